# Optimizing a Trainium2 kernel written in Bass

```python
import jax, jax.numpy as jnp
from jax import lax
import numpy as np

D_MODEL = 2048
BATCH = 4
SEQ = 2048
DEPTH = 2
DEC_BATCH = 128
DEC_SEQ = 4
PAST_LEN = 16384
PAGE_SIZE = 128

N_HEADS = 4
HEAD_K = 128
HEAD_V = 256
MIX_K = N_HEADS * HEAD_K
MIX_V = N_HEADS * HEAD_V
N_BRANCH = 3
GLA_RANK = 16
GLA_TEMP = 16.0
D_FF = 5632
CONV_W = 3
CHUNK = 64
ROPE_BASE = 10000.0
EPS = 1e-6
IN_SIZES = (MIX_K, MIX_K, MIX_V, MIX_V,
            MIX_K, MIX_K, MIX_V, MIX_V, GLA_RANK,
            MIX_K, MIX_K, MIX_V, MIX_V,
            N_BRANCH * D_MODEL)
N_IN = sum(IN_SIZES)

kernel_name = 'hybrid_ret_gla_hgrn2_convffn_step'


def rmsnorm(x, w):
    xf = x.astype(jnp.float32)
    y = xf * lax.rsqrt(jnp.mean(xf * xf, axis=-1, keepdims=True) + EPS)
    return (y * w.astype(jnp.float32)).astype(x.dtype)


def head_rmsnorm(o, w):
    return o * lax.rsqrt(jnp.mean(o * o, axis=-1, keepdims=True) + EPS) * w.astype(jnp.float32)


def rope(x, pos):
    half = x.shape[-1] // 2
    inv = ROPE_BASE ** (-jnp.arange(half, dtype=jnp.float32) / half)
    ang = pos[:, None] * inv[None, :]
    cos = jnp.cos(ang)[None, :, None, :]
    sin = jnp.sin(ang)[None, :, None, :]
    x1, x2 = x[..., :half], x[..., half:]
    return jnp.concatenate([x1 * cos - x2 * sin, x1 * sin + x2 * cos], axis=-1)


def chunked_recurrence(q, k, v, log_f, s0):
    B, T, H, K = q.shape
    V = v.shape[-1]
    c = min(CHUNK, T)
    n = -(-T // c)
    pad = n * c - T
    if pad:
        pw = ((0, 0), (0, pad), (0, 0), (0, 0))
        q, k, v, log_f = (jnp.pad(a, pw) for a in (q, k, v, log_f))

    def chunks(a):
        return a.reshape(B, n, c, H, a.shape[-1]).transpose(1, 0, 2, 3, 4)

    causal = jnp.tril(jnp.ones((c, c), dtype=bool))[None, :, :, None, None]

    def step(S, inp):
        qc, kc, vc, gc = inp
        b = jnp.cumsum(gc, axis=1)
        b_last = b[:, -1]
        o_inter = jnp.einsum('bthk,bhkv->bthv', qc * jnp.exp(b), S)
        diff = jnp.where(causal, b[:, :, None] - b[:, None, :], 0.0)
        decay = jnp.where(causal, jnp.exp(diff), 0.0)
        scores = jnp.einsum('bthk,bshk,btshk->btsh', qc, kc, decay)
        o_intra = jnp.einsum('btsh,bshv->bthv', scores, vc)
        S_new = jnp.exp(b_last)[..., None] * S + jnp.einsum(
            'bshk,bshv->bhkv', kc * jnp.exp(b_last[:, None] - b), vc)
        return S_new, o_inter + o_intra

    S_fin, o = lax.scan(step, s0, tuple(chunks(a) for a in (q, k, v, log_f)))
    o = o.transpose(1, 0, 2, 3, 4).reshape(B, n * c, H, V)[:, :T]
    return o, S_fin


def mixer_branches(h, pos, s_ret, s_gla, s_hgrn, w_in, gla_w_lr, gla_b_lr, lb, head_norm, w_branch, w_out):
    B, T, _ = h.shape
    f32 = jnp.float32
    parts = jnp.split(h @ w_in, np.cumsum(IN_SIZES)[:-1].tolist(), axis=-1)
    (rq, rk, rv, rg, gq, gk, gv, gg, glr, hq, hf, hi, hg, mg) = parts

    def heads(a):
        return a.astype(f32).reshape(B, T, N_HEADS, -1)

    q_r = rope(heads(rq), pos)
    k_r = rope(heads(rk), pos) * (HEAD_K ** -0.5)
    log_gamma = jnp.log1p(-jnp.exp2(-5.0 - jnp.arange(N_HEADS, dtype=f32)))
    g_r = jnp.broadcast_to(log_gamma[None, None, :, None], q_r.shape)
    o_r, s_ret_new = chunked_recurrence(q_r, k_r, heads(rv), g_r, s_ret.astype(f32))

    log_a = jax.nn.log_sigmoid((glr @ gla_w_lr + gla_b_lr).astype(f32)) / GLA_TEMP
    o_g, s_gla_new = chunked_recurrence(heads(gq) * (HEAD_K ** -0.5), heads(gk), heads(gv),
                                        heads(log_a), s_gla.astype(f32))

    z = heads(hf)
    lbh = lb.astype(f32).reshape(N_HEADS, HEAD_K)
    f_h = lbh + (1.0 - lbh) * jax.nn.sigmoid(z)
    log_f = jnp.log(f_h)
    k_h = (1.0 - lbh) * jax.nn.sigmoid(-z)
    o_h, s_hgrn_new = chunked_recurrence(jax.nn.silu(heads(hq)), k_h, heads(hi), log_f, s_hgrn.astype(f32))

    gates = jax.nn.sigmoid(mg.astype(f32)).reshape(B, T, N_BRANCH, D_MODEL)
    branch_out = []
    for n_b, (o, g) in enumerate(((o_r, rg), (o_g, gg), (o_h, hg))):
        o = head_rmsnorm(o, head_norm[n_b]).reshape(B, T, MIX_V) * jax.nn.silu(g.astype(f32))
        branch_out.append(gates[:, :, n_b] * (o.astype(h.dtype) @ w_branch[n_b]).astype(f32))
    merged = (branch_out[0] + branch_out[1] + branch_out[2]).astype(h.dtype)
    return merged @ w_out, s_ret_new, s_gla_new, s_hgrn_new


def conv_ffn(h, buf, w_up, conv_w, conv_b, w_down):
    T = h.shape[1]
    u = h @ w_up
    full = jnp.concatenate([buf.astype(u.dtype), u], axis=1)
    conv = conv_b + conv_w[0] * full[:, 0:T]
    for j in range(1, CONV_W):
        conv = conv + conv_w[j] * full[:, j:j + T]
    a, b = jnp.split(conv, 2, axis=-1)
    return (jax.nn.silu(a) * b) @ w_down, full[:, -(CONV_W - 1):]


def trunk(x, c, pos0, s_ret, s_gla, s_hgrn, s_conv, w_in, gla_w_lr, gla_b_lr, hgrn_lb_logits,
          head_norm, w_branch, w_out, norm_mix, norm_ffn, w_ada, b_ada,
          ffn_w_up, ffn_conv_w, ffn_conv_b, ffn_w_down, final_norm):
    T = x.shape[1]
    pos = pos0 + jnp.arange(T, dtype=jnp.float32)
    sm = jax.nn.softmax(hgrn_lb_logits.astype(jnp.float32), axis=0)
    lbs = jnp.cumsum(sm, axis=0) - sm[0]
    new_ret, new_gla, new_hgrn, new_conv = [], [], [], []
    for l in range(DEPTH):
        mod = (jax.nn.silu(c) @ w_ada[l] + b_ada[l])[:, None, :]
        sh1, sc1, g1, sh2, sc2, g2 = jnp.split(mod, 6, axis=-1)
        h = rmsnorm(x, norm_mix[l]) * (1 + sc1) + sh1
        m, sr, sg, sh = mixer_branches(h, pos, s_ret[l], s_gla[l], s_hgrn[l], w_in[l], gla_w_lr[l],
                                       gla_b_lr[l], lbs[l], head_norm[l], w_branch[l], w_out[l])
        x = x + g1 * m
        h = rmsnorm(x, norm_ffn[l]) * (1 + sc2) + sh2
        f, sc = conv_ffn(h, s_conv[l], ffn_w_up[l], ffn_conv_w[l], ffn_conv_b[l], ffn_w_down[l])
        x = x + g2 * f
        new_ret.append(sr.astype(s_ret.dtype))
        new_gla.append(sg.astype(s_gla.dtype))
        new_hgrn.append(sh.astype(s_hgrn.dtype))
        new_conv.append(sc.astype(s_conv.dtype))
    y = rmsnorm(x, final_norm)
    return y, jnp.stack(new_ret), jnp.stack(new_gla), jnp.stack(new_hgrn), jnp.stack(new_conv)


def setup_inputs(seed: int = 0) -> dict:
    key = jax.random.key(seed)
    ks = jax.random.split(key, 28)
    f32 = jnp.float32

    def nrm(k, shape, s):
        return s * jax.random.normal(k, shape, f32)

    st_shape = (DEPTH, DEC_BATCH, N_HEADS, HEAD_K, HEAD_V)
    return {
        'x_prompt': nrm(ks[0], (BATCH, SEQ, D_MODEL), 1.0),
        'x_sample': nrm(ks[1], (DEC_BATCH, DEC_SEQ, D_MODEL), 1.0),
        'c_prompt': nrm(ks[2], (BATCH, D_MODEL), 1.0),
        'c_sample': nrm(ks[3], (DEC_BATCH, D_MODEL), 1.0),
        'state_ret': nrm(ks[4], st_shape, 0.5),
        'state_gla': nrm(ks[5], st_shape, 0.5),
        'state_hgrn': nrm(ks[6], st_shape, 0.5),
        'state_conv': nrm(ks[7], (DEPTH, DEC_BATCH, CONV_W - 1, 2 * D_FF), 1.0),
        'w_in': nrm(ks[8], (DEPTH, D_MODEL, N_IN), D_MODEL ** -0.5),
        'gla_w_lr': nrm(ks[9], (DEPTH, GLA_RANK, MIX_K), GLA_RANK ** -0.5),
        'gla_b_lr': nrm(ks[10], (DEPTH, MIX_K), 0.01),
        'hgrn_lb_logits': nrm(ks[11], (DEPTH, MIX_K), 1.0),
        'head_norm': 1.0 + nrm(ks[12], (DEPTH, N_BRANCH, HEAD_V), 0.01),
        'w_branch': nrm(ks[13], (DEPTH, N_BRANCH, MIX_V, D_MODEL), MIX_V ** -0.5),
        'w_out': nrm(ks[14], (DEPTH, D_MODEL, D_MODEL), D_MODEL ** -0.5),
        'norm_mix': 1.0 + nrm(ks[15], (DEPTH, D_MODEL), 0.01),
        'norm_ffn': 1.0 + nrm(ks[16], (DEPTH, D_MODEL), 0.01),
        'w_ada': nrm(ks[17], (DEPTH, D_MODEL, 6 * D_MODEL), 0.5 * D_MODEL ** -0.5),
        'b_ada': nrm(ks[18], (DEPTH, 6 * D_MODEL), 0.01),
        'ffn_w_up': nrm(ks[19], (DEPTH, D_MODEL, 2 * D_FF), D_MODEL ** -0.5),
        'ffn_conv_w': nrm(ks[20], (DEPTH, CONV_W, 2 * D_FF), CONV_W ** -0.5),
        'ffn_conv_b': nrm(ks[21], (DEPTH, 2 * D_FF), 0.01),
        'ffn_w_down': nrm(ks[22], (DEPTH, D_FF, D_MODEL), D_FF ** -0.5),
        'final_norm': 1.0 + nrm(ks[23], (D_MODEL,), 0.01),
    }


def reference(x_prompt, x_sample, c_prompt, c_sample, state_ret, state_gla, state_hgrn, state_conv,
              w_in, gla_w_lr, gla_b_lr, hgrn_lb_logits, head_norm, w_branch, w_out, norm_mix, norm_ffn,
              w_ada, b_ada, ffn_w_up, ffn_conv_w, ffn_conv_b, ffn_w_down, final_norm):
    weights = (w_in, gla_w_lr, gla_b_lr, hgrn_lb_logits, head_norm, w_branch, w_out, norm_mix, norm_ffn,
               w_ada, b_ada, ffn_w_up, ffn_conv_w, ffn_conv_b, ffn_w_down, final_norm)
    b = x_prompt.shape[0]
    z_st = jnp.zeros((DEPTH, b, N_HEADS, HEAD_K, HEAD_V), x_prompt.dtype)
    z_cv = jnp.zeros((DEPTH, b, CONV_W - 1, 2 * D_FF), x_prompt.dtype)
    y_prompt, p_ret, p_gla, p_hgrn, p_conv = trunk(x_prompt, c_prompt, 0, z_st, z_st, z_st, z_cv, *weights)
    y_sample, s_ret, s_gla, s_hgrn, s_conv = trunk(x_sample, c_sample, PAST_LEN, state_ret, state_gla,
                                                   state_hgrn, state_conv, *weights)
    return (y_prompt, y_sample, p_ret, p_gla, p_hgrn, p_conv, s_ret, s_gla, s_hgrn, s_conv)
```

```python
import numpy as np
from contextlib import ExitStack
import ml_dtypes
import concourse.bass as bass
import concourse.mybir as mybir
from concourse.bass_utils import run_bass_kernel_spmd

F32 = mybir.dt.float32
BF16 = mybir.dt.bfloat16
ALU = mybir.AluOpType
AF = mybir.ActivationFunctionType
AX = mybir.AxisListType
NPBF = ml_dtypes.bfloat16

D = 2048
NC16 = 16
NH = 4
HK = 128
HV = 256
MV = 1024
DFF = 5632
NFT = 44
NIN = 15376
DEPTH = 2
SEQ = 2048
TP = 512
NPASS = SEQ // TP
NSEQ = 16
TS = 4
TSAMP = NSEQ * TS
PAST = 16384
EPS = 1e-6
N_CORES = 8
OFF = [(0, 512, 1024, 2048), (3072, 3584, 4096, 5120), (6160, 6672, 7184, 8208)]
OFF_GLR = 6144
OFF_MG = 9232

ENGS = ("pe", "dve", "act", "pool", "sp")
SEM_EPOCH = 30000


class Tok:
    __slots__ = ("name", "last_write", "reads")

    def __init__(self, name=""):
        self.name = name
        self.last_write = None
        self.reads = []


class Op:
    __slots__ = ("eng", "fn", "deps", "is_dma", "sig", "needs_sig", "prewait", "wload")

    def __init__(self, eng, fn, is_dma):
        self.eng = eng
        self.fn = fn
        self.deps = set()
        self.is_dma = is_dma
        self.sig = None
        self.needs_sig = False
        self.prewait = None
        self.wload = False


class Prog:
    def __init__(self, nc, n_dma_sems=8):
        self.nc = nc
        self.streams = {e: [] for e in ENGS}
        self.n_dma_sems = n_dma_sems
        self.final_waits = []
        self.fence_deps = set()
        self.since_fence = []

    def add(self, eng, fn, reads=(), writes=(), dma=False, wload=False):
        op = Op(eng, fn, dma)
        op.wload = wload
        for t in reads:
            if t.last_write is not None:
                op.deps.add(t.last_write)
        for t in writes:
            if t.last_write is not None:
                op.deps.add(t.last_write)
            for r in t.reads:
                op.deps.add(r)
        if not wload:
            op.deps |= self.fence_deps
        op.deps.discard(op)
        for t in reads:
            t.reads.append(op)
        for t in writes:
            t.last_write = op
            t.reads = []
        self.streams[eng].append(op)
        if dma and not wload:
            self.since_fence.append(op)
        return op

    def fence(self):
        deps = set()
        for e in ENGS:
            if e in ("sp", "pool"):
                continue
            if self.streams[e]:
                deps.add(self.streams[e][-1])
        for op in self.since_fence:
            deps.add(op)
        self.since_fence = []
        self.fence_deps = deps

    def emit(self, stack):
        nc = self.nc
        for e in ENGS:
            for op in self.streams[e]:
                if op.eng == "pe":
                    op.deps = {d for d in op.deps if d.eng != "pe" or d.is_dma}
                for d in op.deps:
                    d.needs_sig = True
        self.sems = []

        def newsem(name):
            s = stack.enter_context(nc.semaphore(name))
            self.sems.append(s)
            return s

        for e in ENGS:
            cnt = 0
            ep = 0
            sem = None
            dsems = None
            dcnt = None
            di = 0
            for op in self.streams[e]:
                if op.is_dma:
                    if dsems is None:
                        dsems = [newsem(f"d_{e}_{i}") for i in range(self.n_dma_sems)]
                        dcnt = [0] * self.n_dma_sems
                    i = di % self.n_dma_sems
                    di += 1
                    if dcnt[i] + 16 > SEM_EPOCH:
                        dsems[i] = newsem(f"d_{e}_{i}_{di}")
                        dcnt[i] = 0
                    if dcnt[i] > 0:
                        op.prewait = (dsems[i], dcnt[i])
                    dcnt[i] += 16
                    op.sig = (dsems[i], dcnt[i])
                elif op.needs_sig:
                    if sem is None or cnt >= SEM_EPOCH:
                        sem = newsem(f"c_{e}_{ep}")
                        ep += 1
                        cnt = 0
                    cnt += 1
                    op.sig = (sem, cnt)
        block = stack.enter_context(nc.Block())
        self.n_wait = 0

        def run_stream(e):
            def body(eng):
                waited = {}
                for op in self.streams[e]:
                    need = {}
                    for d in op.deps:
                        s, v = d.sig
                        k = id(s)
                        if waited.get(k, 0) >= v:
                            continue
                        if k not in need or need[k][1] < v:
                            need[k] = (s, v)
                    if op.prewait is not None:
                        s, v = op.prewait
                        k = id(s)
                        if waited.get(k, 0) < v and (k not in need or need[k][1] < v):
                            need[k] = (s, v)
                    for k, (s, v) in need.items():
                        eng.wait_ge(s, v)
                        waited[k] = v
                        self.n_wait += 1
                    ins = op.fn(eng)
                    if op.is_dma:
                        ins.then_inc(op.sig[0], 16)
                    elif op.sig is not None:
                        ins.then_inc(op.sig[0], 1)
                if e == "sp":
                    for op in self.final_waits:
                        s, v = op.sig
                        if waited.get(id(s), 0) < v:
                            eng.wait_ge(s, v)
                            waited[id(s)] = v
            return body

        block.tensor(run_stream("pe"))
        block.vector(run_stream("dve"))
        block.scalar(run_stream("act"))
        block.gpsimd(run_stream("pool"))
        block.sync(run_stream("sp"))


def host_consts():
    c = {}
    c["ident_f"] = np.eye(128, dtype=np.float32)
    c["ident_b"] = np.eye(128, dtype=np.float32).astype(NPBF)
    sw = np.zeros((128, 128), np.float32)
    for m in range(128):
        sw[(m + 64) % 128, m] = 1.0
    c["pswap"] = sw.astype(NPBF)
    c["ones_f"] = np.ones((128, 128), np.float32)
    s = np.arange(128)[:, None]
    t = np.arange(128)[None, :]
    c["mask1"] = (s <= t).astype(np.float32).astype(NPBF)
    c["mask4"] = ((s <= t) & (s // 32 == t // 32)).astype(np.float32).astype(NPBF)
    ms = np.zeros((128, 128), np.float32)
    s6 = np.arange(64)[:, None]
    t6 = np.arange(64)[None, :]
    ms[:64, :64] = ((s6 <= t6) & (s6 // 4 == t6 // 4))
    c["masks"] = ms.astype(NPBF)
    cm4 = np.zeros((128, 4, 128), np.float32)
    for j in range(4):
        cm4[:, j, 32 * j:32 * j + 32] = 1.0
    c["cm4"] = cm4.astype(NPBF)
    rm4 = np.zeros((128, 4), np.float32)
    for p in range(128):
        rm4[p, p // 32] = 1.0
    c["rm4"] = rm4
    cms = np.zeros((128, 16, 64), np.float32)
    for j in range(16):
        cms[:, j, 4 * j:4 * j + 4] = 1.0
    c["cms"] = cms.astype(NPBF)
    rms = np.zeros((128, 16), np.float32)
    for p in range(64):
        rms[p, p // 4] = 1.0
    c["rms"] = rms
    tt = np.arange(512)
    c["rst_g"] = np.broadcast_to((tt % 128 != 0).astype(np.float32), (128, 512)).astype(NPBF)
    c["rst_h"] = np.broadcast_to((tt % 32 != 0).astype(np.float32), (128, 512)).astype(NPBF)
    c["rst_s"] = np.broadcast_to((np.arange(64) % 4 != 0).astype(np.float32), (128, 64)).astype(NPBF)
    lg = np.log1p(-np.exp2(-5.0 - np.arange(4, dtype=np.float32))).astype(np.float32)
    tq = np.arange(128, dtype=np.float32) + 1.0
    gq = np.exp(lg[:, None] * tq[None, :]).astype(np.float32)
    gk = (np.exp(-lg[:, None] * tq[None, :]) * (HK ** -0.5)).astype(np.float32)
    c["gq"] = np.broadcast_to(gq, (128, 4, 128)).copy()
    c["gk"] = np.broadcast_to(gk, (128, 4, 128)).copy()
    ts_ = (np.arange(64) % 4).astype(np.float32) + 1.0
    gqs = np.exp(lg[:, None] * ts_[None, :]).astype(np.float32)
    gks = (np.exp(-lg[:, None] * ts_[None, :]) * (HK ** -0.5)).astype(np.float32)
    c["gqs"] = np.broadcast_to(gqs, (128, 4, 64)).copy()
    c["gks"] = np.broadcast_to(gks, (128, 4, 64)).copy()
    c["elr"] = np.broadcast_to(np.exp(lg * 128.0).astype(np.float32), (128, 4)).copy()
    c["elrs"] = np.broadcast_to(np.exp(lg * 4.0).astype(np.float32), (128, 4)).copy()
    half = 64
    inv = (10000.0 ** (-np.arange(half, dtype=np.float32) / half)).astype(np.float32)
    invm = np.concatenate([inv, inv])
    sgn = np.concatenate([-np.ones(64, np.float32), np.ones(64, np.float32)])
    pos = np.arange(SEQ, dtype=np.float32)
    ang = (pos[None, :] * invm[:, None]).astype(np.float32)
    c["cos_p"] = np.cos(ang).astype(np.float32)
    c["sin_p"] = (np.sin(ang) * sgn[:, None]).astype(np.float32)
    poss = (PAST + (np.arange(64) % 4)).astype(np.float32)
    angs = (poss[None, :] * invm[:, None]).astype(np.float32)
    c["cos_s"] = np.cos(angs).astype(np.float32)
    c["sin_s"] = (np.sin(angs) * sgn[:, None]).astype(np.float32)
    return c


CONST_SPECS = [
    ("ident_f", [128, 128], F32), ("ident_b", [128, 128], BF16), ("pswap", [128, 128], BF16),
    ("ones_f", [128, 128], F32), ("mask1", [128, 128], BF16), ("mask4", [128, 128], BF16),
    ("masks", [128, 128], BF16), ("cm4", [128, 4, 128], BF16), ("rm4", [128, 4], F32),
    ("cms", [128, 16, 64], BF16), ("rms", [128, 16], F32), ("rst_g", [128, 512], BF16),
    ("rst_h", [128, 512], BF16), ("rst_s", [128, 64], BF16), ("gq", [128, 4, 128], F32),
    ("gk", [128, 4, 128], F32), ("gqs", [128, 4, 64], F32), ("gks", [128, 4, 64], F32),
    ("elr", [128, 4], F32), ("elrs", [128, 4], F32), ("cos_p", [128, 2048], F32),
    ("sin_p", [128, 2048], F32), ("cos_s", [128, 64], F32), ("sin_s", [128, 64], F32),
]


class Arena:
    def __init__(self, ap, words):
        self.ap = ap
        self.words = words
        self.off = 0

    def alloc(self, shape, dt):
        n = int(np.prod(shape[1:]))
        w = (n + 1) // 2 if dt == BF16 else n
        w = (w + 7) // 8 * 8
        assert self.off + w <= self.words, f"arena overflow {self.off}+{w}>{self.words}"
        v = self.ap[:, self.off:self.off + w]
        self.off += w
        if dt == BF16:
            v = v.bitcast(BF16)
        v = v[:, 0:n]
        if len(shape) == 3:
            v = v.rearrange("p (a b) -> p a b", a=shape[1])
        elif len(shape) == 4:
            v = v.rearrange("p (a b c) -> p a b c", a=shape[1], b=shape[2])
        if shape[0] < 128:
            v = v[0:shape[0]]
        return v


ARENA_WORDS = 49 * 1024


def build_program(debug=False):
    nc = bass.Bass("TRN2", target_bir_lowering=False)

    def din(name, shape, dt=F32):
        return nc.dram_tensor(name, list(shape), dt, kind="ExternalInput").ap()

    def dout(name, shape):
        return nc.dram_tensor(name, list(shape), F32, kind="ExternalOutput").ap()

    xp = din("xp", [SEQ, D])
    xs = din("xs", [TSAMP, D])
    cvec = din("cvec", [17, D])
    st_in = [din(f"st_in{b}", [DEPTH, NSEQ, NH, HK, HV]) for b in range(3)]
    sconv_in = din("sconv_in", [DEPTH, NSEQ * 2, 2 * DFF])
    w_in = din("w_in", [DEPTH, D, NIN])
    gla_w_lr = din("gla_w_lr", [DEPTH, 16, 512])
    pvecs = din("pvecs", [96, 128])
    head_norm = din("head_norm", [DEPTH, 3 * HV])
    w_branch = din("w_branch", [DEPTH, 3, MV, D])
    w_out = din("w_out", [DEPTH, D, D])
    w_ada = din("w_ada", [DEPTH, D, 6 * D])
    b_ada = din("b_ada", [DEPTH, 96, 128])
    w_up = din("ffn_w_up", [DEPTH, D, 2 * DFF])
    convp = din("convp", [DEPTH, 4, 88, 128])
    w_down = din("ffn_w_down", [DEPTH, DFF, D])
    cst = {n: din("k_" + n, s, dt) for n, s, dt in CONST_SPECS}

    yp = dout("yp", [SEQ, D])
    ys = dout("ys", [TSAMP, D])
    pst = [dout(f"pst{b}", [DEPTH, NH, HK, HV]) for b in range(3)]
    pconv = dout("pconv", [DEPTH, 2, 2 * DFF])
    sst = [dout(f"sst{b}", [DEPTH, NSEQ, NH, HK, HV]) for b in range(3)]
    sconv_o = dout("sconv_o", [DEPTH, NSEQ * 2, 2 * DFF])

    st = ExitStack()
    P = Prog(nc)
    arena_t = st.enter_context(nc.sbuf_tensor("arena", [128, ARENA_WORDS], F32))
    AR = Arena(arena_t[:], ARENA_WORDS)
    banks = [st.enter_context(nc.psum_tensor(f"pb{i}", [128, 512], F32)) for i in range(8)]
    bank_tok = [Tok(f"pb{i}") for i in range(8)]
    bank_bf = banks[7][:].bitcast(BF16)

    class Rot:
        def __init__(self, idx):
            self.idx = idx
            self.i = 0

        def get(self):
            k = self.idx[self.i % len(self.idx)]
            self.i += 1
            return banks[k][:], bank_tok[k]

    main_rot = Rot([0, 1, 2, 3])
    small_rot = Rot([4, 5, 6])

    V = lambda fn, r=(), w=(): P.add("dve", fn, r, w)
    A = lambda fn, r=(), w=(): P.add("act", fn, r, w)
    T = lambda fn, r=(), w=(): P.add("pe", fn, r, w)
    LD = lambda fn, r=(), w=(): P.add("sp", fn, r, w, dma=True)
    ST_ = lambda fn, r=(), w=(): P.add("act", fn, r, w, dma=True)

    def mm(out_ap, out_tok, pairs, reads):
        n = len(pairs)
        for i, (l, r) in enumerate(pairs):
            T(lambda e, l=l, r=r, i=i: e.matmul(out_ap, lhsT=l, rhs=r, start=(i == 0), stop=(i == n - 1)),
              reads, [out_tok])

    def act(out, in_, func, r, w, **kw):
        return A(lambda e: e.activation(out=out, in_=in_, func=func, **kw), r, w)

    def tt(out, a, b, op, r, w):
        return V(lambda e: e.tensor_tensor(out=out, in0=a, in1=b, op=op), r, w)

    def ts(out, a, s1, s2, op0, op1, r, w):
        return V(lambda e: e.tensor_scalar(out=out, in0=a, scalar1=s1, scalar2=s2, op0=op0, op1=op1), r, w)

    def stt(out, a, s, b, op0, op1, r, w):
        return V(lambda e: e.scalar_tensor_tensor(out=out, in0=a, scalar=s, in1=b, op0=op0, op1=op1), r, w)

    K = {}
    Kt = {}
    for n, s, dt in CONST_SPECS:
        if n in ("cos_p", "sin_p", "cos_s", "sin_s", "cms", "rms", "rst_s", "gqs", "gks", "elrs", "masks"):
            continue
        K[n] = AR.alloc(s, dt)
        Kt[n] = Tok(n)
        LD(lambda e, n=n: e.dma_start(out=K[n], in_=cst[n]), [], [Kt[n]])
    PV = AR.alloc([128, 96], F32)
    t_PV = Tok("PV")
    CW = [AR.alloc([128, 4, 88], F32) for _ in range(DEPTH)]
    t_CW = [Tok("CW") for _ in range(DEPTH)]
    HNW = [AR.alloc([128, 3 * HV], BF16) for _ in range(DEPTH)]
    t_HNW = [Tok("HNW") for _ in range(DEPTH)]
    GLW = [AR.alloc([16, 512], BF16) for _ in range(DEPTH)]
    t_GLW = [Tok("GLW") for _ in range(DEPTH)]
    LBV = AR.alloc([128, DEPTH, 3, 4], F32)
    t_LBV = Tok("LBV")
    NEGB = AR.alloc([128, DEPTH, 4], F32)
    MODP = [AR.alloc([128, 6, 16], F32) for _ in range(DEPTH)]
    t_MODP = [Tok("MODP") for _ in range(DEPTH)]
    CARRY = [AR.alloc([128, 88, 2], F32) for _ in range(DEPTH)]
    t_CARRY = [[Tok("carry") for _ in range(88)] for _ in range(DEPTH)]
    NWB = 3
    WB = [AR.alloc([128, 4096], BF16) for _ in range(NWB)]
    t_WB = [Tok(f"wb{i}") for i in range(NWB)]
    wb_i = [0]
    persist_mark = AR.off

    scr = {}
    scr_tok = {}
    cast_list = []

    def defblk(key, srcs, kc, ncols):
        t = nc.dram_tensor("s_" + "_".join(str(k) for k in key), [128, kc, ncols], BF16).ap()
        scr[key] = (t, kc, ncols)
        cast_list.append((key, srcs))

    for l in range(DEPTH):
        for b in range(3):
            oq, ok, ov, og = OFF[b]
            for hp in range(2):
                defblk(("k", l, b, hp), [(w_in[l, :, ok + 256 * hp: ok + 256 * hp + 256], 0)], 16, 256)
                defblk(("q", l, b, hp), [(w_in[l, :, oq + 256 * hp: oq + 256 * hp + 256], 0)], 16, 256)
            if b == 1:
                defblk(("glr", l), [(w_in[l, :, OFF_GLR:OFF_GLR + 16], 0)], 16, 16)
            for h in range(4):
                defblk(("v", l, b, h), [(w_in[l, :, ov + 256 * h: ov + 256 * h + 256], 0)], 16, 256)
            for h in range(4):
                defblk(("g", l, b, h), [(w_in[l, :, og + 256 * h: og + 256 * h + 256], 0)], 16, 256)
        for fp in range(8):
            for b in range(3):
                defblk(("br", l, b, fp), [(w_branch[l, b, :, 256 * fp:256 * fp + 256], 0)], 8, 256)
                c0 = OFF_MG + b * D + 256 * fp
                defblk(("mg", l, b, fp), [(w_in[l, :, c0:c0 + 256], 0)], 16, 256)
        for fp in range(8):
            defblk(("out", l, fp), [(w_out[l, :, 256 * fp:256 * fp + 256], 0)], 16, 256)
        for j in range(NFT):
            defblk(("up", l, j), [(w_up[l, :, 128 * j:128 * j + 128], 0),
                                  (w_up[l, :, DFF + 128 * j:DFF + 128 * j + 128], 128)], 16, 256)
        for f in range(16):
            for kh in range(2):
                defblk(("dn", l, f, kh), [(w_down[l, kh * 2816:(kh + 1) * 2816, 128 * f:128 * f + 128], 0)], 22, 128)

    for key, srcs in cast_list:
        dst, kc, ncols = scr[key]
        scr_tok[key] = [Tok(str(key) + str(si)) for si in range(len(srcs))]
        for si, (src, co) in enumerate(srcs):
            n = src.shape[1]
            P.add("pool", lambda e, dst=dst, src=src, co=co, n=n: e.dma_start(
                out=dst[:, :, co:co + n], in_=src.rearrange("(c p) n -> p c n", p=128)),
                [], [scr_tok[key][si]], dma=True, wload=True)

    def load_w(key):
        dst_scr, kc, ncols = scr[key]
        i = wb_i[0] % NWB
        wb_i[0] += 1
        v = WB[i][:, 0:kc * ncols].rearrange("p (c n) -> p c n", c=kc)
        P.add("sp", lambda e: e.dma_start(out=v, in_=dst_scr), scr_tok[key], [t_WB[i]], dma=True, wload=True)
        return v, t_WB[i]

    def setup():
        m0 = AR.off
        stg = AR.alloc([128, 128], F32)
        t_stg = Tok("stg")
        LD(lambda e: e.dma_start(out=stg[0:96, :], in_=pvecs), [], [t_stg])
        pb, pt = small_rot.get()
        T(lambda e, pb=pb: e.transpose(pb[:, 0:96], stg[0:96, :], K["ident_f"][0:96, 0:96]), [t_stg, Kt["ident_f"]], [pt])
        act(PV, pb[:, 0:96], AF.Copy, [pt], [t_PV])
        for l in range(DEPTH):
            for j in range(4):
                LD(lambda e, l=l, j=j: e.dma_start(out=stg[0:88, :], in_=convp[l, j]), [], [t_stg])
                pb, pt = small_rot.get()
                T(lambda e, pb=pb: e.transpose(pb[:, 0:88], stg[0:88, :], K["ident_f"][0:88, 0:88]), [t_stg, Kt["ident_f"]], [pt])
                act(CW[l][:, j, :], pb[:, 0:88], AF.Copy, [pt], [t_CW[l]])
            hn32 = AR.alloc([128, 3 * HV], F32)
            t_hn = Tok("hn32")
            LD(lambda e, l=l, hn32=hn32: e.dma_start(out=hn32, in_=head_norm[l].partition_broadcast(128)), [], [t_hn])
            V(lambda e, l=l, hn32=hn32: e.tensor_copy(out=HNW[l], in_=hn32), [t_hn], [t_HNW[l]])
            gl32 = AR.alloc([16, 512], F32)
            t_gl = Tok("gl32")
            LD(lambda e, l=l, gl32=gl32: e.dma_start(out=gl32, in_=gla_w_lr[l]), [], [t_gl])
            V(lambda e, l=l, gl32=gl32: e.tensor_copy(out=GLW[l], in_=gl32), [t_gl], [t_GLW[l]])
        tmp = AR.alloc([128, 8, 4], F32)
        t_tmp = Tok("lbtmp")
        l0 = PV[:, 80:84]
        l1 = PV[:, 84:88]
        mx, e0, e1, sm, r, s0, s1, cs = [tmp[:, i, :] for i in range(8)]
        tt(mx, l0, l1, ALU.max, [t_PV], [t_tmp])
        tt(e0, l0, mx, ALU.subtract, [t_PV, t_tmp], [t_tmp])
        tt(e1, l1, mx, ALU.subtract, [t_PV, t_tmp], [t_tmp])
        act(e0, e0, AF.Exp, [t_tmp], [t_tmp])
        act(e1, e1, AF.Exp, [t_tmp], [t_tmp])
        tt(sm, e0, e1, ALU.add, [t_tmp], [t_tmp])
        V(lambda e: e.reciprocal(out=r, in_=sm), [t_tmp], [t_tmp])
        tt(s0, e0, r, ALU.mult, [t_tmp], [t_tmp])
        tt(s1, e1, r, ALU.mult, [t_tmp], [t_tmp])
        tt(cs, s0, s1, ALU.add, [t_tmp], [t_tmp])
        tt(LBV[:, 0, 0, :], s0, s0, ALU.subtract, [t_tmp], [t_LBV])
        tt(LBV[:, 1, 0, :], cs, s0, ALU.subtract, [t_tmp], [t_LBV])
        for l in range(DEPTH):
            ts(LBV[:, l, 1, :], LBV[:, l, 0, :], -1.0, 1.0, ALU.mult, ALU.add, [t_LBV], [t_LBV])
            ts(LBV[:, l, 2, :], LBV[:, l, 0, :], 1.0, -1.0, ALU.mult, ALU.add, [t_LBV], [t_LBV])
            ts(NEGB[:, l, :], PV[:, 88 + 4 * l:92 + 4 * l], -1.0, None, ALU.mult, ALU.bypass, [t_PV], [t_LBV])
        for l in range(DEPTH):
            V(lambda e, l=l: e.memset(CARRY[l], 0.0), [], t_CARRY[l])
        return m0

    def compute_mod(MOD, t_MOD):
        m0 = AR.off
        c_sb = AR.alloc([17, D], F32)
        s_sb = AR.alloc([17, D], F32)
        scT = AR.alloc([128, 16, 17], F32)
        badaT = AR.alloc([128, 96], F32)
        stg = AR.alloc([128, 128], F32)
        t_c, t_s, t_scT, t_ba, t_stg = Tok(), Tok(), Tok(), Tok(), Tok()
        LD(lambda e: e.dma_start(out=c_sb, in_=cvec), [], [t_c])
        act(s_sb, c_sb, AF.Sigmoid, [t_c], [t_s])
        tt(s_sb, s_sb, c_sb, ALU.mult, [t_c, t_s], [t_s])
        for c in range(16):
            pb, pt = small_rot.get()
            T(lambda e, c=c, pb=pb: e.transpose(pb[:, 0:17], s_sb[:, c * 128:(c + 1) * 128], K["ident_f"][0:17, 0:17]),
              [t_s, Kt["ident_f"]], [pt])
            act(scT[:, c, :], pb[:, 0:17], AF.Copy, [pt], [t_scT])
        for l in range(DEPTH):
            LD(lambda e, l=l: e.dma_start(out=stg[0:96, :], in_=b_ada[l]), [], [t_stg])
            pb, pt = small_rot.get()
            T(lambda e, pb=pb: e.transpose(pb[:, 0:96], stg[0:96, :], K["ident_f"][0:96, 0:96]), [t_stg, Kt["ident_f"]], [pt])
            act(badaT, pb[:, 0:96], AF.Copy, [pt], [t_ba])
            for fi in range(96):
                i = wb_i[0] % NWB
                wb_i[0] += 1
                wv = WB[i].bitcast(F32).rearrange("p (c n) -> p c n", c=16)
                P.add("sp", lambda e, l=l, fi=fi, wv=wv: e.dma_start(
                    out=wv, in_=w_ada[l, :, fi * 128:(fi + 1) * 128].rearrange("(c p) n -> p c n", p=128)),
                    [], [t_WB[i]], dma=True, wload=True)
                pb, pt = small_rot.get()
                mm(pb[:, 0:17], pt, [(wv[:, c, :], scT[:, c, :]) for c in range(16)], [t_WB[i], t_scT])
                act(MOD[l][:, fi, :], pb[:, 0:17], AF.Identity, [pt, t_ba], [t_MOD[l]], bias=badaT[:, fi:fi + 1], scale=1.0)
            for (c0, pv0) in ((16, 16 * l), (64, 32 + 16 * l)):
                stt(MOD[l][:, c0:c0 + 16, :], MOD[l][:, c0:c0 + 16, :], 1.0,
                    PV[:, pv0:pv0 + 16].unsqueeze(2).broadcast_to([128, 16, 17]),
                    ALU.add, ALU.mult, [t_MOD[l], t_PV], [t_MOD[l]])
            for k, c0 in enumerate((16, 0, 32, 64, 48, 80)):
                V(lambda e, l=l, k=k, c0=c0: e.tensor_copy(out=MODP[l][:, k, :], in_=MOD[l][:, c0:c0 + 16, 16]),
                  [t_MOD[l]], [t_MODP[l]])
        return m0

    dbg_outs = {}

    def run_pass(kind, p, MOD=None, t_MOD=None):
        prompt = kind == "p"
        Tn = TP if prompt else TSAMP
        tiles = [(i * 128, 128) for i in range(4)] if prompt else [(0, 64)]
        nt = len(tiles)
        m0 = AR.off
        xT = AR.alloc([128, 16, Tn], F32)
        t_xT = [Tok(f"xT{c}") for c in range(16)]
        hT = AR.alloc([128, 16, Tn], BF16)
        t_hT = [Tok(f"hT{c}") for c in range(16)]
        r1_words = max(NFT * Tn // 2, 16 * Tn + 2048) + 64
        r1_off = AR.off
        R1 = Arena(AR.ap[:, r1_off:r1_off + r1_words], r1_words)
        AR.off += r1_words
        uT = R1.alloc([128, 3, 8, Tn], BF16)
        v_tok = R1.alloc([128, nt, MV], BF16)
        sg_tok = R1.alloc([128, nt, MV], BF16)
        R1b = Arena(AR.ap[:, r1_off:r1_off + r1_words], r1_words)
        actT = R1b.alloc([128, NFT, Tn], BF16)
        R1c = Arena(AR.ap[:, r1_off:r1_off + r1_words], r1_words)
        yT = R1c.alloc([128, 16, Tn], F32)
        stg = R1c.alloc([128, D], F32)
        mT = Arena(AR.ap[:, r1_off + 3 * 8 * Tn // 2 + 0: r1_off + r1_words], r1_words).alloc([128, 16, Tn], BF16) \
            if prompt else AR.alloc([128, 16, Tn], BF16)
        t_v = [[Tok() for _ in range(4)] for _ in range(nt)]
        t_sg = [[Tok() for _ in range(4)] for _ in range(nt)]
        t_uT = [[Tok() for _ in range(nt)] for _ in range(3)]
        t_stgi = Tok("stgi")
        t_yT = Tok("yT")
        qT = AR.alloc([128, 4, Tn], BF16)
        kT = AR.alloc([128, 4, Tn], BF16)
        t_qT = [Tok() for _ in range(4)]
        t_kT = [Tok() for _ in range(4)]
        NTMP = 6
        TWS = [520] * 6 if prompt else [128, 128, 1040, 1040, 128, 128]
        tmp = [AR.alloc([128, TWS[i]], F32) for i in range(NTMP)]
        t_tmp = [Tok(f"tmp{i}") for i in range(NTMP)]
        acc = AR.alloc([128, 2, Tn], F32)
        t_acc = Tok("acc")
        rstd = AR.alloc([128, Tn], F32)
        t_rstd = Tok("rstd")
        COS = AR.alloc([128, Tn], F32)
        SIN = AR.alloc([128, Tn], F32)
        t_rope = Tok("rope")
        EL = AR.alloc([128, 4, 16], F32)
        t_EL = [Tok() for _ in range(4)]
        PT_ = [AR.alloc([128, 128], BF16) for _ in range(4)]
        t_PT = [Tok() for _ in range(4)]
        ktok = [AR.alloc([128, 128], BF16) for _ in range(4)]
        t_ktok = [Tok() for _ in range(4)]
        u_tok = AR.alloc([128, MV], BF16)
        t_utok = Tok("utok")
        kve = [AR.alloc([128, HV], F32) for _ in range(2)]
        t_kve = [Tok() for _ in range(2)]
        sq256 = [AR.alloc([128, HV], F32) for _ in range(2)]
        t_sq256 = [Tok() for _ in range(2)]
        ssv = AR.alloc([128, 8], F32)
        t_ssv = [Tok() for _ in range(4)]
        glrT = AR.alloc([16, Tn], BF16)
        t_glrT = Tok("glrT")
        if prompt:
            SF = AR.alloc([128, 4, HV], F32)
            Sbf = AR.alloc([128, 4, HV], BF16)
            t_SF = [Tok() for _ in range(4)]
            t_Sbf = [Tok() for _ in range(4)]
            KM = [AR.alloc([128, 4, 128], BF16) for _ in range(4)]
            QM = [AR.alloc([128, 4, 128], BF16) for _ in range(4)]
            t_KM = [Tok() for _ in range(4)]
            t_QM = [Tok() for _ in range(4)]
            cstage = stg[0:2, :]
            t_cstage = Tok("cstage")
            MASKS = {1: K["mask1"], 4: K["mask4"]}
            t_MASKS = {1: Kt["mask1"], 4: Kt["mask4"]}
            CM = K["cm4"]
            RM = K["rm4"]
            t_CMRM = [Kt["cm4"], Kt["rm4"]]
            RST = {1: K["rst_g"], 2: K["rst_h"]}
            t_RST = {1: Kt["rst_g"], 2: Kt["rst_h"]}
            GQ, GK, ELR = K["gq"], K["gk"], K["elr"]
            t_G = [Kt["gq"], Kt["gk"], Kt["elr"]]
            LD(lambda e: e.dma_start(out=COS, in_=cst["cos_p"][:, p * TP:(p + 1) * TP]), [], [t_rope])
            LD(lambda e: e.dma_start(out=SIN, in_=cst["sin_p"][:, p * TP:(p + 1) * TP]), [], [t_rope])
        else:
            r3_off = AR.off
            r3_words = 2 * NSEQ * HV + 64
            R3 = Arena(AR.ap[:, r3_off:r3_off + r3_words], r3_words)
            AR.off += r3_words
            SS = [R3.alloc([128, NSEQ, HV], F32) for _ in range(2)]
            R3b = Arena(AR.ap[:, r3_off:r3_off + r3_words], r3_words)
            t_SS = [Tok() for _ in range(2)]
            SSb = AR.alloc([128, NSEQ, HV], BF16)
            t_SSb = Tok()
            SSo = AR.alloc([128, NSEQ, HV], F32)
            t_SSo = Tok()
            KM = [AR.alloc([64, 16, 128], BF16)]
            QM = [AR.alloc([128, 16, 64], BF16)]
            t_KM = [Tok()]
            t_QM = [Tok()]
            msk = AR.alloc([128, 128], BF16)
            cms = AR.alloc([128, 16, 64], BF16)
            rms = AR.alloc([128, 16], F32)
            rsts = AR.alloc([128, 64], BF16)
            gqs = AR.alloc([128, 4, 64], F32)
            gks = AR.alloc([128, 4, 64], F32)
            elrs = AR.alloc([128, 4], F32)
            t_sc = Tok("sconst")
            for dst, nm in ((msk, "masks"), (cms, "cms"), (rms, "rms"), (rsts, "rst_s"), (gqs, "gqs"), (gks, "gks"),
                            (elrs, "elrs"), (COS, "cos_s"), (SIN, "sin_s")):
                LD(lambda e, dst=dst, nm=nm: e.dma_start(out=dst, in_=cst[nm]), [], [t_sc if nm not in ("cos_s", "sin_s") else t_rope])
            MASKS = {16: msk}
            t_MASKS = {16: t_sc}
            CM, RM = cms, rms
            t_CMRM = [t_sc]
            RST = {1: rsts, 2: rsts}
            t_RST = {1: t_sc, 2: t_sc}
            GQ, GK, ELR = gqs, gks, elrs
            t_G = [t_sc]
            convT = AR.alloc([128, 88, 32], F32)
            t_convT = Tok("convT")
            outc = [R3b.alloc([32, 2816], F32) for _ in range(2)]
            t_outc = [Tok("outc0"), Tok("outc1")]
            cstg = R3b.alloc([32, 2048], F32)
            t_cstg = Tok("cstg")
            U6 = [AR.alloc([128, NSEQ, 6], F32) for _ in range(2)]
            t_U6 = [Tok() for _ in range(2)]

        xsrc = xp[p * TP:(p + 1) * TP] if prompt else xs
        for (t0, rows) in tiles:
            LD(lambda e, t0=t0, rows=rows: e.dma_start(out=stg[0:rows, :], in_=xsrc[t0:t0 + rows, :]), [], [t_stgi])
            for g in range(4):
                pb, pt = main_rot.get()
                for cc in range(4):
                    c = g * 4 + cc
                    T(lambda e, pb=pb, cc=cc, c=c, rows=rows: e.transpose(
                        pb[:, cc * rows:(cc + 1) * rows], stg[0:rows, c * 128:(c + 1) * 128], K["ident_f"][0:rows, 0:rows]),
                      [t_stgi, Kt["ident_f"]], [pt])
                act(xT[:, g * 4:g * 4 + 4, t0:t0 + rows], pb[:, 0:4 * rows].rearrange("p (a b) -> p a b", a=4),
                    AF.Copy, [pt], t_xT[g * 4:g * 4 + 4])
        P.fence()

        def rms_rstd():
            pss, pst_ = small_rot.get()
            for c in range(16):
                k = c % 2
                act(tmp[k][:, 0:Tn], xT[:, c, :], AF.Square, [t_xT[c]], [t_tmp[k]])
                T(lambda e, c=c, k=k: e.matmul(pss[:, 0:Tn], lhsT=K["ones_f"], rhs=tmp[k][:, 0:Tn], start=(c == 0), stop=(c == 15)),
                  [t_tmp[k], Kt["ones_f"]], [pst_])
            ts(rstd, pss[:, 0:Tn], 1.0 / D, EPS, ALU.mult, ALU.add, [pst_], [t_rstd])
            act(rstd, rstd, AF.Sqrt, [t_rstd], [t_rstd])
            V(lambda e: e.reciprocal(out=rstd, in_=rstd), [t_rstd], [t_rstd])

        def modulated_norm(l, which):
            rms_rstd()
            if prompt:
                ka, kb = (0, 1) if which == 1 else (3, 4)
                for c in range(16):
                    k = 2 + c % 2
                    stt(tmp[k][:, 0:Tn], xT[:, c, :], MODP[l][:, ka, c:c + 1], rstd, ALU.mult, ALU.mult,
                        [t_xT[c], t_MODP[l], t_rstd], [t_tmp[k]])
                    act(hT[:, c, :], tmp[k][:, 0:Tn], AF.Identity, [t_tmp[k], t_MODP[l]], [t_hT[c]],
                        bias=MODP[l][:, kb, c:c + 1], scale=1.0)
            else:
                ca, cb = (16, 0) if which == 1 else (64, 48)
                t1 = tmp[2][:, 0:1024].rearrange("p (c t) -> p c t", c=16)
                tt(t1, xT, rstd.unsqueeze(1).broadcast_to([128, 16, Tn]), ALU.mult, t_xT + [t_rstd], [t_tmp[2]])
                t1v = tmp[2][:, 0:1024].rearrange("p (c s t) -> p c s t", c=16, s=NSEQ)
                t2v = tmp[3][:, 0:1024].rearrange("p (c s t) -> p c s t", c=16, s=NSEQ)
                Av = MOD[l][:, ca:ca + 16, 0:NSEQ].unsqueeze(3).broadcast_to([128, 16, NSEQ, TS])
                Bv = MOD[l][:, cb:cb + 16, 0:NSEQ].unsqueeze(3).broadcast_to([128, 16, NSEQ, TS])
                tt(t2v, t1v, Av, ALU.mult, [t_tmp[2], t_MOD[l]], [t_tmp[3]])
                tt(hT.rearrange("p c (s t) -> p c s t", s=NSEQ), t2v, Bv, ALU.add, [t_tmp[3], t_MOD[l]], t_hT)

        def residual_add(l, which, f, pm, pmt):
            if prompt:
                kg = 2 if which == 1 else 5
                stt(xT[:, f, :], pm[:, 0:Tn], MODP[l][:, kg, f:f + 1], xT[:, f, :], ALU.mult, ALU.add,
                    [pmt, t_MODP[l], t_xT[f]], [t_xT[f]])
            else:
                cg = 32 if which == 1 else 80
                Gv = MOD[l][:, cg + f, 0:NSEQ].unsqueeze(2).broadcast_to([128, NSEQ, TS])
                tv = tmp[4][:, 0:Tn].rearrange("p (s t) -> p s t", s=NSEQ)
                tt(tv, pm[:, 0:Tn].rearrange("p (s t) -> p s t", s=NSEQ), Gv, ALU.mult, [pmt, t_MOD[l]], [t_tmp[4]])
                tt(xT[:, f, :], xT[:, f, :], tmp[4][:, 0:Tn], ALU.add, [t_tmp[4], t_xT[f]], [t_xT[f]])

        nbt = {0: 1, 1: 1, 2: 4} if prompt else {0: 16, 1: 16, 2: 16}
        bsz = {b: tiles[0][1] // nbt[b] for b in range(3)}

        def mixer_branch(l, b):
            nb = nbt[b]
            bs = bsz[b]
            if b == 1:
                wg_, wgt = load_w(("glr", l))
                pb, pt = small_rot.get()
                mm(pb[0:16, 0:Tn], pt, [(wg_[:, c, 0:16], hT[:, c, :]) for c in range(16)], [wgt] + t_hT)
                act(glrT, pb[0:16, 0:Tn], AF.Copy, [pt], [t_glrT])
            for hp in range(2):
                kb_, kbt = load_w(("k", l, b, hp))
                qb_, qbt = load_w(("q", l, b, hp))
                for hh in range(2):
                    h = hp * 2 + hh
                    pk, pkt = main_rot.get()
                    mm(pk[:, 0:Tn], pkt, [(kb_[:, c, hh * 128:(hh + 1) * 128], hT[:, c, :]) for c in range(16)], [kbt] + t_hT)
                    X = [tmp[i][:, 0:Tn] for i in range(NTMP)]
                    if b == 0:
                        def rope_path(pz, pzt, G, dstT, dtok):
                            raw = tmp[0].bitcast(BF16)[:, 0:Tn]
                            act(raw, pz[:, 0:Tn], AF.Copy, [pzt], [t_tmp[0]])
                            psw, pswt = small_rot.get()
                            T(lambda e: e.matmul(psw[:, 0:Tn], lhsT=K["pswap"], rhs=raw, start=True, stop=True),
                              [t_tmp[0], Kt["pswap"]], [pswt])
                            tt(X[1], raw, COS, ALU.mult, [t_tmp[0], t_rope], [t_tmp[1]])
                            tt(X[2], psw[:, 0:Tn], SIN, ALU.mult, [pswt, t_rope], [t_tmp[2]])
                            tt(X[3], X[1], X[2], ALU.add, [t_tmp[1], t_tmp[2]], [t_tmp[3]])
                            if prompt:
                                Gv = G[:, h, :].unsqueeze(1).broadcast_to([128, nt, 128])
                                tt(dstT[:, h, :].rearrange("p (i t) -> p i t", i=nt), X[3].rearrange("p (i t) -> p i t", i=nt),
                                   Gv, ALU.mult, [t_tmp[3]] + t_G, [dtok[h]])
                            else:
                                tt(dstT[:, h, :], X[3], G[:, h, :], ALU.mult, [t_tmp[3]] + t_G, [dtok[h]])
                        rope_path(pk, pkt, GK, kT, t_kT)
                        pq, pqt = main_rot.get()
                        mm(pq[:, 0:Tn], pqt, [(qb_[:, c, hh * 128:(hh + 1) * 128], hT[:, c, :]) for c in range(16)], [qbt] + t_hT)
                        rope_path(pq, pqt, GQ, qT, t_qT)
                    else:
                        if b == 1:
                            px, pxt = small_rot.get()
                            T(lambda e, h=h, px=px: e.matmul(px[:, 0:Tn], lhsT=GLW[l][:, h * 128:(h + 1) * 128], rhs=glrT, start=True, stop=True),
                              [t_GLW[l], t_glrT], [pxt])
                            act(X[0], px[:, 0:Tn], AF.Exp, [pxt, t_LBV], [t_tmp[0]], bias=NEGB[:, l, h:h + 1], scale=-1.0)
                            act(X[1], X[0], AF.Ln, [t_tmp[0]], [t_tmp[1]], bias=1.0, scale=1.0)
                            sc_q, sc_k = -1.0 / 16.0, 1.0 / 16.0
                        else:
                            act(X[0], pk[:, 0:Tn], AF.Sigmoid, [pkt], [t_tmp[0]])
                            act(X[1], X[0], AF.Ln, [t_tmp[0], t_LBV], [t_tmp[1]],
                                bias=LBV[:, l, 0, h:h + 1], scale=LBV[:, l, 1, h:h + 1])
                            sc_q, sc_k = 1.0, -1.0
                        V(lambda e, b=b: e.tensor_tensor_scan(out=X[2], data0=RST[b][:, 0:Tn], data1=X[1], initial=0.0,
                                                              op0=ALU.mult, op1=ALU.add), [t_tmp[1], t_RST[b]], [t_tmp[2]])
                        act(X[3], X[2], AF.Exp, [t_tmp[2]], [t_tmp[3]], scale=sc_q)
                        act(X[4], X[2], AF.Exp, [t_tmp[2]], [t_tmp[4]], scale=sc_k)
                        nblk = Tn // bs
                        A(lambda e, h=h, bs=bs, nblk=nblk: e.activation(out=EL[:, h, 0:nblk], in_=tmp[3][:, bs - 1:Tn:bs], func=AF.Copy),
                          [t_tmp[3]], [t_EL[h]])
                        if b == 1:
                            tt(kT[:, h, :], pk[:, 0:Tn], X[4], ALU.mult, [pkt, t_tmp[4]], [t_kT[h]])
                        else:
                            ts(X[5], X[0], LBV[:, l, 2, h:h + 1], LBV[:, l, 1, h:h + 1], ALU.mult, ALU.add, [t_tmp[0], t_LBV], [t_tmp[5]])
                            tt(kT[:, h, :], X[5], X[4], ALU.mult, [t_tmp[5], t_tmp[4]], [t_kT[h]])
                        pq, pqt = main_rot.get()
                        mm(pq[:, 0:Tn], pqt, [(qb_[:, c, hh * 128:(hh + 1) * 128], hT[:, c, :]) for c in range(16)], [qbt] + t_hT)
                        if b == 1:
                            stt(qT[:, h, :], pq[:, 0:Tn], HK ** -0.5, X[3], ALU.mult, ALU.mult, [pqt, t_tmp[3]], [t_qT[h]])
                        else:
                            act(X[0], pq[:, 0:Tn], AF.Sigmoid, [pqt], [t_tmp[0]])
                            tt(X[1], pq[:, 0:Tn], X[0], ALU.mult, [pqt, t_tmp[0]], [t_tmp[1]])
                            tt(qT[:, h, :], X[1], X[3], ALU.mult, [t_tmp[1], t_tmp[3]], [t_qT[h]])
            for h in range(4):
                wv_, wvt = load_w(("v", l, b, h))
                for i, (t0, rows) in enumerate(tiles):
                    pv, pvt = main_rot.get()
                    mm(pv[0:rows, 0:256], pvt, [(hT[:, c, t0:t0 + rows], wv_[:, c, :]) for c in range(16)], [wvt] + t_hT)
                    act(v_tok[0:rows, i, h * 256:(h + 1) * 256], pv[0:rows, 0:256], AF.Copy, [pvt], [t_v[i][h]])
            for h in range(4):
                wg_, wgt = load_w(("g", l, b, h))
                for i, (t0, rows) in enumerate(tiles):
                    pg, pgt = main_rot.get()
                    mm(pg[0:rows, 0:256], pgt, [(hT[:, c, t0:t0 + rows], wg_[:, c, :]) for c in range(16)], [wgt] + t_hT)
                    k = h % 2
                    act(sq256[k][0:rows, :], pg[0:rows, 0:256], AF.Sigmoid, [pgt], [t_sq256[k]])
                    tt(sq256[k][0:rows, :], pg[0:rows, 0:256], sq256[k][0:rows, :], ALU.mult, [pgt, t_sq256[k]], [t_sq256[k]])
                    tt(sg_tok[0:rows, i, h * 256:(h + 1) * 256], sq256[k][0:rows, :], HNW[l][0:rows, b * HV:(b + 1) * HV], ALU.mult,
                       [t_sq256[k], t_HNW[l]], [t_sg[i][h]])
            def el_col(h, blk):
                if b == 0:
                    return ELR[:, h:h + 1], t_G
                return EL[:, h, blk:blk + 1], [t_EL[h]]

            if prompt:
                if p == 0:
                    V(lambda e: e.memset(SF, 0.0), [], t_SF)
                else:
                    LD(lambda e: e.dma_start(out=SF, in_=pst[b][l].rearrange("h k v -> k h v")), [t_pst[b][l]], t_SF)
                act(Sbf, SF, AF.Copy, t_SF, t_Sbf)
            hgroups = [[0, 1, 2, 3]] if prompt else [[0], [1], [2], [3]]
            ss_i = [0]
            for i, (t0, C) in enumerate(tiles):
                for hg in hgroups:
                    ctx = {}
                    for h in hg:
                        if not prompt:
                            k = ss_i[0] % 2
                            ss_i[0] += 1
                            LD(lambda e, h=h, k=k: e.dma_start(out=SS[k], in_=st_in[b][l, :, h].rearrange("s k v -> k s v")), [], [t_SS[k]])
                            V(lambda e, k=k: e.tensor_copy(out=SSb[:, 0:8, :], in_=SS[k][:, 0:8, :]), [t_SS[k]], [t_SSb])
                            act(SSb[:, 8:16, :], SS[k][:, 8:16, :], AF.Copy, [t_SS[k]], [t_SSb])
                            ctx[h] = k
                        qt_ = qT[:, h, t0:t0 + C]
                        kt_ = kT[:, h, t0:t0 + C]
                        psc, psct = small_rot.get()
                        T(lambda e, psc=psc, kt_=kt_, qt_=qt_, C=C: e.matmul(psc[0:C, 0:C], lhsT=kt_, rhs=qt_, start=True, stop=True),
                          [t_kT[h], t_qT[h]], [psct])
                        k2 = h
                        tt(PT_[k2][0:C, 0:C], psc[0:C, 0:C], MASKS[nb][0:C, 0:C], ALU.mult, [psct, t_MASKS[nb]], [t_PT[k2]])
                        T(lambda e, kt_=kt_, C=C: e.transpose(bank_bf[0:C, 0:128], kt_, K["ident_b"]), [t_kT[h], Kt["ident_b"]], [bank_tok[7]])
                        act(ktok[k2][0:C, :], bank_bf[0:C, 0:128], AF.Copy, [bank_tok[7]], [t_ktok[k2]])
                        if nb > 1:
                            km = KM[k2 % len(KM)]
                            qm = QM[k2 % len(QM)]
                            tkm = t_KM[k2 % len(KM)]
                            tqm = t_QM[k2 % len(QM)]
                            tt(km[0:C, 0:nb, :], ktok[k2][0:C, :].unsqueeze(1).broadcast_to([C, nb, 128]),
                               RM[0:C, 0:nb].unsqueeze(2).broadcast_to([C, nb, 128]), ALU.mult, [t_ktok[k2]] + t_CMRM, [tkm])
                            tt(qm[:, 0:nb, 0:C], qt_.unsqueeze(1).broadcast_to([128, nb, C]), CM[:, 0:nb, 0:C], ALU.mult,
                               [t_qT[h]] + t_CMRM, [tqm])
                    po = {}
                    for hi, h in enumerate(hg):
                        po[h] = (banks[hi][:], bank_tok[hi])
                    for j in range(nb):
                        for hi, h in enumerate(hg):
                            k2 = h
                            pob, pot = po[h]
                            vt = v_tok[0:C, i, h * 256:(h + 1) * 256]
                            qt_ = qT[:, h, t0:t0 + C]
                            if j == 0:
                                T(lambda e, pob=pob, k2=k2, vt=vt, C=C: e.matmul(pob[0:C, 0:256], lhsT=PT_[k2][0:C, 0:C], rhs=vt, start=True, stop=False),
                                  [t_PT[k2], t_v[i][h]], [pot])
                            lhs = qt_ if nb == 1 else QM[k2 % len(QM)][:, j, 0:C]
                            lt = [t_qT[h]] if nb == 1 else [t_QM[k2 % len(QM)]]
                            if prompt:
                                srhs, srt = Sbf[:, h, :], [t_Sbf[h]]
                            else:
                                srhs, srt = SSb[:, j, :], [t_SSb]
                            T(lambda e, pob=pob, lhs=lhs, srhs=srhs, C=C, j=j: e.matmul(pob[0:C, 0:256], lhsT=lhs, rhs=srhs, start=False, stop=(j == nb - 1)),
                              lt + srt, [pot])
                            pkv, pkvt = small_rot.get()
                            klhs = ktok[k2][0:C, :] if nb == 1 else KM[k2 % len(KM)][0:C, j, :]
                            klt = [t_ktok[k2]] if nb == 1 else [t_KM[k2 % len(KM)]]
                            T(lambda e, pkv=pkv, klhs=klhs, vt=vt: e.matmul(pkv[:, 0:256], lhsT=klhs, rhs=vt, start=True, stop=True),
                              klt + [t_v[i][h]], [pkvt])
                            blk = i * nb + j
                            elc, elt = el_col(h, blk)
                            kk = (j + hi) % 2
                            act(kve[kk], pkv[:, 0:256], AF.Identity, [pkvt] + elt, [t_kve[kk]], scale=elc)
                            if prompt:
                                stt(SF[:, h, :], SF[:, h, :], elc, kve[kk], ALU.mult, ALU.add, [t_SF[h], t_kve[kk]] + elt, [t_SF[h]])
                                act(Sbf[:, h, :], SF[:, h, :], AF.Copy, [t_SF[h]], [t_Sbf[h]])
                            else:
                                stt(SSo[:, j, :], SS[ctx[h]][:, j, :], elc, kve[kk], ALU.mult, ALU.add, [t_SS[ctx[h]], t_kve[kk]] + elt, [t_SSo])
                    for hi, h in enumerate(hg):
                        pob, pot = po[h]
                        k2 = h % 2
                        act(sq256[k2][0:C, :], pob[0:C, 0:256], AF.Square, [pot], [t_sq256[k2]])
                        V(lambda e, k2=k2, h=h, C=C: e.tensor_reduce(out=ssv[0:C, h:h + 1], in_=sq256[k2][0:C, :], axis=AX.X, op=ALU.add),
                          [t_sq256[k2]], [t_ssv[h]])
                        ts(ssv[0:C, h:h + 1], ssv[0:C, h:h + 1], 1.0 / HV, EPS, ALU.mult, ALU.add, [t_ssv[h]], [t_ssv[h]])
                        act(ssv[0:C, h:h + 1], ssv[0:C, h:h + 1], AF.Sqrt, [t_ssv[h]], [t_ssv[h]])
                        V(lambda e, h=h, C=C: e.reciprocal(out=ssv[0:C, h:h + 1], in_=ssv[0:C, h:h + 1]), [t_ssv[h]], [t_ssv[h]])
                        stt(u_tok[0:C, h * 256:(h + 1) * 256], pob[0:C, 0:256], ssv[0:C, h:h + 1], sg_tok[0:C, i, h * 256:(h + 1) * 256],
                            ALU.mult, ALU.mult, [pot, t_ssv[h], t_sg[i][h]], [t_utok])
                        if not prompt:
                            ST_(lambda e, h=h: e.dma_start(out=sst[b][l, :, h].rearrange("s k v -> k s v"), in_=SSo), [t_SSo], [t_sst])
                            outs_final.append(P.streams["act"][-1])
                for c in range(8):
                    T(lambda e, c=c, C=C: e.transpose(bank_bf[:, c * C:(c + 1) * C], u_tok[0:C, c * 128:(c + 1) * 128], K["ident_b"][0:C, 0:C]),
                      [t_utok, Kt["ident_b"]], [bank_tok[7]])
                act(uT[:, b, :, t0:t0 + C], bank_bf[:, 0:8 * C].rearrange("p (c t) -> p c t", c=8), AF.Copy, [bank_tok[7]], [t_uT[b][i]])
            if prompt:
                ST_(lambda e: e.dma_start(out=pst[b][l].rearrange("h k v -> k h v"), in_=SF), t_SF, [t_pst[b][l]])
                if p == NPASS - 1:
                    outs_final.append(P.streams["act"][-1])

        def merge_and_out(l):
            P.fence()
            for fp in range(8):
                for b in range(3):
                    wbr, wbrt = load_w(("br", l, b, fp))
                    wmg, wmgt = load_w(("mg", l, b, fp))
                    for ft in range(2):
                        f = fp * 2 + ft
                        po_, pot = main_rot.get()
                        mm(po_[:, 0:Tn], pot, [(wbr[:, c, ft * 128:(ft + 1) * 128], uT[:, b, c, :]) for c in range(8)], [wbrt] + t_uT[b])
                        pg, pgt = main_rot.get()
                        mm(pg[:, 0:Tn], pgt, [(wmg[:, c, ft * 128:(ft + 1) * 128], hT[:, c, :]) for c in range(16)], [wmgt] + t_hT)
                        k = ft
                        act(tmp[k][:, 0:Tn], pg[:, 0:Tn], AF.Sigmoid, [pgt], [t_tmp[k]])
                        if b == 0:
                            tt(acc[:, ft, :], po_[:, 0:Tn], tmp[k][:, 0:Tn], ALU.mult, [pot, t_tmp[k]], [t_acc])
                        elif b == 1:
                            tt(tmp[k][:, 0:Tn], po_[:, 0:Tn], tmp[k][:, 0:Tn], ALU.mult, [pot, t_tmp[k]], [t_tmp[k]])
                            tt(acc[:, ft, :], acc[:, ft, :], tmp[k][:, 0:Tn], ALU.add, [t_tmp[k], t_acc], [t_acc])
                        else:
                            tt(tmp[k][:, 0:Tn], po_[:, 0:Tn], tmp[k][:, 0:Tn], ALU.mult, [pot, t_tmp[k]], [t_tmp[k]])
                            tt(mT[:, f, :], acc[:, ft, :], tmp[k][:, 0:Tn], ALU.add, [t_tmp[k], t_acc], [t_mT])
            for fp in range(8):
                wo, wot = load_w(("out", l, fp))
                for ft in range(2):
                    f = fp * 2 + ft
                    pm, pmt = main_rot.get()
                    mm(pm[:, 0:Tn], pmt, [(wo[:, c, ft * 128:(ft + 1) * 128], mT[:, c, :]) for c in range(16)], [wot, t_mT])
                    residual_add(l, 1, f, pm, pmt)

        def ffn(l):
            modulated_norm(l, 2)
            P.fence()
            if not prompt:
                for g in range(6):
                    ncol = min(2048, 2 * DFF - g * 2048)
                    LD(lambda e, g=g, ncol=ncol: e.dma_start(out=cstg[:, 0:ncol], in_=sconv_in[l, :, g * 2048:g * 2048 + ncol]), [], [t_cstg])
                    for q4 in range(0, ncol // 128, 4):
                        pb, pt = small_rot.get()
                        for cc in range(4):
                            T(lambda e, pb=pb, cc=cc, q4=q4: e.transpose(pb[:, cc * 32:(cc + 1) * 32], cstg[:, (q4 + cc) * 128:(q4 + cc + 1) * 128],
                                                                          K["ident_f"][0:32, 0:32]), [t_cstg, Kt["ident_f"]], [pt])
                        act(convT[:, g * 16 + q4:g * 16 + q4 + 4, :], pb[:, 0:128].rearrange("p (a b) -> p a b", a=4), AF.Copy, [pt], [t_convT])
            for j in range(NFT):
                wu, wut = load_w(("up", l, j))
                res = []
                for half in range(2):
                    tile_idx = half * NFT + j
                    pu, put = main_rot.get()
                    mm(pu[:, 0:Tn], put, [(wu[:, c, half * 128:(half + 1) * 128], hT[:, c, :]) for c in range(16)], [wut] + t_hT)
                    o3 = 3 * half
                    w0 = CW[l][:, 0, tile_idx:tile_idx + 1]
                    w1 = CW[l][:, 1, tile_idx:tile_idx + 1]
                    w2 = CW[l][:, 2, tile_idx:tile_idx + 1]
                    cb = CW[l][:, 3, tile_idx:tile_idx + 1]
                    if prompt:
                        U = tmp[o3]
                        tc_ = t_CARRY[l][tile_idx]
                        V(lambda e, U=U, tile_idx=tile_idx: e.tensor_copy(out=U[:, 0:2], in_=CARRY[l][:, tile_idx, :]), [tc_], [t_tmp[o3]])
                        act(U[:, 2:2 + Tn], pu[:, 0:Tn], AF.Copy, [put], [t_tmp[o3]])
                        V(lambda e, U=U, tile_idx=tile_idx: e.tensor_copy(out=CARRY[l][:, tile_idx, :], in_=U[:, Tn:Tn + 2]), [t_tmp[o3]], [tc_])
                        c1, c2 = tmp[o3 + 1][:, 0:Tn], tmp[o3 + 2][:, 0:Tn]
                        ts(c1, U[:, 0:Tn], w0, cb, ALU.mult, ALU.add, [t_tmp[o3], t_CW[l]], [t_tmp[o3 + 1]])
                        stt(c2, U[:, 1:Tn + 1], w1, c1, ALU.mult, ALU.add, [t_tmp[o3], t_tmp[o3 + 1], t_CW[l]], [t_tmp[o3 + 2]])
                        stt(c1, U[:, 2:Tn + 2], w2, c2, ALU.mult, ALU.add, [t_tmp[o3], t_tmp[o3 + 2], t_CW[l]], [t_tmp[o3 + 1]])
                        res.append((c1, t_tmp[o3 + 1], tmp[o3 + 2][:, 0:Tn], t_tmp[o3 + 2]))
                    else:
                        U = U6[half]
                        tu = t_U6[half]
                        V(lambda e, U=U, tile_idx=tile_idx: e.tensor_copy(out=U[:, :, 0:2], in_=convT[:, tile_idx, :].rearrange("p (s r) -> p s r", r=2)),
                          [t_convT], [tu])
                        act(U[:, :, 2:6], pu[:, 0:Tn].rearrange("p (s t) -> p s t", t=TS), AF.Copy, [put], [tu])
                        pb, pt = small_rot.get()
                        raw = tmp[o3][:, 0:32].rearrange("p (s r) -> p s r", r=2)
                        V(lambda e, U=U, raw=raw: e.tensor_copy(out=raw, in_=U[:, :, 4:6]), [tu], [t_tmp[o3]])
                        T(lambda e, pb=pb, o3=o3: e.transpose(pb[0:32, 0:128], tmp[o3][:, 0:32], K["ident_f"]), [t_tmp[o3], Kt["ident_f"]], [pt])
                        act(outc[half][:, (tile_idx % 22) * 128:(tile_idx % 22 + 1) * 128], pb[0:32, 0:128], AF.Copy, [pt], [t_outc[half]])
                        c1 = tmp[o3 + 1][:, 0:Tn].rearrange("p (s t) -> p s t", t=TS)
                        c2 = tmp[o3 + 2][:, 0:Tn].rearrange("p (s t) -> p s t", t=TS)
                        ts(c1, U[:, :, 0:4], w0, cb, ALU.mult, ALU.add, [tu, t_CW[l]], [t_tmp[o3 + 1]])
                        stt(c2, U[:, :, 1:5], w1, c1, ALU.mult, ALU.add, [tu, t_tmp[o3 + 1], t_CW[l]], [t_tmp[o3 + 2]])
                        stt(c1, U[:, :, 2:6], w2, c2, ALU.mult, ALU.add, [tu, t_tmp[o3 + 2], t_CW[l]], [t_tmp[o3 + 1]])
                        res.append((tmp[o3 + 1][:, 0:Tn], t_tmp[o3 + 1], tmp[o3 + 2][:, 0:Tn], t_tmp[o3 + 2]))
                (ca, tca, sa, tsa), (cbv, tcb, _, _) = res
                act(sa, ca, AF.Silu, [tca], [tsa])
                tt(actT[:, j, :], sa, cbv, ALU.mult, [tsa, tcb], [t_act[j]])
                if (not prompt) and j % 22 == 21:
                    for half in range(2):
                        grp = half * 2 + j // 22
                        ST_(lambda e, half=half, grp=grp: e.dma_start(out=sconv_o[l][:, grp * 2816:(grp + 1) * 2816], in_=outc[half]),
                            [t_outc[half]], [t_sconv_o])
                        outs_final.append(P.streams["act"][-1])
            for f in range(16):
                pf, pft = main_rot.get()
                for kh in range(2):
                    wd, wdt = load_w(("dn", l, f, kh))
                    for c in range(22):
                        T(lambda e, pf=pf, wd=wd, c=c, kh=kh: e.matmul(pf[:, 0:Tn], lhsT=wd[:, c, :], rhs=actT[:, kh * 22 + c, :],
                                                                     start=(kh == 0 and c == 0), stop=(kh == 1 and c == 21)),
                          [wdt, t_act[kh * 22 + c]], [pft])
                residual_add(l, 2, f, pf, pft)
            P.fence()

        t_mT = Tok("mT")
        t_act = [Tok(f"act{j}") for j in range(NFT)]
        for l in range(DEPTH):
            modulated_norm(l, 1)
            for b in range(3):
                mixer_branch(l, b)
            merge_and_out(l)
            ffn(l)
            if prompt and p == NPASS - 1:
                for g in range(6):
                    n = min(16, 88 - g * 16)
                    for q4 in range(0, n, 4):
                        pb, pt = small_rot.get()
                        for cc in range(4):
                            ti = g * 16 + q4 + cc
                            T(lambda e, pb=pb, cc=cc, ti=ti, l=l: e.transpose(pb[0:2, cc * 128:(cc + 1) * 128], CARRY[l][:, ti, :], K["ident_f"]),
                              [t_CARRY[l][ti], Kt["ident_f"]], [pt])
                        act(cstage[:, q4 * 128:(q4 + 4) * 128], pb[0:2, 0:512], AF.Copy, [pt], [t_cstage])
                    ST_(lambda e, g=g, n=n, l=l: e.dma_start(out=pconv[l][:, g * 2048:g * 2048 + n * 128], in_=cstage[:, 0:n * 128]), [t_cstage], [t_pconv])
                    outs_final.append(P.streams["act"][-1])
                P.fence()
        rms_rstd()
        for c in range(16):
            stt(yT[:, c, :], xT[:, c, :], PV[:, 64 + c:65 + c], rstd, ALU.mult, ALU.mult, [t_xT[c], t_PV, t_rstd], [t_yT])
        ydst = yp[p * TP:(p + 1) * TP] if prompt else ys
        for (t0, rows) in tiles:
            for g in range(4):
                pb, pt = main_rot.get()
                for cc in range(4):
                    c = g * 4 + cc
                    T(lambda e, pb=pb, cc=cc, c=c, t0=t0, rows=rows: e.transpose(pb[0:rows, cc * 128:(cc + 1) * 128], yT[:, c, t0:t0 + rows], K["ident_f"]),
                      [t_yT, Kt["ident_f"]], [pt])
                act(stg[0:rows, g * 512:(g + 1) * 512], pb[0:rows, 0:512], AF.Copy, [pt], [t_stgo])
            ST_(lambda e, t0=t0, rows=rows: e.dma_start(out=ydst[t0:t0 + rows, :], in_=stg[0:rows, :]), [t_stgo], [])
            outs_final.append(P.streams["act"][-1])
        P.fence()
        AR.off = m0

    t_pst = [[Tok() for _ in range(DEPTH)] for _ in range(3)]
    t_sst = Tok()
    t_sconv_o = Tok()
    t_pconv = Tok()
    t_stgo = Tok("stgo")
    outs_final = []

    setup()
    MOD = [AR.alloc([128, 96, 17], F32) for _ in range(DEPTH)]
    t_MOD = [Tok("MOD") for _ in range(DEPTH)]
    m_mod = compute_mod(MOD, t_MOD)
    P.fence()
    AR.off = m_mod
    run_pass("s", 0, MOD, t_MOD)
    AR.off = persist_mark
    for p in range(NPASS):
        run_pass("p", p)
    P.final_waits = outs_final
    P.emit(st)
    st.close()
    return nc


_CACHE = {}


def _prep_inputs(inp):
    f32 = lambda a: np.ascontiguousarray(np.asarray(a, dtype=np.float32))
    consts = host_consts()
    pv = np.zeros((96, 128), np.float32)
    nm, nf, fn = f32(inp["norm_mix"]), f32(inp["norm_ffn"]), f32(inp["final_norm"])
    pv[0:16] = nm[0].reshape(16, 128)
    pv[16:32] = nm[1].reshape(16, 128)
    pv[32:48] = nf[0].reshape(16, 128)
    pv[48:64] = nf[1].reshape(16, 128)
    pv[64:80] = fn.reshape(16, 128)
    lbl = f32(inp["hgrn_lb_logits"])
    pv[80:84] = lbl[0].reshape(4, 128)
    pv[84:88] = lbl[1].reshape(4, 128)
    gb = f32(inp["gla_b_lr"])
    pv[88:92] = gb[0].reshape(4, 128)
    pv[92:96] = gb[1].reshape(4, 128)
    cw, cb = f32(inp["ffn_conv_w"]), f32(inp["ffn_conv_b"])
    convp = np.concatenate([cw.reshape(DEPTH, 3, 88, 128), cb.reshape(DEPTH, 1, 88, 128)], axis=1)
    shared = {
        "w_in": f32(inp["w_in"]), "gla_w_lr": f32(inp["gla_w_lr"]), "pvecs": pv,
        "head_norm": f32(inp["head_norm"]).reshape(DEPTH, 3 * HV), "w_branch": f32(inp["w_branch"]),
        "w_out": f32(inp["w_out"]), "w_ada": f32(inp["w_ada"]), "b_ada": f32(inp["b_ada"]).reshape(DEPTH, 96, 128),
        "ffn_w_up": f32(inp["ffn_w_up"]), "convp": np.ascontiguousarray(convp), "ffn_w_down": f32(inp["ffn_w_down"]),
    }
    for n, s, dt in CONST_SPECS:
        shared["k_" + n] = np.ascontiguousarray(consts[n])
    x_prompt, x_sample = f32(inp["x_prompt"]), f32(inp["x_sample"])
    c_prompt, c_sample = f32(inp["c_prompt"]), f32(inp["c_sample"])
    sts = [f32(inp["state_ret"]), f32(inp["state_gla"]), f32(inp["state_hgrn"])]
    sc = f32(inp["state_conv"])
    maps = []
    for c in range(N_CORES):
        b = c % 4
        s0 = c * NSEQ
        m = dict(shared)
        m["xp"] = x_prompt[b]
        m["xs"] = np.ascontiguousarray(x_sample[s0:s0 + NSEQ].reshape(TSAMP, D))
        m["cvec"] = np.ascontiguousarray(np.concatenate([c_sample[s0:s0 + NSEQ], c_prompt[b:b + 1]], axis=0))
        for k in range(3):
            m[f"st_in{k}"] = np.ascontiguousarray(sts[k][:, s0:s0 + NSEQ])
        m["sconv_in"] = np.ascontiguousarray(sc[:, s0:s0 + NSEQ].reshape(DEPTH, NSEQ * 2, 2 * DFF))
        maps.append(m)
    return maps


def kernel(**inputs):
    if "nc" not in _CACHE:
        _CACHE["nc"] = build_program()
    nc = _CACHE["nc"]
    maps = _prep_inputs(inputs)
    res = run_bass_kernel_spmd(nc, maps, core_ids=list(range(N_CORES)))
    R = res.results
    y_prompt = np.stack([R[b]["yp"] for b in range(4)]).astype(np.float32)
    y_sample = np.concatenate([R[c]["ys"].reshape(NSEQ, TS, D) for c in range(N_CORES)], axis=0).astype(np.float32)
    outs = [y_prompt, y_sample]
    for k in range(3):
        outs.append(np.stack([R[b][f"pst{k}"] for b in range(4)], axis=1).astype(np.float32))
    outs.append(np.stack([R[b]["pconv"] for b in range(4)], axis=1).astype(np.float32))
    for k in range(3):
        outs.append(np.concatenate([R[c][f"sst{k}"] for c in range(N_CORES)], axis=1).astype(np.float32))
    outs.append(np.concatenate([R[c]["sconv_o"].reshape(DEPTH, NSEQ, 2, 2 * DFF) for c in range(N_CORES)], axis=1).astype(np.float32))
    return tuple(outs)
```

```python
import numpy as np
from contextlib import ExitStack
import ml_dtypes
import concourse.bass as bass
import concourse.mybir as mybir
from concourse.bass_utils import run_bass_kernel_spmd

F32 = mybir.dt.float32
BF16 = mybir.dt.bfloat16
ALU = mybir.AluOpType
AF = mybir.ActivationFunctionType
AX = mybir.AxisListType
NPBF = ml_dtypes.bfloat16

D = 2048
NC16 = 16
NH = 4
HK = 128
HV = 256
MV = 1024
DFF = 5632
NFT = 44
NIN = 15376
DEPTH = 2
SEQ = 2048
TP = 512
NSP = 2
HALF = NSP * TP
PAIRS = [[0, 1], [2, 3], [4, 5], [6, 7]]
NSEQ = 16
TS = 4
TSAMP = NSEQ * TS
PAST = 16384
EPS = 1e-6
N_CORES = 8
OFF = [(0, 512, 1024, 2048), (3072, 3584, 4096, 5120), (6160, 6672, 7184, 8208)]
OFF_GLR = 6144
OFF_MG = 9232

ENGS = ("pe", "dve", "act", "pool", "sp")
SEM_EPOCH = 30000


class Tok:
    __slots__ = ("name", "last_write", "reads")

    def __init__(self, name=""):
        self.name = name
        self.last_write = None
        self.reads = []


class Op:
    __slots__ = ("eng", "fn", "deps", "is_dma", "sig", "needs_sig", "prewait", "wload")

    def __init__(self, eng, fn, is_dma):
        self.eng = eng
        self.fn = fn
        self.deps = set()
        self.is_dma = is_dma
        self.sig = None
        self.needs_sig = False
        self.prewait = None
        self.wload = False


class Prog:
    def __init__(self, nc, n_dma_sems=8):
        self.nc = nc
        self.streams = {e: [] for e in ENGS}
        self.n_dma_sems = n_dma_sems
        self.final_waits = []
        self.fence_deps = set()
        self.since_fence = []

    def add(self, eng, fn, reads=(), writes=(), dma=False, wload=False):
        op = Op(eng, fn, dma)
        op.wload = wload
        for t in reads:
            if t.last_write is not None:
                op.deps.add(t.last_write)
        for t in writes:
            if t.last_write is not None:
                op.deps.add(t.last_write)
            for r in t.reads:
                op.deps.add(r)
        if not wload:
            op.deps |= self.fence_deps
        op.deps.discard(op)
        for t in reads:
            t.reads.append(op)
        for t in writes:
            t.last_write = op
            t.reads = []
        self.streams[eng].append(op)
        if dma and not wload:
            self.since_fence.append(op)
        return op

    def fence(self):
        deps = set()
        for e in ENGS:
            if e in ("sp", "pool"):
                continue
            if self.streams[e]:
                deps.add(self.streams[e][-1])
        for op in self.since_fence:
            deps.add(op)
        self.since_fence = []
        self.fence_deps = deps

    def emit(self, stack):
        nc = self.nc
        for e in ENGS:
            for op in self.streams[e]:
                if op.eng == "pe":
                    op.deps = {d for d in op.deps if d.eng != "pe" or d.is_dma}
                for d in op.deps:
                    d.needs_sig = True
        self.sems = []

        def newsem(name):
            s = stack.enter_context(nc.semaphore(name))
            self.sems.append(s)
            return s

        for e in ENGS:
            cnt = 0
            ep = 0
            sem = None
            dsems = None
            dcnt = None
            di = 0
            for op in self.streams[e]:
                if op.is_dma:
                    if dsems is None:
                        dsems = [newsem(f"d_{e}_{i}") for i in range(self.n_dma_sems)]
                        dcnt = [0] * self.n_dma_sems
                    i = di % self.n_dma_sems
                    di += 1
                    if dcnt[i] + 16 > SEM_EPOCH:
                        dsems[i] = newsem(f"d_{e}_{i}_{di}")
                        dcnt[i] = 0
                    if dcnt[i] > 0:
                        op.prewait = (dsems[i], dcnt[i])
                    dcnt[i] += 16
                    op.sig = (dsems[i], dcnt[i])
                elif op.needs_sig:
                    if sem is None or cnt >= SEM_EPOCH:
                        sem = newsem(f"c_{e}_{ep}")
                        ep += 1
                        cnt = 0
                    cnt += 1
                    op.sig = (sem, cnt)
        block = stack.enter_context(nc.Block())
        self.n_wait = 0

        def run_stream(e):
            def body(eng):
                waited = {}
                for op in self.streams[e]:
                    need = {}
                    for d in op.deps:
                        s, v = d.sig
                        k = id(s)
                        if waited.get(k, 0) >= v:
                            continue
                        if k not in need or need[k][1] < v:
                            need[k] = (s, v)
                    if op.prewait is not None:
                        s, v = op.prewait
                        k = id(s)
                        if waited.get(k, 0) < v and (k not in need or need[k][1] < v):
                            need[k] = (s, v)
                    for k, (s, v) in need.items():
                        eng.wait_ge(s, v)
                        waited[k] = v
                        self.n_wait += 1
                    ins = op.fn(eng)
                    if op.is_dma:
                        ins.then_inc(op.sig[0], 16)
                    elif op.sig is not None:
                        ins.then_inc(op.sig[0], 1)
                if e == "sp":
                    for op in self.final_waits:
                        s, v = op.sig
                        if waited.get(id(s), 0) < v:
                            eng.wait_ge(s, v)
                            waited[id(s)] = v
            return body

        block.tensor(run_stream("pe"))
        block.vector(run_stream("dve"))
        block.scalar(run_stream("act"))
        block.gpsimd(run_stream("pool"))
        block.sync(run_stream("sp"))


def host_consts():
    c = {}
    c["ident_f"] = np.eye(128, dtype=np.float32)
    c["ident_b"] = np.eye(128, dtype=np.float32).astype(NPBF)
    sw = np.zeros((128, 128), np.float32)
    for m in range(128):
        sw[(m + 64) % 128, m] = 1.0
    c["pswap"] = sw.astype(NPBF)
    c["ones_f"] = np.ones((128, 128), np.float32)
    s = np.arange(128)[:, None]
    t = np.arange(128)[None, :]
    c["mask1"] = (s <= t).astype(np.float32).astype(NPBF)
    c["mask4"] = ((s <= t) & (s // 32 == t // 32)).astype(np.float32).astype(NPBF)
    ms = np.zeros((128, 128), np.float32)
    s6 = np.arange(64)[:, None]
    t6 = np.arange(64)[None, :]
    ms[:64, :64] = ((s6 <= t6) & (s6 // 4 == t6 // 4))
    c["masks"] = ms.astype(NPBF)
    cm4 = np.zeros((128, 4, 128), np.float32)
    for j in range(4):
        cm4[:, j, 32 * j:32 * j + 32] = 1.0
    c["cm4"] = cm4.astype(NPBF)
    rm4 = np.zeros((128, 4), np.float32)
    for p in range(128):
        rm4[p, p // 32] = 1.0
    c["rm4"] = rm4
    cms = np.zeros((128, 16, 64), np.float32)
    for j in range(16):
        cms[:, j, 4 * j:4 * j + 4] = 1.0
    c["cms"] = cms.astype(NPBF)
    rms = np.zeros((128, 16), np.float32)
    for p in range(64):
        rms[p, p // 4] = 1.0
    c["rms"] = rms
    tt = np.arange(512)
    c["rst_g"] = np.broadcast_to((tt % 128 != 0).astype(np.float32), (128, 512)).astype(NPBF)
    c["rst_h"] = np.broadcast_to((tt % 32 != 0).astype(np.float32), (128, 512)).astype(NPBF)
    c["rst_s"] = np.broadcast_to((np.arange(64) % 4 != 0).astype(np.float32), (128, 64)).astype(NPBF)
    lg = np.log1p(-np.exp2(-5.0 - np.arange(4, dtype=np.float32))).astype(np.float32)
    tq = np.arange(128, dtype=np.float32) + 1.0
    gq = np.exp(lg[:, None] * tq[None, :]).astype(np.float32)
    gk = (np.exp(-lg[:, None] * tq[None, :]) * (HK ** -0.5)).astype(np.float32)
    c["gq"] = np.broadcast_to(gq, (128, 4, 128)).copy()
    c["gk"] = np.broadcast_to(gk, (128, 4, 128)).copy()
    ts_ = (np.arange(64) % 4).astype(np.float32) + 1.0
    gqs = np.exp(lg[:, None] * ts_[None, :]).astype(np.float32)
    gks = (np.exp(-lg[:, None] * ts_[None, :]) * (HK ** -0.5)).astype(np.float32)
    c["gqs"] = np.broadcast_to(gqs, (128, 4, 64)).copy()
    c["gks"] = np.broadcast_to(gks, (128, 4, 64)).copy()
    c["elr"] = np.broadcast_to(np.exp(lg * 128.0).astype(np.float32), (128, 4)).copy()
    c["elrs"] = np.broadcast_to(np.exp(lg * 4.0).astype(np.float32), (128, 4)).copy()
    half = 64
    inv = (10000.0 ** (-np.arange(half, dtype=np.float32) / half)).astype(np.float32)
    invm = np.concatenate([inv, inv])
    sgn = np.concatenate([-np.ones(64, np.float32), np.ones(64, np.float32)])
    pos = np.arange(SEQ, dtype=np.float32)
    ang = (pos[None, :] * invm[:, None]).astype(np.float32)
    c["cos_p"] = np.cos(ang).astype(np.float32)
    c["sin_p"] = (np.sin(ang) * sgn[:, None]).astype(np.float32)
    poss = (PAST + (np.arange(64) % 4)).astype(np.float32)
    angs = (poss[None, :] * invm[:, None]).astype(np.float32)
    c["cos_s"] = np.cos(angs).astype(np.float32)
    c["sin_s"] = (np.sin(angs) * sgn[:, None]).astype(np.float32)
    return c


CONST_SPECS = [
    ("ident_f", [128, 128], F32), ("ident_b", [128, 128], BF16), ("pswap", [128, 128], BF16),
    ("ones_f", [128, 128], F32), ("mask1", [128, 128], BF16), ("mask4", [128, 128], BF16),
    ("masks", [128, 128], BF16), ("cm4", [128, 4, 128], BF16), ("rm4", [128, 4], F32),
    ("cms", [128, 16, 64], BF16), ("rms", [128, 16], F32), ("rst_g", [128, 512], BF16),
    ("rst_h", [128, 512], BF16), ("rst_s", [128, 64], BF16), ("gq", [128, 4, 128], F32),
    ("gk", [128, 4, 128], F32), ("gqs", [128, 4, 64], F32), ("gks", [128, 4, 64], F32),
    ("elr", [128, 4], F32), ("elrs", [128, 4], F32), ("cos_p", [128, HALF], F32),
    ("sin_p", [128, HALF], F32), ("isb", [128, 1], F32), ("cos_s", [128, 64], F32), ("sin_s", [128, 64], F32),
]


class Arena:
    def __init__(self, ap, words):
        self.ap = ap
        self.words = words
        self.off = 0

    def alloc(self, shape, dt):
        n = int(np.prod(shape[1:]))
        w = (n + 1) // 2 if dt == BF16 else n
        w = (w + 7) // 8 * 8
        assert self.off + w <= self.words, f"arena overflow {self.off}+{w}>{self.words}"
        v = self.ap[:, self.off:self.off + w]
        self.off += w
        if dt == BF16:
            v = v.bitcast(BF16)
        v = v[:, 0:n]
        if len(shape) == 3:
            v = v.rearrange("p (a b) -> p a b", a=shape[1])
        elif len(shape) == 4:
            v = v.rearrange("p (a b c) -> p a b c", a=shape[1], b=shape[2])
        if shape[0] < 128:
            v = v[0:shape[0]]
        return v


ARENA_WORDS = 49 * 1024


def build_program(debug=False, n_cores=N_CORES, no_cc=False):
    pairs = [[2 * i, 2 * i + 1] for i in range(n_cores // 2)]
    nc = bass.Bass("TRN2", target_bir_lowering=False)

    def din(name, shape, dt=F32):
        return nc.dram_tensor(name, list(shape), dt, kind="ExternalInput").ap()

    def dout(name, shape):
        return nc.dram_tensor(name, list(shape), F32, kind="ExternalOutput").ap()

    xp = din("xp", [HALF, D])
    xs = din("xs", [TSAMP, D])
    cvec = din("cvec", [17, D])
    st_in = [din(f"st_in{b}", [DEPTH, NSEQ, NH, HK, HV]) for b in range(3)]
    sconv_in = din("sconv_in", [DEPTH, NSEQ * 2, 2 * DFF])
    w_in = din("w_in", [DEPTH, D, NIN])
    gla_w_lr = din("gla_w_lr", [DEPTH, 16, 512])
    pvecs = din("pvecs", [96, 128])
    head_norm = din("head_norm", [DEPTH, 3 * HV])
    w_branch = din("w_branch", [DEPTH, 3, MV, D])
    w_out = din("w_out", [DEPTH, D, D])
    w_ada = din("w_ada", [DEPTH, D, 6 * D])
    b_ada = din("b_ada", [DEPTH, 96, 128])
    w_up = din("ffn_w_up", [DEPTH, D, 2 * DFF])
    convp = din("convp", [DEPTH, 4, 88, 128])
    w_down = din("ffn_w_down", [DEPTH, DFF, D])
    cst = {n: din("k_" + n, s, dt) for n, s, dt in CONST_SPECS}

    yp = dout("yp", [HALF, D])
    ys = dout("ys", [TSAMP, D])
    pst = [dout(f"pst{b}", [DEPTH, NH, HK, HV]) for b in range(3)]
    pconv = dout("pconv", [DEPTH, 2, 2 * DFF])
    sst = [dout(f"sst{b}", [DEPTH, NSEQ, NH, HK, HV]) for b in range(3)]
    sconv_o = dout("sconv_o", [DEPTH, NSEQ * 2, 2 * DFF])

    st = ExitStack()
    P = Prog(nc)
    dscr = lambda name, shape: nc.dram_tensor(name, list(shape), F32).ap()
    xsw = [dscr(f'xsw{sp}', [128, 16, TP]) for sp in range(NSP)]
    t_xsw = [[Tok() for _ in range(16)] for _ in range(NSP)]
    ph1st = [dscr(f'ph1st{b}', [NH, HK, HV]) for b in range(3)]
    t_ph1st = [Tok() for _ in range(3)]
    exs = [dscr(f'exs{l}', [3 * NH * HK, HV]) for l in range(DEPTH)]
    exd = [dscr(f'exd{l}', [2 * 3 * NH * HK, HV]) for l in range(DEPTH)]
    t_exs = [[Tok() for _ in range(3)] for _ in range(DEPTH)]
    t_exd = [Tok() for _ in range(DEPTH)]
    exs2 = [dscr(f'exs2_{l}', [128, 32]) for l in range(DEPTH)]
    exd2 = [dscr(f'exd2_{l}', [256, 32]) for l in range(DEPTH)]
    t_exs2 = [Tok() for _ in range(DEPTH)]
    t_exd2 = [Tok() for _ in range(DEPTH)]
    modsc = [dscr(f'modsc{l}', [128, 96 * 17]) for l in range(DEPTH)]
    t_modsc = [Tok() for _ in range(DEPTH)]
    t_ada_done = Tok('ada_done')
    arena_t = st.enter_context(nc.sbuf_tensor("arena", [128, ARENA_WORDS], F32))
    AR = Arena(arena_t[:], ARENA_WORDS)
    banks = [st.enter_context(nc.psum_tensor(f"pb{i}", [128, 512], F32)) for i in range(8)]
    bank_tok = [Tok(f"pb{i}") for i in range(8)]
    bank_bf = banks[7][:].bitcast(BF16)

    class Rot:
        def __init__(self, idx):
            self.idx = idx
            self.i = 0

        def get(self):
            k = self.idx[self.i % len(self.idx)]
            self.i += 1
            return banks[k][:], bank_tok[k]

    main_rot = Rot([0, 1, 2, 3])
    small_rot = Rot([4, 5, 6])

    V = lambda fn, r=(), w=(): P.add("dve", fn, r, w)
    A = lambda fn, r=(), w=(): P.add("act", fn, r, w)
    T = lambda fn, r=(), w=(): P.add("pe", fn, r, w)
    LD = lambda fn, r=(), w=(): P.add("sp", fn, r, w, dma=True)
    ST_ = lambda fn, r=(), w=(): P.add("act", fn, r, w, dma=True)

    def mm(out_ap, out_tok, pairs, reads):
        n = len(pairs)
        for i, (l, r) in enumerate(pairs):
            T(lambda e, l=l, r=r, i=i: e.matmul(out_ap, lhsT=l, rhs=r, start=(i == 0), stop=(i == n - 1)),
              reads, [out_tok])

    def act(out, in_, func, r, w, **kw):
        return A(lambda e: e.activation(out=out, in_=in_, func=func, **kw), r, w)

    def tt(out, a, b, op, r, w):
        return V(lambda e: e.tensor_tensor(out=out, in0=a, in1=b, op=op), r, w)

    def ts(out, a, s1, s2, op0, op1, r, w):
        return V(lambda e: e.tensor_scalar(out=out, in0=a, scalar1=s1, scalar2=s2, op0=op0, op1=op1), r, w)

    def stt(out, a, s, b, op0, op1, r, w):
        return V(lambda e: e.scalar_tensor_tensor(out=out, in0=a, scalar=s, in1=b, op0=op0, op1=op1), r, w)

    K = {}
    Kt = {}
    for n, s, dt in CONST_SPECS:
        if n in ("cos_p", "sin_p", "cos_s", "sin_s", "cms", "rms", "rst_s", "gqs", "gks", "elrs", "masks"):
            continue
        K[n] = AR.alloc(s, dt)
        Kt[n] = Tok(n)
        LD(lambda e, n=n: e.dma_start(out=K[n], in_=cst[n]), [], [Kt[n]])
    PV = AR.alloc([128, 96], F32)
    t_PV = Tok("PV")
    CW = [AR.alloc([128, 4, 88], F32) for _ in range(DEPTH)]
    t_CW = [Tok("CW") for _ in range(DEPTH)]
    HNW = [AR.alloc([128, 3 * HV], BF16) for _ in range(DEPTH)]
    t_HNW = [Tok("HNW") for _ in range(DEPTH)]
    GLW = [AR.alloc([16, 512], BF16) for _ in range(DEPTH)]
    t_GLW = [Tok("GLW") for _ in range(DEPTH)]
    LBV = AR.alloc([128, DEPTH, 3, 4], F32)
    t_LBV = Tok("LBV")
    NEGB = AR.alloc([128, DEPTH, 4], F32)
    MODP = [AR.alloc([128, 6, 16], F32) for _ in range(DEPTH)]
    t_MODP = [Tok("MODP") for _ in range(DEPTH)]
    CARRY = [AR.alloc([128, 88, 2], F32) for _ in range(DEPTH)]
    t_CARRY = [[Tok("carry") for _ in range(88)] for _ in range(DEPTH)]
    NWB = 3
    WB = [AR.alloc([128, 4096], BF16) for _ in range(NWB)]
    t_WB = [Tok(f"wb{i}") for i in range(NWB)]
    wb_i = [0]
    persist_mark = AR.off

    scr = {}
    scr_tok = {}
    cast_list = []

    def defblk(key, srcs, kc, ncols):
        t = nc.dram_tensor("s_" + "_".join(str(k) for k in key), [128, kc, ncols], BF16).ap()
        scr[key] = (t, kc, ncols)
        cast_list.append((key, srcs))

    for l in range(DEPTH):
        for b in range(3):
            oq, ok, ov, og = OFF[b]
            for hp in range(2):
                defblk(("k", l, b, hp), [(w_in[l, :, ok + 256 * hp: ok + 256 * hp + 256], 0)], 16, 256)
                defblk(("q", l, b, hp), [(w_in[l, :, oq + 256 * hp: oq + 256 * hp + 256], 0)], 16, 256)
            if b == 1:
                defblk(("glr", l), [(w_in[l, :, OFF_GLR:OFF_GLR + 16], 0)], 16, 16)
            for h in range(4):
                defblk(("v", l, b, h), [(w_in[l, :, ov + 256 * h: ov + 256 * h + 256], 0)], 16, 256)
            for h in range(4):
                defblk(("g", l, b, h), [(w_in[l, :, og + 256 * h: og + 256 * h + 256], 0)], 16, 256)
        for fp in range(8):
            for b in range(3):
                defblk(("br", l, b, fp), [(w_branch[l, b, :, 256 * fp:256 * fp + 256], 0)], 8, 256)
                c0 = OFF_MG + b * D + 256 * fp
                defblk(("mg", l, b, fp), [(w_in[l, :, c0:c0 + 256], 0)], 16, 256)
        for fp in range(8):
            defblk(("out", l, fp), [(w_out[l, :, 256 * fp:256 * fp + 256], 0)], 16, 256)
        for j in range(NFT):
            defblk(("up", l, j), [(w_up[l, :, 128 * j:128 * j + 128], 0),
                                  (w_up[l, :, DFF + 128 * j:DFF + 128 * j + 128], 128)], 16, 256)
        for f in range(16):
            for kh in range(2):
                defblk(("dn", l, f, kh), [(w_down[l, kh * 2816:(kh + 1) * 2816, 128 * f:128 * f + 128], 0)], 22, 128)

    def emit_casts(stage):
        for key, srcs in cast_list:
            st_key = ("ffn", key[1]) if key[0] in ("up", "dn") else ("mix", key[1])
            if st_key != stage:
                continue
            dst, kc, ncols = scr[key]
            scr_tok[key] = [Tok(str(key) + str(si)) for si in range(len(srcs))]
            for si, (src, co) in enumerate(srcs):
                n = src.shape[1]
                P.add("pool", lambda e, dst=dst, src=src, co=co, n=n: e.dma_start(
                    out=dst[:, :, co:co + n], in_=src.rearrange("(c p) n -> p c n", p=128)),
                    [t_ada_done], [scr_tok[key][si]], dma=True, wload=True)

    def load_w(key):
        dst_scr, kc, ncols = scr[key]
        i = wb_i[0] % NWB
        wb_i[0] += 1
        v = WB[i][:, 0:kc * ncols].rearrange("p (c n) -> p c n", c=kc)
        P.add("sp", lambda e: e.dma_start(out=v, in_=dst_scr), scr_tok[key], [t_WB[i]], dma=True, wload=True)
        return v, t_WB[i]

    def setup():
        m0 = AR.off
        stg = AR.alloc([128, 128], F32)
        t_stg = Tok("stg")
        LD(lambda e: e.dma_start(out=stg[0:96, :], in_=pvecs), [], [t_stg])
        pb, pt = small_rot.get()
        T(lambda e, pb=pb: e.transpose(pb[:, 0:96], stg[0:96, :], K["ident_f"][0:96, 0:96]), [t_stg, Kt["ident_f"]], [pt])
        act(PV, pb[:, 0:96], AF.Copy, [pt], [t_PV])
        for l in range(DEPTH):
            for j in range(4):
                LD(lambda e, l=l, j=j: e.dma_start(out=stg[0:88, :], in_=convp[l, j]), [], [t_stg])
                pb, pt = small_rot.get()
                T(lambda e, pb=pb: e.transpose(pb[:, 0:88], stg[0:88, :], K["ident_f"][0:88, 0:88]), [t_stg, Kt["ident_f"]], [pt])
                act(CW[l][:, j, :], pb[:, 0:88], AF.Copy, [pt], [t_CW[l]])
            hn32 = AR.alloc([128, 3 * HV], F32)
            t_hn = Tok("hn32")
            LD(lambda e, l=l, hn32=hn32: e.dma_start(out=hn32, in_=head_norm[l].partition_broadcast(128)), [], [t_hn])
            V(lambda e, l=l, hn32=hn32: e.tensor_copy(out=HNW[l], in_=hn32), [t_hn], [t_HNW[l]])
            gl32 = AR.alloc([16, 512], F32)
            t_gl = Tok("gl32")
            LD(lambda e, l=l, gl32=gl32: e.dma_start(out=gl32, in_=gla_w_lr[l]), [], [t_gl])
            V(lambda e, l=l, gl32=gl32: e.tensor_copy(out=GLW[l], in_=gl32), [t_gl], [t_GLW[l]])
        tmp = AR.alloc([128, 8, 4], F32)
        t_tmp = Tok("lbtmp")
        l0 = PV[:, 80:84]
        l1 = PV[:, 84:88]
        mx, e0, e1, sm, r, s0, s1, cs = [tmp[:, i, :] for i in range(8)]
        tt(mx, l0, l1, ALU.max, [t_PV], [t_tmp])
        tt(e0, l0, mx, ALU.subtract, [t_PV, t_tmp], [t_tmp])
        tt(e1, l1, mx, ALU.subtract, [t_PV, t_tmp], [t_tmp])
        act(e0, e0, AF.Exp, [t_tmp], [t_tmp])
        act(e1, e1, AF.Exp, [t_tmp], [t_tmp])
        tt(sm, e0, e1, ALU.add, [t_tmp], [t_tmp])
        V(lambda e: e.reciprocal(out=r, in_=sm), [t_tmp], [t_tmp])
        tt(s0, e0, r, ALU.mult, [t_tmp], [t_tmp])
        tt(s1, e1, r, ALU.mult, [t_tmp], [t_tmp])
        tt(cs, s0, s1, ALU.add, [t_tmp], [t_tmp])
        tt(LBV[:, 0, 0, :], s0, s0, ALU.subtract, [t_tmp], [t_LBV])
        tt(LBV[:, 1, 0, :], cs, s0, ALU.subtract, [t_tmp], [t_LBV])
        for l in range(DEPTH):
            ts(LBV[:, l, 1, :], LBV[:, l, 0, :], -1.0, 1.0, ALU.mult, ALU.add, [t_LBV], [t_LBV])
            ts(LBV[:, l, 2, :], LBV[:, l, 0, :], 1.0, -1.0, ALU.mult, ALU.add, [t_LBV], [t_LBV])
            ts(NEGB[:, l, :], PV[:, 88 + 4 * l:92 + 4 * l], -1.0, None, ALU.mult, ALU.bypass, [t_PV], [t_LBV])
        for l in range(DEPTH):
            V(lambda e, l=l: e.memset(CARRY[l], 0.0), [], t_CARRY[l])
        return m0

    def compute_mod(MOD, t_MOD):
        m0 = AR.off
        c_sb = AR.alloc([17, D], F32)
        s_sb = AR.alloc([17, D], F32)
        scT = AR.alloc([128, 16, 17], F32)
        badaT = AR.alloc([128, 96], F32)
        stg = AR.alloc([128, 128], F32)
        t_c, t_s, t_scT, t_ba, t_stg = Tok(), Tok(), Tok(), Tok(), Tok()
        LD(lambda e: e.dma_start(out=c_sb, in_=cvec), [], [t_c])
        act(s_sb, c_sb, AF.Sigmoid, [t_c], [t_s])
        tt(s_sb, s_sb, c_sb, ALU.mult, [t_c, t_s], [t_s])
        for c in range(16):
            pb, pt = small_rot.get()
            T(lambda e, c=c, pb=pb: e.transpose(pb[:, 0:17], s_sb[:, c * 128:(c + 1) * 128], K["ident_f"][0:17, 0:17]),
              [t_s, Kt["ident_f"]], [pt])
            act(scT[:, c, :], pb[:, 0:17], AF.Copy, [pt], [t_scT])
        for l in range(DEPTH):
            LD(lambda e, l=l: e.dma_start(out=stg[0:96, :], in_=b_ada[l]), [], [t_stg])
            pb, pt = small_rot.get()
            T(lambda e, pb=pb: e.transpose(pb[:, 0:96], stg[0:96, :], K["ident_f"][0:96, 0:96]), [t_stg, Kt["ident_f"]], [pt])
            act(badaT, pb[:, 0:96], AF.Copy, [pt], [t_ba])
            for fi in range(96):
                i = wb_i[0] % NWB
                wb_i[0] += 1
                wv = WB[i].bitcast(F32).rearrange("p (c n) -> p c n", c=16)
                P.add("sp", lambda e, l=l, fi=fi, wv=wv: e.dma_start(
                    out=wv, in_=w_ada[l, :, fi * 128:(fi + 1) * 128].rearrange("(c p) n -> p c n", p=128)),
                    [], [t_WB[i]] + ([t_ada_done] if (l == DEPTH - 1 and fi == 95) else []), dma=True, wload=True)
                pb, pt = small_rot.get()
                mm(pb[:, 0:17], pt, [(wv[:, c, :], scT[:, c, :]) for c in range(16)], [t_WB[i], t_scT])
                act(MOD[l][:, fi, :], pb[:, 0:17], AF.Identity, [pt, t_ba], [t_MOD[l]], bias=badaT[:, fi:fi + 1], scale=1.0)
            for (c0, pv0) in ((16, 16 * l), (64, 32 + 16 * l)):
                stt(MOD[l][:, c0:c0 + 16, :], MOD[l][:, c0:c0 + 16, :], 1.0,
                    PV[:, pv0:pv0 + 16].unsqueeze(2).broadcast_to([128, 16, 17]),
                    ALU.add, ALU.mult, [t_MOD[l], t_PV], [t_MOD[l]])
            for k, c0 in enumerate((16, 0, 32, 64, 48, 80)):
                V(lambda e, l=l, k=k, c0=c0: e.tensor_copy(out=MODP[l][:, k, :], in_=MOD[l][:, c0:c0 + 16, 16]),
                  [t_MOD[l]], [t_MODP[l]])
        return m0

    dbg_outs = {}

    def run_pass(kind, MOD=None, t_MOD=None):
        prompt = kind == "p"
        cx = {"sp": 0, "ph1": False, "store_x": False}
        Tn = TP if prompt else TSAMP
        tiles = [(i * 128, 128) for i in range(4)] if prompt else [(0, 64)]
        nt = len(tiles)
        m0 = AR.off
        xT = AR.alloc([128, 16, Tn], F32)
        t_xT = [Tok(f"xT{c}") for c in range(16)]
        hT = AR.alloc([128, 16, Tn], BF16)
        t_hT = [Tok(f"hT{c}") for c in range(16)]
        r1_words = max(NFT * Tn // 2, 16 * Tn + 2048) + 64
        r1_off = AR.off
        R1 = Arena(AR.ap[:, r1_off:r1_off + r1_words], r1_words)
        AR.off += r1_words
        uT = R1.alloc([128, 3, 8, Tn], BF16)
        v_tok = R1.alloc([128, nt, MV], BF16)
        sg_tok = R1.alloc([128, nt, MV], BF16)
        R1b = Arena(AR.ap[:, r1_off:r1_off + r1_words], r1_words)
        actT = R1b.alloc([128, NFT, Tn], BF16)
        R1c = Arena(AR.ap[:, r1_off:r1_off + r1_words], r1_words)
        yT = R1c.alloc([128, 16, Tn], F32)
        stg = R1c.alloc([128, D], F32)
        mT = Arena(AR.ap[:, r1_off + 3 * 8 * Tn // 2 + 0: r1_off + r1_words], r1_words).alloc([128, 16, Tn], BF16) \
            if prompt else AR.alloc([128, 16, Tn], BF16)
        t_v = [[Tok() for _ in range(4)] for _ in range(nt)]
        t_sg = [[Tok() for _ in range(4)] for _ in range(nt)]
        t_uT = [[Tok() for _ in range(nt)] for _ in range(3)]
        t_stgi = Tok("stgi")
        t_yT = Tok("yT")
        qT = AR.alloc([128, 4, Tn], BF16)
        kT = AR.alloc([128, 4, Tn], BF16)
        t_qT = [Tok() for _ in range(4)]
        t_kT = [Tok() for _ in range(4)]
        NTMP = 6
        TWS = [520] * 6 if prompt else [128, 128, 1040, 1040, 128, 128]
        tmp = [AR.alloc([128, TWS[i]], F32) for i in range(NTMP)]
        t_tmp = [Tok(f"tmp{i}") for i in range(NTMP)]
        acc = AR.alloc([128, 2, Tn], F32)
        t_acc = Tok("acc")
        rstd = AR.alloc([128, Tn], F32)
        t_rstd = Tok("rstd")
        COS = AR.alloc([128, Tn], F32)
        SIN = AR.alloc([128, Tn], F32)
        t_rope = Tok("rope")
        EL = AR.alloc([128, 4, 16], F32)
        t_EL = [Tok() for _ in range(4)]
        PT_ = [AR.alloc([128, 128], BF16) for _ in range(4)]
        t_PT = [Tok() for _ in range(4)]
        ktok = [AR.alloc([128, 128], BF16) for _ in range(4)]
        t_ktok = [Tok() for _ in range(4)]
        u_tok = AR.alloc([128, MV], BF16)
        t_utok = Tok("utok")
        kve = [AR.alloc([128, HV], F32) for _ in range(2)]
        t_kve = [Tok() for _ in range(2)]
        sq256 = [AR.alloc([128, HV], F32) for _ in range(2)]
        t_sq256 = [Tok() for _ in range(2)]
        ssv = AR.alloc([128, 8], F32)
        t_ssv = [Tok() for _ in range(4)]
        glrT = AR.alloc([16, Tn], BF16)
        t_glrT = Tok("glrT")
        if prompt:
            SF = AR.alloc([128, 4, HV], F32)
            Sbf = AR.alloc([128, 4, HV], BF16)
            t_SF = [Tok() for _ in range(4)]
            t_Sbf = [Tok() for _ in range(4)]
            KM = [AR.alloc([128, 4, 128], BF16) for _ in range(4)]
            QM = [AR.alloc([128, 4, 128], BF16) for _ in range(4)]
            t_KM = [Tok() for _ in range(4)]
            t_QM = [Tok() for _ in range(4)]
            cstage = stg[0:2, :]
            t_cstage = Tok("cstage")
            MASKS = {1: K["mask1"], 4: K["mask4"]}
            t_MASKS = {1: Kt["mask1"], 4: Kt["mask4"]}
            CM = K["cm4"]
            RM = K["rm4"]
            t_CMRM = [Kt["cm4"], Kt["rm4"]]
            RST = {1: K["rst_g"], 2: K["rst_h"]}
            t_RST = {1: Kt["rst_g"], 2: Kt["rst_h"]}
            GQ, GK, ELR = K["gq"], K["gk"], K["elr"]
            t_G = [Kt["gq"], Kt["gk"], Kt["elr"]]
            xtl = AR.alloc([128, 32], F32)
            t_xtl = Tok("xtl")
            xtin = AR.alloc([128, 32], F32)
            sqt = AR.alloc([128, 32], F32)
            rs2 = AR.alloc([128, 2], F32)
            h2t = AR.alloc([128, 16, 2], BF16)
            t_xtin, t_sqt, t_rs2, t_h2t = Tok(), Tok(), Tok(), Tok()
        else:
            r3_off = AR.off
            r3_words = 2 * NSEQ * HV + 64
            R3 = Arena(AR.ap[:, r3_off:r3_off + r3_words], r3_words)
            AR.off += r3_words
            SS = [R3.alloc([128, NSEQ, HV], F32) for _ in range(2)]
            R3b = Arena(AR.ap[:, r3_off:r3_off + r3_words], r3_words)
            t_SS = [Tok() for _ in range(2)]
            SSb = AR.alloc([128, NSEQ, HV], BF16)
            t_SSb = Tok()
            SSo = AR.alloc([128, NSEQ, HV], F32)
            t_SSo = Tok()
            KM = [AR.alloc([64, 16, 128], BF16)]
            QM = [AR.alloc([128, 16, 64], BF16)]
            t_KM = [Tok()]
            t_QM = [Tok()]
            msk = AR.alloc([128, 128], BF16)
            cms = AR.alloc([128, 16, 64], BF16)
            rms = AR.alloc([128, 16], F32)
            rsts = AR.alloc([128, 64], BF16)
            gqs = AR.alloc([128, 4, 64], F32)
            gks = AR.alloc([128, 4, 64], F32)
            elrs = AR.alloc([128, 4], F32)
            t_sc = Tok("sconst")
            for dst, nm in ((msk, "masks"), (cms, "cms"), (rms, "rms"), (rsts, "rst_s"), (gqs, "gqs"), (gks, "gks"),
                            (elrs, "elrs"), (COS, "cos_s"), (SIN, "sin_s")):
                LD(lambda e, dst=dst, nm=nm: e.dma_start(out=dst, in_=cst[nm]), [], [t_sc if nm not in ("cos_s", "sin_s") else t_rope])
            MASKS = {16: msk}
            t_MASKS = {16: t_sc}
            CM, RM = cms, rms
            t_CMRM = [t_sc]
            RST = {1: rsts, 2: rsts}
            t_RST = {1: t_sc, 2: t_sc}
            GQ, GK, ELR = gqs, gks, elrs
            t_G = [t_sc]
            convT = AR.alloc([128, 88, 32], F32)
            t_convT = Tok("convT")
            outc = [R3b.alloc([32, 2816], F32) for _ in range(2)]
            t_outc = [Tok("outc0"), Tok("outc1")]
            cstg = R3b.alloc([32, 2048], F32)
            t_cstg = Tok("cstg")
            U6 = [AR.alloc([128, NSEQ, 6], F32) for _ in range(2)]
            t_U6 = [Tok() for _ in range(2)]

        def load_tokens(xsrc):
            for (t0, rows) in tiles:
                LD(lambda e, t0=t0, rows=rows: e.dma_start(out=stg[0:rows, :], in_=xsrc[t0:t0 + rows, :]), [], [t_stgi])
                for g in range(4):
                    pb, pt = main_rot.get()
                    for cc in range(4):
                        c = g * 4 + cc
                        T(lambda e, pb=pb, cc=cc, c=c, rows=rows: e.transpose(
                            pb[:, cc * rows:(cc + 1) * rows], stg[0:rows, c * 128:(c + 1) * 128], K["ident_f"][0:rows, 0:rows]),
                          [t_stgi, Kt["ident_f"]], [pt])
                    act(xT[:, g * 4:g * 4 + 4, t0:t0 + rows], pb[:, 0:4 * rows].rearrange("p (a b) -> p a b", a=4),
                        AF.Copy, [pt], t_xT[g * 4:g * 4 + 4])

        def store_x(sp, c):
            ST_(lambda e: e.dma_start(out=xsw[sp][:, c, :], in_=xT[:, c, :]), [t_xT[c]], [t_xsw[sp][c]])

        def load_x(sp):
            for c in range(16):
                LD(lambda e, c=c: e.dma_start(out=xT[:, c, :], in_=xsw[sp][:, c, :]), [t_xsw[sp][c]], [t_xT[c]])

        def load_rope(sp):
            LD(lambda e: e.dma_start(out=COS, in_=cst["cos_p"][:, sp * TP:(sp + 1) * TP]), [], [t_rope])
            LD(lambda e: e.dma_start(out=SIN, in_=cst["sin_p"][:, sp * TP:(sp + 1) * TP]), [], [t_rope])

        if prompt:
            for sp in range(NSP):
                load_tokens(xp[sp * TP:(sp + 1) * TP])
                for c in range(16):
                    store_x(sp, c)
        else:
            load_tokens(xs)
        P.fence()

        def rms_rstd():
            pss, pst_ = small_rot.get()
            for c in range(16):
                k = c % 2
                act(tmp[k][:, 0:Tn], xT[:, c, :], AF.Square, [t_xT[c]], [t_tmp[k]])
                T(lambda e, c=c, k=k: e.matmul(pss[:, 0:Tn], lhsT=K["ones_f"], rhs=tmp[k][:, 0:Tn], start=(c == 0), stop=(c == 15)),
                  [t_tmp[k], Kt["ones_f"]], [pst_])
            ts(rstd, pss[:, 0:Tn], 1.0 / D, EPS, ALU.mult, ALU.add, [pst_], [t_rstd])
            act(rstd, rstd, AF.Sqrt, [t_rstd], [t_rstd])
            V(lambda e: e.reciprocal(out=rstd, in_=rstd), [t_rstd], [t_rstd])

        def modulated_norm(l, which):
            rms_rstd()
            if prompt:
                ka, kb = (0, 1) if which == 1 else (3, 4)
                for c in range(16):
                    k = 2 + c % 2
                    stt(tmp[k][:, 0:Tn], xT[:, c, :], MODP[l][:, ka, c:c + 1], rstd, ALU.mult, ALU.mult,
                        [t_xT[c], t_MODP[l], t_rstd], [t_tmp[k]])
                    act(hT[:, c, :], tmp[k][:, 0:Tn], AF.Identity, [t_tmp[k], t_MODP[l]], [t_hT[c]],
                        bias=MODP[l][:, kb, c:c + 1], scale=1.0)
            else:
                ca, cb = (16, 0) if which == 1 else (64, 48)
                t1 = tmp[2][:, 0:1024].rearrange("p (c t) -> p c t", c=16)
                tt(t1, xT, rstd.unsqueeze(1).broadcast_to([128, 16, Tn]), ALU.mult, t_xT + [t_rstd], [t_tmp[2]])
                t1v = tmp[2][:, 0:1024].rearrange("p (c s t) -> p c s t", c=16, s=NSEQ)
                t2v = tmp[3][:, 0:1024].rearrange("p (c s t) -> p c s t", c=16, s=NSEQ)
                Av = MOD[l][:, ca:ca + 16, 0:NSEQ].unsqueeze(3).broadcast_to([128, 16, NSEQ, TS])
                Bv = MOD[l][:, cb:cb + 16, 0:NSEQ].unsqueeze(3).broadcast_to([128, 16, NSEQ, TS])
                tt(t2v, t1v, Av, ALU.mult, [t_tmp[2], t_MOD[l]], [t_tmp[3]])
                tt(hT.rearrange("p c (s t) -> p c s t", s=NSEQ), t2v, Bv, ALU.add, [t_tmp[3], t_MOD[l]], t_hT)

        def residual_add(l, which, f, pm, pmt):
            if prompt:
                kg = 2 if which == 1 else 5
                stt(xT[:, f, :], pm[:, 0:Tn], MODP[l][:, kg, f:f + 1], xT[:, f, :], ALU.mult, ALU.add,
                    [pmt, t_MODP[l], t_xT[f]], [t_xT[f]])
                store_x(cx["sp"], f)
            else:
                cg = 32 if which == 1 else 80
                Gv = MOD[l][:, cg + f, 0:NSEQ].unsqueeze(2).broadcast_to([128, NSEQ, TS])
                tv = tmp[4][:, 0:Tn].rearrange("p (s t) -> p s t", s=NSEQ)
                tt(tv, pm[:, 0:Tn].rearrange("p (s t) -> p s t", s=NSEQ), Gv, ALU.mult, [pmt, t_MOD[l]], [t_tmp[4]])
                tt(xT[:, f, :], xT[:, f, :], tmp[4][:, 0:Tn], ALU.add, [t_tmp[4], t_xT[f]], [t_xT[f]])

        nbt = {0: 1, 1: 1, 2: 4} if prompt else {0: 16, 1: 16, 2: 16}
        bsz = {b: tiles[0][1] // nbt[b] for b in range(3)}

        def mixer_branch(l, b):
            nb = nbt[b]
            bs = bsz[b]
            ph1 = cx["ph1"]
            sp = cx["sp"]
            if b == 1:
                wg_, wgt = load_w(("glr", l))
                pb, pt = small_rot.get()
                mm(pb[0:16, 0:Tn], pt, [(wg_[:, c, 0:16], hT[:, c, :]) for c in range(16)], [wgt] + t_hT)
                act(glrT, pb[0:16, 0:Tn], AF.Copy, [pt], [t_glrT])
            for hp in range(2):
                kb_, kbt = load_w(("k", l, b, hp))
                if not ph1:
                    qb_, qbt = load_w(("q", l, b, hp))
                for hh in range(2):
                    h = hp * 2 + hh
                    pk, pkt = main_rot.get()
                    mm(pk[:, 0:Tn], pkt, [(kb_[:, c, hh * 128:(hh + 1) * 128], hT[:, c, :]) for c in range(16)], [kbt] + t_hT)
                    X = [tmp[i][:, 0:Tn] for i in range(NTMP)]
                    if b == 0:
                        def rope_path(pz, pzt, G, dstT, dtok):
                            raw = tmp[0].bitcast(BF16)[:, 0:Tn]
                            act(raw, pz[:, 0:Tn], AF.Copy, [pzt], [t_tmp[0]])
                            psw, pswt = small_rot.get()
                            T(lambda e: e.matmul(psw[:, 0:Tn], lhsT=K["pswap"], rhs=raw, start=True, stop=True),
                              [t_tmp[0], Kt["pswap"]], [pswt])
                            tt(X[1], raw, COS, ALU.mult, [t_tmp[0], t_rope], [t_tmp[1]])
                            tt(X[2], psw[:, 0:Tn], SIN, ALU.mult, [pswt, t_rope], [t_tmp[2]])
                            tt(X[3], X[1], X[2], ALU.add, [t_tmp[1], t_tmp[2]], [t_tmp[3]])
                            if prompt:
                                Gv = G[:, h, :].unsqueeze(1).broadcast_to([128, nt, 128])
                                tt(dstT[:, h, :].rearrange("p (i t) -> p i t", i=nt), X[3].rearrange("p (i t) -> p i t", i=nt),
                                   Gv, ALU.mult, [t_tmp[3]] + t_G, [dtok[h]])
                            else:
                                tt(dstT[:, h, :], X[3], G[:, h, :], ALU.mult, [t_tmp[3]] + t_G, [dtok[h]])
                        rope_path(pk, pkt, GK, kT, t_kT)
                        if not ph1:
                            pq, pqt = main_rot.get()
                            mm(pq[:, 0:Tn], pqt, [(qb_[:, c, hh * 128:(hh + 1) * 128], hT[:, c, :]) for c in range(16)], [qbt] + t_hT)
                            rope_path(pq, pqt, GQ, qT, t_qT)
                    else:
                        if b == 1:
                            px, pxt = small_rot.get()
                            T(lambda e, h=h, px=px: e.matmul(px[:, 0:Tn], lhsT=GLW[l][:, h * 128:(h + 1) * 128], rhs=glrT, start=True, stop=True),
                              [t_GLW[l], t_glrT], [pxt])
                            act(X[0], px[:, 0:Tn], AF.Exp, [pxt, t_LBV], [t_tmp[0]], bias=NEGB[:, l, h:h + 1], scale=-1.0)
                            act(X[1], X[0], AF.Ln, [t_tmp[0]], [t_tmp[1]], bias=1.0, scale=1.0)
                            sc_q, sc_k = -1.0 / 16.0, 1.0 / 16.0
                        else:
                            act(X[0], pk[:, 0:Tn], AF.Sigmoid, [pkt], [t_tmp[0]])
                            act(X[1], X[0], AF.Ln, [t_tmp[0], t_LBV], [t_tmp[1]],
                                bias=LBV[:, l, 0, h:h + 1], scale=LBV[:, l, 1, h:h + 1])
                            sc_q, sc_k = 1.0, -1.0
                        V(lambda e, b=b: e.tensor_tensor_scan(out=X[2], data0=RST[b][:, 0:Tn], data1=X[1], initial=0.0,
                                                              op0=ALU.mult, op1=ALU.add), [t_tmp[1], t_RST[b]], [t_tmp[2]])
                        act(X[3], X[2], AF.Exp, [t_tmp[2]], [t_tmp[3]], scale=sc_q)
                        act(X[4], X[2], AF.Exp, [t_tmp[2]], [t_tmp[4]], scale=sc_k)
                        nblk = Tn // bs
                        A(lambda e, h=h, bs=bs, nblk=nblk: e.activation(out=EL[:, h, 0:nblk], in_=tmp[3][:, bs - 1:Tn:bs], func=AF.Copy),
                          [t_tmp[3]], [t_EL[h]])
                        if b == 1:
                            tt(kT[:, h, :], pk[:, 0:Tn], X[4], ALU.mult, [pkt, t_tmp[4]], [t_kT[h]])
                        else:
                            ts(X[5], X[0], LBV[:, l, 2, h:h + 1], LBV[:, l, 1, h:h + 1], ALU.mult, ALU.add, [t_tmp[0], t_LBV], [t_tmp[5]])
                            tt(kT[:, h, :], X[5], X[4], ALU.mult, [t_tmp[5], t_tmp[4]], [t_kT[h]])
                        if ph1:
                            continue
                        pq, pqt = main_rot.get()
                        mm(pq[:, 0:Tn], pqt, [(qb_[:, c, hh * 128:(hh + 1) * 128], hT[:, c, :]) for c in range(16)], [qbt] + t_hT)
                        if b == 1:
                            stt(qT[:, h, :], pq[:, 0:Tn], HK ** -0.5, X[3], ALU.mult, ALU.mult, [pqt, t_tmp[3]], [t_qT[h]])
                        else:
                            act(X[0], pq[:, 0:Tn], AF.Sigmoid, [pqt], [t_tmp[0]])
                            tt(X[1], pq[:, 0:Tn], X[0], ALU.mult, [pqt, t_tmp[0]], [t_tmp[1]])
                            tt(qT[:, h, :], X[1], X[3], ALU.mult, [t_tmp[1], t_tmp[3]], [t_qT[h]])
            for h in range(4):
                wv_, wvt = load_w(("v", l, b, h))
                for i, (t0, rows) in enumerate(tiles):
                    pv, pvt = main_rot.get()
                    mm(pv[0:rows, 0:256], pvt, [(hT[:, c, t0:t0 + rows], wv_[:, c, :]) for c in range(16)], [wvt] + t_hT)
                    act(v_tok[0:rows, i, h * 256:(h + 1) * 256], pv[0:rows, 0:256], AF.Copy, [pvt], [t_v[i][h]])
            for h in range(0 if not ph1 else 4, 4):
                wg_, wgt = load_w(("g", l, b, h))
                for i, (t0, rows) in enumerate(tiles):
                    pg, pgt = main_rot.get()
                    mm(pg[0:rows, 0:256], pgt, [(hT[:, c, t0:t0 + rows], wg_[:, c, :]) for c in range(16)], [wgt] + t_hT)
                    k = h % 2
                    act(sq256[k][0:rows, :], pg[0:rows, 0:256], AF.Sigmoid, [pgt], [t_sq256[k]])
                    tt(sq256[k][0:rows, :], pg[0:rows, 0:256], sq256[k][0:rows, :], ALU.mult, [pgt, t_sq256[k]], [t_sq256[k]])
                    tt(sg_tok[0:rows, i, h * 256:(h + 1) * 256], sq256[k][0:rows, :], HNW[l][0:rows, b * HV:(b + 1) * HV], ALU.mult,
                       [t_sq256[k], t_HNW[l]], [t_sg[i][h]])
            def el_col(h, blk):
                if b == 0:
                    return ELR[:, h:h + 1], t_G
                return EL[:, h, blk:blk + 1], [t_EL[h]]

            if prompt:
                if ph1:
                    if sp == 0:
                        V(lambda e: e.memset(SF, 0.0), [], t_SF)
                    else:
                        LD(lambda e: e.dma_start(out=SF, in_=ph1st[b].rearrange("h k v -> k h v")), [t_ph1st[b]], t_SF)
                elif sp == 0:
                    LD(lambda e: e.dma_start(out=SF, in_=exd[l][b * 512:(b + 1) * 512, :].rearrange("(h k) v -> k h v", k=HK)), [t_exd[l]], t_SF)
                    ts(SF, SF, K["isb"][:, 0:1], None, ALU.mult, ALU.bypass, t_SF + [Kt["isb"]], t_SF)
                else:
                    LD(lambda e: e.dma_start(out=SF, in_=pst[b][l].rearrange("h k v -> k h v")), [t_pst[b][l]], t_SF)
                if not ph1:
                    act(Sbf, SF, AF.Copy, t_SF, t_Sbf)
            hgroups = [[0, 1, 2, 3]] if prompt else [[0], [1], [2], [3]]
            ss_i = [0]
            for i, (t0, C) in enumerate(tiles):
                for hg in hgroups:
                    ctx = {}
                    for h in hg:
                        if not prompt:
                            k = ss_i[0] % 2
                            ss_i[0] += 1
                            LD(lambda e, h=h, k=k: e.dma_start(out=SS[k], in_=st_in[b][l, :, h].rearrange("s k v -> k s v")), [], [t_SS[k]])
                            V(lambda e, k=k: e.tensor_copy(out=SSb[:, 0:8, :], in_=SS[k][:, 0:8, :]), [t_SS[k]], [t_SSb])
                            act(SSb[:, 8:16, :], SS[k][:, 8:16, :], AF.Copy, [t_SS[k]], [t_SSb])
                            ctx[h] = k
                        qt_ = qT[:, h, t0:t0 + C]
                        kt_ = kT[:, h, t0:t0 + C]
                        k2 = h
                        if not ph1:
                            psc, psct = small_rot.get()
                            T(lambda e, psc=psc, kt_=kt_, qt_=qt_, C=C: e.matmul(psc[0:C, 0:C], lhsT=kt_, rhs=qt_, start=True, stop=True),
                              [t_kT[h], t_qT[h]], [psct])
                            tt(PT_[k2][0:C, 0:C], psc[0:C, 0:C], MASKS[nb][0:C, 0:C], ALU.mult, [psct, t_MASKS[nb]], [t_PT[k2]])
                        T(lambda e, kt_=kt_, C=C: e.transpose(bank_bf[0:C, 0:128], kt_, K["ident_b"]), [t_kT[h], Kt["ident_b"]], [bank_tok[7]])
                        act(ktok[k2][0:C, :], bank_bf[0:C, 0:128], AF.Copy, [bank_tok[7]], [t_ktok[k2]])
                        if nb > 1:
                            km = KM[k2 % len(KM)]
                            qm = QM[k2 % len(QM)]
                            tkm = t_KM[k2 % len(KM)]
                            tqm = t_QM[k2 % len(QM)]
                            tt(km[0:C, 0:nb, :], ktok[k2][0:C, :].unsqueeze(1).broadcast_to([C, nb, 128]),
                               RM[0:C, 0:nb].unsqueeze(2).broadcast_to([C, nb, 128]), ALU.mult, [t_ktok[k2]] + t_CMRM, [tkm])
                            if not ph1:
                                tt(qm[:, 0:nb, 0:C], qt_.unsqueeze(1).broadcast_to([128, nb, C]), CM[:, 0:nb, 0:C], ALU.mult,
                                   [t_qT[h]] + t_CMRM, [tqm])
                    po = {}
                    for hi, h in enumerate(hg):
                        po[h] = (banks[hi][:], bank_tok[hi])
                    for j in range(nb):
                        for hi, h in enumerate(hg):
                            k2 = h
                            pob, pot = po[h]
                            vt = v_tok[0:C, i, h * 256:(h + 1) * 256]
                            qt_ = qT[:, h, t0:t0 + C]
                            if j == 0 and not ph1:
                                T(lambda e, pob=pob, k2=k2, vt=vt, C=C: e.matmul(pob[0:C, 0:256], lhsT=PT_[k2][0:C, 0:C], rhs=vt, start=True, stop=False),
                                  [t_PT[k2], t_v[i][h]], [pot])
                            lhs = qt_ if nb == 1 else QM[k2 % len(QM)][:, j, 0:C]
                            lt = [t_qT[h]] if nb == 1 else [t_QM[k2 % len(QM)]]
                            if prompt:
                                srhs, srt = Sbf[:, h, :], [t_Sbf[h]]
                            else:
                                srhs, srt = SSb[:, j, :], [t_SSb]
                            if not ph1:
                                T(lambda e, pob=pob, lhs=lhs, srhs=srhs, C=C, j=j: e.matmul(pob[0:C, 0:256], lhsT=lhs, rhs=srhs, start=False, stop=(j == nb - 1)),
                                  lt + srt, [pot])
                            pkv, pkvt = small_rot.get()
                            klhs = ktok[k2][0:C, :] if nb == 1 else KM[k2 % len(KM)][0:C, j, :]
                            klt = [t_ktok[k2]] if nb == 1 else [t_KM[k2 % len(KM)]]
                            T(lambda e, pkv=pkv, klhs=klhs, vt=vt: e.matmul(pkv[:, 0:256], lhsT=klhs, rhs=vt, start=True, stop=True),
                              klt + [t_v[i][h]], [pkvt])
                            blk = i * nb + j
                            elc, elt = el_col(h, blk)
                            kk = (j + hi) % 2
                            act(kve[kk], pkv[:, 0:256], AF.Identity, [pkvt] + elt, [t_kve[kk]], scale=elc)
                            if prompt:
                                stt(SF[:, h, :], SF[:, h, :], elc, kve[kk], ALU.mult, ALU.add, [t_SF[h], t_kve[kk]] + elt, [t_SF[h]])
                                if not ph1:
                                    act(Sbf[:, h, :], SF[:, h, :], AF.Copy, [t_SF[h]], [t_Sbf[h]])
                            else:
                                stt(SSo[:, j, :], SS[ctx[h]][:, j, :], elc, kve[kk], ALU.mult, ALU.add, [t_SS[ctx[h]], t_kve[kk]] + elt, [t_SSo])
                    for hi, h in enumerate(hg):
                        if ph1:
                            continue
                        pob, pot = po[h]
                        k2 = h % 2
                        act(sq256[k2][0:C, :], pob[0:C, 0:256], AF.Square, [pot], [t_sq256[k2]])
                        V(lambda e, k2=k2, h=h, C=C: e.tensor_reduce(out=ssv[0:C, h:h + 1], in_=sq256[k2][0:C, :], axis=AX.X, op=ALU.add),
                          [t_sq256[k2]], [t_ssv[h]])
                        ts(ssv[0:C, h:h + 1], ssv[0:C, h:h + 1], 1.0 / HV, EPS, ALU.mult, ALU.add, [t_ssv[h]], [t_ssv[h]])
                        act(ssv[0:C, h:h + 1], ssv[0:C, h:h + 1], AF.Sqrt, [t_ssv[h]], [t_ssv[h]])
                        V(lambda e, h=h, C=C: e.reciprocal(out=ssv[0:C, h:h + 1], in_=ssv[0:C, h:h + 1]), [t_ssv[h]], [t_ssv[h]])
                        stt(u_tok[0:C, h * 256:(h + 1) * 256], pob[0:C, 0:256], ssv[0:C, h:h + 1], sg_tok[0:C, i, h * 256:(h + 1) * 256],
                            ALU.mult, ALU.mult, [pot, t_ssv[h], t_sg[i][h]], [t_utok])
                        if not prompt:
                            ST_(lambda e, h=h: e.dma_start(out=sst[b][l, :, h].rearrange("s k v -> k s v"), in_=SSo), [t_SSo], [t_sst])
                            outs_final.append(P.streams["act"][-1])
                for c in range(8 if not ph1 else 0):
                    T(lambda e, c=c, C=C: e.transpose(bank_bf[:, c * C:(c + 1) * C], u_tok[0:C, c * 128:(c + 1) * 128], K["ident_b"][0:C, 0:C]),
                      [t_utok, Kt["ident_b"]], [bank_tok[7]])
                if not ph1:
                    act(uT[:, b, :, t0:t0 + C], bank_bf[:, 0:8 * C].rearrange("p (c t) -> p c t", c=8), AF.Copy, [bank_tok[7]], [t_uT[b][i]])
            if prompt:
                if ph1 and sp == 0:
                    ST_(lambda e: e.dma_start(out=ph1st[b].rearrange("h k v -> k h v"), in_=SF), t_SF, [t_ph1st[b]])
                elif ph1:
                    ST_(lambda e: e.dma_start(out=exs[l][b * 512:(b + 1) * 512, :].rearrange("(h k) v -> k h v", k=HK), in_=SF), t_SF, [t_exs[l][b]])
                else:
                    ST_(lambda e: e.dma_start(out=pst[b][l].rearrange("h k v -> k h v"), in_=SF), t_SF, [t_pst[b][l]])
                    if sp == NSP - 1:
                        outs_final.append(P.streams["act"][-1])

        def merge_and_out(l):
            P.fence()
            for fp in range(8):
                for b in range(3):
                    wbr, wbrt = load_w(("br", l, b, fp))
                    wmg, wmgt = load_w(("mg", l, b, fp))
                    for ft in range(2):
                        f = fp * 2 + ft
                        po_, pot = main_rot.get()
                        mm(po_[:, 0:Tn], pot, [(wbr[:, c, ft * 128:(ft + 1) * 128], uT[:, b, c, :]) for c in range(8)], [wbrt] + t_uT[b])
                        pg, pgt = main_rot.get()
                        mm(pg[:, 0:Tn], pgt, [(wmg[:, c, ft * 128:(ft + 1) * 128], hT[:, c, :]) for c in range(16)], [wmgt] + t_hT)
                        k = ft
                        act(tmp[k][:, 0:Tn], pg[:, 0:Tn], AF.Sigmoid, [pgt], [t_tmp[k]])
                        if b == 0:
                            tt(acc[:, ft, :], po_[:, 0:Tn], tmp[k][:, 0:Tn], ALU.mult, [pot, t_tmp[k]], [t_acc])
                        elif b == 1:
                            tt(tmp[k][:, 0:Tn], po_[:, 0:Tn], tmp[k][:, 0:Tn], ALU.mult, [pot, t_tmp[k]], [t_tmp[k]])
                            tt(acc[:, ft, :], acc[:, ft, :], tmp[k][:, 0:Tn], ALU.add, [t_tmp[k], t_acc], [t_acc])
                        else:
                            tt(tmp[k][:, 0:Tn], po_[:, 0:Tn], tmp[k][:, 0:Tn], ALU.mult, [pot, t_tmp[k]], [t_tmp[k]])
                            tt(mT[:, f, :], acc[:, ft, :], tmp[k][:, 0:Tn], ALU.add, [t_tmp[k], t_acc], [t_mT])
            for fp in range(8):
                wo, wot = load_w(("out", l, fp))
                for ft in range(2):
                    f = fp * 2 + ft
                    pm, pmt = main_rot.get()
                    mm(pm[:, 0:Tn], pmt, [(wo[:, c, ft * 128:(ft + 1) * 128], mT[:, c, :]) for c in range(16)], [wot, t_mT])
                    residual_add(l, 1, f, pm, pmt)
            if prompt and cx["sp"] == NSP - 1:
                V(lambda e: e.tensor_copy(out=xtl[:].rearrange("p (c t) -> p c t", t=2), in_=xT[:, :, Tn - 2:Tn]), t_xT, [t_xtl])
                ST_(lambda e: e.dma_start(out=exs2[l], in_=xtl), [t_xtl], [t_exs2[l]])
            P.fence()

        def ffn(l):
            modulated_norm(l, 2)
            P.fence()
            if prompt and cx["sp"] == 0:
                LD(lambda e: e.dma_start(out=xtin, in_=exd2[l][0:128, :]), [t_exd2[l]], [t_xtin])
                act(sqt, xtin, AF.Square, [t_xtin], [t_sqt])
                pss2, pss2t = small_rot.get()
                for c in range(16):
                    T(lambda e, c=c, pss2=pss2: e.matmul(pss2[:, 0:2], lhsT=K["ones_f"], rhs=sqt[:, 2 * c:2 * c + 2], start=(c == 0), stop=(c == 15)),
                      [t_sqt, Kt["ones_f"]], [pss2t])
                ts(rs2, pss2[:, 0:2], 1.0 / D, EPS, ALU.mult, ALU.add, [pss2t], [t_rs2])
                act(rs2, rs2, AF.Sqrt, [t_rs2], [t_rs2])
                V(lambda e: e.reciprocal(out=rs2, in_=rs2), [t_rs2], [t_rs2])
                for c in range(16):
                    stt(sqt[:, 2 * c:2 * c + 2], xtin[:, 2 * c:2 * c + 2], MODP[l][:, 3, c:c + 1], rs2, ALU.mult, ALU.mult,
                        [t_xtin, t_MODP[l], t_rs2, t_sqt], [t_sqt])
                    act(h2t[:, c, :], sqt[:, 2 * c:2 * c + 2], AF.Identity, [t_sqt, t_MODP[l]], [t_h2t], bias=MODP[l][:, 4, c:c + 1], scale=1.0)
            if not prompt:
                for g in range(6):
                    ncol = min(2048, 2 * DFF - g * 2048)
                    LD(lambda e, g=g, ncol=ncol: e.dma_start(out=cstg[:, 0:ncol], in_=sconv_in[l, :, g * 2048:g * 2048 + ncol]), [], [t_cstg])
                    for q4 in range(0, ncol // 128, 4):
                        pb, pt = small_rot.get()
                        for cc in range(4):
                            T(lambda e, pb=pb, cc=cc, q4=q4: e.transpose(pb[:, cc * 32:(cc + 1) * 32], cstg[:, (q4 + cc) * 128:(q4 + cc + 1) * 128],
                                                                          K["ident_f"][0:32, 0:32]), [t_cstg, Kt["ident_f"]], [pt])
                        act(convT[:, g * 16 + q4:g * 16 + q4 + 4, :], pb[:, 0:128].rearrange("p (a b) -> p a b", a=4), AF.Copy, [pt], [t_convT])
            for j in range(NFT):
                wu, wut = load_w(("up", l, j))
                res = []
                for half in range(2):
                    tile_idx = half * NFT + j
                    pu, put = main_rot.get()
                    mm(pu[:, 0:Tn], put, [(wu[:, c, half * 128:(half + 1) * 128], hT[:, c, :]) for c in range(16)], [wut] + t_hT)
                    o3 = 3 * half
                    w0 = CW[l][:, 0, tile_idx:tile_idx + 1]
                    w1 = CW[l][:, 1, tile_idx:tile_idx + 1]
                    w2 = CW[l][:, 2, tile_idx:tile_idx + 1]
                    cb = CW[l][:, 3, tile_idx:tile_idx + 1]
                    if prompt:
                        U = tmp[o3]
                        tc_ = t_CARRY[l][tile_idx]
                        if cx["sp"] == 0:
                            pu2, pu2t = small_rot.get()
                            mm(pu2[:, 0:2], pu2t, [(wu[:, c, half * 128:(half + 1) * 128], h2t[:, c, :]) for c in range(16)], [wut, t_h2t])
                            ts(U[:, 0:2], pu2[:, 0:2], K["isb"][:, 0:1], None, ALU.mult, ALU.bypass, [pu2t, Kt["isb"]], [t_tmp[o3]])
                        else:
                            V(lambda e, U=U, tile_idx=tile_idx: e.tensor_copy(out=U[:, 0:2], in_=CARRY[l][:, tile_idx, :]), [tc_], [t_tmp[o3]])
                        act(U[:, 2:2 + Tn], pu[:, 0:Tn], AF.Copy, [put], [t_tmp[o3]])
                        V(lambda e, U=U, tile_idx=tile_idx: e.tensor_copy(out=CARRY[l][:, tile_idx, :], in_=U[:, Tn:Tn + 2]), [t_tmp[o3]], [tc_])
                        c1, c2 = tmp[o3 + 1][:, 0:Tn], tmp[o3 + 2][:, 0:Tn]
                        ts(c1, U[:, 0:Tn], w0, cb, ALU.mult, ALU.add, [t_tmp[o3], t_CW[l]], [t_tmp[o3 + 1]])
                        stt(c2, U[:, 1:Tn + 1], w1, c1, ALU.mult, ALU.add, [t_tmp[o3], t_tmp[o3 + 1], t_CW[l]], [t_tmp[o3 + 2]])
                        stt(c1, U[:, 2:Tn + 2], w2, c2, ALU.mult, ALU.add, [t_tmp[o3], t_tmp[o3 + 2], t_CW[l]], [t_tmp[o3 + 1]])
                        res.append((c1, t_tmp[o3 + 1], tmp[o3 + 2][:, 0:Tn], t_tmp[o3 + 2]))
                    else:
                        U = U6[half]
                        tu = t_U6[half]
                        V(lambda e, U=U, tile_idx=tile_idx: e.tensor_copy(out=U[:, :, 0:2], in_=convT[:, tile_idx, :].rearrange("p (s r) -> p s r", r=2)),
                          [t_convT], [tu])
                        act(U[:, :, 2:6], pu[:, 0:Tn].rearrange("p (s t) -> p s t", t=TS), AF.Copy, [put], [tu])
                        pb, pt = small_rot.get()
                        raw = tmp[o3][:, 0:32].rearrange("p (s r) -> p s r", r=2)
                        V(lambda e, U=U, raw=raw: e.tensor_copy(out=raw, in_=U[:, :, 4:6]), [tu], [t_tmp[o3]])
                        T(lambda e, pb=pb, o3=o3: e.transpose(pb[0:32, 0:128], tmp[o3][:, 0:32], K["ident_f"]), [t_tmp[o3], Kt["ident_f"]], [pt])
                        act(outc[half][:, (tile_idx % 22) * 128:(tile_idx % 22 + 1) * 128], pb[0:32, 0:128], AF.Copy, [pt], [t_outc[half]])
                        c1 = tmp[o3 + 1][:, 0:Tn].rearrange("p (s t) -> p s t", t=TS)
                        c2 = tmp[o3 + 2][:, 0:Tn].rearrange("p (s t) -> p s t", t=TS)
                        ts(c1, U[:, :, 0:4], w0, cb, ALU.mult, ALU.add, [tu, t_CW[l]], [t_tmp[o3 + 1]])
                        stt(c2, U[:, :, 1:5], w1, c1, ALU.mult, ALU.add, [tu, t_tmp[o3 + 1], t_CW[l]], [t_tmp[o3 + 2]])
                        stt(c1, U[:, :, 2:6], w2, c2, ALU.mult, ALU.add, [tu, t_tmp[o3 + 2], t_CW[l]], [t_tmp[o3 + 1]])
                        res.append((tmp[o3 + 1][:, 0:Tn], t_tmp[o3 + 1], tmp[o3 + 2][:, 0:Tn], t_tmp[o3 + 2]))
                (ca, tca, sa, tsa), (cbv, tcb, _, _) = res
                act(sa, ca, AF.Silu, [tca], [tsa])
                tt(actT[:, j, :], sa, cbv, ALU.mult, [tsa, tcb], [t_act[j]])
                if (not prompt) and j % 22 == 21:
                    for half in range(2):
                        grp = half * 2 + j // 22
                        ST_(lambda e, half=half, grp=grp: e.dma_start(out=sconv_o[l][:, grp * 2816:(grp + 1) * 2816], in_=outc[half]),
                            [t_outc[half]], [t_sconv_o])
                        outs_final.append(P.streams["act"][-1])
            for f in range(16):
                pf, pft = main_rot.get()
                for kh in range(2):
                    wd, wdt = load_w(("dn", l, f, kh))
                    for c in range(22):
                        T(lambda e, pf=pf, wd=wd, c=c, kh=kh: e.matmul(pf[:, 0:Tn], lhsT=wd[:, c, :], rhs=actT[:, kh * 22 + c, :],
                                                                     start=(kh == 0 and c == 0), stop=(kh == 1 and c == 21)),
                          [wdt, t_act[kh * 22 + c]], [pft])
                residual_add(l, 2, f, pf, pft)
            P.fence()

        t_mT = Tok("mT")
        t_act = [Tok(f"act{j}") for j in range(NFT)]

        def conv_out(l):
            for g in range(6):
                n = min(16, 88 - g * 16)
                for q4 in range(0, n, 4):
                    pb, pt = small_rot.get()
                    for cc in range(4):
                        ti = g * 16 + q4 + cc
                        T(lambda e, pb=pb, cc=cc, ti=ti, l=l: e.transpose(pb[0:2, cc * 128:(cc + 1) * 128], CARRY[l][:, ti, :], K["ident_f"]),
                          [t_CARRY[l][ti], Kt["ident_f"]], [pt])
                    act(cstage[:, q4 * 128:(q4 + 4) * 128], pb[0:2, 0:512], AF.Copy, [pt], [t_cstage])
                ST_(lambda e, g=g, n=n, l=l: e.dma_start(out=pconv[l][:, g * 2048:g * 2048 + n * 128], in_=cstage[:, 0:n * 128]), [t_cstage], [t_pconv])
                outs_final.append(P.streams["act"][-1])
            P.fence()

        def final_out(ydst):
            rms_rstd()
            for c in range(16):
                stt(yT[:, c, :], xT[:, c, :], PV[:, 64 + c:65 + c], rstd, ALU.mult, ALU.mult, [t_xT[c], t_PV, t_rstd], [t_yT])
            for (t0, rows) in tiles:
                for g in range(4):
                    pb, pt = main_rot.get()
                    for cc in range(4):
                        c = g * 4 + cc
                        T(lambda e, pb=pb, cc=cc, c=c, t0=t0, rows=rows: e.transpose(pb[0:rows, cc * 128:(cc + 1) * 128], yT[:, c, t0:t0 + rows], K["ident_f"]),
                          [t_yT, Kt["ident_f"]], [pt])
                    act(stg[0:rows, g * 512:(g + 1) * 512], pb[0:rows, 0:512], AF.Copy, [pt], [t_stgo])
                ST_(lambda e, t0=t0, rows=rows: e.dma_start(out=ydst[t0:t0 + rows, :], in_=stg[0:rows, :]), [t_stgo], [])
                outs_final.append(P.streams["act"][-1])
            P.fence()

        if not prompt:
            for l in range(DEPTH):
                modulated_norm(l, 1)
                for b in range(3):
                    mixer_branch(l, b)
                merge_and_out(l)
                ffn(l)
            final_out(ys)
        else:
            for l in range(DEPTH):
                cx["ph1"] = True
                for sp in range(NSP):
                    cx["sp"] = sp
                    load_x(sp)
                    load_rope(sp)
                    modulated_norm(l, 1)
                    for b in range(3):
                        mixer_branch(l, b)
                    P.fence()
                cx["ph1"] = False
                if no_cc:
                    LD(lambda e, l=l: e.dma_start(out=exd[l][0:1536, :], in_=exs[l]), t_exs[l], [t_exd[l]])
                else:
                    P.add("pool", lambda e, l=l: e.collective_compute("AllGather", ALU.bypass, replica_groups=pairs,
                                                                      ins=[exs[l].opt()], outs=[exd[l].opt()]), t_exs[l], [t_exd[l]])
                emit_casts(("ffn", l))
                for sp in range(NSP):
                    cx["sp"] = sp
                    load_x(sp)
                    load_rope(sp)
                    modulated_norm(l, 1)
                    for b in range(3):
                        mixer_branch(l, b)
                    merge_and_out(l)
                if no_cc:
                    LD(lambda e, l=l: e.dma_start(out=exd2[l][0:128, :], in_=exs2[l]), [t_exs2[l]], [t_exd2[l]])
                else:
                    P.add("pool", lambda e, l=l: e.collective_compute("AllGather", ALU.bypass, replica_groups=pairs,
                                                                      ins=[exs2[l].opt()], outs=[exd2[l].opt()]), [t_exs2[l]], [t_exd2[l]])
                if l + 1 < DEPTH:
                    emit_casts(("mix", l + 1))
                for sp in range(NSP):
                    cx["sp"] = sp
                    load_x(sp)
                    ffn(l)
                conv_out(l)
            for sp in range(NSP):
                load_x(sp)
                final_out(yp[sp * TP:(sp + 1) * TP])
        AR.off = m0

    t_pst = [[Tok() for _ in range(DEPTH)] for _ in range(3)]
    t_sst = Tok()
    t_sconv_o = Tok()
    t_pconv = Tok()
    t_stgo = Tok("stgo")
    outs_final = []

    setup()
    MOD = [AR.alloc([128, 96, 17], F32) for _ in range(DEPTH)]
    t_MOD = [Tok("MOD") for _ in range(DEPTH)]
    compute_mod(MOD, t_MOD)
    for l in range(DEPTH):
        LD(lambda e, l=l, M=MOD: e.dma_start(out=modsc[l], in_=M[l][:].rearrange("p a b -> p (a b)")), [t_MOD[l]], [t_modsc[l]])
    emit_casts(("mix", 0))
    P.fence()
    AR.off = persist_mark
    run_pass("p")
    AR.off = persist_mark
    MOD = [AR.alloc([128, 96, 17], F32) for _ in range(DEPTH)]
    t_MOD = [Tok("MOD") for _ in range(DEPTH)]
    for l in range(DEPTH):
        LD(lambda e, l=l, M=MOD: e.dma_start(out=M[l][:].rearrange("p a b -> p (a b)"), in_=modsc[l]), [t_modsc[l]], [t_MOD[l]])
    run_pass("s", MOD, t_MOD)
    P.final_waits = outs_final
    P.emit(st)
    st.close()
    return nc


_CACHE = {}


def _prep_inputs(inp):
    f32 = lambda a: np.ascontiguousarray(np.asarray(a, dtype=np.float32))
    consts = host_consts()
    pv = np.zeros((96, 128), np.float32)
    nm, nf, fn = f32(inp["norm_mix"]), f32(inp["norm_ffn"]), f32(inp["final_norm"])
    pv[0:16] = nm[0].reshape(16, 128)
    pv[16:32] = nm[1].reshape(16, 128)
    pv[32:48] = nf[0].reshape(16, 128)
    pv[48:64] = nf[1].reshape(16, 128)
    pv[64:80] = fn.reshape(16, 128)
    lbl = f32(inp["hgrn_lb_logits"])
    pv[80:84] = lbl[0].reshape(4, 128)
    pv[84:88] = lbl[1].reshape(4, 128)
    gb = f32(inp["gla_b_lr"])
    pv[88:92] = gb[0].reshape(4, 128)
    pv[92:96] = gb[1].reshape(4, 128)
    cw, cb = f32(inp["ffn_conv_w"]), f32(inp["ffn_conv_b"])
    convp = np.concatenate([cw.reshape(DEPTH, 3, 88, 128), cb.reshape(DEPTH, 1, 88, 128)], axis=1)
    shared = {
        "w_in": f32(inp["w_in"]), "gla_w_lr": f32(inp["gla_w_lr"]), "pvecs": pv,
        "head_norm": f32(inp["head_norm"]).reshape(DEPTH, 3 * HV), "w_branch": f32(inp["w_branch"]),
        "w_out": f32(inp["w_out"]), "w_ada": f32(inp["w_ada"]), "b_ada": f32(inp["b_ada"]).reshape(DEPTH, 96, 128),
        "ffn_w_up": f32(inp["ffn_w_up"]), "convp": np.ascontiguousarray(convp), "ffn_w_down": f32(inp["ffn_w_down"]),
    }
    for n, s, dt in CONST_SPECS:
        if n not in ("cos_p", "sin_p", "isb"):
            shared["k_" + n] = np.ascontiguousarray(consts[n])
    x_prompt, x_sample = f32(inp["x_prompt"]), f32(inp["x_sample"])
    c_prompt, c_sample = f32(inp["c_prompt"]), f32(inp["c_sample"])
    sts = [f32(inp["state_ret"]), f32(inp["state_gla"]), f32(inp["state_hgrn"])]
    sc = f32(inp["state_conv"])
    maps = []
    for c in range(N_CORES):
        b = c // 2
        hf = c % 2
        s0 = c * NSEQ
        m = dict(shared)
        m["xp"] = np.ascontiguousarray(x_prompt[b, hf * HALF:(hf + 1) * HALF])
        m["k_cos_p"] = np.ascontiguousarray(consts["cos_p"][:, hf * HALF:(hf + 1) * HALF])
        m["k_sin_p"] = np.ascontiguousarray(consts["sin_p"][:, hf * HALF:(hf + 1) * HALF])
        m["k_isb"] = np.full((128, 1), float(hf), np.float32)
        m["xs"] = np.ascontiguousarray(x_sample[s0:s0 + NSEQ].reshape(TSAMP, D))
        m["cvec"] = np.ascontiguousarray(np.concatenate([c_sample[s0:s0 + NSEQ], c_prompt[b:b + 1]], axis=0))
        for k in range(3):
            m[f"st_in{k}"] = np.ascontiguousarray(sts[k][:, s0:s0 + NSEQ])
        m["sconv_in"] = np.ascontiguousarray(sc[:, s0:s0 + NSEQ].reshape(DEPTH, NSEQ * 2, 2 * DFF))
        maps.append(m)
    return maps


def kernel(**inputs):
    if "nc" not in _CACHE:
        _CACHE["nc"] = build_program()
    nc = _CACHE["nc"]
    maps = _prep_inputs(inputs)
    res = run_bass_kernel_spmd(nc, maps, core_ids=list(range(N_CORES)))
    R = res.results
    y_prompt = np.stack([np.concatenate([R[2 * b]["yp"], R[2 * b + 1]["yp"]], axis=0) for b in range(4)]).astype(np.float32)
    y_sample = np.concatenate([R[c]["ys"].reshape(NSEQ, TS, D) for c in range(N_CORES)], axis=0).astype(np.float32)
    outs = [y_prompt, y_sample]
    for k in range(3):
        outs.append(np.stack([R[2 * b + 1][f"pst{k}"] for b in range(4)], axis=1).astype(np.float32))
    outs.append(np.stack([R[2 * b + 1]["pconv"] for b in range(4)], axis=1).astype(np.float32))
    for k in range(3):
        outs.append(np.concatenate([R[c][f"sst{k}"] for c in range(N_CORES)], axis=1).astype(np.float32))
    outs.append(np.concatenate([R[c]["sconv_o"].reshape(DEPTH, NSEQ, 2, 2 * DFF) for c in range(N_CORES)], axis=1).astype(np.float32))
    return tuple(outs)
```

```python
import numpy as np
from contextlib import ExitStack
import ml_dtypes
import concourse.bass as bass
import concourse.mybir as mybir
from concourse.bass_utils import run_bass_kernel_spmd

F32 = mybir.dt.float32
BF16 = mybir.dt.bfloat16
ALU = mybir.AluOpType
AF = mybir.ActivationFunctionType
AX = mybir.AxisListType
NPBF = ml_dtypes.bfloat16

D = 2048
NC16 = 16
NH = 4
HK = 128
HV = 256
MV = 1024
DFF = 5632
NFT = 44
NIN = 15376
DEPTH = 2
SEQ = 2048
TP = 512
NSP = 2
HALF = NSP * TP
PAIRS = [[0, 1], [2, 3], [4, 5], [6, 7]]
NSEQ = 16
TS = 4
TSAMP = NSEQ * TS
PAST = 16384
EPS = 1e-6
N_CORES = 8
OFF = [(0, 512, 1024, 2048), (3072, 3584, 4096, 5120), (6160, 6672, 7184, 8208)]
OFF_GLR = 6144
OFF_MG = 9232

ENGS = ("pe", "dve", "act", "pool", "sp")
SEM_EPOCH = 30000


class Tok:
    __slots__ = ("name", "last_write", "reads")

    def __init__(self, name=""):
        self.name = name
        self.last_write = None
        self.reads = []


class Op:
    __slots__ = ("eng", "fn", "deps", "is_dma", "sig", "needs_sig", "prewait", "wload")

    def __init__(self, eng, fn, is_dma):
        self.eng = eng
        self.fn = fn
        self.deps = set()
        self.is_dma = is_dma
        self.sig = None
        self.needs_sig = False
        self.prewait = None
        self.wload = False


class Prog:
    def __init__(self, nc, n_dma_sems=8):
        self.nc = nc
        self.streams = {e: [] for e in ENGS}
        self.n_dma_sems = n_dma_sems
        self.final_waits = []
        self.fence_deps = set()
        self.since_fence = []

    def add(self, eng, fn, reads=(), writes=(), dma=False, wload=False):
        op = Op(eng, fn, dma)
        op.wload = wload
        for t in reads:
            if t.last_write is not None:
                op.deps.add(t.last_write)
        for t in writes:
            if t.last_write is not None:
                op.deps.add(t.last_write)
            for r in t.reads:
                op.deps.add(r)
        if not wload:
            op.deps |= self.fence_deps
        op.deps.discard(op)
        for t in reads:
            t.reads.append(op)
        for t in writes:
            t.last_write = op
            t.reads = []
        self.streams[eng].append(op)
        if dma and not wload:
            self.since_fence.append(op)
        return op

    def fence(self):
        deps = set()
        for e in ENGS:
            if e in ("sp", "pool"):
                continue
            if self.streams[e]:
                deps.add(self.streams[e][-1])
        for op in self.since_fence:
            deps.add(op)
        self.since_fence = []
        self.fence_deps = deps

    def emit(self, stack):
        nc = self.nc
        for e in ENGS:
            for op in self.streams[e]:
                if op.eng == "pe":
                    op.deps = {d for d in op.deps if d.eng != "pe" or d.is_dma}
                for d in op.deps:
                    d.needs_sig = True
        self.sems = []

        def newsem(name):
            s = stack.enter_context(nc.semaphore(name))
            self.sems.append(s)
            return s

        for e in ENGS:
            cnt = 0
            ep = 0
            sem = None
            dsems = None
            dcnt = None
            di = 0
            for op in self.streams[e]:
                if op.is_dma:
                    if dsems is None:
                        dsems = [newsem(f"d_{e}_{i}") for i in range(self.n_dma_sems)]
                        dcnt = [0] * self.n_dma_sems
                    i = di % self.n_dma_sems
                    di += 1
                    if dcnt[i] + 16 > SEM_EPOCH:
                        dsems[i] = newsem(f"d_{e}_{i}_{di}")
                        dcnt[i] = 0
                    if dcnt[i] > 0:
                        op.prewait = (dsems[i], dcnt[i])
                    dcnt[i] += 16
                    op.sig = (dsems[i], dcnt[i])
                elif op.needs_sig:
                    if sem is None or cnt >= SEM_EPOCH:
                        sem = newsem(f"c_{e}_{ep}")
                        ep += 1
                        cnt = 0
                    cnt += 1
                    op.sig = (sem, cnt)
        block = stack.enter_context(nc.Block())
        self.n_wait = 0

        def run_stream(e):
            def body(eng):
                waited = {}
                for op in self.streams[e]:
                    need = {}
                    for d in op.deps:
                        s, v = d.sig
                        k = id(s)
                        if waited.get(k, 0) >= v:
                            continue
                        if k not in need or need[k][1] < v:
                            need[k] = (s, v)
                    if op.prewait is not None:
                        s, v = op.prewait
                        k = id(s)
                        if waited.get(k, 0) < v and (k not in need or need[k][1] < v):
                            need[k] = (s, v)
                    for k, (s, v) in need.items():
                        eng.wait_ge(s, v)
                        waited[k] = v
                        self.n_wait += 1
                    ins = op.fn(eng)
                    if op.is_dma:
                        ins.then_inc(op.sig[0], 16)
                    elif op.sig is not None:
                        ins.then_inc(op.sig[0], 1)
                if e == "sp":
                    for op in self.final_waits:
                        s, v = op.sig
                        if waited.get(id(s), 0) < v:
                            eng.wait_ge(s, v)
                            waited[id(s)] = v
            return body

        block.tensor(run_stream("pe"))
        block.vector(run_stream("dve"))
        block.scalar(run_stream("act"))
        block.gpsimd(run_stream("pool"))
        block.sync(run_stream("sp"))


def host_consts():
    c = {}
    c["ident_f"] = np.eye(128, dtype=np.float32)
    c["ident_b"] = np.eye(128, dtype=np.float32).astype(NPBF)
    sw = np.zeros((128, 128), np.float32)
    for m in range(128):
        sw[(m + 64) % 128, m] = 1.0
    c["pswap"] = sw.astype(NPBF)
    c["ones_f"] = np.ones((128, 128), np.float32)
    s = np.arange(128)[:, None]
    t = np.arange(128)[None, :]
    c["mask1"] = (s <= t).astype(np.float32).astype(NPBF)
    c["mask4"] = ((s <= t) & (s // 32 == t // 32)).astype(np.float32).astype(NPBF)
    ms = np.zeros((128, 128), np.float32)
    s6 = np.arange(64)[:, None]
    t6 = np.arange(64)[None, :]
    ms[:64, :64] = ((s6 <= t6) & (s6 // 4 == t6 // 4))
    c["masks"] = ms.astype(NPBF)
    cm4 = np.zeros((128, 4, 128), np.float32)
    for j in range(4):
        cm4[:, j, 32 * j:32 * j + 32] = 1.0
    c["cm4"] = cm4.astype(NPBF)
    rm4 = np.zeros((128, 4), np.float32)
    for p in range(128):
        rm4[p, p // 32] = 1.0
    c["rm4"] = rm4
    cms = np.zeros((128, 16, 64), np.float32)
    for j in range(16):
        cms[:, j, 4 * j:4 * j + 4] = 1.0
    c["cms"] = cms.astype(NPBF)
    rms = np.zeros((128, 16), np.float32)
    for p in range(64):
        rms[p, p // 4] = 1.0
    c["rms"] = rms
    tt = np.arange(512)
    c["rst_g"] = np.broadcast_to((tt % 128 != 0).astype(np.float32), (128, 512)).astype(NPBF)
    c["rst_h"] = np.broadcast_to((tt % 32 != 0).astype(np.float32), (128, 512)).astype(NPBF)
    c["rst_s"] = np.broadcast_to((np.arange(64) % 4 != 0).astype(np.float32), (128, 64)).astype(NPBF)
    lg = np.log1p(-np.exp2(-5.0 - np.arange(4, dtype=np.float32))).astype(np.float32)
    tq = np.arange(128, dtype=np.float32) + 1.0
    gq = np.exp(lg[:, None] * tq[None, :]).astype(np.float32)
    gk = (np.exp(-lg[:, None] * tq[None, :]) * (HK ** -0.5)).astype(np.float32)
    c["gq"] = np.broadcast_to(gq, (128, 4, 128)).copy()
    c["gk"] = np.broadcast_to(gk, (128, 4, 128)).copy()
    ts_ = (np.arange(64) % 4).astype(np.float32) + 1.0
    gqs = np.exp(lg[:, None] * ts_[None, :]).astype(np.float32)
    gks = (np.exp(-lg[:, None] * ts_[None, :]) * (HK ** -0.5)).astype(np.float32)
    c["gqs"] = np.broadcast_to(gqs, (128, 4, 64)).copy()
    c["gks"] = np.broadcast_to(gks, (128, 4, 64)).copy()
    c["elr"] = np.broadcast_to(np.exp(lg * 128.0).astype(np.float32), (128, 4)).copy()
    c["elrs"] = np.broadcast_to(np.exp(lg * 4.0).astype(np.float32), (128, 4)).copy()
    half = 64
    inv = (10000.0 ** (-np.arange(half, dtype=np.float32) / half)).astype(np.float32)
    invm = np.concatenate([inv, inv])
    sgn = np.concatenate([-np.ones(64, np.float32), np.ones(64, np.float32)])
    pos = np.arange(SEQ, dtype=np.float32)
    ang = (pos[None, :] * invm[:, None]).astype(np.float32)
    c["cos_p"] = np.cos(ang).astype(np.float32)
    c["sin_p"] = (np.sin(ang) * sgn[:, None]).astype(np.float32)
    poss = (PAST + (np.arange(64) % 4)).astype(np.float32)
    angs = (poss[None, :] * invm[:, None]).astype(np.float32)
    c["cos_s"] = np.cos(angs).astype(np.float32)
    c["sin_s"] = (np.sin(angs) * sgn[:, None]).astype(np.float32)
    return c


CONST_SPECS = [
    ("ident_f", [128, 128], F32), ("ident_b", [128, 128], BF16), ("pswap", [128, 128], BF16),
    ("ones_f", [128, 128], F32), ("mask1", [128, 128], BF16), ("mask4", [128, 128], BF16),
    ("masks", [128, 128], BF16), ("cm4", [128, 4, 128], BF16), ("rm4", [128, 4], F32),
    ("cms", [128, 16, 64], BF16), ("rms", [128, 16], F32), ("rst_g", [128, 512], BF16),
    ("rst_h", [128, 512], BF16), ("rst_s", [128, 64], BF16), ("gq", [128, 4, 128], F32),
    ("gk", [128, 4, 128], F32), ("gqs", [128, 4, 64], F32), ("gks", [128, 4, 64], F32),
    ("elr", [128, 4], F32), ("elrs", [128, 4], F32), ("cos_p", [128, HALF], F32),
    ("sin_p", [128, HALF], F32), ("isb", [128, 1], F32), ("cos_s", [128, 64], F32), ("sin_s", [128, 64], F32),
]


class Arena:
    def __init__(self, ap, words):
        self.ap = ap
        self.words = words
        self.off = 0

    def alloc(self, shape, dt):
        n = int(np.prod(shape[1:]))
        w = (n + 1) // 2 if dt == BF16 else n
        w = (w + 7) // 8 * 8
        assert self.off + w <= self.words, f"arena overflow {self.off}+{w}>{self.words}"
        v = self.ap[:, self.off:self.off + w]
        self.off += w
        if dt == BF16:
            v = v.bitcast(BF16)
        v = v[:, 0:n]
        if len(shape) == 3:
            v = v.rearrange("p (a b) -> p a b", a=shape[1])
        elif len(shape) == 4:
            v = v.rearrange("p (a b c) -> p a b c", a=shape[1], b=shape[2])
        if shape[0] < 128:
            v = v[0:shape[0]]
        return v


ARENA_WORDS = 53200


def build_program(debug=False, n_cores=N_CORES, no_cc=False):
    pairs = [[2 * i, 2 * i + 1] for i in range(n_cores // 2)]
    nc = bass.Bass("TRN2", target_bir_lowering=False)

    def din(name, shape, dt=F32):
        return nc.dram_tensor(name, list(shape), dt, kind="ExternalInput").ap()

    def dout(name, shape):
        return nc.dram_tensor(name, list(shape), F32, kind="ExternalOutput").ap()

    xp = din("xp", [HALF, D])
    xs = din("xs", [TSAMP, D])
    cvec = din("cvec", [17, D])
    st_in = [din(f"st_in{b}", [DEPTH, NSEQ, NH, HK, HV]) for b in range(3)]
    sconv_in = din("sconv_in", [DEPTH, NSEQ * 2, 2 * DFF])
    w_in = din("w_in", [DEPTH, D, NIN])
    gla_w_lr = din("gla_w_lr", [DEPTH, 16, 512])
    pvecs = din("pvecs", [96, 128])
    head_norm = din("head_norm", [DEPTH, 3 * HV])
    w_branch = din("w_branch", [DEPTH, 3, MV, D])
    w_out = din("w_out", [DEPTH, D, D])
    w_ada = din("w_ada", [DEPTH, D, 6 * D])
    b_ada = din("b_ada", [DEPTH, 96, 128])
    w_up = din("ffn_w_up", [DEPTH, D, 2 * DFF])
    convp = din("convp", [DEPTH, 4, 88, 128])
    w_down = din("ffn_w_down", [DEPTH, DFF, D])
    cst = {n: din("k_" + n, s, dt) for n, s, dt in CONST_SPECS}

    yp = dout("yp", [HALF, D])
    ys = dout("ys", [TSAMP, D])
    pst = [dout(f"pst{b}", [DEPTH, NH, HK, HV]) for b in range(3)]
    pconv = dout("pconv", [DEPTH, 2, 2 * DFF])
    sst = [dout(f"sst{b}", [DEPTH, NSEQ, NH, HK, HV]) for b in range(3)]
    sconv_o = dout("sconv_o", [DEPTH, NSEQ * 2, 2 * DFF])

    st = ExitStack()
    P = Prog(nc)
    dscr = lambda name, shape: nc.dram_tensor(name, list(shape), F32).ap()
    xsw = [dscr(f'xsw{sp}', [128, 16, TP]) for sp in range(NSP)]
    t_xsw = [[Tok() for _ in range(16)] for _ in range(NSP)]
    ph1st = [dscr(f'ph1st{b}', [NH, HK, HV]) for b in range(3)]
    t_ph1st = [Tok() for _ in range(3)]
    exs = [dscr(f'exs{l}', [3 * NH * HK, HV]) for l in range(DEPTH)]
    exd = [dscr(f'exd{l}', [2 * 3 * NH * HK, HV]) for l in range(DEPTH)]
    t_exs = [[Tok() for _ in range(3)] for _ in range(DEPTH)]
    t_exd = [Tok() for _ in range(DEPTH)]
    exs2 = [dscr(f'exs2_{l}', [128, 32]) for l in range(DEPTH)]
    exd2 = [dscr(f'exd2_{l}', [256, 32]) for l in range(DEPTH)]
    t_exs2 = [Tok() for _ in range(DEPTH)]
    t_exd2 = [Tok() for _ in range(DEPTH)]
    modsc = [dscr(f'modsc{l}', [128, 96 * 17]) for l in range(DEPTH)]
    t_modsc = [Tok() for _ in range(DEPTH)]
    t_ada_done = Tok('ada_done')
    arena_t = st.enter_context(nc.sbuf_tensor("arena", [128, ARENA_WORDS], F32))
    AR = Arena(arena_t[:], ARENA_WORDS)
    banks = [st.enter_context(nc.psum_tensor(f"pb{i}", [128, 512], F32)) for i in range(8)]
    bank_tok = [Tok(f"pb{i}") for i in range(8)]
    bank_bf = banks[7][:].bitcast(BF16)

    class Rot:
        def __init__(self, idx):
            self.idx = idx
            self.i = 0

        def get(self):
            k = self.idx[self.i % len(self.idx)]
            self.i += 1
            return banks[k][:], bank_tok[k]

    main_rot = Rot([0, 1, 2, 3])
    small_rot = Rot([4, 5, 6])

    V = lambda fn, r=(), w=(): P.add("dve", fn, r, w)
    A = lambda fn, r=(), w=(): P.add("act", fn, r, w)
    T = lambda fn, r=(), w=(): P.add("pe", fn, r, w)
    LD = lambda fn, r=(), w=(): P.add("sp", fn, r, w, dma=True)
    ST_ = lambda fn, r=(), w=(): P.add("act", fn, r, w, dma=True)

    def mm(out_ap, out_tok, pairs, reads):
        n = len(pairs)
        for i, (l, r) in enumerate(pairs):
            T(lambda e, l=l, r=r, i=i: e.matmul(out_ap, lhsT=l, rhs=r, start=(i == 0), stop=(i == n - 1)),
              reads, [out_tok])

    def act(out, in_, func, r, w, **kw):
        return A(lambda e: e.activation(out=out, in_=in_, func=func, **kw), r, w)

    def tt(out, a, b, op, r, w):
        return V(lambda e: e.tensor_tensor(out=out, in0=a, in1=b, op=op), r, w)

    def ts(out, a, s1, s2, op0, op1, r, w):
        return V(lambda e: e.tensor_scalar(out=out, in0=a, scalar1=s1, scalar2=s2, op0=op0, op1=op1), r, w)

    def stt(out, a, s, b, op0, op1, r, w):
        return V(lambda e: e.scalar_tensor_tensor(out=out, in0=a, scalar=s, in1=b, op0=op0, op1=op1), r, w)

    K = {}
    Kt = {}
    for n, s, dt in CONST_SPECS:
        if n in ("cos_p", "sin_p", "cos_s", "sin_s", "cms", "rms", "rst_s", "gqs", "gks", "elrs", "masks"):
            continue
        K[n] = AR.alloc(s, dt)
        Kt[n] = Tok(n)
        LD(lambda e, n=n: e.dma_start(out=K[n], in_=cst[n]), [], [Kt[n]])
    PV = AR.alloc([128, 96], F32)
    t_PV = Tok("PV")
    CW = [AR.alloc([128, 4, 88], F32) for _ in range(DEPTH)]
    t_CW = [Tok("CW") for _ in range(DEPTH)]
    HNW = [AR.alloc([128, 3 * HV], BF16) for _ in range(DEPTH)]
    t_HNW = [Tok("HNW") for _ in range(DEPTH)]
    GLW = [AR.alloc([16, 512], BF16) for _ in range(DEPTH)]
    t_GLW = [Tok("GLW") for _ in range(DEPTH)]
    LBV = AR.alloc([128, DEPTH, 3, 4], F32)
    t_LBV = Tok("LBV")
    NEGB = AR.alloc([128, DEPTH, 4], F32)
    MODP = [AR.alloc([128, 6, 16], F32) for _ in range(DEPTH)]
    t_MODP = [Tok("MODP") for _ in range(DEPTH)]
    CARRY = [AR.alloc([128, 88, 2], F32) for _ in range(DEPTH)]
    t_CARRY = [[Tok("carry") for _ in range(88)] for _ in range(DEPTH)]
    NWB = 4
    WB = [AR.alloc([128, 4096], BF16) for _ in range(NWB)]
    t_WB = [Tok(f"wb{i}") for i in range(NWB)]
    wb_i = [0]
    persist_mark = AR.off

    scr = {}
    scr_tok = {}
    cast_list = []

    def defblk(key, srcs, kc, ncols):
        t = nc.dram_tensor("s_" + "_".join(str(k) for k in key), [128, kc, ncols], BF16).ap()
        scr[key] = (t, kc, ncols)
        cast_list.append((key, srcs))

    for l in range(DEPTH):
        for b in range(3):
            oq, ok, ov, og = OFF[b]
            for hp in range(2):
                defblk(("k", l, b, hp), [(w_in[l, :, ok + 256 * hp: ok + 256 * hp + 256], 0)], 16, 256)
                defblk(("q", l, b, hp), [(w_in[l, :, oq + 256 * hp: oq + 256 * hp + 256], 0)], 16, 256)
            if b == 1:
                defblk(("glr", l), [(w_in[l, :, OFF_GLR:OFF_GLR + 16], 0)], 16, 16)
            for h in range(4):
                defblk(("v", l, b, h), [(w_in[l, :, ov + 256 * h: ov + 256 * h + 256], 0)], 16, 256)
            for h in range(4):
                defblk(("g", l, b, h), [(w_in[l, :, og + 256 * h: og + 256 * h + 256], 0)], 16, 256)
        for fp in range(8):
            for b in range(3):
                defblk(("br", l, b, fp), [(w_branch[l, b, :, 256 * fp:256 * fp + 256], 0)], 8, 256)
                c0 = OFF_MG + b * D + 256 * fp
                defblk(("mg", l, b, fp), [(w_in[l, :, c0:c0 + 256], 0)], 16, 256)
        for fp in range(8):
            defblk(("out", l, fp), [(w_out[l, :, 256 * fp:256 * fp + 256], 0)], 16, 256)
        for j in range(NFT):
            defblk(("up", l, j), [(w_up[l, :, 128 * j:128 * j + 128], 0),
                                  (w_up[l, :, DFF + 128 * j:DFF + 128 * j + 128], 128)], 16, 256)
        for f in range(16):
            for kh in range(2):
                defblk(("dn", l, f, kh), [(w_down[l, kh * 2816:(kh + 1) * 2816, 128 * f:128 * f + 128], 0)], 22, 128)

    cast_src = dict(cast_list)
    cast_done = set()
    t_WB2 = [Tok(f"wbb{i}") for i in range(16)]
    wb_extra = [[]]

    def emit_casts(stage):
        return

    def load_w(key):
        dst_scr, kc, ncols = scr[key]
        i = wb_i[0] % len(WB)
        wb_i[0] += 1
        v = WB[i][:, 0:kc * ncols].rearrange("p (c n) -> p c n", c=kc)
        if key not in cast_done:
            cast_done.add(key)
            two = len(cast_src[key]) == 2
            op0 = None
            for si, (src, co) in enumerate(cast_src[key]):
                n = src.shape[1]
                fn = lambda e, src=src, co=co, n=n: e.dma_start(out=v[:, :, co:co + n], in_=src.rearrange("(c p) n -> p c n", p=128))
                if si == 0:
                    op0 = P.add("pool", fn, [], [t_WB[i], t_WB2[i]] if two else [t_WB[i]], dma=True, wload=True)
                else:
                    op1 = P.add("pool", fn, [], [], dma=True, wload=True)
                    op1.deps = set(op0.deps)
                    t_WB2[i].last_write = op1
                    t_WB2[i].reads = []
            wb_extra[0] = [t_WB2[i]] if two else []
            scr_tok[key] = [Tok(str(key))]
            P.add("act", lambda e: e.dma_start(out=dst_scr, in_=v), [t_WB[i]] + wb_extra[0], scr_tok[key], dma=True, wload=True)
        else:
            wb_extra[0] = []
            P.add("sp", lambda e: e.dma_start(out=v, in_=dst_scr), scr_tok[key] + [t_WB2[i]], [t_WB[i]], dma=True, wload=True)
        return v, t_WB[i]

    def setup():
        m0 = AR.off
        stg = AR.alloc([128, 128], F32)
        t_stg = Tok("stg")
        LD(lambda e: e.dma_start(out=stg[0:96, :], in_=pvecs), [], [t_stg])
        pb, pt = small_rot.get()
        T(lambda e, pb=pb: e.transpose(pb[:, 0:96], stg[0:96, :], K["ident_f"][0:96, 0:96]), [t_stg, Kt["ident_f"]], [pt])
        act(PV, pb[:, 0:96], AF.Copy, [pt], [t_PV])
        for l in range(DEPTH):
            for j in range(4):
                LD(lambda e, l=l, j=j: e.dma_start(out=stg[0:88, :], in_=convp[l, j]), [], [t_stg])
                pb, pt = small_rot.get()
                T(lambda e, pb=pb: e.transpose(pb[:, 0:88], stg[0:88, :], K["ident_f"][0:88, 0:88]), [t_stg, Kt["ident_f"]], [pt])
                act(CW[l][:, j, :], pb[:, 0:88], AF.Copy, [pt], [t_CW[l]])
            hn32 = AR.alloc([128, 3 * HV], F32)
            t_hn = Tok("hn32")
            LD(lambda e, l=l, hn32=hn32: e.dma_start(out=hn32, in_=head_norm[l].partition_broadcast(128)), [], [t_hn])
            V(lambda e, l=l, hn32=hn32: e.tensor_copy(out=HNW[l], in_=hn32), [t_hn], [t_HNW[l]])
            gl32 = AR.alloc([16, 512], F32)
            t_gl = Tok("gl32")
            LD(lambda e, l=l, gl32=gl32: e.dma_start(out=gl32, in_=gla_w_lr[l]), [], [t_gl])
            V(lambda e, l=l, gl32=gl32: e.tensor_copy(out=GLW[l], in_=gl32), [t_gl], [t_GLW[l]])
        tmp = AR.alloc([128, 8, 4], F32)
        t_tmp = Tok("lbtmp")
        l0 = PV[:, 80:84]
        l1 = PV[:, 84:88]
        mx, e0, e1, sm, r, s0, s1, cs = [tmp[:, i, :] for i in range(8)]
        tt(mx, l0, l1, ALU.max, [t_PV], [t_tmp])
        tt(e0, l0, mx, ALU.subtract, [t_PV, t_tmp], [t_tmp])
        tt(e1, l1, mx, ALU.subtract, [t_PV, t_tmp], [t_tmp])
        act(e0, e0, AF.Exp, [t_tmp], [t_tmp])
        act(e1, e1, AF.Exp, [t_tmp], [t_tmp])
        tt(sm, e0, e1, ALU.add, [t_tmp], [t_tmp])
        V(lambda e: e.reciprocal(out=r, in_=sm), [t_tmp], [t_tmp])
        tt(s0, e0, r, ALU.mult, [t_tmp], [t_tmp])
        tt(s1, e1, r, ALU.mult, [t_tmp], [t_tmp])
        tt(cs, s0, s1, ALU.add, [t_tmp], [t_tmp])
        tt(LBV[:, 0, 0, :], s0, s0, ALU.subtract, [t_tmp], [t_LBV])
        tt(LBV[:, 1, 0, :], cs, s0, ALU.subtract, [t_tmp], [t_LBV])
        for l in range(DEPTH):
            ts(LBV[:, l, 1, :], LBV[:, l, 0, :], -1.0, 1.0, ALU.mult, ALU.add, [t_LBV], [t_LBV])
            ts(LBV[:, l, 2, :], LBV[:, l, 0, :], 1.0, -1.0, ALU.mult, ALU.add, [t_LBV], [t_LBV])
            ts(NEGB[:, l, :], PV[:, 88 + 4 * l:92 + 4 * l], -1.0, None, ALU.mult, ALU.bypass, [t_PV], [t_LBV])
        for l in range(DEPTH):
            V(lambda e, l=l: e.memset(CARRY[l], 0.0), [], t_CARRY[l])
        return m0

    def compute_mod(MOD, t_MOD):
        m0 = AR.off
        c_sb = AR.alloc([17, D], F32)
        s_sb = AR.alloc([17, D], F32)
        scT = AR.alloc([128, 16, 17], F32)
        badaT = AR.alloc([128, 96], F32)
        stg = AR.alloc([128, 128], F32)
        scTb = AR.alloc([128, 16, 17], BF16)
        wcb = [AR.alloc([128, 16, 128], BF16) for _ in range(2)]
        t_wcb = [Tok(), Tok()]
        t_scTb = Tok()
        t_c, t_s, t_scT, t_ba, t_stg = Tok(), Tok(), Tok(), Tok(), Tok()
        LD(lambda e: e.dma_start(out=c_sb, in_=cvec), [], [t_c])
        act(s_sb, c_sb, AF.Sigmoid, [t_c], [t_s])
        tt(s_sb, s_sb, c_sb, ALU.mult, [t_c, t_s], [t_s])
        for c in range(16):
            pb, pt = small_rot.get()
            T(lambda e, c=c, pb=pb: e.transpose(pb[:, 0:17], s_sb[:, c * 128:(c + 1) * 128], K["ident_f"][0:17, 0:17]),
              [t_s, Kt["ident_f"]], [pt])
            act(scT[:, c, :], pb[:, 0:17], AF.Copy, [pt], [t_scT])
        V(lambda e: e.tensor_copy(out=scTb, in_=scT), [t_scT], [t_scTb])
        nblk_ = [0]
        for l in range(DEPTH):
            LD(lambda e, l=l: e.dma_start(out=stg[0:96, :], in_=b_ada[l]), [], [t_stg])
            pb, pt = small_rot.get()
            T(lambda e, pb=pb: e.transpose(pb[:, 0:96], stg[0:96, :], K["ident_f"][0:96, 0:96]), [t_stg, Kt["ident_f"]], [pt])
            act(badaT, pb[:, 0:96], AF.Copy, [pt], [t_ba])
            for fb in range(48):
                i = wb_i[0] % len(WB)
                wb_i[0] += 1
                wv = WB[i].rearrange("p (c n) -> p c n", c=16)
                P.add("pool", lambda e, l=l, fb=fb, wv=wv: e.dma_start(
                    out=wv, in_=w_ada[l, :, fb * 256:(fb + 1) * 256].rearrange("(c p) n -> p c n", p=128)),
                    [], [t_WB[i]], dma=True, wload=True)
                for ft in range(2):
                    fi = fb * 2 + ft
                    pb, pt = small_rot.get()
                    mm(pb[:, 0:17], pt, [(wv[:, c, ft * 128:(ft + 1) * 128], scTb[:, c, :]) for c in range(16)], [t_WB[i], t_scTb])
                    act(MOD[l][:, fi, :], pb[:, 0:17], AF.Identity, [pt, t_ba], [t_MOD[l]], bias=badaT[:, fi:fi + 1], scale=1.0)
            for (c0, pv0) in ((16, 16 * l), (64, 32 + 16 * l)):
                stt(MOD[l][:, c0:c0 + 16, :], MOD[l][:, c0:c0 + 16, :], 1.0,
                    PV[:, pv0:pv0 + 16].unsqueeze(2).broadcast_to([128, 16, 17]),
                    ALU.add, ALU.mult, [t_MOD[l], t_PV], [t_MOD[l]])
            for k, c0 in enumerate((16, 0, 32, 64, 48, 80)):
                V(lambda e, l=l, k=k, c0=c0: e.tensor_copy(out=MODP[l][:, k, :], in_=MOD[l][:, c0:c0 + 16, 16]),
                  [t_MOD[l]], [t_MODP[l]])
        return m0

    dbg_outs = {}

    def run_pass(kind, MOD=None, t_MOD=None):
        prompt = kind == "p"
        cx = {"sp": 0, "ph1": False, "store_x": False}
        Tn = TP if prompt else TSAMP
        tiles = [(i * 128, 128) for i in range(4)] if prompt else [(0, 64)]
        nt = len(tiles)
        m0 = AR.off
        xT = AR.alloc([128, 16, Tn], F32)
        t_xT = [Tok(f"xT{c}") for c in range(16)]
        hT = AR.alloc([128, 16, Tn], BF16)
        t_hT = [Tok(f"hT{c}") for c in range(16)]
        r1_words = max(NFT * Tn // 2, 16 * Tn + 2048) + 64
        r1_off = AR.off
        R1 = Arena(AR.ap[:, r1_off:r1_off + r1_words], r1_words)
        AR.off += r1_words
        uT = R1.alloc([128, 3, 8, Tn], BF16)
        v_tok = R1.alloc([128, nt, MV], BF16)
        sg_tok = R1.alloc([128, nt, MV], BF16)
        R1b = Arena(AR.ap[:, r1_off:r1_off + r1_words], r1_words)
        actT = R1b.alloc([128, NFT, Tn], BF16)
        R1c = Arena(AR.ap[:, r1_off:r1_off + r1_words], r1_words)
        yT = R1c.alloc([128, 16, Tn], F32)
        stg = R1c.alloc([128, D], F32)
        mT = Arena(AR.ap[:, r1_off + 3 * 8 * Tn // 2 + 0: r1_off + r1_words], r1_words).alloc([128, 16, Tn], BF16) \
            if prompt else AR.alloc([128, 16, Tn], BF16)
        t_v = [[Tok() for _ in range(4)] for _ in range(nt)]
        t_sg = [[Tok() for _ in range(4)] for _ in range(nt)]
        t_uT = [[Tok() for _ in range(nt)] for _ in range(3)]
        t_stgi = Tok("stgi")
        t_yT = Tok("yT")
        qT = AR.alloc([128, 4, Tn], BF16)
        kT = AR.alloc([128, 4, Tn], BF16)
        t_qT = [Tok() for _ in range(4)]
        t_kT = [Tok() for _ in range(4)]
        NTMP = 6
        TWS = [520] * 6 if prompt else [128, 128, 1040, 1040, 128, 128]
        tmp = [AR.alloc([128, TWS[i]], F32) for i in range(NTMP)]
        t_tmp = [Tok(f"tmp{i}") for i in range(NTMP)]
        acc = AR.alloc([128, 2, Tn], F32)
        t_acc = Tok("acc")
        rstd = AR.alloc([128, Tn], F32)
        t_rstd = Tok("rstd")
        COS = AR.alloc([128, Tn], F32)
        SIN = AR.alloc([128, Tn], F32)
        t_rope = Tok("rope")
        EL = AR.alloc([128, 4, 16], F32)
        t_EL = [Tok() for _ in range(4)]
        PT_ = [AR.alloc([128, 128], BF16) for _ in range(4)]
        t_PT = [Tok() for _ in range(4)]
        ktok = [AR.alloc([128, 128], BF16) for _ in range(4)]
        t_ktok = [Tok() for _ in range(4)]
        u_tok = AR.alloc([128, MV], BF16)
        t_utok = Tok("utok")
        kve = [AR.alloc([128, HV], F32) for _ in range(4)]
        t_kve = [Tok() for _ in range(4)]
        sq256 = [AR.alloc([128, HV], F32) for _ in range(4)]
        t_sq256 = [Tok() for _ in range(4)]
        ssv = AR.alloc([128, 8], F32)
        t_ssv = Tok("ssv")
        glrT = AR.alloc([16, Tn], BF16)
        t_glrT = Tok("glrT")
        if prompt:
            SF = AR.alloc([128, 4, HV], F32)
            Sbf = AR.alloc([128, 4, HV], BF16)
            t_SF = [Tok() for _ in range(4)]
            t_Sbf = [Tok() for _ in range(4)]
            KM = [AR.alloc([128, 4, 128], BF16) for _ in range(4)]
            QM = [AR.alloc([128, 4, 128], BF16) for _ in range(4)]
            t_KM = [Tok() for _ in range(4)]
            t_QM = [Tok() for _ in range(4)]
            cstage = stg[0:2, :]
            t_cstage = Tok("cstage")
            MASKS = {1: K["mask1"], 4: K["mask4"]}
            t_MASKS = {1: Kt["mask1"], 4: Kt["mask4"]}
            CM = K["cm4"]
            RM = K["rm4"]
            t_CMRM = [Kt["cm4"], Kt["rm4"]]
            RST = {1: K["rst_g"], 2: K["rst_h"]}
            t_RST = {1: Kt["rst_g"], 2: Kt["rst_h"]}
            GQ, GK, ELR = K["gq"], K["gk"], K["elr"]
            t_G = [Kt["gq"], Kt["gk"], Kt["elr"]]
            xtl = AR.alloc([128, 32], F32)
            t_xtl = Tok("xtl")
            xtin = AR.alloc([128, 32], F32)
            sqt = AR.alloc([128, 32], F32)
            rs2 = AR.alloc([128, 2], F32)
            h2t = AR.alloc([128, 16, 2], BF16)
            t_xtin, t_sqt, t_rs2, t_h2t = Tok(), Tok(), Tok(), Tok()
        else:
            r3_off = AR.off
            r3_words = 2 * NSEQ * HV + 64
            R3 = Arena(AR.ap[:, r3_off:r3_off + r3_words], r3_words)
            AR.off += r3_words
            SS = [R3.alloc([128, NSEQ, HV], F32) for _ in range(2)]
            R3b = Arena(AR.ap[:, r3_off:r3_off + r3_words], r3_words)
            t_SS = [Tok() for _ in range(2)]
            SSb = AR.alloc([128, NSEQ, HV], BF16)
            t_SSb = Tok()
            SSo = AR.alloc([128, NSEQ, HV], F32)
            t_SSo = Tok()
            KM = [AR.alloc([64, 16, 128], BF16)]
            QM = [AR.alloc([128, 16, 64], BF16)]
            t_KM = [Tok()]
            t_QM = [Tok()]
            msk = AR.alloc([128, 128], BF16)
            cms = AR.alloc([128, 16, 64], BF16)
            rms = AR.alloc([128, 16], F32)
            rsts = AR.alloc([128, 64], BF16)
            gqs = AR.alloc([128, 4, 64], F32)
            gks = AR.alloc([128, 4, 64], F32)
            elrs = AR.alloc([128, 4], F32)
            t_sc = Tok("sconst")
            for dst, nm in ((msk, "masks"), (cms, "cms"), (rms, "rms"), (rsts, "rst_s"), (gqs, "gqs"), (gks, "gks"),
                            (elrs, "elrs"), (COS, "cos_s"), (SIN, "sin_s")):
                LD(lambda e, dst=dst, nm=nm: e.dma_start(out=dst, in_=cst[nm]), [], [t_sc if nm not in ("cos_s", "sin_s") else t_rope])
            MASKS = {16: msk}
            t_MASKS = {16: t_sc}
            CM, RM = cms, rms
            t_CMRM = [t_sc]
            RST = {1: rsts, 2: rsts}
            t_RST = {1: t_sc, 2: t_sc}
            GQ, GK, ELR = gqs, gks, elrs
            t_G = [t_sc]
            convT = AR.alloc([128, 88, 32], F32)
            t_convT = Tok("convT")
            outc = [R3b.alloc([32, 2816], F32) for _ in range(2)]
            t_outc = [Tok("outc0"), Tok("outc1")]
            cstg = R3b.alloc([32, 2048], F32)
            t_cstg = Tok("cstg")
            U6 = [AR.alloc([128, NSEQ, 6], F32) for _ in range(2)]
            t_U6 = [Tok() for _ in range(2)]
            while len(WB) < 12 and AR.off + 2056 <= AR.words:
                WB.append(AR.alloc([128, 4096], BF16))
                t_WB.append(Tok(f"wbx{len(WB)}"))

        def load_tokens(xsrc):
            for (t0, rows) in tiles:
                LD(lambda e, t0=t0, rows=rows: e.dma_start(out=stg[0:rows, :], in_=xsrc[t0:t0 + rows, :]), [], [t_stgi])
                for g in range(4):
                    pb, pt = main_rot.get()
                    for cc in range(4):
                        c = g * 4 + cc
                        T(lambda e, pb=pb, cc=cc, c=c, rows=rows: e.transpose(
                            pb[:, cc * rows:(cc + 1) * rows], stg[0:rows, c * 128:(c + 1) * 128], K["ident_f"][0:rows, 0:rows]),
                          [t_stgi, Kt["ident_f"]], [pt])
                    act(xT[:, g * 4:g * 4 + 4, t0:t0 + rows], pb[:, 0:4 * rows].rearrange("p (a b) -> p a b", a=4),
                        AF.Copy, [pt], t_xT[g * 4:g * 4 + 4])

        def store_x(sp, c):
            ST_(lambda e: e.dma_start(out=xsw[sp][:, c, :], in_=xT[:, c, :]), [t_xT[c]], [t_xsw[sp][c]])

        def load_x(sp):
            for c in range(16):
                LD(lambda e, c=c: e.dma_start(out=xT[:, c, :], in_=xsw[sp][:, c, :]), [t_xsw[sp][c]], [t_xT[c]])

        def load_rope(sp):
            LD(lambda e: e.dma_start(out=COS, in_=cst["cos_p"][:, sp * TP:(sp + 1) * TP]), [], [t_rope])
            LD(lambda e: e.dma_start(out=SIN, in_=cst["sin_p"][:, sp * TP:(sp + 1) * TP]), [], [t_rope])

        if prompt:
            for sp in range(NSP):
                load_tokens(xp[sp * TP:(sp + 1) * TP])
                for c in range(16):
                    store_x(sp, c)
        else:
            load_tokens(xs)
        P.fence()

        def rms_rstd():
            pss, pst_ = small_rot.get()
            for c in range(16):
                k = c % 2
                act(tmp[k][:, 0:Tn], xT[:, c, :], AF.Square, [t_xT[c]], [t_tmp[k]])
                T(lambda e, c=c, k=k: e.matmul(pss[:, 0:Tn], lhsT=K["ones_f"], rhs=tmp[k][:, 0:Tn], start=(c == 0), stop=(c == 15)),
                  [t_tmp[k], Kt["ones_f"]], [pst_])
            ts(rstd, pss[:, 0:Tn], 1.0 / D, EPS, ALU.mult, ALU.add, [pst_], [t_rstd])
            act(rstd, rstd, AF.Sqrt, [t_rstd], [t_rstd])
            V(lambda e: e.reciprocal(out=rstd, in_=rstd), [t_rstd], [t_rstd])

        def modulated_norm(l, which):
            rms_rstd()
            if prompt:
                ka, kb = (0, 1) if which == 1 else (3, 4)
                for c in range(16):
                    k = 2 + c % 2
                    stt(tmp[k][:, 0:Tn], xT[:, c, :], MODP[l][:, ka, c:c + 1], rstd, ALU.mult, ALU.mult,
                        [t_xT[c], t_MODP[l], t_rstd], [t_tmp[k]])
                    act(hT[:, c, :], tmp[k][:, 0:Tn], AF.Identity, [t_tmp[k], t_MODP[l]], [t_hT[c]],
                        bias=MODP[l][:, kb, c:c + 1], scale=1.0)
            else:
                ca, cb = (16, 0) if which == 1 else (64, 48)
                t1 = tmp[2][:, 0:1024].rearrange("p (c t) -> p c t", c=16)
                tt(t1, xT, rstd.unsqueeze(1).broadcast_to([128, 16, Tn]), ALU.mult, t_xT + [t_rstd], [t_tmp[2]])
                t1v = tmp[2][:, 0:1024].rearrange("p (c s t) -> p c s t", c=16, s=NSEQ)
                t2v = tmp[3][:, 0:1024].rearrange("p (c s t) -> p c s t", c=16, s=NSEQ)
                Av = MOD[l][:, ca:ca + 16, 0:NSEQ].unsqueeze(3).broadcast_to([128, 16, NSEQ, TS])
                Bv = MOD[l][:, cb:cb + 16, 0:NSEQ].unsqueeze(3).broadcast_to([128, 16, NSEQ, TS])
                tt(t2v, t1v, Av, ALU.mult, [t_tmp[2], t_MOD[l]], [t_tmp[3]])
                tt(hT.rearrange("p c (s t) -> p c s t", s=NSEQ), t2v, Bv, ALU.add, [t_tmp[3], t_MOD[l]], t_hT)

        def residual_add(l, which, f, pm, pmt):
            if prompt:
                kg = 2 if which == 1 else 5
                stt(xT[:, f, :], pm[:, 0:Tn], MODP[l][:, kg, f:f + 1], xT[:, f, :], ALU.mult, ALU.add,
                    [pmt, t_MODP[l], t_xT[f]], [t_xT[f]])
                store_x(cx["sp"], f)
            else:
                cg = 32 if which == 1 else 80
                Gv = MOD[l][:, cg + f, 0:NSEQ].unsqueeze(2).broadcast_to([128, NSEQ, TS])
                tv = tmp[4][:, 0:Tn].rearrange("p (s t) -> p s t", s=NSEQ)
                tt(tv, pm[:, 0:Tn].rearrange("p (s t) -> p s t", s=NSEQ), Gv, ALU.mult, [pmt, t_MOD[l]], [t_tmp[4]])
                tt(xT[:, f, :], xT[:, f, :], tmp[4][:, 0:Tn], ALU.add, [t_tmp[4], t_xT[f]], [t_xT[f]])

        nbt = {0: 1, 1: 1, 2: 4} if prompt else {0: 16, 1: 16, 2: 16}
        bsz = {b: tiles[0][1] // nbt[b] for b in range(3)}

        def mixer_branch(l, b):
            nb = nbt[b]
            bs = bsz[b]
            ph1 = cx["ph1"]
            sp = cx["sp"]
            if b == 1:
                wg_, wgt = load_w(("glr", l))
                pb, pt = small_rot.get()
                mm(pb[0:16, 0:Tn], pt, [(wg_[:, c, 0:16], hT[:, c, :]) for c in range(16)], [wgt] + t_hT)
                act(glrT, pb[0:16, 0:Tn], AF.Copy, [pt], [t_glrT])
            for hp in range(2):
                kb_, kbt = load_w(("k", l, b, hp))
                if not ph1:
                    qb_, qbt = load_w(("q", l, b, hp))
                for hh in range(2):
                    h = hp * 2 + hh
                    pk, pkt = main_rot.get()
                    mm(pk[:, 0:Tn], pkt, [(kb_[:, c, hh * 128:(hh + 1) * 128], hT[:, c, :]) for c in range(16)], [kbt] + t_hT)
                    X = [tmp[i][:, 0:Tn] for i in range(NTMP)]
                    if b == 0:
                        def rope_path(pz, pzt, G, dstT, dtok):
                            raw = tmp[0].bitcast(BF16)[:, 0:Tn]
                            act(raw, pz[:, 0:Tn], AF.Copy, [pzt], [t_tmp[0]])
                            psw, pswt = small_rot.get()
                            T(lambda e: e.matmul(psw[:, 0:Tn], lhsT=K["pswap"], rhs=raw, start=True, stop=True),
                              [t_tmp[0], Kt["pswap"]], [pswt])
                            tt(X[1], raw, COS, ALU.mult, [t_tmp[0], t_rope], [t_tmp[1]])
                            tt(X[2], psw[:, 0:Tn], SIN, ALU.mult, [pswt, t_rope], [t_tmp[2]])
                            tt(X[3], X[1], X[2], ALU.add, [t_tmp[1], t_tmp[2]], [t_tmp[3]])
                            if prompt:
                                Gv = G[:, h, :].unsqueeze(1).broadcast_to([128, nt, 128])
                                tt(dstT[:, h, :].rearrange("p (i t) -> p i t", i=nt), X[3].rearrange("p (i t) -> p i t", i=nt),
                                   Gv, ALU.mult, [t_tmp[3]] + t_G, [dtok[h]])
                            else:
                                tt(dstT[:, h, :], X[3], G[:, h, :], ALU.mult, [t_tmp[3]] + t_G, [dtok[h]])
                        rope_path(pk, pkt, GK, kT, t_kT)
                        if not ph1:
                            pq, pqt = main_rot.get()
                            mm(pq[:, 0:Tn], pqt, [(qb_[:, c, hh * 128:(hh + 1) * 128], hT[:, c, :]) for c in range(16)], [qbt] + t_hT)
                            rope_path(pq, pqt, GQ, qT, t_qT)
                    else:
                        if b == 1:
                            px, pxt = small_rot.get()
                            T(lambda e, h=h, px=px: e.matmul(px[:, 0:Tn], lhsT=GLW[l][:, h * 128:(h + 1) * 128], rhs=glrT, start=True, stop=True),
                              [t_GLW[l], t_glrT], [pxt])
                            act(X[0], px[:, 0:Tn], AF.Exp, [pxt, t_LBV], [t_tmp[0]], bias=NEGB[:, l, h:h + 1], scale=-1.0)
                            act(X[1], X[0], AF.Ln, [t_tmp[0]], [t_tmp[1]], bias=1.0, scale=1.0)
                            sc_q, sc_k = -1.0 / 16.0, 1.0 / 16.0
                        else:
                            act(X[0], pk[:, 0:Tn], AF.Sigmoid, [pkt], [t_tmp[0]])
                            act(X[1], X[0], AF.Ln, [t_tmp[0], t_LBV], [t_tmp[1]],
                                bias=LBV[:, l, 0, h:h + 1], scale=LBV[:, l, 1, h:h + 1])
                            sc_q, sc_k = 1.0, -1.0
                        V(lambda e, b=b: e.tensor_tensor_scan(out=X[2], data0=RST[b][:, 0:Tn], data1=X[1], initial=0.0,
                                                              op0=ALU.mult, op1=ALU.add), [t_tmp[1], t_RST[b]], [t_tmp[2]])
                        act(X[3], X[2], AF.Exp, [t_tmp[2]], [t_tmp[3]], scale=sc_q)
                        act(X[4], X[2], AF.Exp, [t_tmp[2]], [t_tmp[4]], scale=sc_k)
                        nblk = Tn // bs
                        A(lambda e, h=h, bs=bs, nblk=nblk: e.activation(out=EL[:, h, 0:nblk], in_=tmp[3][:, bs - 1:Tn:bs], func=AF.Copy),
                          [t_tmp[3]], [t_EL[h]])
                        if b == 1:
                            tt(kT[:, h, :], pk[:, 0:Tn], X[4], ALU.mult, [pkt, t_tmp[4]], [t_kT[h]])
                        else:
                            ts(X[5], X[0], LBV[:, l, 2, h:h + 1], LBV[:, l, 1, h:h + 1], ALU.mult, ALU.add, [t_tmp[0], t_LBV], [t_tmp[5]])
                            tt(kT[:, h, :], X[5], X[4], ALU.mult, [t_tmp[5], t_tmp[4]], [t_kT[h]])
                        if ph1:
                            continue
                        pq, pqt = main_rot.get()
                        mm(pq[:, 0:Tn], pqt, [(qb_[:, c, hh * 128:(hh + 1) * 128], hT[:, c, :]) for c in range(16)], [qbt] + t_hT)
                        if b == 1:
                            stt(qT[:, h, :], pq[:, 0:Tn], HK ** -0.5, X[3], ALU.mult, ALU.mult, [pqt, t_tmp[3]], [t_qT[h]])
                        else:
                            act(X[0], pq[:, 0:Tn], AF.Sigmoid, [pqt], [t_tmp[0]])
                            tt(X[1], pq[:, 0:Tn], X[0], ALU.mult, [pqt, t_tmp[0]], [t_tmp[1]])
                            tt(qT[:, h, :], X[1], X[3], ALU.mult, [t_tmp[1], t_tmp[3]], [t_qT[h]])
            for h in range(4):
                wv_, wvt = load_w(("v", l, b, h))
                for i, (t0, rows) in enumerate(tiles):
                    pv, pvt = main_rot.get()
                    mm(pv[0:rows, 0:256], pvt, [(hT[:, c, t0:t0 + rows], wv_[:, c, :]) for c in range(16)], [wvt] + t_hT)
                    act(v_tok[0:rows, i, h * 256:(h + 1) * 256], pv[0:rows, 0:256], AF.Copy, [pvt], [t_v[i][h]])
            for h in range(0 if not ph1 else 4, 4):
                wg_, wgt = load_w(("g", l, b, h))
                for i, (t0, rows) in enumerate(tiles):
                    pg, pgt = main_rot.get()
                    mm(pg[0:rows, 0:256], pgt, [(hT[:, c, t0:t0 + rows], wg_[:, c, :]) for c in range(16)], [wgt] + t_hT)
                    k = h % 2
                    act(sq256[k][0:rows, :], pg[0:rows, 0:256], AF.Sigmoid, [pgt], [t_sq256[k]])
                    tt(sq256[k][0:rows, :], pg[0:rows, 0:256], sq256[k][0:rows, :], ALU.mult, [pgt, t_sq256[k]], [t_sq256[k]])
                    tt(sg_tok[0:rows, i, h * 256:(h + 1) * 256], sq256[k][0:rows, :], HNW[l][0:rows, b * HV:(b + 1) * HV], ALU.mult,
                       [t_sq256[k], t_HNW[l]], [t_sg[i][h]])
            def el_col(h, blk):
                if b == 0:
                    return ELR[:, h:h + 1], t_G
                return EL[:, h, blk:blk + 1], [t_EL[h]]

            if prompt:
                if ph1:
                    if sp == 0:
                        V(lambda e: e.memset(SF, 0.0), [], t_SF)
                    else:
                        LD(lambda e: e.dma_start(out=SF, in_=ph1st[b].rearrange("h k v -> k h v")), [t_ph1st[b]], t_SF)
                elif sp == 0:
                    LD(lambda e: e.dma_start(out=SF, in_=exd[l][b * 512:(b + 1) * 512, :].rearrange("(h k) v -> k h v", k=HK)), [t_exd[l]], t_SF)
                    ts(SF, SF, K["isb"][:, 0:1], None, ALU.mult, ALU.bypass, t_SF + [Kt["isb"]], t_SF)
                else:
                    LD(lambda e: e.dma_start(out=SF, in_=pst[b][l].rearrange("h k v -> k h v")), [t_pst[b][l]], t_SF)
                if not ph1:
                    act(Sbf, SF, AF.Copy, t_SF, t_Sbf)
            hgroups = [[0, 1, 2, 3]] if prompt else [[0], [1], [2], [3]]
            ss_i = [0]
            for i, (t0, C) in enumerate(tiles):
                for hg in hgroups:
                    ctx = {}
                    H = list(enumerate(hg))
                    if not prompt:
                        for hi, h in H:
                            k = ss_i[0] % 2
                            ss_i[0] += 1
                            LD(lambda e, h=h, k=k: e.dma_start(out=SS[k], in_=st_in[b][l, :, h].rearrange("s k v -> k s v")), [], [t_SS[k]])
                            V(lambda e, k=k: e.tensor_copy(out=SSb[:, 0:8, :], in_=SS[k][:, 0:8, :]), [t_SS[k]], [t_SSb])
                            act(SSb[:, 8:16, :], SS[k][:, 8:16, :], AF.Copy, [t_SS[k]], [t_SSb])
                            ctx[h] = k
                    qts = {h: qT[:, h, t0:t0 + C] for _, h in H}
                    kts = {h: kT[:, h, t0:t0 + C] for _, h in H}
                    vts = {h: v_tok[0:C, i, h * 256:(h + 1) * 256] for _, h in H}
                    pscb, psct = banks[6][:], bank_tok[6]
                    if not ph1:
                        for hi, h in H:
                            T(lambda e, hi=hi, C=C, pscb=pscb, kt_=kts[h], qt_=qts[h]: e.matmul(pscb[0:C, hi * 128:hi * 128 + C], lhsT=kt_, rhs=qt_, start=True, stop=True),
                              [t_kT[h], t_qT[h]], [psct])
                        for hi, h in H:
                            tt(PT_[h][0:C, 0:C], pscb[0:C, hi * 128:hi * 128 + C], MASKS[nb][0:C, 0:C], ALU.mult, [psct, t_MASKS[nb]], [t_PT[h]])
                    for hi, h in H:
                        T(lambda e, hi=hi, C=C, kt_=kts[h]: e.transpose(bank_bf[0:C, hi * 128:(hi + 1) * 128], kt_, K["ident_b"]),
                          [t_kT[h], Kt["ident_b"]], [bank_tok[7]])
                    for hi, h in H:
                        act(ktok[h][0:C, :], bank_bf[0:C, hi * 128:(hi + 1) * 128], AF.Copy, [bank_tok[7]], [t_ktok[h]])
                    if nb > 1:
                        for hi, h in H:
                            km, tkm = KM[h % len(KM)], t_KM[h % len(KM)]
                            tt(km[0:C, 0:nb, :], ktok[h][0:C, :].unsqueeze(1).broadcast_to([C, nb, 128]),
                               RM[0:C, 0:nb].unsqueeze(2).broadcast_to([C, nb, 128]), ALU.mult, [t_ktok[h]] + t_CMRM, [tkm])
                        if not ph1:
                            for hi, h in H:
                                qm, tqm = QM[h % len(QM)], t_QM[h % len(QM)]
                                tt(qm[:, 0:nb, 0:C], qts[h].unsqueeze(1).broadcast_to([128, nb, C]), CM[:, 0:nb, 0:C], ALU.mult,
                                   [t_qT[h]] + t_CMRM, [tqm])
                    po = {h: (banks[hi][:], bank_tok[hi]) for hi, h in H}
                    pkvs = {h: (banks[4 + hi // 2][:, (hi % 2) * 256:(hi % 2) * 256 + 256], bank_tok[4 + hi // 2]) for hi, h in H}
                    for j in range(nb):
                        blk = i * nb + j
                        if not ph1:
                            for hi, h in H:
                                pob, pot = po[h]
                                if j == 0:
                                    T(lambda e, pob=pob, h=h, C=C, vt_=vts[h]: e.matmul(pob[0:C, 0:256], lhsT=PT_[h][0:C, 0:C], rhs=vt_, start=True, stop=False),
                                      [t_PT[h], t_v[i][h]], [pot])
                                lhs = qts[h] if nb == 1 else QM[h % len(QM)][:, j, 0:C]
                                lt = [t_qT[h]] if nb == 1 else [t_QM[h % len(QM)]]
                                if prompt:
                                    srhs, srt = Sbf[:, h, :], [t_Sbf[h]]
                                else:
                                    srhs, srt = SSb[:, j, :], [t_SSb]
                                T(lambda e, pob=pob, lhs=lhs, srhs=srhs, C=C, j=j: e.matmul(pob[0:C, 0:256], lhsT=lhs, rhs=srhs, start=False, stop=(j == nb - 1)),
                                  lt + srt, [pot])
                        for hi, h in H:
                            pkv, pkvt = pkvs[h]
                            klhs = ktok[h][0:C, :] if nb == 1 else KM[h % len(KM)][0:C, j, :]
                            klt = [t_ktok[h]] if nb == 1 else [t_KM[h % len(KM)]]
                            T(lambda e, pkv=pkv, klhs=klhs, vt_=vts[h]: e.matmul(pkv, lhsT=klhs, rhs=vt_, start=True, stop=True),
                              klt + [t_v[i][h]], [pkvt])
                        for hi, h in H:
                            pkv, pkvt = pkvs[h]
                            elc, elt = el_col(h, blk)
                            act(kve[hi], pkv, AF.Identity, [pkvt] + elt, [t_kve[hi]], scale=elc)
                        for hi, h in H:
                            elc, elt = el_col(h, blk)
                            if prompt:
                                stt(SF[:, h, :], SF[:, h, :], elc, kve[hi], ALU.mult, ALU.add, [t_SF[h], t_kve[hi]] + elt, [t_SF[h]])
                            else:
                                stt(SSo[:, j, :], SS[ctx[h]][:, j, :], elc, kve[hi], ALU.mult, ALU.add, [t_SS[ctx[h]], t_kve[hi]] + elt, [t_SSo])
                        if prompt and not ph1:
                            for hi, h in H:
                                act(Sbf[:, h, :], SF[:, h, :], AF.Copy, [t_SF[h]], [t_Sbf[h]])
                    if not ph1:
                        h0 = hg[0]
                        nh_ = len(hg)
                        for hi, h in H:
                            pob, pot = po[h]
                            act(sq256[hi][0:C, :], pob[0:C, 0:256], AF.Square, [pot], [t_sq256[hi]])
                        for hi, h in H:
                            V(lambda e, hi=hi, h=h, C=C: e.tensor_reduce(out=ssv[0:C, h:h + 1], in_=sq256[hi][0:C, :], axis=AX.X, op=ALU.add),
                              [t_sq256[hi]], [t_ssv])
                        sv_ = ssv[0:C, h0:h0 + nh_]
                        ts(sv_, sv_, 1.0 / HV, EPS, ALU.mult, ALU.add, [t_ssv], [t_ssv])
                        act(sv_, sv_, AF.Sqrt, [t_ssv], [t_ssv])
                        V(lambda e, sv_=sv_: e.reciprocal(out=sv_, in_=sv_), [t_ssv], [t_ssv])
                        for hi, h in H:
                            pob, pot = po[h]
                            stt(u_tok[0:C, h * 256:(h + 1) * 256], pob[0:C, 0:256], ssv[0:C, h:h + 1], sg_tok[0:C, i, h * 256:(h + 1) * 256],
                                ALU.mult, ALU.mult, [pot, t_ssv, t_sg[i][h]], [t_utok])
                    if not prompt:
                        for hi, h in H:
                            ST_(lambda e, h=h: e.dma_start(out=sst[b][l, :, h].rearrange("s k v -> k s v"), in_=SSo), [t_SSo], [t_sst])
                            outs_final.append(P.streams["act"][-1])
                for c in range(8 if not ph1 else 0):
                    T(lambda e, c=c, C=C: e.transpose(bank_bf[:, c * C:(c + 1) * C], u_tok[0:C, c * 128:(c + 1) * 128], K["ident_b"][0:C, 0:C]),
                      [t_utok, Kt["ident_b"]], [bank_tok[7]])
                if not ph1:
                    act(uT[:, b, :, t0:t0 + C], bank_bf[:, 0:8 * C].rearrange("p (c t) -> p c t", c=8), AF.Copy, [bank_tok[7]], [t_uT[b][i]])
            if prompt:
                if ph1 and sp == 0:
                    ST_(lambda e: e.dma_start(out=ph1st[b].rearrange("h k v -> k h v"), in_=SF), t_SF, [t_ph1st[b]])
                elif ph1:
                    ST_(lambda e: e.dma_start(out=exs[l][b * 512:(b + 1) * 512, :].rearrange("(h k) v -> k h v", k=HK), in_=SF), t_SF, [t_exs[l][b]])
                else:
                    ST_(lambda e: e.dma_start(out=pst[b][l].rearrange("h k v -> k h v"), in_=SF), t_SF, [t_pst[b][l]])
                    if sp == NSP - 1:
                        outs_final.append(P.streams["act"][-1])

        def merge_and_out(l):
            P.fence()
            for fp in range(8):
                for b in range(3):
                    wbr, wbrt = load_w(("br", l, b, fp))
                    wmg, wmgt = load_w(("mg", l, b, fp))
                    for ft in range(2):
                        f = fp * 2 + ft
                        po_, pot = main_rot.get()
                        mm(po_[:, 0:Tn], pot, [(wbr[:, c, ft * 128:(ft + 1) * 128], uT[:, b, c, :]) for c in range(8)], [wbrt] + t_uT[b])
                        pg, pgt = main_rot.get()
                        mm(pg[:, 0:Tn], pgt, [(wmg[:, c, ft * 128:(ft + 1) * 128], hT[:, c, :]) for c in range(16)], [wmgt] + t_hT)
                        k = ft
                        act(tmp[k][:, 0:Tn], pg[:, 0:Tn], AF.Sigmoid, [pgt], [t_tmp[k]])
                        if b == 0:
                            tt(acc[:, ft, :], po_[:, 0:Tn], tmp[k][:, 0:Tn], ALU.mult, [pot, t_tmp[k]], [t_acc])
                        elif b == 1:
                            tt(tmp[k][:, 0:Tn], po_[:, 0:Tn], tmp[k][:, 0:Tn], ALU.mult, [pot, t_tmp[k]], [t_tmp[k]])
                            tt(acc[:, ft, :], acc[:, ft, :], tmp[k][:, 0:Tn], ALU.add, [t_tmp[k], t_acc], [t_acc])
                        else:
                            tt(tmp[k][:, 0:Tn], po_[:, 0:Tn], tmp[k][:, 0:Tn], ALU.mult, [pot, t_tmp[k]], [t_tmp[k]])
                            tt(mT[:, f, :], acc[:, ft, :], tmp[k][:, 0:Tn], ALU.add, [t_tmp[k], t_acc], [t_mT])
            for fp in range(8):
                wo, wot = load_w(("out", l, fp))
                for ft in range(2):
                    f = fp * 2 + ft
                    pm, pmt = main_rot.get()
                    mm(pm[:, 0:Tn], pmt, [(wo[:, c, ft * 128:(ft + 1) * 128], mT[:, c, :]) for c in range(16)], [wot, t_mT])
                    residual_add(l, 1, f, pm, pmt)
            if prompt and cx["sp"] == NSP - 1:
                V(lambda e: e.tensor_copy(out=xtl[:].rearrange("p (c t) -> p c t", t=2), in_=xT[:, :, Tn - 2:Tn]), t_xT, [t_xtl])
                ST_(lambda e: e.dma_start(out=exs2[l], in_=xtl), [t_xtl], [t_exs2[l]])
            P.fence()

        def ffn(l):
            modulated_norm(l, 2)
            P.fence()
            if prompt and cx["sp"] == 0:
                LD(lambda e: e.dma_start(out=xtin, in_=exd2[l][0:128, :]), [t_exd2[l]], [t_xtin])
                act(sqt, xtin, AF.Square, [t_xtin], [t_sqt])
                pss2, pss2t = small_rot.get()
                for c in range(16):
                    T(lambda e, c=c, pss2=pss2: e.matmul(pss2[:, 0:2], lhsT=K["ones_f"], rhs=sqt[:, 2 * c:2 * c + 2], start=(c == 0), stop=(c == 15)),
                      [t_sqt, Kt["ones_f"]], [pss2t])
                ts(rs2, pss2[:, 0:2], 1.0 / D, EPS, ALU.mult, ALU.add, [pss2t], [t_rs2])
                act(rs2, rs2, AF.Sqrt, [t_rs2], [t_rs2])
                V(lambda e: e.reciprocal(out=rs2, in_=rs2), [t_rs2], [t_rs2])
                for c in range(16):
                    stt(sqt[:, 2 * c:2 * c + 2], xtin[:, 2 * c:2 * c + 2], MODP[l][:, 3, c:c + 1], rs2, ALU.mult, ALU.mult,
                        [t_xtin, t_MODP[l], t_rs2, t_sqt], [t_sqt])
                    act(h2t[:, c, :], sqt[:, 2 * c:2 * c + 2], AF.Identity, [t_sqt, t_MODP[l]], [t_h2t], bias=MODP[l][:, 4, c:c + 1], scale=1.0)
            if not prompt:
                for g in range(6):
                    ncol = min(2048, 2 * DFF - g * 2048)
                    LD(lambda e, g=g, ncol=ncol: e.dma_start(out=cstg[:, 0:ncol], in_=sconv_in[l, :, g * 2048:g * 2048 + ncol]), [], [t_cstg])
                    for q4 in range(0, ncol // 128, 4):
                        pb, pt = small_rot.get()
                        for cc in range(4):
                            T(lambda e, pb=pb, cc=cc, q4=q4: e.transpose(pb[:, cc * 32:(cc + 1) * 32], cstg[:, (q4 + cc) * 128:(q4 + cc + 1) * 128],
                                                                          K["ident_f"][0:32, 0:32]), [t_cstg, Kt["ident_f"]], [pt])
                        act(convT[:, g * 16 + q4:g * 16 + q4 + 4, :], pb[:, 0:128].rearrange("p (a b) -> p a b", a=4), AF.Copy, [pt], [t_convT])
            for j in range(NFT):
                wu, wut = load_w(("up", l, j))
                wux = list(wb_extra[0])
                res = []
                for half in range(2):
                    tile_idx = half * NFT + j
                    pu, put = main_rot.get()
                    mm(pu[:, 0:Tn], put, [(wu[:, c, half * 128:(half + 1) * 128], hT[:, c, :]) for c in range(16)], [wut] + wux + t_hT)
                    o3 = 3 * half
                    w0 = CW[l][:, 0, tile_idx:tile_idx + 1]
                    w1 = CW[l][:, 1, tile_idx:tile_idx + 1]
                    w2 = CW[l][:, 2, tile_idx:tile_idx + 1]
                    cb = CW[l][:, 3, tile_idx:tile_idx + 1]
                    if prompt:
                        U = tmp[o3]
                        tc_ = t_CARRY[l][tile_idx]
                        if cx["sp"] == 0:
                            pu2, pu2t = small_rot.get()
                            mm(pu2[:, 0:2], pu2t, [(wu[:, c, half * 128:(half + 1) * 128], h2t[:, c, :]) for c in range(16)], [wut, t_h2t] + wux)
                            ts(U[:, 0:2], pu2[:, 0:2], K["isb"][:, 0:1], None, ALU.mult, ALU.bypass, [pu2t, Kt["isb"]], [t_tmp[o3]])
                        else:
                            V(lambda e, U=U, tile_idx=tile_idx: e.tensor_copy(out=U[:, 0:2], in_=CARRY[l][:, tile_idx, :]), [tc_], [t_tmp[o3]])
                        act(U[:, 2:2 + Tn], pu[:, 0:Tn], AF.Copy, [put], [t_tmp[o3]])
                        V(lambda e, U=U, tile_idx=tile_idx: e.tensor_copy(out=CARRY[l][:, tile_idx, :], in_=U[:, Tn:Tn + 2]), [t_tmp[o3]], [tc_])
                        c1, c2 = tmp[o3 + 1][:, 0:Tn], tmp[o3 + 2][:, 0:Tn]
                        ts(c1, U[:, 0:Tn], w0, cb, ALU.mult, ALU.add, [t_tmp[o3], t_CW[l]], [t_tmp[o3 + 1]])
                        stt(c2, U[:, 1:Tn + 1], w1, c1, ALU.mult, ALU.add, [t_tmp[o3], t_tmp[o3 + 1], t_CW[l]], [t_tmp[o3 + 2]])
                        stt(c1, U[:, 2:Tn + 2], w2, c2, ALU.mult, ALU.add, [t_tmp[o3], t_tmp[o3 + 2], t_CW[l]], [t_tmp[o3 + 1]])
                        res.append((c1, t_tmp[o3 + 1], tmp[o3 + 2][:, 0:Tn], t_tmp[o3 + 2]))
                    else:
                        U = U6[half]
                        tu = t_U6[half]
                        V(lambda e, U=U, tile_idx=tile_idx: e.tensor_copy(out=U[:, :, 0:2], in_=convT[:, tile_idx, :].rearrange("p (s r) -> p s r", r=2)),
                          [t_convT], [tu])
                        act(U[:, :, 2:6], pu[:, 0:Tn].rearrange("p (s t) -> p s t", t=TS), AF.Copy, [put], [tu])
                        pb, pt = small_rot.get()
                        raw = tmp[o3][:, 0:32].rearrange("p (s r) -> p s r", r=2)
                        V(lambda e, U=U, raw=raw: e.tensor_copy(out=raw, in_=U[:, :, 4:6]), [tu], [t_tmp[o3]])
                        T(lambda e, pb=pb, o3=o3: e.transpose(pb[0:32, 0:128], tmp[o3][:, 0:32], K["ident_f"]), [t_tmp[o3], Kt["ident_f"]], [pt])
                        act(outc[half][:, (tile_idx % 22) * 128:(tile_idx % 22 + 1) * 128], pb[0:32, 0:128], AF.Copy, [pt], [t_outc[half]])
                        c1 = tmp[o3 + 1][:, 0:Tn].rearrange("p (s t) -> p s t", t=TS)
                        c2 = tmp[o3 + 2][:, 0:Tn].rearrange("p (s t) -> p s t", t=TS)
                        ts(c1, U[:, :, 0:4], w0, cb, ALU.mult, ALU.add, [tu, t_CW[l]], [t_tmp[o3 + 1]])
                        stt(c2, U[:, :, 1:5], w1, c1, ALU.mult, ALU.add, [tu, t_tmp[o3 + 1], t_CW[l]], [t_tmp[o3 + 2]])
                        stt(c1, U[:, :, 2:6], w2, c2, ALU.mult, ALU.add, [tu, t_tmp[o3 + 2], t_CW[l]], [t_tmp[o3 + 1]])
                        res.append((tmp[o3 + 1][:, 0:Tn], t_tmp[o3 + 1], tmp[o3 + 2][:, 0:Tn], t_tmp[o3 + 2]))
                (ca, tca, sa, tsa), (cbv, tcb, _, _) = res
                act(sa, ca, AF.Silu, [tca], [tsa])
                tt(actT[:, j, :], sa, cbv, ALU.mult, [tsa, tcb], [t_act[j]])
                if (not prompt) and j % 22 == 21:
                    for half in range(2):
                        grp = half * 2 + j // 22
                        ST_(lambda e, half=half, grp=grp: e.dma_start(out=sconv_o[l][:, grp * 2816:(grp + 1) * 2816], in_=outc[half]),
                            [t_outc[half]], [t_sconv_o])
                        outs_final.append(P.streams["act"][-1])
            for f in range(16):
                pf, pft = main_rot.get()
                for kh in range(2):
                    wd, wdt = load_w(("dn", l, f, kh))
                    for c in range(22):
                        T(lambda e, pf=pf, wd=wd, c=c, kh=kh: e.matmul(pf[:, 0:Tn], lhsT=wd[:, c, :], rhs=actT[:, kh * 22 + c, :],
                                                                     start=(kh == 0 and c == 0), stop=(kh == 1 and c == 21)),
                          [wdt, t_act[kh * 22 + c]], [pft])
                residual_add(l, 2, f, pf, pft)
            P.fence()

        t_mT = Tok("mT")
        t_act = [Tok(f"act{j}") for j in range(NFT)]

        def conv_out(l):
            for g in range(6):
                n = min(16, 88 - g * 16)
                for q4 in range(0, n, 4):
                    pb, pt = small_rot.get()
                    for cc in range(4):
                        ti = g * 16 + q4 + cc
                        T(lambda e, pb=pb, cc=cc, ti=ti, l=l: e.transpose(pb[0:2, cc * 128:(cc + 1) * 128], CARRY[l][:, ti, :], K["ident_f"]),
                          [t_CARRY[l][ti], Kt["ident_f"]], [pt])
                    act(cstage[:, q4 * 128:(q4 + 4) * 128], pb[0:2, 0:512], AF.Copy, [pt], [t_cstage])
                ST_(lambda e, g=g, n=n, l=l: e.dma_start(out=pconv[l][:, g * 2048:g * 2048 + n * 128], in_=cstage[:, 0:n * 128]), [t_cstage], [t_pconv])
                outs_final.append(P.streams["act"][-1])
            P.fence()

        def final_out(ydst):
            rms_rstd()
            for c in range(16):
                stt(yT[:, c, :], xT[:, c, :], PV[:, 64 + c:65 + c], rstd, ALU.mult, ALU.mult, [t_xT[c], t_PV, t_rstd], [t_yT])
            for (t0, rows) in tiles:
                for g in range(4):
                    pb, pt = main_rot.get()
                    for cc in range(4):
                        c = g * 4 + cc
                        T(lambda e, pb=pb, cc=cc, c=c, t0=t0, rows=rows: e.transpose(pb[0:rows, cc * 128:(cc + 1) * 128], yT[:, c, t0:t0 + rows], K["ident_f"]),
                          [t_yT, Kt["ident_f"]], [pt])
                    act(stg[0:rows, g * 512:(g + 1) * 512], pb[0:rows, 0:512], AF.Copy, [pt], [t_stgo])
                ST_(lambda e, t0=t0, rows=rows: e.dma_start(out=ydst[t0:t0 + rows, :], in_=stg[0:rows, :]), [t_stgo], [])
                outs_final.append(P.streams["act"][-1])
            P.fence()

        if not prompt:
            for l in range(DEPTH):
                modulated_norm(l, 1)
                for b in range(3):
                    mixer_branch(l, b)
                merge_and_out(l)
                ffn(l)
            final_out(ys)
        else:
            for l in range(DEPTH):
                cx["ph1"] = True
                for sp in range(NSP):
                    cx["sp"] = sp
                    load_x(sp)
                    load_rope(sp)
                    modulated_norm(l, 1)
                    for b in range(3):
                        mixer_branch(l, b)
                    P.fence()
                cx["ph1"] = False
                if no_cc:
                    LD(lambda e, l=l: e.dma_start(out=exd[l][0:1536, :], in_=exs[l]), t_exs[l], [t_exd[l]])
                else:
                    P.add("pool", lambda e, l=l: e.collective_compute("AllGather", ALU.bypass, replica_groups=pairs,
                                                                      ins=[exs[l].opt()], outs=[exd[l].opt()]), t_exs[l], [t_exd[l]])
                emit_casts(("ffn", l))
                for sp in range(NSP):
                    cx["sp"] = sp
                    load_x(sp)
                    load_rope(sp)
                    modulated_norm(l, 1)
                    for b in range(3):
                        mixer_branch(l, b)
                    merge_and_out(l)
                if no_cc:
                    LD(lambda e, l=l: e.dma_start(out=exd2[l][0:128, :], in_=exs2[l]), [t_exs2[l]], [t_exd2[l]])
                else:
                    P.add("pool", lambda e, l=l: e.collective_compute("AllGather", ALU.bypass, replica_groups=pairs,
                                                                      ins=[exs2[l].opt()], outs=[exd2[l].opt()]), [t_exs2[l]], [t_exd2[l]])
                if l + 1 < DEPTH:
                    emit_casts(("mix", l + 1))
                for sp in range(NSP):
                    cx["sp"] = sp
                    load_x(sp)
                    ffn(l)
                conv_out(l)
            for sp in range(NSP):
                load_x(sp)
                final_out(yp[sp * TP:(sp + 1) * TP])
        AR.off = m0

    t_pst = [[Tok() for _ in range(DEPTH)] for _ in range(3)]
    t_sst = Tok()
    t_sconv_o = Tok()
    t_pconv = Tok()
    t_stgo = Tok("stgo")
    outs_final = []

    setup()
    MOD = [AR.alloc([128, 96, 17], F32) for _ in range(DEPTH)]
    t_MOD = [Tok("MOD") for _ in range(DEPTH)]
    compute_mod(MOD, t_MOD)
    for l in range(DEPTH):
        LD(lambda e, l=l, M=MOD: e.dma_start(out=modsc[l], in_=M[l][:].rearrange("p a b -> p (a b)")), [t_MOD[l]], [t_modsc[l]])
    emit_casts(("mix", 0))
    P.fence()
    AR.off = persist_mark
    run_pass("p")
    AR.off = persist_mark
    MOD = [AR.alloc([128, 96, 17], F32) for _ in range(DEPTH)]
    t_MOD = [Tok("MOD") for _ in range(DEPTH)]
    for l in range(DEPTH):
        LD(lambda e, l=l, M=MOD: e.dma_start(out=M[l][:].rearrange("p a b -> p (a b)"), in_=modsc[l]), [t_modsc[l]], [t_MOD[l]])
    run_pass("s", MOD, t_MOD)
    P.final_waits = outs_final
    P.emit(st)
    st.close()
    return nc


_CACHE = {}


def _prep_inputs(inp):
    f32 = lambda a: np.ascontiguousarray(np.asarray(a, dtype=np.float32))
    consts = host_consts()
    pv = np.zeros((96, 128), np.float32)
    nm, nf, fn = f32(inp["norm_mix"]), f32(inp["norm_ffn"]), f32(inp["final_norm"])
    pv[0:16] = nm[0].reshape(16, 128)
    pv[16:32] = nm[1].reshape(16, 128)
    pv[32:48] = nf[0].reshape(16, 128)
    pv[48:64] = nf[1].reshape(16, 128)
    pv[64:80] = fn.reshape(16, 128)
    lbl = f32(inp["hgrn_lb_logits"])
    pv[80:84] = lbl[0].reshape(4, 128)
    pv[84:88] = lbl[1].reshape(4, 128)
    gb = f32(inp["gla_b_lr"])
    pv[88:92] = gb[0].reshape(4, 128)
    pv[92:96] = gb[1].reshape(4, 128)
    cw, cb = f32(inp["ffn_conv_w"]), f32(inp["ffn_conv_b"])
    convp = np.concatenate([cw.reshape(DEPTH, 3, 88, 128), cb.reshape(DEPTH, 1, 88, 128)], axis=1)
    shared = {
        "w_in": f32(inp["w_in"]), "gla_w_lr": f32(inp["gla_w_lr"]), "pvecs": pv,
        "head_norm": f32(inp["head_norm"]).reshape(DEPTH, 3 * HV), "w_branch": f32(inp["w_branch"]),
        "w_out": f32(inp["w_out"]), "w_ada": f32(inp["w_ada"]), "b_ada": f32(inp["b_ada"]).reshape(DEPTH, 96, 128),
        "ffn_w_up": f32(inp["ffn_w_up"]), "convp": np.ascontiguousarray(convp), "ffn_w_down": f32(inp["ffn_w_down"]),
    }
    for n, s, dt in CONST_SPECS:
        if n not in ("cos_p", "sin_p", "isb"):
            shared["k_" + n] = np.ascontiguousarray(consts[n])
    x_prompt, x_sample = f32(inp["x_prompt"]), f32(inp["x_sample"])
    c_prompt, c_sample = f32(inp["c_prompt"]), f32(inp["c_sample"])
    sts = [f32(inp["state_ret"]), f32(inp["state_gla"]), f32(inp["state_hgrn"])]
    sc = f32(inp["state_conv"])
    maps = []
    for c in range(N_CORES):
        b = c // 2
        hf = c % 2
        s0 = c * NSEQ
        m = dict(shared)
        m["xp"] = np.ascontiguousarray(x_prompt[b, hf * HALF:(hf + 1) * HALF])
        m["k_cos_p"] = np.ascontiguousarray(consts["cos_p"][:, hf * HALF:(hf + 1) * HALF])
        m["k_sin_p"] = np.ascontiguousarray(consts["sin_p"][:, hf * HALF:(hf + 1) * HALF])
        m["k_isb"] = np.full((128, 1), float(hf), np.float32)
        m["xs"] = np.ascontiguousarray(x_sample[s0:s0 + NSEQ].reshape(TSAMP, D))
        m["cvec"] = np.ascontiguousarray(np.concatenate([c_sample[s0:s0 + NSEQ], c_prompt[b:b + 1]], axis=0))
        for k in range(3):
            m[f"st_in{k}"] = np.ascontiguousarray(sts[k][:, s0:s0 + NSEQ])
        m["sconv_in"] = np.ascontiguousarray(sc[:, s0:s0 + NSEQ].reshape(DEPTH, NSEQ * 2, 2 * DFF))
        maps.append(m)
    return maps


def kernel(**inputs):
    if "nc" not in _CACHE:
        _CACHE["nc"] = build_program()
    nc = _CACHE["nc"]
    maps = _prep_inputs(inputs)
    res = run_bass_kernel_spmd(nc, maps, core_ids=list(range(N_CORES)))
    R = res.results
    y_prompt = np.stack([np.concatenate([R[2 * b]["yp"], R[2 * b + 1]["yp"]], axis=0) for b in range(4)]).astype(np.float32)
    y_sample = np.concatenate([R[c]["ys"].reshape(NSEQ, TS, D) for c in range(N_CORES)], axis=0).astype(np.float32)
    outs = [y_prompt, y_sample]
    for k in range(3):
        outs.append(np.stack([R[2 * b + 1][f"pst{k}"] for b in range(4)], axis=1).astype(np.float32))
    outs.append(np.stack([R[2 * b + 1]["pconv"] for b in range(4)], axis=1).astype(np.float32))
    for k in range(3):
        outs.append(np.concatenate([R[c][f"sst{k}"] for c in range(N_CORES)], axis=1).astype(np.float32))
    outs.append(np.concatenate([R[c]["sconv_o"].reshape(DEPTH, NSEQ, 2, 2 * DFF) for c in range(N_CORES)], axis=1).astype(np.float32))
    return tuple(outs)
```

```python
import numpy as np
from contextlib import ExitStack
import ml_dtypes
import concourse.bass as bass
import concourse.mybir as mybir
from concourse.bass_utils import run_bass_kernel_spmd

F32 = mybir.dt.float32
BF16 = mybir.dt.bfloat16
ALU = mybir.AluOpType
AF = mybir.ActivationFunctionType
AX = mybir.AxisListType
NPBF = ml_dtypes.bfloat16

D = 2048
NC16 = 16
NH = 4
HK = 128
HV = 256
MV = 1024
DFF = 5632
NFT = 44
NIN = 15376
DEPTH = 2
SEQ = 2048
TP = 512
NSP = 2
HALF = NSP * TP
PAIRS = [[0, 1], [2, 3], [4, 5], [6, 7]]
NSEQ = 16
TS = 4
TSAMP = NSEQ * TS
PAST = 16384
EPS = 1e-6
N_CORES = 8
OFF = [(0, 512, 1024, 2048), (3072, 3584, 4096, 5120), (6160, 6672, 7184, 8208)]
OFF_GLR = 6144
OFF_MG = 9232

ENGS = ("pe", "dve", "act", "pool", "sp")
SEM_EPOCH = 30000


class Tok:
    __slots__ = ("name", "last_write", "reads")

    def __init__(self, name=""):
        self.name = name
        self.last_write = None
        self.reads = []


class Op:
    __slots__ = ("eng", "fn", "deps", "is_dma", "sig", "needs_sig", "prewait", "wload")

    def __init__(self, eng, fn, is_dma):
        self.eng = eng
        self.fn = fn
        self.deps = set()
        self.is_dma = is_dma
        self.sig = None
        self.needs_sig = False
        self.prewait = None
        self.wload = False


class Prog:
    def __init__(self, nc, n_dma_sems=8):
        self.nc = nc
        self.streams = {e: [] for e in ENGS}
        self.n_dma_sems = n_dma_sems
        self.final_waits = []
        self.fence_deps = set()
        self.since_fence = []

    def add(self, eng, fn, reads=(), writes=(), dma=False, wload=False):
        op = Op(eng, fn, dma)
        op.wload = wload
        for t in reads:
            if t.last_write is not None:
                op.deps.add(t.last_write)
        for t in writes:
            if t.last_write is not None:
                op.deps.add(t.last_write)
            for r in t.reads:
                op.deps.add(r)
        if not wload:
            op.deps |= self.fence_deps
        op.deps.discard(op)
        for t in reads:
            t.reads.append(op)
        for t in writes:
            t.last_write = op
            t.reads = []
        self.streams[eng].append(op)
        if dma and not wload:
            self.since_fence.append(op)
        return op

    def fence(self):
        deps = set()
        for e in ENGS:
            if e in ("sp", "pool"):
                continue
            if self.streams[e]:
                deps.add(self.streams[e][-1])
        for op in self.since_fence:
            deps.add(op)
        self.since_fence = []
        self.fence_deps = deps

    def emit(self, stack):
        nc = self.nc
        for e in ENGS:
            for op in self.streams[e]:
                if op.eng == "pe":
                    op.deps = {d for d in op.deps if d.eng != "pe" or d.is_dma}
                for d in op.deps:
                    d.needs_sig = True
        self.sems = []

        def newsem(name):
            s = stack.enter_context(nc.semaphore(name))
            self.sems.append(s)
            return s

        for e in ENGS:
            cnt = 0
            ep = 0
            sem = None
            dsems = None
            dcnt = None
            di = 0
            for op in self.streams[e]:
                if op.is_dma:
                    if dsems is None:
                        dsems = [newsem(f"d_{e}_{i}") for i in range(self.n_dma_sems)]
                        dcnt = [0] * self.n_dma_sems
                    i = di % self.n_dma_sems
                    di += 1
                    if dcnt[i] + 16 > SEM_EPOCH:
                        dsems[i] = newsem(f"d_{e}_{i}_{di}")
                        dcnt[i] = 0
                    if dcnt[i] > 0:
                        op.prewait = (dsems[i], dcnt[i])
                    dcnt[i] += 16
                    op.sig = (dsems[i], dcnt[i])
                elif op.needs_sig:
                    if sem is None or cnt >= SEM_EPOCH:
                        sem = newsem(f"c_{e}_{ep}")
                        ep += 1
                        cnt = 0
                    cnt += 1
                    op.sig = (sem, cnt)
        block = stack.enter_context(nc.Block())
        self.n_wait = 0

        def run_stream(e):
            def body(eng):
                waited = {}
                for op in self.streams[e]:
                    need = {}
                    for d in op.deps:
                        s, v = d.sig
                        k = id(s)
                        if waited.get(k, 0) >= v:
                            continue
                        if k not in need or need[k][1] < v:
                            need[k] = (s, v)
                    if op.prewait is not None:
                        s, v = op.prewait
                        k = id(s)
                        if waited.get(k, 0) < v and (k not in need or need[k][1] < v):
                            need[k] = (s, v)
                    for k, (s, v) in need.items():
                        eng.wait_ge(s, v)
                        waited[k] = v
                        self.n_wait += 1
                    ins = op.fn(eng)
                    if op.is_dma:
                        ins.then_inc(op.sig[0], 16)
                    elif op.sig is not None:
                        ins.then_inc(op.sig[0], 1)
                if e == "sp":
                    for op in self.final_waits:
                        s, v = op.sig
                        if waited.get(id(s), 0) < v:
                            eng.wait_ge(s, v)
                            waited[id(s)] = v
            return body

        block.tensor(run_stream("pe"))
        block.vector(run_stream("dve"))
        block.scalar(run_stream("act"))
        block.gpsimd(run_stream("pool"))
        block.sync(run_stream("sp"))


def host_consts():
    c = {}
    c["ident_f"] = np.eye(128, dtype=np.float32)
    c["ident_b"] = np.eye(128, dtype=np.float32).astype(NPBF)
    sw = np.zeros((128, 128), np.float32)
    for m in range(128):
        sw[(m + 64) % 128, m] = 1.0
    c["pswap"] = sw.astype(NPBF)
    c["ones_f"] = np.ones((128, 128), np.float32)
    s = np.arange(128)[:, None]
    t = np.arange(128)[None, :]
    c["mask1"] = (s <= t).astype(np.float32).astype(NPBF)
    c["mask4"] = ((s <= t) & (s // 32 == t // 32)).astype(np.float32).astype(NPBF)
    ms = np.zeros((128, 128), np.float32)
    s6 = np.arange(64)[:, None]
    t6 = np.arange(64)[None, :]
    ms[:64, :64] = ((s6 <= t6) & (s6 // 4 == t6 // 4))
    c["masks"] = ms.astype(NPBF)
    cm4 = np.zeros((128, 4, 128), np.float32)
    for j in range(4):
        cm4[:, j, 32 * j:32 * j + 32] = 1.0
    c["cm4"] = cm4.astype(NPBF)
    rm4 = np.zeros((128, 4), np.float32)
    for p in range(128):
        rm4[p, p // 32] = 1.0
    c["rm4"] = rm4
    cms = np.zeros((128, 16, 64), np.float32)
    for j in range(16):
        cms[:, j, 4 * j:4 * j + 4] = 1.0
    c["cms"] = cms.astype(NPBF)
    rms = np.zeros((128, 16), np.float32)
    for p in range(64):
        rms[p, p // 4] = 1.0
    c["rms"] = rms
    tt = np.arange(512)
    c["rst_g"] = np.broadcast_to((tt % 128 != 0).astype(np.float32), (128, 512)).astype(NPBF)
    c["rst_h"] = np.broadcast_to((tt % 32 != 0).astype(np.float32), (128, 512)).astype(NPBF)
    c["rst_s"] = np.broadcast_to((np.arange(64) % 4 != 0).astype(np.float32), (128, 64)).astype(NPBF)
    lg = np.log1p(-np.exp2(-5.0 - np.arange(4, dtype=np.float32))).astype(np.float32)
    tq = np.arange(128, dtype=np.float32) + 1.0
    gq = np.exp(lg[:, None] * tq[None, :]).astype(np.float32)
    gk = (np.exp(-lg[:, None] * tq[None, :]) * (HK ** -0.5)).astype(np.float32)
    c["gq"] = np.broadcast_to(gq, (128, 4, 128)).copy()
    c["gk"] = np.broadcast_to(gk, (128, 4, 128)).copy()
    ts_ = (np.arange(64) % 4).astype(np.float32) + 1.0
    gqs = np.exp(lg[:, None] * ts_[None, :]).astype(np.float32)
    gks = (np.exp(-lg[:, None] * ts_[None, :]) * (HK ** -0.5)).astype(np.float32)
    c["gqs"] = np.broadcast_to(gqs, (128, 4, 64)).copy()
    c["gks"] = np.broadcast_to(gks, (128, 4, 64)).copy()
    c["elr"] = np.broadcast_to(np.exp(lg * 128.0).astype(np.float32), (128, 4)).copy()
    c["elrs"] = np.broadcast_to(np.exp(lg * 4.0).astype(np.float32), (128, 4)).copy()
    half = 64
    inv = (10000.0 ** (-np.arange(half, dtype=np.float32) / half)).astype(np.float32)
    invm = np.concatenate([inv, inv])
    sgn = np.concatenate([-np.ones(64, np.float32), np.ones(64, np.float32)])
    pos = np.arange(SEQ, dtype=np.float32)
    ang = (pos[None, :] * invm[:, None]).astype(np.float32)
    c["cos_p"] = np.cos(ang).astype(np.float32)
    c["sin_p"] = (np.sin(ang) * sgn[:, None]).astype(np.float32)
    poss = (PAST + (np.arange(64) % 4)).astype(np.float32)
    angs = (poss[None, :] * invm[:, None]).astype(np.float32)
    c["cos_s"] = np.cos(angs).astype(np.float32)
    c["sin_s"] = (np.sin(angs) * sgn[:, None]).astype(np.float32)
    return c


CONST_SPECS = [
    ("ident_f", [128, 128], F32), ("ident_b", [128, 128], BF16), ("pswap", [128, 128], BF16),
    ("ones_f", [128, 128], F32), ("mask1", [128, 128], BF16), ("mask4", [128, 128], BF16),
    ("masks", [128, 128], BF16), ("cm4", [128, 4, 128], BF16), ("rm4", [128, 4], F32),
    ("cms", [128, 16, 64], BF16), ("rms", [128, 16], F32), ("rst_g", [128, 512], BF16),
    ("rst_h", [128, 512], BF16), ("rst_s", [128, 64], BF16), ("gq", [128, 4, 128], F32),
    ("gk", [128, 4, 128], F32), ("gqs", [128, 4, 64], F32), ("gks", [128, 4, 64], F32),
    ("elr", [128, 4], F32), ("elrs", [128, 4], F32), ("cos_p", [128, HALF], F32),
    ("sin_p", [128, HALF], F32), ("isb", [128, 1], F32), ("cos_s", [128, 64], F32), ("sin_s", [128, 64], F32),
]


class Arena:
    def __init__(self, ap, words):
        self.ap = ap
        self.words = words
        self.off = 0

    def alloc(self, shape, dt):
        n = int(np.prod(shape[1:]))
        w = (n + 1) // 2 if dt == BF16 else n
        w = (w + 7) // 8 * 8
        assert self.off + w <= self.words, f"arena overflow {self.off}+{w}>{self.words}"
        v = self.ap[:, self.off:self.off + w]
        self.off += w
        if dt == BF16:
            v = v.bitcast(BF16)
        v = v[:, 0:n]
        if len(shape) == 3:
            v = v.rearrange("p (a b) -> p a b", a=shape[1])
        elif len(shape) == 4:
            v = v.rearrange("p (a b c) -> p a b c", a=shape[1], b=shape[2])
        if shape[0] < 128:
            v = v[0:shape[0]]
        return v


ARENA_WORDS = 53200


def build_program(debug=False, n_cores=N_CORES, no_cc=False):
    pairs = [[2 * i, 2 * i + 1] for i in range(n_cores // 2)]
    nc = bass.Bass("TRN2", target_bir_lowering=False)

    def din(name, shape, dt=F32):
        return nc.dram_tensor(name, list(shape), dt, kind="ExternalInput").ap()

    def dout(name, shape):
        return nc.dram_tensor(name, list(shape), F32, kind="ExternalOutput").ap()

    xp = din("xp", [HALF, D])
    xs = din("xs", [TSAMP, D])
    cvec = din("cvec", [17, D])
    st_in = [din(f"st_in{b}", [DEPTH, NSEQ, NH, HK, HV]) for b in range(3)]
    sconv_in = din("sconv_in", [DEPTH, NSEQ * 2, 2 * DFF])
    w_in = din("w_in", [DEPTH, D, NIN])
    gla_w_lr = din("gla_w_lr", [DEPTH, 16, 512])
    pvecs = din("pvecs", [96, 128])
    head_norm = din("head_norm", [DEPTH, 3 * HV])
    w_branch = din("w_branch", [DEPTH, 3, MV, D])
    w_out = din("w_out", [DEPTH, D, D])
    w_ada = din("w_ada", [DEPTH, D, 6 * D])
    b_ada = din("b_ada", [DEPTH, 96, 128])
    w_up = din("ffn_w_up", [DEPTH, D, 2 * DFF])
    convp = din("convp", [DEPTH, 4, 88, 128])
    w_down = din("ffn_w_down", [DEPTH, DFF, D])
    cst = {n: din("k_" + n, s, dt) for n, s, dt in CONST_SPECS}

    yp = dout("yp", [HALF, D])
    ys = dout("ys", [TSAMP, D])
    pst = [dout(f"pst{b}", [DEPTH, NH, HK, HV]) for b in range(3)]
    pconv = dout("pconv", [DEPTH, 2, 2 * DFF])
    sst = [dout(f"sst{b}", [DEPTH, NSEQ, NH, HK, HV]) for b in range(3)]
    sconv_o = dout("sconv_o", [DEPTH, NSEQ * 2, 2 * DFF])

    st = ExitStack()
    P = Prog(nc)
    dscr = lambda name, shape: nc.dram_tensor(name, list(shape), F32).ap()
    xsw = [dscr(f'xsw{sp}', [128, 16, TP]) for sp in range(NSP)]
    t_xsw = [[Tok() for _ in range(16)] for _ in range(NSP)]
    ph1st = [dscr(f'ph1st{b}', [NH, HK, HV]) for b in range(3)]
    t_ph1st = [Tok() for _ in range(3)]
    exs = [dscr(f'exs{l}', [3 * NH * HK, HV]) for l in range(DEPTH)]
    exd = [dscr(f'exd{l}', [2 * 3 * NH * HK, HV]) for l in range(DEPTH)]
    t_exs = [[Tok() for _ in range(3)] for _ in range(DEPTH)]
    t_exd = [Tok() for _ in range(DEPTH)]
    exs2 = [dscr(f'exs2_{l}', [128, 32]) for l in range(DEPTH)]
    exd2 = [dscr(f'exd2_{l}', [256, 32]) for l in range(DEPTH)]
    t_exs2 = [Tok() for _ in range(DEPTH)]
    t_exd2 = [Tok() for _ in range(DEPTH)]
    modsc = [dscr(f'modsc{l}', [128, 96 * 17]) for l in range(DEPTH)]
    kTs = [[nc.dram_tensor(f'kTs{sp}_{b}', [128, 4, TP], BF16).ap() for b in range(3)] for sp in range(NSP)]
    vTs = [[nc.dram_tensor(f'vTs{sp}_{b}', [128, 4, MV], BF16).ap() for b in range(3)] for sp in range(NSP)]
    eqs = [[dscr(f'eqs{sp}_{b}', [128, 4, TP]) for b in range(3)] for sp in range(NSP)]
    els = [[dscr(f'els{sp}_{b}', [128, 64]) for b in range(3)] for sp in range(NSP)]
    t_kTs = [[Tok() for b in range(3)] for sp in range(NSP)]
    t_vTs = [[Tok() for b in range(3)] for sp in range(NSP)]
    t_eqs = [[[Tok() for h in range(4)] for b in range(3)] for sp in range(NSP)]
    t_els = [[Tok() for b in range(3)] for sp in range(NSP)]
    t_modsc = [Tok() for _ in range(DEPTH)]
    t_ada_done = Tok('ada_done')
    arena_t = st.enter_context(nc.sbuf_tensor("arena", [128, ARENA_WORDS], F32))
    AR = Arena(arena_t[:], ARENA_WORDS)
    banks = [st.enter_context(nc.psum_tensor(f"pb{i}", [128, 512], F32)) for i in range(8)]
    bank_tok = [Tok(f"pb{i}") for i in range(8)]
    bank_bf = banks[7][:].bitcast(BF16)

    class Rot:
        def __init__(self, idx):
            self.idx = idx
            self.i = 0

        def get(self):
            k = self.idx[self.i % len(self.idx)]
            self.i += 1
            return banks[k][:], bank_tok[k]

    main_rot = Rot([0, 1, 2, 3])
    small_rot = Rot([4, 5, 6])

    V = lambda fn, r=(), w=(): P.add("dve", fn, r, w)
    A = lambda fn, r=(), w=(): P.add("act", fn, r, w)
    T = lambda fn, r=(), w=(): P.add("pe", fn, r, w)
    LD = lambda fn, r=(), w=(): P.add("sp", fn, r, w, dma=True)
    ST_ = lambda fn, r=(), w=(): P.add("act", fn, r, w, dma=True)

    def mm(out_ap, out_tok, pairs, reads):
        n = len(pairs)
        for i, (l, r) in enumerate(pairs):
            T(lambda e, l=l, r=r, i=i: e.matmul(out_ap, lhsT=l, rhs=r, start=(i == 0), stop=(i == n - 1)),
              reads, [out_tok])

    def act(out, in_, func, r, w, **kw):
        return A(lambda e: e.activation(out=out, in_=in_, func=func, **kw), r, w)

    def tt(out, a, b, op, r, w):
        return V(lambda e: e.tensor_tensor(out=out, in0=a, in1=b, op=op), r, w)

    def ts(out, a, s1, s2, op0, op1, r, w):
        return V(lambda e: e.tensor_scalar(out=out, in0=a, scalar1=s1, scalar2=s2, op0=op0, op1=op1), r, w)

    def stt(out, a, s, b, op0, op1, r, w):
        return V(lambda e: e.scalar_tensor_tensor(out=out, in0=a, scalar=s, in1=b, op0=op0, op1=op1), r, w)

    K = {}
    Kt = {}
    for n, s, dt in CONST_SPECS:
        if n in ("cos_p", "sin_p", "cos_s", "sin_s", "cms", "rms", "rst_s", "gqs", "gks", "elrs", "masks"):
            continue
        K[n] = AR.alloc(s, dt)
        Kt[n] = Tok(n)
        LD(lambda e, n=n: e.dma_start(out=K[n], in_=cst[n]), [], [Kt[n]])
    PV = AR.alloc([128, 96], F32)
    t_PV = Tok("PV")
    CW = [AR.alloc([128, 4, 88], F32) for _ in range(DEPTH)]
    t_CW = [Tok("CW") for _ in range(DEPTH)]
    HNW = [AR.alloc([128, 3 * HV], BF16) for _ in range(DEPTH)]
    t_HNW = [Tok("HNW") for _ in range(DEPTH)]
    GLW = [AR.alloc([16, 512], BF16) for _ in range(DEPTH)]
    t_GLW = [Tok("GLW") for _ in range(DEPTH)]
    LBV = AR.alloc([128, DEPTH, 3, 4], F32)
    t_LBV = Tok("LBV")
    NEGB = AR.alloc([128, DEPTH, 4], F32)
    MODP = [AR.alloc([128, 6, 16], F32) for _ in range(DEPTH)]
    t_MODP = [Tok("MODP") for _ in range(DEPTH)]
    CARRY = [AR.alloc([128, 88, 2], F32) for _ in range(DEPTH)]
    t_CARRY = [[Tok("carry") for _ in range(88)] for _ in range(DEPTH)]
    NWB = 4
    WB = [AR.alloc([128, 4096], BF16) for _ in range(NWB)]
    t_WB = [Tok(f"wb{i}") for i in range(NWB)]
    wb_i = [0]
    persist_mark = AR.off

    scr = {}
    scr_tok = {}
    cast_list = []

    def defblk(key, srcs, kc, ncols):
        t = nc.dram_tensor("s_" + "_".join(str(k) for k in key), [128, kc, ncols], BF16).ap()
        scr[key] = (t, kc, ncols)
        cast_list.append((key, srcs))

    for l in range(DEPTH):
        for b in range(3):
            oq, ok, ov, og = OFF[b]
            for hp in range(2):
                defblk(("k", l, b, hp), [(w_in[l, :, ok + 256 * hp: ok + 256 * hp + 256], 0)], 16, 256)
                defblk(("q", l, b, hp), [(w_in[l, :, oq + 256 * hp: oq + 256 * hp + 256], 0)], 16, 256)
            if b == 1:
                defblk(("glr", l), [(w_in[l, :, OFF_GLR:OFF_GLR + 16], 0)], 16, 16)
            for h in range(4):
                defblk(("v", l, b, h), [(w_in[l, :, ov + 256 * h: ov + 256 * h + 256], 0)], 16, 256)
            for h in range(4):
                defblk(("g", l, b, h), [(w_in[l, :, og + 256 * h: og + 256 * h + 256], 0)], 16, 256)
        for fp in range(8):
            for b in range(3):
                defblk(("br", l, b, fp), [(w_branch[l, b, :, 256 * fp:256 * fp + 256], 0)], 8, 256)
                c0 = OFF_MG + b * D + 256 * fp
                defblk(("mg", l, b, fp), [(w_in[l, :, c0:c0 + 256], 0)], 16, 256)
        for fp in range(8):
            defblk(("out", l, fp), [(w_out[l, :, 256 * fp:256 * fp + 256], 0)], 16, 256)
        for j in range(NFT):
            defblk(("up", l, j), [(w_up[l, :, 128 * j:128 * j + 128], 0),
                                  (w_up[l, :, DFF + 128 * j:DFF + 128 * j + 128], 128)], 16, 256)
        for f in range(16):
            for kh in range(2):
                defblk(("dn", l, f, kh), [(w_down[l, kh * 2816:(kh + 1) * 2816, 128 * f:128 * f + 128], 0)], 22, 128)

    cast_src = dict(cast_list)
    cast_done = set()
    t_WB2 = [Tok(f"wbb{i}") for i in range(16)]
    wb_extra = [[]]

    def emit_casts(stage):
        return

    def load_w(key):
        dst_scr, kc, ncols = scr[key]
        i = wb_i[0] % len(WB)
        wb_i[0] += 1
        v = WB[i][:, 0:kc * ncols].rearrange("p (c n) -> p c n", c=kc)
        if key not in cast_done:
            cast_done.add(key)
            two = len(cast_src[key]) == 2
            op0 = None
            for si, (src, co) in enumerate(cast_src[key]):
                n = src.shape[1]
                fn = lambda e, src=src, co=co, n=n: e.dma_start(out=v[:, :, co:co + n], in_=src.rearrange("(c p) n -> p c n", p=128))
                if si == 0:
                    op0 = P.add("pool", fn, [], [t_WB[i], t_WB2[i]] if two else [t_WB[i]], dma=True, wload=True)
                else:
                    op1 = P.add("pool", fn, [], [], dma=True, wload=True)
                    op1.deps = set(op0.deps)
                    t_WB2[i].last_write = op1
                    t_WB2[i].reads = []
            wb_extra[0] = [t_WB2[i]] if two else []
            scr_tok[key] = [Tok(str(key))]
            P.add("act", lambda e: e.dma_start(out=dst_scr, in_=v), [t_WB[i]] + wb_extra[0], scr_tok[key], dma=True, wload=True)
        else:
            wb_extra[0] = []
            P.add("sp", lambda e: e.dma_start(out=v, in_=dst_scr), scr_tok[key] + [t_WB2[i]], [t_WB[i]], dma=True, wload=True)
        return v, t_WB[i]

    def setup():
        m0 = AR.off
        stg = AR.alloc([128, 128], F32)
        t_stg = Tok("stg")
        LD(lambda e: e.dma_start(out=stg[0:96, :], in_=pvecs), [], [t_stg])
        pb, pt = small_rot.get()
        T(lambda e, pb=pb: e.transpose(pb[:, 0:96], stg[0:96, :], K["ident_f"][0:96, 0:96]), [t_stg, Kt["ident_f"]], [pt])
        act(PV, pb[:, 0:96], AF.Copy, [pt], [t_PV])
        for l in range(DEPTH):
            for j in range(4):
                LD(lambda e, l=l, j=j: e.dma_start(out=stg[0:88, :], in_=convp[l, j]), [], [t_stg])
                pb, pt = small_rot.get()
                T(lambda e, pb=pb: e.transpose(pb[:, 0:88], stg[0:88, :], K["ident_f"][0:88, 0:88]), [t_stg, Kt["ident_f"]], [pt])
                act(CW[l][:, j, :], pb[:, 0:88], AF.Copy, [pt], [t_CW[l]])
            hn32 = AR.alloc([128, 3 * HV], F32)
            t_hn = Tok("hn32")
            LD(lambda e, l=l, hn32=hn32: e.dma_start(out=hn32, in_=head_norm[l].partition_broadcast(128)), [], [t_hn])
            V(lambda e, l=l, hn32=hn32: e.tensor_copy(out=HNW[l], in_=hn32), [t_hn], [t_HNW[l]])
            gl32 = AR.alloc([16, 512], F32)
            t_gl = Tok("gl32")
            LD(lambda e, l=l, gl32=gl32: e.dma_start(out=gl32, in_=gla_w_lr[l]), [], [t_gl])
            V(lambda e, l=l, gl32=gl32: e.tensor_copy(out=GLW[l], in_=gl32), [t_gl], [t_GLW[l]])
        tmp = AR.alloc([128, 8, 4], F32)
        t_tmp = Tok("lbtmp")
        l0 = PV[:, 80:84]
        l1 = PV[:, 84:88]
        mx, e0, e1, sm, r, s0, s1, cs = [tmp[:, i, :] for i in range(8)]
        tt(mx, l0, l1, ALU.max, [t_PV], [t_tmp])
        tt(e0, l0, mx, ALU.subtract, [t_PV, t_tmp], [t_tmp])
        tt(e1, l1, mx, ALU.subtract, [t_PV, t_tmp], [t_tmp])
        act(e0, e0, AF.Exp, [t_tmp], [t_tmp])
        act(e1, e1, AF.Exp, [t_tmp], [t_tmp])
        tt(sm, e0, e1, ALU.add, [t_tmp], [t_tmp])
        V(lambda e: e.reciprocal(out=r, in_=sm), [t_tmp], [t_tmp])
        tt(s0, e0, r, ALU.mult, [t_tmp], [t_tmp])
        tt(s1, e1, r, ALU.mult, [t_tmp], [t_tmp])
        tt(cs, s0, s1, ALU.add, [t_tmp], [t_tmp])
        tt(LBV[:, 0, 0, :], s0, s0, ALU.subtract, [t_tmp], [t_LBV])
        tt(LBV[:, 1, 0, :], cs, s0, ALU.subtract, [t_tmp], [t_LBV])
        for l in range(DEPTH):
            ts(LBV[:, l, 1, :], LBV[:, l, 0, :], -1.0, 1.0, ALU.mult, ALU.add, [t_LBV], [t_LBV])
            ts(LBV[:, l, 2, :], LBV[:, l, 0, :], 1.0, -1.0, ALU.mult, ALU.add, [t_LBV], [t_LBV])
            ts(NEGB[:, l, :], PV[:, 88 + 4 * l:92 + 4 * l], -1.0, None, ALU.mult, ALU.bypass, [t_PV], [t_LBV])
        for l in range(DEPTH):
            V(lambda e, l=l: e.memset(CARRY[l], 0.0), [], t_CARRY[l])
        return m0

    def compute_mod(MOD, t_MOD):
        m0 = AR.off
        c_sb = AR.alloc([17, D], F32)
        s_sb = AR.alloc([17, D], F32)
        scT = AR.alloc([128, 16, 17], F32)
        badaT = AR.alloc([128, 96], F32)
        stg = AR.alloc([128, 128], F32)
        scTb = AR.alloc([128, 16, 17], BF16)
        wcb = [AR.alloc([128, 16, 128], BF16) for _ in range(2)]
        t_wcb = [Tok(), Tok()]
        t_scTb = Tok()
        t_c, t_s, t_scT, t_ba, t_stg = Tok(), Tok(), Tok(), Tok(), Tok()
        LD(lambda e: e.dma_start(out=c_sb, in_=cvec), [], [t_c])
        act(s_sb, c_sb, AF.Sigmoid, [t_c], [t_s])
        tt(s_sb, s_sb, c_sb, ALU.mult, [t_c, t_s], [t_s])
        for c in range(16):
            pb, pt = small_rot.get()
            T(lambda e, c=c, pb=pb: e.transpose(pb[:, 0:17], s_sb[:, c * 128:(c + 1) * 128], K["ident_f"][0:17, 0:17]),
              [t_s, Kt["ident_f"]], [pt])
            act(scT[:, c, :], pb[:, 0:17], AF.Copy, [pt], [t_scT])
        V(lambda e: e.tensor_copy(out=scTb, in_=scT), [t_scT], [t_scTb])
        nblk_ = [0]
        for l in range(DEPTH):
            LD(lambda e, l=l: e.dma_start(out=stg[0:96, :], in_=b_ada[l]), [], [t_stg])
            pb, pt = small_rot.get()
            T(lambda e, pb=pb: e.transpose(pb[:, 0:96], stg[0:96, :], K["ident_f"][0:96, 0:96]), [t_stg, Kt["ident_f"]], [pt])
            act(badaT, pb[:, 0:96], AF.Copy, [pt], [t_ba])
            for fb in range(48):
                i = wb_i[0] % len(WB)
                wb_i[0] += 1
                wv = WB[i].rearrange("p (c n) -> p c n", c=16)
                P.add("pool", lambda e, l=l, fb=fb, wv=wv: e.dma_start(
                    out=wv, in_=w_ada[l, :, fb * 256:(fb + 1) * 256].rearrange("(c p) n -> p c n", p=128)),
                    [], [t_WB[i]], dma=True, wload=True)
                for ft in range(2):
                    fi = fb * 2 + ft
                    pb, pt = small_rot.get()
                    mm(pb[:, 0:17], pt, [(wv[:, c, ft * 128:(ft + 1) * 128], scTb[:, c, :]) for c in range(16)], [t_WB[i], t_scTb])
                    act(MOD[l][:, fi, :], pb[:, 0:17], AF.Identity, [pt, t_ba], [t_MOD[l]], bias=badaT[:, fi:fi + 1], scale=1.0)
            for (c0, pv0) in ((16, 16 * l), (64, 32 + 16 * l)):
                stt(MOD[l][:, c0:c0 + 16, :], MOD[l][:, c0:c0 + 16, :], 1.0,
                    PV[:, pv0:pv0 + 16].unsqueeze(2).broadcast_to([128, 16, 17]),
                    ALU.add, ALU.mult, [t_MOD[l], t_PV], [t_MOD[l]])
            for k, c0 in enumerate((16, 0, 32, 64, 48, 80)):
                V(lambda e, l=l, k=k, c0=c0: e.tensor_copy(out=MODP[l][:, k, :], in_=MOD[l][:, c0:c0 + 16, 16]),
                  [t_MOD[l]], [t_MODP[l]])
        return m0

    dbg_outs = {}

    def run_pass(kind, MOD=None, t_MOD=None):
        prompt = kind == "p"
        cx = {"sp": 0, "ph1": False, "store_x": False}
        Tn = TP if prompt else TSAMP
        tiles = [(i * 128, 128) for i in range(4)] if prompt else [(0, 64)]
        nt = len(tiles)
        m0 = AR.off
        xT = AR.alloc([128, 16, Tn], F32)
        t_xT = [Tok(f"xT{c}") for c in range(16)]
        hT = AR.alloc([128, 16, Tn], BF16)
        t_hT = [Tok(f"hT{c}") for c in range(16)]
        r1_words = max(NFT * Tn // 2, 16 * Tn + 2048) + 64
        r1_off = AR.off
        R1 = Arena(AR.ap[:, r1_off:r1_off + r1_words], r1_words)
        AR.off += r1_words
        uT = R1.alloc([128, 3, 8, Tn], BF16)
        v_tok = R1.alloc([128, nt, MV], BF16)
        sg_tok = R1.alloc([128, nt, MV], BF16)
        R1b = Arena(AR.ap[:, r1_off:r1_off + r1_words], r1_words)
        actT = R1b.alloc([128, NFT, Tn], BF16)
        R1c = Arena(AR.ap[:, r1_off:r1_off + r1_words], r1_words)
        yT = R1c.alloc([128, 16, Tn], F32)
        stg = R1c.alloc([128, D], F32)
        mT = Arena(AR.ap[:, r1_off + 3 * 8 * Tn // 2 + 0: r1_off + r1_words], r1_words).alloc([128, 16, Tn], BF16) \
            if prompt else AR.alloc([128, 16, Tn], BF16)
        t_v = [[Tok() for _ in range(4)] for _ in range(nt)]
        t_sg = [[Tok() for _ in range(4)] for _ in range(nt)]
        t_uT = [[Tok() for _ in range(nt)] for _ in range(3)]
        t_stgi = Tok("stgi")
        t_yT = Tok("yT")
        qT = AR.alloc([128, 4, Tn], BF16)
        kT = AR.alloc([128, 4, Tn], BF16)
        t_qT = [Tok() for _ in range(4)]
        t_kT = [Tok() for _ in range(4)]
        NTMP = 6
        TWS = [520] * 6 if prompt else [128, 128, 1040, 1040, 128, 128]
        tmp = [AR.alloc([128, TWS[i]], F32) for i in range(NTMP)]
        t_tmp = [Tok(f"tmp{i}") for i in range(NTMP)]
        acc = AR.alloc([128, 2, Tn], F32)
        t_acc = Tok("acc")
        rstd = AR.alloc([128, Tn], F32)
        t_rstd = Tok("rstd")
        COS = AR.alloc([128, Tn], F32)
        SIN = AR.alloc([128, Tn], F32)
        t_rope = Tok("rope")
        EL = AR.alloc([128, 4, 16], F32)
        t_EL = [Tok() for _ in range(4)]
        PT_ = [AR.alloc([128, 128], BF16) for _ in range(4)]
        t_PT = [Tok() for _ in range(4)]
        ktok = [AR.alloc([128, 128], BF16) for _ in range(4)]
        t_ktok = [Tok() for _ in range(4)]
        u_tok = AR.alloc([128, MV], BF16)
        t_utok = Tok("utok")
        kve = [AR.alloc([128, HV], F32) for _ in range(4)]
        t_kve = [Tok() for _ in range(4)]
        sq256 = [AR.alloc([128, HV], F32) for _ in range(4)]
        t_sq256 = [Tok() for _ in range(4)]
        ssv = AR.alloc([128, 8], F32)
        t_ssv = Tok("ssv")
        glrT = AR.alloc([16, Tn], BF16)
        t_glrT = Tok("glrT")
        if prompt:
            SF = AR.alloc([128, 4, HV], F32)
            Sbf = AR.alloc([128, 4, HV], BF16)
            t_SF = [Tok() for _ in range(4)]
            t_Sbf = [Tok() for _ in range(4)]
            KM = [AR.alloc([128, 4, 128], BF16) for _ in range(4)]
            QM = [AR.alloc([128, 4, 128], BF16) for _ in range(4)]
            t_KM = [Tok() for _ in range(4)]
            t_QM = [Tok() for _ in range(4)]
            cstage = stg[0:2, :]
            t_cstage = Tok("cstage")
            MASKS = {1: K["mask1"], 4: K["mask4"]}
            t_MASKS = {1: Kt["mask1"], 4: Kt["mask4"]}
            CM = K["cm4"]
            RM = K["rm4"]
            t_CMRM = [Kt["cm4"], Kt["rm4"]]
            RST = {1: K["rst_g"], 2: K["rst_h"]}
            t_RST = {1: Kt["rst_g"], 2: Kt["rst_h"]}
            GQ, GK, ELR = K["gq"], K["gk"], K["elr"]
            t_G = [Kt["gq"], Kt["gk"], Kt["elr"]]
            xtl = AR.alloc([128, 32], F32)
            t_xtl = Tok("xtl")
            xtin = AR.alloc([128, 32], F32)
            sqt = AR.alloc([128, 32], F32)
            rs2 = AR.alloc([128, 2], F32)
            h2t = AR.alloc([128, 16, 2], BF16)
            t_xtin, t_sqt, t_rs2, t_h2t = Tok(), Tok(), Tok(), Tok()
        else:
            r3_off = AR.off
            r3_words = 2 * NSEQ * HV + 64
            R3 = Arena(AR.ap[:, r3_off:r3_off + r3_words], r3_words)
            AR.off += r3_words
            SS = [R3.alloc([128, NSEQ, HV], F32) for _ in range(2)]
            R3b = Arena(AR.ap[:, r3_off:r3_off + r3_words], r3_words)
            t_SS = [Tok() for _ in range(2)]
            SSb = AR.alloc([128, NSEQ, HV], BF16)
            t_SSb = Tok()
            SSo = AR.alloc([128, NSEQ, HV], F32)
            t_SSo = Tok()
            KM = [AR.alloc([64, 16, 128], BF16)]
            QM = [AR.alloc([128, 16, 64], BF16)]
            t_KM = [Tok()]
            t_QM = [Tok()]
            msk = AR.alloc([128, 128], BF16)
            cms = AR.alloc([128, 16, 64], BF16)
            rms = AR.alloc([128, 16], F32)
            rsts = AR.alloc([128, 64], BF16)
            gqs = AR.alloc([128, 4, 64], F32)
            gks = AR.alloc([128, 4, 64], F32)
            elrs = AR.alloc([128, 4], F32)
            t_sc = Tok("sconst")
            for dst, nm in ((msk, "masks"), (cms, "cms"), (rms, "rms"), (rsts, "rst_s"), (gqs, "gqs"), (gks, "gks"),
                            (elrs, "elrs"), (COS, "cos_s"), (SIN, "sin_s")):
                LD(lambda e, dst=dst, nm=nm: e.dma_start(out=dst, in_=cst[nm]), [], [t_sc if nm not in ("cos_s", "sin_s") else t_rope])
            MASKS = {16: msk}
            t_MASKS = {16: t_sc}
            CM, RM = cms, rms
            t_CMRM = [t_sc]
            RST = {1: rsts, 2: rsts}
            t_RST = {1: t_sc, 2: t_sc}
            GQ, GK, ELR = gqs, gks, elrs
            t_G = [t_sc]
            convT = AR.alloc([128, 88, 32], F32)
            t_convT = Tok("convT")
            outc = [R3b.alloc([32, 2816], F32) for _ in range(2)]
            t_outc = [Tok("outc0"), Tok("outc1")]
            cstg = R3b.alloc([32, 2048], F32)
            t_cstg = Tok("cstg")
            U6 = [AR.alloc([128, NSEQ, 6], F32) for _ in range(2)]
            t_U6 = [Tok() for _ in range(2)]
            while len(WB) < 12 and AR.off + 2056 <= AR.words:
                WB.append(AR.alloc([128, 4096], BF16))
                t_WB.append(Tok(f"wbx{len(WB)}"))

        def load_tokens(xsrc):
            for (t0, rows) in tiles:
                LD(lambda e, t0=t0, rows=rows: e.dma_start(out=stg[0:rows, :], in_=xsrc[t0:t0 + rows, :]), [], [t_stgi])
                for g in range(4):
                    pb, pt = main_rot.get()
                    for cc in range(4):
                        c = g * 4 + cc
                        T(lambda e, pb=pb, cc=cc, c=c, rows=rows: e.transpose(
                            pb[:, cc * rows:(cc + 1) * rows], stg[0:rows, c * 128:(c + 1) * 128], K["ident_f"][0:rows, 0:rows]),
                          [t_stgi, Kt["ident_f"]], [pt])
                    act(xT[:, g * 4:g * 4 + 4, t0:t0 + rows], pb[:, 0:4 * rows].rearrange("p (a b) -> p a b", a=4),
                        AF.Copy, [pt], t_xT[g * 4:g * 4 + 4])

        def store_x(sp, c):
            ST_(lambda e: e.dma_start(out=xsw[sp][:, c, :], in_=xT[:, c, :]), [t_xT[c]], [t_xsw[sp][c]])

        def load_x(sp):
            for c in range(16):
                LD(lambda e, c=c: e.dma_start(out=xT[:, c, :], in_=xsw[sp][:, c, :]), [t_xsw[sp][c]], [t_xT[c]])

        def load_rope(sp):
            LD(lambda e: e.dma_start(out=COS, in_=cst["cos_p"][:, sp * TP:(sp + 1) * TP]), [], [t_rope])
            LD(lambda e: e.dma_start(out=SIN, in_=cst["sin_p"][:, sp * TP:(sp + 1) * TP]), [], [t_rope])

        if prompt:
            for sp in range(NSP):
                load_tokens(xp[sp * TP:(sp + 1) * TP])
                for c in range(16):
                    store_x(sp, c)
        else:
            load_tokens(xs)
        P.fence()

        def rms_rstd():
            pss, pst_ = small_rot.get()
            for c in range(16):
                k = c % 2
                act(tmp[k][:, 0:Tn], xT[:, c, :], AF.Square, [t_xT[c]], [t_tmp[k]])
                T(lambda e, c=c, k=k: e.matmul(pss[:, 0:Tn], lhsT=K["ones_f"], rhs=tmp[k][:, 0:Tn], start=(c == 0), stop=(c == 15)),
                  [t_tmp[k], Kt["ones_f"]], [pst_])
            ts(rstd, pss[:, 0:Tn], 1.0 / D, EPS, ALU.mult, ALU.add, [pst_], [t_rstd])
            act(rstd, rstd, AF.Sqrt, [t_rstd], [t_rstd])
            V(lambda e: e.reciprocal(out=rstd, in_=rstd), [t_rstd], [t_rstd])

        def modulated_norm(l, which):
            rms_rstd()
            if prompt:
                ka, kb = (0, 1) if which == 1 else (3, 4)
                for c in range(16):
                    k = 2 + c % 2
                    stt(tmp[k][:, 0:Tn], xT[:, c, :], MODP[l][:, ka, c:c + 1], rstd, ALU.mult, ALU.mult,
                        [t_xT[c], t_MODP[l], t_rstd], [t_tmp[k]])
                    act(hT[:, c, :], tmp[k][:, 0:Tn], AF.Identity, [t_tmp[k], t_MODP[l]], [t_hT[c]],
                        bias=MODP[l][:, kb, c:c + 1], scale=1.0)
            else:
                ca, cb = (16, 0) if which == 1 else (64, 48)
                t1 = tmp[2][:, 0:1024].rearrange("p (c t) -> p c t", c=16)
                tt(t1, xT, rstd.unsqueeze(1).broadcast_to([128, 16, Tn]), ALU.mult, t_xT + [t_rstd], [t_tmp[2]])
                t1v = tmp[2][:, 0:1024].rearrange("p (c s t) -> p c s t", c=16, s=NSEQ)
                t2v = tmp[3][:, 0:1024].rearrange("p (c s t) -> p c s t", c=16, s=NSEQ)
                Av = MOD[l][:, ca:ca + 16, 0:NSEQ].unsqueeze(3).broadcast_to([128, 16, NSEQ, TS])
                Bv = MOD[l][:, cb:cb + 16, 0:NSEQ].unsqueeze(3).broadcast_to([128, 16, NSEQ, TS])
                tt(t2v, t1v, Av, ALU.mult, [t_tmp[2], t_MOD[l]], [t_tmp[3]])
                tt(hT.rearrange("p c (s t) -> p c s t", s=NSEQ), t2v, Bv, ALU.add, [t_tmp[3], t_MOD[l]], t_hT)

        def residual_add(l, which, f, pm, pmt):
            if prompt:
                kg = 2 if which == 1 else 5
                stt(xT[:, f, :], pm[:, 0:Tn], MODP[l][:, kg, f:f + 1], xT[:, f, :], ALU.mult, ALU.add,
                    [pmt, t_MODP[l], t_xT[f]], [t_xT[f]])
                store_x(cx["sp"], f)
            else:
                cg = 32 if which == 1 else 80
                Gv = MOD[l][:, cg + f, 0:NSEQ].unsqueeze(2).broadcast_to([128, NSEQ, TS])
                tv = tmp[4][:, 0:Tn].rearrange("p (s t) -> p s t", s=NSEQ)
                tt(tv, pm[:, 0:Tn].rearrange("p (s t) -> p s t", s=NSEQ), Gv, ALU.mult, [pmt, t_MOD[l]], [t_tmp[4]])
                tt(xT[:, f, :], xT[:, f, :], tmp[4][:, 0:Tn], ALU.add, [t_tmp[4], t_xT[f]], [t_xT[f]])

        nbt = {0: 1, 1: 1, 2: 4} if prompt else {0: 16, 1: 16, 2: 16}
        bsz = {b: tiles[0][1] // nbt[b] for b in range(3)}

        def mixer_branch(l, b):
            nb = nbt[b]
            bs = bsz[b]
            ph1 = cx["ph1"]
            sp = cx["sp"]
            sv = prompt and ph1
            ld = prompt and not ph1
            if b == 1 and not ld:
                wg_, wgt = load_w(("glr", l))
                pb, pt = small_rot.get()
                mm(pb[0:16, 0:Tn], pt, [(wg_[:, c, 0:16], hT[:, c, :]) for c in range(16)], [wgt] + t_hT)
                act(glrT, pb[0:16, 0:Tn], AF.Copy, [pt], [t_glrT])
            if ld:
                ST_(lambda e: e.dma_start(out=kT, in_=kTs[sp][b]), [t_kTs[sp][b]], t_kT)
                if b != 0:
                    ST_(lambda e: e.dma_start(out=EL[:].rearrange("p a b -> p (a b)"), in_=els[sp][b]), [t_els[sp][b]], t_EL)
            for hp in range(2):
                if not ld:
                    kb_, kbt = load_w(("k", l, b, hp))
                if not ph1:
                    qb_, qbt = load_w(("q", l, b, hp))
                for hh in range(2):
                    h = hp * 2 + hh
                    if not ld:
                        pk, pkt = main_rot.get()
                        mm(pk[:, 0:Tn], pkt, [(kb_[:, c, hh * 128:(hh + 1) * 128], hT[:, c, :]) for c in range(16)], [kbt] + t_hT)
                    X = [tmp[i][:, 0:Tn] for i in range(NTMP)]
                    if b == 0:
                        def rope_path(pz, pzt, G, dstT, dtok):
                            raw = tmp[0].bitcast(BF16)[:, 0:Tn]
                            act(raw, pz[:, 0:Tn], AF.Copy, [pzt], [t_tmp[0]])
                            psw, pswt = small_rot.get()
                            T(lambda e: e.matmul(psw[:, 0:Tn], lhsT=K["pswap"], rhs=raw, start=True, stop=True),
                              [t_tmp[0], Kt["pswap"]], [pswt])
                            tt(X[1], raw, COS, ALU.mult, [t_tmp[0], t_rope], [t_tmp[1]])
                            tt(X[2], psw[:, 0:Tn], SIN, ALU.mult, [pswt, t_rope], [t_tmp[2]])
                            tt(X[3], X[1], X[2], ALU.add, [t_tmp[1], t_tmp[2]], [t_tmp[3]])
                            if prompt:
                                Gv = G[:, h, :].unsqueeze(1).broadcast_to([128, nt, 128])
                                tt(dstT[:, h, :].rearrange("p (i t) -> p i t", i=nt), X[3].rearrange("p (i t) -> p i t", i=nt),
                                   Gv, ALU.mult, [t_tmp[3]] + t_G, [dtok[h]])
                            else:
                                tt(dstT[:, h, :], X[3], G[:, h, :], ALU.mult, [t_tmp[3]] + t_G, [dtok[h]])
                        if not ld:
                            rope_path(pk, pkt, GK, kT, t_kT)
                        if not ph1:
                            pq, pqt = main_rot.get()
                            mm(pq[:, 0:Tn], pqt, [(qb_[:, c, hh * 128:(hh + 1) * 128], hT[:, c, :]) for c in range(16)], [qbt] + t_hT)
                            rope_path(pq, pqt, GQ, qT, t_qT)
                    else:
                        if not ld:
                            if b == 1:
                                px, pxt = small_rot.get()
                                T(lambda e, h=h, px=px: e.matmul(px[:, 0:Tn], lhsT=GLW[l][:, h * 128:(h + 1) * 128], rhs=glrT, start=True, stop=True),
                                  [t_GLW[l], t_glrT], [pxt])
                                act(X[0], px[:, 0:Tn], AF.Exp, [pxt, t_LBV], [t_tmp[0]], bias=NEGB[:, l, h:h + 1], scale=-1.0)
                                act(X[1], X[0], AF.Ln, [t_tmp[0]], [t_tmp[1]], bias=1.0, scale=1.0)
                                sc_q, sc_k = -1.0 / 16.0, 1.0 / 16.0
                            else:
                                act(X[0], pk[:, 0:Tn], AF.Sigmoid, [pkt], [t_tmp[0]])
                                act(X[1], X[0], AF.Ln, [t_tmp[0], t_LBV], [t_tmp[1]],
                                    bias=LBV[:, l, 0, h:h + 1], scale=LBV[:, l, 1, h:h + 1])
                                sc_q, sc_k = 1.0, -1.0
                            V(lambda e, b=b: e.tensor_tensor_scan(out=X[2], data0=RST[b][:, 0:Tn], data1=X[1], initial=0.0,
                                                                  op0=ALU.mult, op1=ALU.add), [t_tmp[1], t_RST[b]], [t_tmp[2]])
                            act(X[3], X[2], AF.Exp, [t_tmp[2]], [t_tmp[3]], scale=sc_q)
                            act(X[4], X[2], AF.Exp, [t_tmp[2]], [t_tmp[4]], scale=sc_k)
                            nblk = Tn // bs
                            A(lambda e, h=h, bs=bs, nblk=nblk: e.activation(out=EL[:, h, 0:nblk], in_=tmp[3][:, bs - 1:Tn:bs], func=AF.Copy),
                              [t_tmp[3]], [t_EL[h]])
                            if b == 1:
                                tt(kT[:, h, :], pk[:, 0:Tn], X[4], ALU.mult, [pkt, t_tmp[4]], [t_kT[h]])
                            else:
                                ts(X[5], X[0], LBV[:, l, 2, h:h + 1], LBV[:, l, 1, h:h + 1], ALU.mult, ALU.add, [t_tmp[0], t_LBV], [t_tmp[5]])
                                tt(kT[:, h, :], X[5], X[4], ALU.mult, [t_tmp[5], t_tmp[4]], [t_kT[h]])
                            if sv:
                                ST_(lambda e, h=h: e.dma_start(out=eqs[sp][b][:, h, :], in_=tmp[3][:, 0:Tn]), [t_tmp[3]], [t_eqs[sp][b][h]])
                        else:
                            ST_(lambda e, h=h: e.dma_start(out=tmp[3][:, 0:Tn], in_=eqs[sp][b][:, h, :]), [t_eqs[sp][b][h]], [t_tmp[3]])
                        if ph1:
                            continue
                        pq, pqt = main_rot.get()
                        mm(pq[:, 0:Tn], pqt, [(qb_[:, c, hh * 128:(hh + 1) * 128], hT[:, c, :]) for c in range(16)], [qbt] + t_hT)
                        if b == 1:
                            stt(qT[:, h, :], pq[:, 0:Tn], HK ** -0.5, X[3], ALU.mult, ALU.mult, [pqt, t_tmp[3]], [t_qT[h]])
                        else:
                            act(X[0], pq[:, 0:Tn], AF.Sigmoid, [pqt], [t_tmp[0]])
                            tt(X[1], pq[:, 0:Tn], X[0], ALU.mult, [pqt, t_tmp[0]], [t_tmp[1]])
                            tt(qT[:, h, :], X[1], X[3], ALU.mult, [t_tmp[1], t_tmp[3]], [t_qT[h]])
            if sv:
                ST_(lambda e: e.dma_start(out=kTs[sp][b], in_=kT), t_kT, [t_kTs[sp][b]])
                if b != 0:
                    ST_(lambda e: e.dma_start(out=els[sp][b], in_=EL[:].rearrange("p a b -> p (a b)")), t_EL, [t_els[sp][b]])
            if ld:
                ST_(lambda e: e.dma_start(out=v_tok, in_=vTs[sp][b]), [t_vTs[sp][b]], [t for row in t_v for t in row])
            for h in range(4 if not ld else 0):
                wv_, wvt = load_w(("v", l, b, h))
                for i, (t0, rows) in enumerate(tiles):
                    pv, pvt = main_rot.get()
                    mm(pv[0:rows, 0:256], pvt, [(hT[:, c, t0:t0 + rows], wv_[:, c, :]) for c in range(16)], [wvt] + t_hT)
                    act(v_tok[0:rows, i, h * 256:(h + 1) * 256], pv[0:rows, 0:256], AF.Copy, [pvt], [t_v[i][h]])
            if sv:
                ST_(lambda e: e.dma_start(out=vTs[sp][b], in_=v_tok), [t for row in t_v for t in row], [t_vTs[sp][b]])
            for h in range(0 if not ph1 else 4, 4):
                wg_, wgt = load_w(("g", l, b, h))
                for i, (t0, rows) in enumerate(tiles):
                    pg, pgt = main_rot.get()
                    mm(pg[0:rows, 0:256], pgt, [(hT[:, c, t0:t0 + rows], wg_[:, c, :]) for c in range(16)], [wgt] + t_hT)
                    k = h % 2
                    act(sq256[k][0:rows, :], pg[0:rows, 0:256], AF.Sigmoid, [pgt], [t_sq256[k]])
                    tt(sq256[k][0:rows, :], pg[0:rows, 0:256], sq256[k][0:rows, :], ALU.mult, [pgt, t_sq256[k]], [t_sq256[k]])
                    tt(sg_tok[0:rows, i, h * 256:(h + 1) * 256], sq256[k][0:rows, :], HNW[l][0:rows, b * HV:(b + 1) * HV], ALU.mult,
                       [t_sq256[k], t_HNW[l]], [t_sg[i][h]])
            def el_col(h, blk):
                if b == 0:
                    return ELR[:, h:h + 1], t_G
                return EL[:, h, blk:blk + 1], [t_EL[h]]

            if prompt:
                if ph1:
                    if sp == 0:
                        V(lambda e: e.memset(SF, 0.0), [], t_SF)
                    else:
                        LD(lambda e: e.dma_start(out=SF, in_=ph1st[b].rearrange("h k v -> k h v")), [t_ph1st[b]], t_SF)
                elif sp == 0:
                    LD(lambda e: e.dma_start(out=SF, in_=exd[l][b * 512:(b + 1) * 512, :].rearrange("(h k) v -> k h v", k=HK)), [t_exd[l]], t_SF)
                    ts(SF, SF, K["isb"][:, 0:1], None, ALU.mult, ALU.bypass, t_SF + [Kt["isb"]], t_SF)
                else:
                    LD(lambda e: e.dma_start(out=SF, in_=pst[b][l].rearrange("h k v -> k h v")), [t_pst[b][l]], t_SF)
                if not ph1:
                    act(Sbf, SF, AF.Copy, t_SF, t_Sbf)
            hgroups = [[0, 1, 2, 3]] if prompt else [[0], [1], [2], [3]]
            ss_i = [0]
            for i, (t0, C) in enumerate(tiles):
                for hg in hgroups:
                    ctx = {}
                    H = list(enumerate(hg))
                    if not prompt:
                        for hi, h in H:
                            k = ss_i[0] % 2
                            ss_i[0] += 1
                            LD(lambda e, h=h, k=k: e.dma_start(out=SS[k], in_=st_in[b][l, :, h].rearrange("s k v -> k s v")), [], [t_SS[k]])
                            V(lambda e, k=k: e.tensor_copy(out=SSb[:, 0:8, :], in_=SS[k][:, 0:8, :]), [t_SS[k]], [t_SSb])
                            act(SSb[:, 8:16, :], SS[k][:, 8:16, :], AF.Copy, [t_SS[k]], [t_SSb])
                            ctx[h] = k
                    qts = {h: qT[:, h, t0:t0 + C] for _, h in H}
                    kts = {h: kT[:, h, t0:t0 + C] for _, h in H}
                    vts = {h: v_tok[0:C, i, h * 256:(h + 1) * 256] for _, h in H}
                    pscb, psct = banks[6][:], bank_tok[6]
                    if not ph1:
                        for hi, h in H:
                            T(lambda e, hi=hi, C=C, pscb=pscb, kt_=kts[h], qt_=qts[h]: e.matmul(pscb[0:C, hi * 128:hi * 128 + C], lhsT=kt_, rhs=qt_, start=True, stop=True),
                              [t_kT[h], t_qT[h]], [psct])
                        for hi, h in H:
                            tt(PT_[h][0:C, 0:C], pscb[0:C, hi * 128:hi * 128 + C], MASKS[nb][0:C, 0:C], ALU.mult, [psct, t_MASKS[nb]], [t_PT[h]])
                    for hi, h in H:
                        T(lambda e, hi=hi, C=C, kt_=kts[h]: e.transpose(bank_bf[0:C, hi * 128:(hi + 1) * 128], kt_, K["ident_b"]),
                          [t_kT[h], Kt["ident_b"]], [bank_tok[7]])
                    for hi, h in H:
                        act(ktok[h][0:C, :], bank_bf[0:C, hi * 128:(hi + 1) * 128], AF.Copy, [bank_tok[7]], [t_ktok[h]])
                    if nb > 1:
                        for hi, h in H:
                            km, tkm = KM[h % len(KM)], t_KM[h % len(KM)]
                            tt(km[0:C, 0:nb, :], ktok[h][0:C, :].unsqueeze(1).broadcast_to([C, nb, 128]),
                               RM[0:C, 0:nb].unsqueeze(2).broadcast_to([C, nb, 128]), ALU.mult, [t_ktok[h]] + t_CMRM, [tkm])
                        if not ph1:
                            for hi, h in H:
                                qm, tqm = QM[h % len(QM)], t_QM[h % len(QM)]
                                tt(qm[:, 0:nb, 0:C], qts[h].unsqueeze(1).broadcast_to([128, nb, C]), CM[:, 0:nb, 0:C], ALU.mult,
                                   [t_qT[h]] + t_CMRM, [tqm])
                    po = {h: (banks[hi][:], bank_tok[hi]) for hi, h in H}
                    pkvs = {h: (banks[4 + hi // 2][:, (hi % 2) * 256:(hi % 2) * 256 + 256], bank_tok[4 + hi // 2]) for hi, h in H}
                    for j in range(nb):
                        blk = i * nb + j
                        if not ph1:
                            for hi, h in H:
                                pob, pot = po[h]
                                if j == 0:
                                    T(lambda e, pob=pob, h=h, C=C, vt_=vts[h]: e.matmul(pob[0:C, 0:256], lhsT=PT_[h][0:C, 0:C], rhs=vt_, start=True, stop=False),
                                      [t_PT[h], t_v[i][h]], [pot])
                                lhs = qts[h] if nb == 1 else QM[h % len(QM)][:, j, 0:C]
                                lt = [t_qT[h]] if nb == 1 else [t_QM[h % len(QM)]]
                                if prompt:
                                    srhs, srt = Sbf[:, h, :], [t_Sbf[h]]
                                else:
                                    srhs, srt = SSb[:, j, :], [t_SSb]
                                T(lambda e, pob=pob, lhs=lhs, srhs=srhs, C=C, j=j: e.matmul(pob[0:C, 0:256], lhsT=lhs, rhs=srhs, start=False, stop=(j == nb - 1)),
                                  lt + srt, [pot])
                        for hi, h in H:
                            pkv, pkvt = pkvs[h]
                            klhs = ktok[h][0:C, :] if nb == 1 else KM[h % len(KM)][0:C, j, :]
                            klt = [t_ktok[h]] if nb == 1 else [t_KM[h % len(KM)]]
                            T(lambda e, pkv=pkv, klhs=klhs, vt_=vts[h]: e.matmul(pkv, lhsT=klhs, rhs=vt_, start=True, stop=True),
                              klt + [t_v[i][h]], [pkvt])
                        for hi, h in H:
                            pkv, pkvt = pkvs[h]
                            elc, elt = el_col(h, blk)
                            act(kve[hi], pkv, AF.Identity, [pkvt] + elt, [t_kve[hi]], scale=elc)
                        for hi, h in H:
                            elc, elt = el_col(h, blk)
                            if prompt:
                                stt(SF[:, h, :], SF[:, h, :], elc, kve[hi], ALU.mult, ALU.add, [t_SF[h], t_kve[hi]] + elt, [t_SF[h]])
                            else:
                                stt(SSo[:, j, :], SS[ctx[h]][:, j, :], elc, kve[hi], ALU.mult, ALU.add, [t_SS[ctx[h]], t_kve[hi]] + elt, [t_SSo])
                        if prompt and not ph1:
                            for hi, h in H:
                                act(Sbf[:, h, :], SF[:, h, :], AF.Copy, [t_SF[h]], [t_Sbf[h]])
                    if not ph1:
                        h0 = hg[0]
                        nh_ = len(hg)
                        for hi, h in H:
                            pob, pot = po[h]
                            act(sq256[hi][0:C, :], pob[0:C, 0:256], AF.Square, [pot], [t_sq256[hi]])
                        for hi, h in H:
                            V(lambda e, hi=hi, h=h, C=C: e.tensor_reduce(out=ssv[0:C, h:h + 1], in_=sq256[hi][0:C, :], axis=AX.X, op=ALU.add),
                              [t_sq256[hi]], [t_ssv])
                        sv_ = ssv[0:C, h0:h0 + nh_]
                        ts(sv_, sv_, 1.0 / HV, EPS, ALU.mult, ALU.add, [t_ssv], [t_ssv])
                        act(sv_, sv_, AF.Sqrt, [t_ssv], [t_ssv])
                        V(lambda e, sv_=sv_: e.reciprocal(out=sv_, in_=sv_), [t_ssv], [t_ssv])
                        for hi, h in H:
                            pob, pot = po[h]
                            stt(u_tok[0:C, h * 256:(h + 1) * 256], pob[0:C, 0:256], ssv[0:C, h:h + 1], sg_tok[0:C, i, h * 256:(h + 1) * 256],
                                ALU.mult, ALU.mult, [pot, t_ssv, t_sg[i][h]], [t_utok])
                    if not prompt:
                        for hi, h in H:
                            ST_(lambda e, h=h: e.dma_start(out=sst[b][l, :, h].rearrange("s k v -> k s v"), in_=SSo), [t_SSo], [t_sst])
                            outs_final.append(P.streams["act"][-1])
                for c in range(8 if not ph1 else 0):
                    T(lambda e, c=c, C=C: e.transpose(bank_bf[:, c * C:(c + 1) * C], u_tok[0:C, c * 128:(c + 1) * 128], K["ident_b"][0:C, 0:C]),
                      [t_utok, Kt["ident_b"]], [bank_tok[7]])
                if not ph1:
                    act(uT[:, b, :, t0:t0 + C], bank_bf[:, 0:8 * C].rearrange("p (c t) -> p c t", c=8), AF.Copy, [bank_tok[7]], [t_uT[b][i]])
            if prompt:
                if ph1 and sp == 0:
                    ST_(lambda e: e.dma_start(out=ph1st[b].rearrange("h k v -> k h v"), in_=SF), t_SF, [t_ph1st[b]])
                elif ph1:
                    ST_(lambda e: e.dma_start(out=exs[l][b * 512:(b + 1) * 512, :].rearrange("(h k) v -> k h v", k=HK), in_=SF), t_SF, [t_exs[l][b]])
                else:
                    ST_(lambda e: e.dma_start(out=pst[b][l].rearrange("h k v -> k h v"), in_=SF), t_SF, [t_pst[b][l]])
                    if sp == NSP - 1:
                        outs_final.append(P.streams["act"][-1])

        def merge_and_out(l):
            P.fence()
            for fp in range(8):
                for b in range(3):
                    wbr, wbrt = load_w(("br", l, b, fp))
                    wmg, wmgt = load_w(("mg", l, b, fp))
                    for ft in range(2):
                        f = fp * 2 + ft
                        po_, pot = main_rot.get()
                        mm(po_[:, 0:Tn], pot, [(wbr[:, c, ft * 128:(ft + 1) * 128], uT[:, b, c, :]) for c in range(8)], [wbrt] + t_uT[b])
                        pg, pgt = main_rot.get()
                        mm(pg[:, 0:Tn], pgt, [(wmg[:, c, ft * 128:(ft + 1) * 128], hT[:, c, :]) for c in range(16)], [wmgt] + t_hT)
                        k = ft
                        act(tmp[k][:, 0:Tn], pg[:, 0:Tn], AF.Sigmoid, [pgt], [t_tmp[k]])
                        if b == 0:
                            tt(acc[:, ft, :], po_[:, 0:Tn], tmp[k][:, 0:Tn], ALU.mult, [pot, t_tmp[k]], [t_acc])
                        elif b == 1:
                            tt(tmp[k][:, 0:Tn], po_[:, 0:Tn], tmp[k][:, 0:Tn], ALU.mult, [pot, t_tmp[k]], [t_tmp[k]])
                            tt(acc[:, ft, :], acc[:, ft, :], tmp[k][:, 0:Tn], ALU.add, [t_tmp[k], t_acc], [t_acc])
                        else:
                            tt(tmp[k][:, 0:Tn], po_[:, 0:Tn], tmp[k][:, 0:Tn], ALU.mult, [pot, t_tmp[k]], [t_tmp[k]])
                            tt(mT[:, f, :], acc[:, ft, :], tmp[k][:, 0:Tn], ALU.add, [t_tmp[k], t_acc], [t_mT])
            for fp in range(8):
                wo, wot = load_w(("out", l, fp))
                for ft in range(2):
                    f = fp * 2 + ft
                    pm, pmt = main_rot.get()
                    mm(pm[:, 0:Tn], pmt, [(wo[:, c, ft * 128:(ft + 1) * 128], mT[:, c, :]) for c in range(16)], [wot, t_mT])
                    residual_add(l, 1, f, pm, pmt)
            if prompt and cx["sp"] == NSP - 1:
                V(lambda e: e.tensor_copy(out=xtl[:].rearrange("p (c t) -> p c t", t=2), in_=xT[:, :, Tn - 2:Tn]), t_xT, [t_xtl])
                ST_(lambda e: e.dma_start(out=exs2[l], in_=xtl), [t_xtl], [t_exs2[l]])
            P.fence()

        def ffn(l):
            modulated_norm(l, 2)
            P.fence()
            if prompt and cx["sp"] == 0:
                LD(lambda e: e.dma_start(out=xtin, in_=exd2[l][0:128, :]), [t_exd2[l]], [t_xtin])
                act(sqt, xtin, AF.Square, [t_xtin], [t_sqt])
                pss2, pss2t = small_rot.get()
                for c in range(16):
                    T(lambda e, c=c, pss2=pss2: e.matmul(pss2[:, 0:2], lhsT=K["ones_f"], rhs=sqt[:, 2 * c:2 * c + 2], start=(c == 0), stop=(c == 15)),
                      [t_sqt, Kt["ones_f"]], [pss2t])
                ts(rs2, pss2[:, 0:2], 1.0 / D, EPS, ALU.mult, ALU.add, [pss2t], [t_rs2])
                act(rs2, rs2, AF.Sqrt, [t_rs2], [t_rs2])
                V(lambda e: e.reciprocal(out=rs2, in_=rs2), [t_rs2], [t_rs2])
                for c in range(16):
                    stt(sqt[:, 2 * c:2 * c + 2], xtin[:, 2 * c:2 * c + 2], MODP[l][:, 3, c:c + 1], rs2, ALU.mult, ALU.mult,
                        [t_xtin, t_MODP[l], t_rs2, t_sqt], [t_sqt])
                    act(h2t[:, c, :], sqt[:, 2 * c:2 * c + 2], AF.Identity, [t_sqt, t_MODP[l]], [t_h2t], bias=MODP[l][:, 4, c:c + 1], scale=1.0)
            if not prompt:
                for g in range(6):
                    ncol = min(2048, 2 * DFF - g * 2048)
                    LD(lambda e, g=g, ncol=ncol: e.dma_start(out=cstg[:, 0:ncol], in_=sconv_in[l, :, g * 2048:g * 2048 + ncol]), [], [t_cstg])
                    for q4 in range(0, ncol // 128, 4):
                        pb, pt = small_rot.get()
                        for cc in range(4):
                            T(lambda e, pb=pb, cc=cc, q4=q4: e.transpose(pb[:, cc * 32:(cc + 1) * 32], cstg[:, (q4 + cc) * 128:(q4 + cc + 1) * 128],
                                                                          K["ident_f"][0:32, 0:32]), [t_cstg, Kt["ident_f"]], [pt])
                        act(convT[:, g * 16 + q4:g * 16 + q4 + 4, :], pb[:, 0:128].rearrange("p (a b) -> p a b", a=4), AF.Copy, [pt], [t_convT])
            for j in range(NFT):
                wu, wut = load_w(("up", l, j))
                wux = list(wb_extra[0])
                res = []
                for half in range(2):
                    tile_idx = half * NFT + j
                    pu, put = main_rot.get()
                    mm(pu[:, 0:Tn], put, [(wu[:, c, half * 128:(half + 1) * 128], hT[:, c, :]) for c in range(16)], [wut] + wux + t_hT)
                    o3 = 3 * half
                    w0 = CW[l][:, 0, tile_idx:tile_idx + 1]
                    w1 = CW[l][:, 1, tile_idx:tile_idx + 1]
                    w2 = CW[l][:, 2, tile_idx:tile_idx + 1]
                    cb = CW[l][:, 3, tile_idx:tile_idx + 1]
                    if prompt:
                        U = tmp[o3]
                        tc_ = t_CARRY[l][tile_idx]
                        if cx["sp"] == 0:
                            pu2, pu2t = small_rot.get()
                            mm(pu2[:, 0:2], pu2t, [(wu[:, c, half * 128:(half + 1) * 128], h2t[:, c, :]) for c in range(16)], [wut, t_h2t] + wux)
                            ts(U[:, 0:2], pu2[:, 0:2], K["isb"][:, 0:1], None, ALU.mult, ALU.bypass, [pu2t, Kt["isb"]], [t_tmp[o3]])
                        else:
                            V(lambda e, U=U, tile_idx=tile_idx: e.tensor_copy(out=U[:, 0:2], in_=CARRY[l][:, tile_idx, :]), [tc_], [t_tmp[o3]])
                        act(U[:, 2:2 + Tn], pu[:, 0:Tn], AF.Copy, [put], [t_tmp[o3]])
                        V(lambda e, U=U, tile_idx=tile_idx: e.tensor_copy(out=CARRY[l][:, tile_idx, :], in_=U[:, Tn:Tn + 2]), [t_tmp[o3]], [tc_])
                        c1, c2 = tmp[o3 + 1][:, 0:Tn], tmp[o3 + 2][:, 0:Tn]
                        ts(c1, U[:, 0:Tn], w0, cb, ALU.mult, ALU.add, [t_tmp[o3], t_CW[l]], [t_tmp[o3 + 1]])
                        stt(c2, U[:, 1:Tn + 1], w1, c1, ALU.mult, ALU.add, [t_tmp[o3], t_tmp[o3 + 1], t_CW[l]], [t_tmp[o3 + 2]])
                        stt(c1, U[:, 2:Tn + 2], w2, c2, ALU.mult, ALU.add, [t_tmp[o3], t_tmp[o3 + 2], t_CW[l]], [t_tmp[o3 + 1]])
                        res.append((c1, t_tmp[o3 + 1], tmp[o3 + 2][:, 0:Tn], t_tmp[o3 + 2]))
                    else:
                        U = U6[half]
                        tu = t_U6[half]
                        V(lambda e, U=U, tile_idx=tile_idx: e.tensor_copy(out=U[:, :, 0:2], in_=convT[:, tile_idx, :].rearrange("p (s r) -> p s r", r=2)),
                          [t_convT], [tu])
                        act(U[:, :, 2:6], pu[:, 0:Tn].rearrange("p (s t) -> p s t", t=TS), AF.Copy, [put], [tu])
                        pb, pt = small_rot.get()
                        raw = tmp[o3][:, 0:32].rearrange("p (s r) -> p s r", r=2)
                        V(lambda e, U=U, raw=raw: e.tensor_copy(out=raw, in_=U[:, :, 4:6]), [tu], [t_tmp[o3]])
                        T(lambda e, pb=pb, o3=o3: e.transpose(pb[0:32, 0:128], tmp[o3][:, 0:32], K["ident_f"]), [t_tmp[o3], Kt["ident_f"]], [pt])
                        act(outc[half][:, (tile_idx % 22) * 128:(tile_idx % 22 + 1) * 128], pb[0:32, 0:128], AF.Copy, [pt], [t_outc[half]])
                        c1 = tmp[o3 + 1][:, 0:Tn].rearrange("p (s t) -> p s t", t=TS)
                        c2 = tmp[o3 + 2][:, 0:Tn].rearrange("p (s t) -> p s t", t=TS)
                        ts(c1, U[:, :, 0:4], w0, cb, ALU.mult, ALU.add, [tu, t_CW[l]], [t_tmp[o3 + 1]])
                        stt(c2, U[:, :, 1:5], w1, c1, ALU.mult, ALU.add, [tu, t_tmp[o3 + 1], t_CW[l]], [t_tmp[o3 + 2]])
                        stt(c1, U[:, :, 2:6], w2, c2, ALU.mult, ALU.add, [tu, t_tmp[o3 + 2], t_CW[l]], [t_tmp[o3 + 1]])
                        res.append((tmp[o3 + 1][:, 0:Tn], t_tmp[o3 + 1], tmp[o3 + 2][:, 0:Tn], t_tmp[o3 + 2]))
                (ca, tca, sa, tsa), (cbv, tcb, _, _) = res
                act(sa, ca, AF.Silu, [tca], [tsa])
                tt(actT[:, j, :], sa, cbv, ALU.mult, [tsa, tcb], [t_act[j]])
                if (not prompt) and j % 22 == 21:
                    for half in range(2):
                        grp = half * 2 + j // 22
                        ST_(lambda e, half=half, grp=grp: e.dma_start(out=sconv_o[l][:, grp * 2816:(grp + 1) * 2816], in_=outc[half]),
                            [t_outc[half]], [t_sconv_o])
                        outs_final.append(P.streams["act"][-1])
            for f in range(16):
                pf, pft = main_rot.get()
                for kh in range(2):
                    wd, wdt = load_w(("dn", l, f, kh))
                    for c in range(22):
                        T(lambda e, pf=pf, wd=wd, c=c, kh=kh: e.matmul(pf[:, 0:Tn], lhsT=wd[:, c, :], rhs=actT[:, kh * 22 + c, :],
                                                                     start=(kh == 0 and c == 0), stop=(kh == 1 and c == 21)),
                          [wdt, t_act[kh * 22 + c]], [pft])
                residual_add(l, 2, f, pf, pft)
            P.fence()

        t_mT = Tok("mT")
        t_act = [Tok(f"act{j}") for j in range(NFT)]

        def conv_out(l):
            for g in range(6):
                n = min(16, 88 - g * 16)
                for q4 in range(0, n, 4):
                    pb, pt = small_rot.get()
                    for cc in range(4):
                        ti = g * 16 + q4 + cc
                        T(lambda e, pb=pb, cc=cc, ti=ti, l=l: e.transpose(pb[0:2, cc * 128:(cc + 1) * 128], CARRY[l][:, ti, :], K["ident_f"]),
                          [t_CARRY[l][ti], Kt["ident_f"]], [pt])
                    act(cstage[:, q4 * 128:(q4 + 4) * 128], pb[0:2, 0:512], AF.Copy, [pt], [t_cstage])
                ST_(lambda e, g=g, n=n, l=l: e.dma_start(out=pconv[l][:, g * 2048:g * 2048 + n * 128], in_=cstage[:, 0:n * 128]), [t_cstage], [t_pconv])
                outs_final.append(P.streams["act"][-1])
            P.fence()

        def final_out(ydst):
            rms_rstd()
            for c in range(16):
                stt(yT[:, c, :], xT[:, c, :], PV[:, 64 + c:65 + c], rstd, ALU.mult, ALU.mult, [t_xT[c], t_PV, t_rstd], [t_yT])
            for (t0, rows) in tiles:
                for g in range(4):
                    pb, pt = main_rot.get()
                    for cc in range(4):
                        c = g * 4 + cc
                        T(lambda e, pb=pb, cc=cc, c=c, t0=t0, rows=rows: e.transpose(pb[0:rows, cc * 128:(cc + 1) * 128], yT[:, c, t0:t0 + rows], K["ident_f"]),
                          [t_yT, Kt["ident_f"]], [pt])
                    act(stg[0:rows, g * 512:(g + 1) * 512], pb[0:rows, 0:512], AF.Copy, [pt], [t_stgo])
                ST_(lambda e, t0=t0, rows=rows: e.dma_start(out=ydst[t0:t0 + rows, :], in_=stg[0:rows, :]), [t_stgo], [])
                outs_final.append(P.streams["act"][-1])
            P.fence()

        if not prompt:
            for l in range(DEPTH):
                modulated_norm(l, 1)
                for b in range(3):
                    mixer_branch(l, b)
                merge_and_out(l)
                ffn(l)
            final_out(ys)
        else:
            for l in range(DEPTH):
                cx["ph1"] = True
                for sp in range(NSP):
                    cx["sp"] = sp
                    load_x(sp)
                    load_rope(sp)
                    modulated_norm(l, 1)
                    for b in range(3):
                        mixer_branch(l, b)
                    P.fence()
                cx["ph1"] = False
                if no_cc:
                    LD(lambda e, l=l: e.dma_start(out=exd[l][0:1536, :], in_=exs[l]), t_exs[l], [t_exd[l]])
                else:
                    P.add("pool", lambda e, l=l: e.collective_compute("AllGather", ALU.bypass, replica_groups=pairs,
                                                                      ins=[exs[l].opt()], outs=[exd[l].opt()]), t_exs[l], [t_exd[l]])
                emit_casts(("ffn", l))
                for sp in range(NSP):
                    cx["sp"] = sp
                    load_x(sp)
                    load_rope(sp)
                    modulated_norm(l, 1)
                    for b in range(3):
                        mixer_branch(l, b)
                    merge_and_out(l)
                if no_cc:
                    LD(lambda e, l=l: e.dma_start(out=exd2[l][0:128, :], in_=exs2[l]), [t_exs2[l]], [t_exd2[l]])
                else:
                    P.add("pool", lambda e, l=l: e.collective_compute("AllGather", ALU.bypass, replica_groups=pairs,
                                                                      ins=[exs2[l].opt()], outs=[exd2[l].opt()]), [t_exs2[l]], [t_exd2[l]])
                if l + 1 < DEPTH:
                    emit_casts(("mix", l + 1))
                for sp in range(NSP):
                    cx["sp"] = sp
                    load_x(sp)
                    ffn(l)
                conv_out(l)
            for sp in range(NSP):
                load_x(sp)
                final_out(yp[sp * TP:(sp + 1) * TP])
        AR.off = m0

    t_pst = [[Tok() for _ in range(DEPTH)] for _ in range(3)]
    t_sst = Tok()
    t_sconv_o = Tok()
    t_pconv = Tok()
    t_stgo = Tok("stgo")
    outs_final = []

    setup()
    MOD = [AR.alloc([128, 96, 17], F32) for _ in range(DEPTH)]
    t_MOD = [Tok("MOD") for _ in range(DEPTH)]
    compute_mod(MOD, t_MOD)
    for l in range(DEPTH):
        LD(lambda e, l=l, M=MOD: e.dma_start(out=modsc[l], in_=M[l][:].rearrange("p a b -> p (a b)")), [t_MOD[l]], [t_modsc[l]])
    emit_casts(("mix", 0))
    P.fence()
    AR.off = persist_mark
    run_pass("p")
    AR.off = persist_mark
    MOD = [AR.alloc([128, 96, 17], F32) for _ in range(DEPTH)]
    t_MOD = [Tok("MOD") for _ in range(DEPTH)]
    for l in range(DEPTH):
        LD(lambda e, l=l, M=MOD: e.dma_start(out=M[l][:].rearrange("p a b -> p (a b)"), in_=modsc[l]), [t_modsc[l]], [t_MOD[l]])
    run_pass("s", MOD, t_MOD)
    P.final_waits = outs_final
    P.emit(st)
    st.close()
    return nc


_CACHE = {}


def _prep_inputs(inp):
    f32 = lambda a: np.ascontiguousarray(np.asarray(a, dtype=np.float32))
    consts = host_consts()
    pv = np.zeros((96, 128), np.float32)
    nm, nf, fn = f32(inp["norm_mix"]), f32(inp["norm_ffn"]), f32(inp["final_norm"])
    pv[0:16] = nm[0].reshape(16, 128)
    pv[16:32] = nm[1].reshape(16, 128)
    pv[32:48] = nf[0].reshape(16, 128)
    pv[48:64] = nf[1].reshape(16, 128)
    pv[64:80] = fn.reshape(16, 128)
    lbl = f32(inp["hgrn_lb_logits"])
    pv[80:84] = lbl[0].reshape(4, 128)
    pv[84:88] = lbl[1].reshape(4, 128)
    gb = f32(inp["gla_b_lr"])
    pv[88:92] = gb[0].reshape(4, 128)
    pv[92:96] = gb[1].reshape(4, 128)
    cw, cb = f32(inp["ffn_conv_w"]), f32(inp["ffn_conv_b"])
    convp = np.concatenate([cw.reshape(DEPTH, 3, 88, 128), cb.reshape(DEPTH, 1, 88, 128)], axis=1)
    shared = {
        "w_in": f32(inp["w_in"]), "gla_w_lr": f32(inp["gla_w_lr"]), "pvecs": pv,
        "head_norm": f32(inp["head_norm"]).reshape(DEPTH, 3 * HV), "w_branch": f32(inp["w_branch"]),
        "w_out": f32(inp["w_out"]), "w_ada": f32(inp["w_ada"]), "b_ada": f32(inp["b_ada"]).reshape(DEPTH, 96, 128),
        "ffn_w_up": f32(inp["ffn_w_up"]), "convp": np.ascontiguousarray(convp), "ffn_w_down": f32(inp["ffn_w_down"]),
    }
    for n, s, dt in CONST_SPECS:
        if n not in ("cos_p", "sin_p", "isb"):
            shared["k_" + n] = np.ascontiguousarray(consts[n])
    x_prompt, x_sample = f32(inp["x_prompt"]), f32(inp["x_sample"])
    c_prompt, c_sample = f32(inp["c_prompt"]), f32(inp["c_sample"])
    sts = [f32(inp["state_ret"]), f32(inp["state_gla"]), f32(inp["state_hgrn"])]
    sc = f32(inp["state_conv"])
    maps = []
    for c in range(N_CORES):
        b = c // 2
        hf = c % 2
        s0 = c * NSEQ
        m = dict(shared)
        m["xp"] = np.ascontiguousarray(x_prompt[b, hf * HALF:(hf + 1) * HALF])
        m["k_cos_p"] = np.ascontiguousarray(consts["cos_p"][:, hf * HALF:(hf + 1) * HALF])
        m["k_sin_p"] = np.ascontiguousarray(consts["sin_p"][:, hf * HALF:(hf + 1) * HALF])
        m["k_isb"] = np.full((128, 1), float(hf), np.float32)
        m["xs"] = np.ascontiguousarray(x_sample[s0:s0 + NSEQ].reshape(TSAMP, D))
        m["cvec"] = np.ascontiguousarray(np.concatenate([c_sample[s0:s0 + NSEQ], c_prompt[b:b + 1]], axis=0))
        for k in range(3):
            m[f"st_in{k}"] = np.ascontiguousarray(sts[k][:, s0:s0 + NSEQ])
        m["sconv_in"] = np.ascontiguousarray(sc[:, s0:s0 + NSEQ].reshape(DEPTH, NSEQ * 2, 2 * DFF))
        maps.append(m)
    return maps


def kernel(**inputs):
    if "nc" not in _CACHE:
        _CACHE["nc"] = build_program()
    nc = _CACHE["nc"]
    maps = _prep_inputs(inputs)
    res = run_bass_kernel_spmd(nc, maps, core_ids=list(range(N_CORES)))
    R = res.results
    y_prompt = np.stack([np.concatenate([R[2 * b]["yp"], R[2 * b + 1]["yp"]], axis=0) for b in range(4)]).astype(np.float32)
    y_sample = np.concatenate([R[c]["ys"].reshape(NSEQ, TS, D) for c in range(N_CORES)], axis=0).astype(np.float32)
    outs = [y_prompt, y_sample]
    for k in range(3):
        outs.append(np.stack([R[2 * b + 1][f"pst{k}"] for b in range(4)], axis=1).astype(np.float32))
    outs.append(np.stack([R[2 * b + 1]["pconv"] for b in range(4)], axis=1).astype(np.float32))
    for k in range(3):
        outs.append(np.concatenate([R[c][f"sst{k}"] for c in range(N_CORES)], axis=1).astype(np.float32))
    outs.append(np.concatenate([R[c]["sconv_o"].reshape(DEPTH, NSEQ, 2, 2 * DFF) for c in range(N_CORES)], axis=1).astype(np.float32))
    return tuple(outs)
```

```python
import numpy as np
from contextlib import ExitStack
import ml_dtypes
import concourse.bass as bass
import concourse.mybir as mybir
from concourse.bass_utils import run_bass_kernel_spmd

F32 = mybir.dt.float32
BF16 = mybir.dt.bfloat16
ALU = mybir.AluOpType
AF = mybir.ActivationFunctionType
AX = mybir.AxisListType
NPBF = ml_dtypes.bfloat16

D = 2048
NC16 = 16
NH = 4
HK = 128
HV = 256
MV = 1024
DFF = 5632
NFT = 44
NIN = 15376
DEPTH = 2
SEQ = 2048
TP = 512
NSP = 2
HALF = NSP * TP
PAIRS = [[0, 1], [2, 3], [4, 5], [6, 7]]
NSEQ = 16
TS = 4
TSAMP = NSEQ * TS
PAST = 16384
EPS = 1e-6
N_CORES = 8
OFF = [(0, 512, 1024, 2048), (3072, 3584, 4096, 5120), (6160, 6672, 7184, 8208)]
OFF_GLR = 6144
OFF_MG = 9232

ENGS = ("pe", "dve", "act", "pool", "sp")
SEM_EPOCH = 30000


class Tok:
    __slots__ = ("name", "last_write", "reads")

    def __init__(self, name=""):
        self.name = name
        self.last_write = None
        self.reads = []


class Op:
    __slots__ = ("eng", "fn", "deps", "is_dma", "sig", "needs_sig", "prewait", "wload")

    def __init__(self, eng, fn, is_dma):
        self.eng = eng
        self.fn = fn
        self.deps = set()
        self.is_dma = is_dma
        self.sig = None
        self.needs_sig = False
        self.prewait = None
        self.wload = False


class Prog:
    def __init__(self, nc, n_dma_sems=8):
        self.nc = nc
        self.streams = {e: [] for e in ENGS}
        self.n_dma_sems = n_dma_sems
        self.final_waits = []
        self.fence_deps = set()
        self.since_fence = []

    def add(self, eng, fn, reads=(), writes=(), dma=False, wload=False):
        op = Op(eng, fn, dma)
        op.wload = wload
        for t in reads:
            if t.last_write is not None:
                op.deps.add(t.last_write)
        for t in writes:
            if t.last_write is not None:
                op.deps.add(t.last_write)
            for r in t.reads:
                op.deps.add(r)
        if not wload:
            op.deps |= self.fence_deps
        op.deps.discard(op)
        for t in reads:
            t.reads.append(op)
        for t in writes:
            t.last_write = op
            t.reads = []
        self.streams[eng].append(op)
        if dma and not wload:
            self.since_fence.append(op)
        return op

    def fence(self):
        deps = set()
        for e in ENGS:
            if e in ("sp", "pool"):
                continue
            if self.streams[e]:
                deps.add(self.streams[e][-1])
        for op in self.since_fence:
            deps.add(op)
        self.since_fence = []
        self.fence_deps = deps

    def emit(self, stack):
        nc = self.nc
        for e in ENGS:
            for op in self.streams[e]:
                if op.eng == "pe":
                    op.deps = {d for d in op.deps if d.eng != "pe" or d.is_dma}
                for d in op.deps:
                    d.needs_sig = True
        self.sems = []

        def newsem(name):
            s = stack.enter_context(nc.semaphore(name))
            self.sems.append(s)
            return s

        for e in ENGS:
            cnt = 0
            ep = 0
            sem = None
            dsems = None
            dcnt = None
            di = 0
            for op in self.streams[e]:
                if op.is_dma:
                    if dsems is None:
                        dsems = [newsem(f"d_{e}_{i}") for i in range(self.n_dma_sems)]
                        dcnt = [0] * self.n_dma_sems
                    i = di % self.n_dma_sems
                    di += 1
                    if dcnt[i] + 16 > SEM_EPOCH:
                        dsems[i] = newsem(f"d_{e}_{i}_{di}")
                        dcnt[i] = 0
                    if dcnt[i] > 0:
                        op.prewait = (dsems[i], dcnt[i])
                    dcnt[i] += 16
                    op.sig = (dsems[i], dcnt[i])
                elif op.needs_sig:
                    if sem is None or cnt >= SEM_EPOCH:
                        sem = newsem(f"c_{e}_{ep}")
                        ep += 1
                        cnt = 0
                    cnt += 1
                    op.sig = (sem, cnt)
        block = stack.enter_context(nc.Block())
        self.n_wait = 0

        def run_stream(e):
            def body(eng):
                waited = {}
                for op in self.streams[e]:
                    need = {}
                    for d in op.deps:
                        s, v = d.sig
                        k = id(s)
                        if waited.get(k, 0) >= v:
                            continue
                        if k not in need or need[k][1] < v:
                            need[k] = (s, v)
                    if op.prewait is not None:
                        s, v = op.prewait
                        k = id(s)
                        if waited.get(k, 0) < v and (k not in need or need[k][1] < v):
                            need[k] = (s, v)
                    for k, (s, v) in need.items():
                        eng.wait_ge(s, v)
                        waited[k] = v
                        self.n_wait += 1
                    ins = op.fn(eng)
                    if op.is_dma:
                        ins.then_inc(op.sig[0], 16)
                    elif op.sig is not None:
                        ins.then_inc(op.sig[0], 1)
                if e == "sp":
                    for op in self.final_waits:
                        s, v = op.sig
                        if waited.get(id(s), 0) < v:
                            eng.wait_ge(s, v)
                            waited[id(s)] = v
            return body

        block.tensor(run_stream("pe"))
        block.vector(run_stream("dve"))
        block.scalar(run_stream("act"))
        block.gpsimd(run_stream("pool"))
        block.sync(run_stream("sp"))


def host_consts():
    c = {}
    c["ident_f"] = np.eye(128, dtype=np.float32)
    c["ident_b"] = np.eye(128, dtype=np.float32).astype(NPBF)
    sw = np.zeros((128, 128), np.float32)
    for m in range(128):
        sw[(m + 64) % 128, m] = 1.0
    c["pswap"] = sw.astype(NPBF)
    c["ones_f"] = np.ones((128, 128), np.float32)
    s = np.arange(128)[:, None]
    t = np.arange(128)[None, :]
    c["mask1"] = (s <= t).astype(np.float32).astype(NPBF)
    c["mask4"] = ((s <= t) & (s // 32 == t // 32)).astype(np.float32).astype(NPBF)
    ms = np.zeros((128, 128), np.float32)
    s6 = np.arange(64)[:, None]
    t6 = np.arange(64)[None, :]
    ms[:64, :64] = ((s6 <= t6) & (s6 // 4 == t6 // 4))
    c["masks"] = ms.astype(NPBF)
    cm4 = np.zeros((128, 4, 128), np.float32)
    for j in range(4):
        cm4[:, j, 32 * j:32 * j + 32] = 1.0
    c["cm4"] = cm4.astype(NPBF)
    rm4 = np.zeros((128, 4), np.float32)
    for p in range(128):
        rm4[p, p // 32] = 1.0
    c["rm4"] = rm4
    cms = np.zeros((128, 16, 64), np.float32)
    for j in range(16):
        cms[:, j, 4 * j:4 * j + 4] = 1.0
    c["cms"] = cms.astype(NPBF)
    rms = np.zeros((128, 16), np.float32)
    for p in range(64):
        rms[p, p // 4] = 1.0
    c["rms"] = rms
    tt = np.arange(512)
    c["rst_g"] = np.broadcast_to((tt % 128 != 0).astype(np.float32), (128, 512)).astype(NPBF)
    c["rst_h"] = np.broadcast_to((tt % 32 != 0).astype(np.float32), (128, 512)).astype(NPBF)
    c["rst_s"] = np.broadcast_to((np.arange(64) % 4 != 0).astype(np.float32), (128, 64)).astype(NPBF)
    lg = np.log1p(-np.exp2(-5.0 - np.arange(4, dtype=np.float32))).astype(np.float32)
    tq = np.arange(128, dtype=np.float32) + 1.0
    gq = np.exp(lg[:, None] * tq[None, :]).astype(np.float32)
    gk = (np.exp(-lg[:, None] * tq[None, :]) * (HK ** -0.5)).astype(np.float32)
    c["gq"] = np.broadcast_to(gq, (128, 4, 128)).copy()
    c["gk"] = np.broadcast_to(gk, (128, 4, 128)).copy()
    ts_ = (np.arange(64) % 4).astype(np.float32) + 1.0
    gqs = np.exp(lg[:, None] * ts_[None, :]).astype(np.float32)
    gks = (np.exp(-lg[:, None] * ts_[None, :]) * (HK ** -0.5)).astype(np.float32)
    c["gqs"] = np.broadcast_to(gqs, (128, 4, 64)).copy()
    c["gks"] = np.broadcast_to(gks, (128, 4, 64)).copy()
    c["elr"] = np.broadcast_to(np.exp(lg * 128.0).astype(np.float32), (128, 4)).copy()
    c["elrs"] = np.broadcast_to(np.exp(lg * 4.0).astype(np.float32), (128, 4)).copy()
    half = 64
    inv = (10000.0 ** (-np.arange(half, dtype=np.float32) / half)).astype(np.float32)
    invm = np.concatenate([inv, inv])
    sgn = np.concatenate([-np.ones(64, np.float32), np.ones(64, np.float32)])
    pos = np.arange(SEQ, dtype=np.float32)
    ang = (pos[None, :] * invm[:, None]).astype(np.float32)
    c["cos_p"] = np.cos(ang).astype(np.float32)
    c["sin_p"] = (np.sin(ang) * sgn[:, None]).astype(np.float32)
    poss = (PAST + (np.arange(64) % 4)).astype(np.float32)
    angs = (poss[None, :] * invm[:, None]).astype(np.float32)
    c["cos_s"] = np.cos(angs).astype(np.float32)
    c["sin_s"] = (np.sin(angs) * sgn[:, None]).astype(np.float32)
    return c


CONST_SPECS = [
    ("ident_f", [128, 128], F32), ("ident_b", [128, 128], BF16), ("pswap", [128, 128], BF16),
    ("ones_f", [128, 128], F32), ("mask1", [128, 128], BF16), ("mask4", [128, 128], BF16),
    ("masks", [128, 128], BF16), ("cm4", [128, 4, 128], BF16), ("rm4", [128, 4], F32),
    ("cms", [128, 16, 64], BF16), ("rms", [128, 16], F32), ("rst_g", [128, 512], BF16),
    ("rst_h", [128, 512], BF16), ("rst_s", [128, 64], BF16), ("gq", [128, 4, 128], F32),
    ("gk", [128, 4, 128], F32), ("gqs", [128, 4, 64], F32), ("gks", [128, 4, 64], F32),
    ("elr", [128, 4], F32), ("elrs", [128, 4], F32), ("cos_p", [128, HALF], F32),
    ("sin_p", [128, HALF], F32), ("isb", [128, 1], F32), ("cos_s", [128, 64], F32), ("sin_s", [128, 64], F32),
]


class Arena:
    def __init__(self, ap, words):
        self.ap = ap
        self.words = words
        self.off = 0

    def alloc(self, shape, dt):
        n = int(np.prod(shape[1:]))
        w = (n + 1) // 2 if dt == BF16 else n
        w = (w + 7) // 8 * 8
        assert self.off + w <= self.words, f"arena overflow {self.off}+{w}>{self.words}"
        v = self.ap[:, self.off:self.off + w]
        self.off += w
        if dt == BF16:
            v = v.bitcast(BF16)
        v = v[:, 0:n]
        if len(shape) == 3:
            v = v.rearrange("p (a b) -> p a b", a=shape[1])
        elif len(shape) == 4:
            v = v.rearrange("p (a b c) -> p a b c", a=shape[1], b=shape[2])
        if shape[0] < 128:
            v = v[0:shape[0]]
        return v


ARENA_WORDS = 53200


def build_program(debug=False, n_cores=N_CORES, no_cc=False):
    pairs = [[2 * i, 2 * i + 1] for i in range(n_cores // 2)]
    nc = bass.Bass("TRN2", target_bir_lowering=False)

    def din(name, shape, dt=F32):
        return nc.dram_tensor(name, list(shape), dt, kind="ExternalInput").ap()

    def dout(name, shape):
        return nc.dram_tensor(name, list(shape), F32, kind="ExternalOutput").ap()

    xp = din("xp", [HALF, D])
    xs = din("xs", [TSAMP, D])
    cvec = din("cvec", [33, D])
    st_in = [din(f"st_in{b}", [DEPTH, NSEQ, NH, HK, HV]) for b in range(3)]
    sconv_in = din("sconv_in", [DEPTH, NSEQ * 2, 2 * DFF])
    w_in = din("w_in", [DEPTH, D, NIN])
    gla_w_lr = din("gla_w_lr", [DEPTH, 16, 512])
    pvecs = din("pvecs", [96, 128])
    head_norm = din("head_norm", [DEPTH, 3 * HV])
    w_branch = din("w_branch", [DEPTH, 3, MV, D])
    w_out = din("w_out", [DEPTH, D, D])
    w_ada = din("w_ada", [DEPTH, D, 3 * D])
    b_ada = din("b_ada", [DEPTH, 48, 128])
    w_up = din("ffn_w_up", [DEPTH, D, 2 * DFF])
    convp = din("convp", [DEPTH, 4, 88, 128])
    w_down = din("ffn_w_down", [DEPTH, DFF, D])
    cst = {n: din("k_" + n, s, dt) for n, s, dt in CONST_SPECS}

    yp = dout("yp", [HALF, D])
    ys = dout("ys", [TSAMP, D])
    pst = [dout(f"pst{b}", [DEPTH, NH, HK, HV]) for b in range(3)]
    pconv = dout("pconv", [DEPTH, 2, 2 * DFF])
    sst = [dout(f"sst{b}", [DEPTH, NSEQ, NH, HK, HV]) for b in range(3)]
    sconv_o = dout("sconv_o", [DEPTH, NSEQ * 2, 2 * DFF])

    st = ExitStack()
    P = Prog(nc)
    dscr = lambda name, shape: nc.dram_tensor(name, list(shape), F32).ap()
    xsw = [dscr(f'xsw{sp}', [128, 16, TP]) for sp in range(NSP)]
    t_xsw = [[Tok() for _ in range(16)] for _ in range(NSP)]
    ph1st = [dscr(f'ph1st{b}', [NH, HK, HV]) for b in range(3)]
    t_ph1st = [Tok() for _ in range(3)]
    exs = [dscr(f'exs{l}', [3 * NH * HK, HV]) for l in range(DEPTH)]
    exd = [dscr(f'exd{l}', [2 * 3 * NH * HK, HV]) for l in range(DEPTH)]
    t_exs = [[Tok() for _ in range(3)] for _ in range(DEPTH)]
    t_exd = [Tok() for _ in range(DEPTH)]
    exs2 = [dscr(f'exs2_{l}', [128, 32]) for l in range(DEPTH)]
    exd2 = [dscr(f'exd2_{l}', [256, 32]) for l in range(DEPTH)]
    t_exs2 = [Tok() for _ in range(DEPTH)]
    t_exd2 = [Tok() for _ in range(DEPTH)]
    modsc = [dscr(f'modsc{l}', [128, 96 * 17]) for l in range(DEPTH)]
    exm_s = [dscr(f'exm_s{l}', [128, 48 * 33]) for l in range(DEPTH)]
    exm_d = [dscr(f'exm_d{l}', [256, 48 * 33]) for l in range(DEPTH)]
    t_exm_s = [Tok() for _ in range(DEPTH)]
    t_exm_d = [Tok() for _ in range(DEPTH)]
    kTs = [[nc.dram_tensor(f'kTs{sp}_{b}', [128, 4, TP], BF16).ap() for b in range(3)] for sp in range(NSP)]
    vTs = [[nc.dram_tensor(f'vTs{sp}_{b}', [128, 4, MV], BF16).ap() for b in range(3)] for sp in range(NSP)]
    eqs = [[dscr(f'eqs{sp}_{b}', [128, 4, TP]) for b in range(3)] for sp in range(NSP)]
    els = [[dscr(f'els{sp}_{b}', [128, 64]) for b in range(3)] for sp in range(NSP)]
    t_kTs = [[Tok() for b in range(3)] for sp in range(NSP)]
    t_vTs = [[Tok() for b in range(3)] for sp in range(NSP)]
    t_eqs = [[[Tok() for h in range(4)] for b in range(3)] for sp in range(NSP)]
    t_els = [[Tok() for b in range(3)] for sp in range(NSP)]
    t_modsc = [Tok() for _ in range(DEPTH)]
    t_ada_done = Tok('ada_done')
    arena_t = st.enter_context(nc.sbuf_tensor("arena", [128, ARENA_WORDS], F32))
    AR = Arena(arena_t[:], ARENA_WORDS)
    banks = [st.enter_context(nc.psum_tensor(f"pb{i}", [128, 512], F32)) for i in range(8)]
    bank_tok = [Tok(f"pb{i}") for i in range(8)]
    bank_bf = banks[7][:].bitcast(BF16)

    class Rot:
        def __init__(self, idx):
            self.idx = idx
            self.i = 0

        def get(self):
            k = self.idx[self.i % len(self.idx)]
            self.i += 1
            return banks[k][:], bank_tok[k]

    main_rot = Rot([0, 1, 2, 3])
    small_rot = Rot([4, 5, 6])

    V = lambda fn, r=(), w=(): P.add("dve", fn, r, w)
    A = lambda fn, r=(), w=(): P.add("act", fn, r, w)
    T = lambda fn, r=(), w=(): P.add("pe", fn, r, w)
    LD = lambda fn, r=(), w=(): P.add("sp", fn, r, w, dma=True)
    ST_ = lambda fn, r=(), w=(): P.add("act", fn, r, w, dma=True)

    def mm(out_ap, out_tok, pairs, reads):
        n = len(pairs)
        for i, (l, r) in enumerate(pairs):
            T(lambda e, l=l, r=r, i=i: e.matmul(out_ap, lhsT=l, rhs=r, start=(i == 0), stop=(i == n - 1)),
              reads, [out_tok])

    def act(out, in_, func, r, w, **kw):
        return A(lambda e: e.activation(out=out, in_=in_, func=func, **kw), r, w)

    def tt(out, a, b, op, r, w):
        return V(lambda e: e.tensor_tensor(out=out, in0=a, in1=b, op=op), r, w)

    def ts(out, a, s1, s2, op0, op1, r, w):
        return V(lambda e: e.tensor_scalar(out=out, in0=a, scalar1=s1, scalar2=s2, op0=op0, op1=op1), r, w)

    def stt(out, a, s, b, op0, op1, r, w):
        return V(lambda e: e.scalar_tensor_tensor(out=out, in0=a, scalar=s, in1=b, op0=op0, op1=op1), r, w)

    K = {}
    Kt = {}
    for n, s, dt in CONST_SPECS:
        if n in ("cos_p", "sin_p", "cos_s", "sin_s", "cms", "rms", "rst_s", "gqs", "gks", "elrs", "masks"):
            continue
        K[n] = AR.alloc(s, dt)
        Kt[n] = Tok(n)
        LD(lambda e, n=n: e.dma_start(out=K[n], in_=cst[n]), [], [Kt[n]])
    PV = AR.alloc([128, 96], F32)
    t_PV = Tok("PV")
    CW = [AR.alloc([128, 4, 88], F32) for _ in range(DEPTH)]
    t_CW = [Tok("CW") for _ in range(DEPTH)]
    HNW = [AR.alloc([128, 3 * HV], BF16) for _ in range(DEPTH)]
    t_HNW = [Tok("HNW") for _ in range(DEPTH)]
    GLW = [AR.alloc([16, 512], BF16) for _ in range(DEPTH)]
    t_GLW = [Tok("GLW") for _ in range(DEPTH)]
    LBV = AR.alloc([128, DEPTH, 3, 4], F32)
    t_LBV = Tok("LBV")
    NEGB = AR.alloc([128, DEPTH, 4], F32)
    MODP = [AR.alloc([128, 6, 16], F32) for _ in range(DEPTH)]
    t_MODP = [Tok("MODP") for _ in range(DEPTH)]
    CARRY = [AR.alloc([128, 88, 2], F32) for _ in range(DEPTH)]
    t_CARRY = [[Tok("carry") for _ in range(88)] for _ in range(DEPTH)]
    NWB = 4
    WB = [AR.alloc([128, 4096], BF16) for _ in range(NWB)]
    t_WB = [Tok(f"wb{i}") for i in range(NWB)]
    wb_i = [0]
    persist_mark = AR.off

    scr = {}
    scr_tok = {}
    cast_list = []

    def defblk(key, srcs, kc, ncols):
        t = nc.dram_tensor("s_" + "_".join(str(k) for k in key), [128, kc, ncols], BF16).ap()
        scr[key] = (t, kc, ncols)
        cast_list.append((key, srcs))

    for l in range(DEPTH):
        for b in range(3):
            oq, ok, ov, og = OFF[b]
            for hp in range(2):
                defblk(("k", l, b, hp), [(w_in[l, :, ok + 256 * hp: ok + 256 * hp + 256], 0)], 16, 256)
                defblk(("q", l, b, hp), [(w_in[l, :, oq + 256 * hp: oq + 256 * hp + 256], 0)], 16, 256)
            if b == 1:
                defblk(("glr", l), [(w_in[l, :, OFF_GLR:OFF_GLR + 16], 0)], 16, 16)
            for h in range(4):
                defblk(("v", l, b, h), [(w_in[l, :, ov + 256 * h: ov + 256 * h + 256], 0)], 16, 256)
            for h in range(4):
                defblk(("g", l, b, h), [(w_in[l, :, og + 256 * h: og + 256 * h + 256], 0)], 16, 256)
        for fp in range(8):
            for b in range(3):
                defblk(("br", l, b, fp), [(w_branch[l, b, :, 256 * fp:256 * fp + 256], 0)], 8, 256)
                c0 = OFF_MG + b * D + 256 * fp
                defblk(("mg", l, b, fp), [(w_in[l, :, c0:c0 + 256], 0)], 16, 256)
        for fp in range(8):
            defblk(("out", l, fp), [(w_out[l, :, 256 * fp:256 * fp + 256], 0)], 16, 256)
        for j in range(NFT):
            defblk(("up", l, j), [(w_up[l, :, 128 * j:128 * j + 128], 0),
                                  (w_up[l, :, DFF + 128 * j:DFF + 128 * j + 128], 128)], 16, 256)
        for f in range(16):
            for kh in range(2):
                defblk(("dn", l, f, kh), [(w_down[l, kh * 2816:(kh + 1) * 2816, 128 * f:128 * f + 128], 0)], 22, 128)

    cast_src = dict(cast_list)
    cast_done = set()
    t_WB2 = [Tok(f"wbb{i}") for i in range(16)]
    wb_extra = [[]]

    def emit_casts(stage):
        return

    def load_w(key):
        dst_scr, kc, ncols = scr[key]
        i = wb_i[0] % len(WB)
        wb_i[0] += 1
        v = WB[i][:, 0:kc * ncols].rearrange("p (c n) -> p c n", c=kc)
        if key not in cast_done:
            cast_done.add(key)
            two = len(cast_src[key]) == 2
            op0 = None
            for si, (src, co) in enumerate(cast_src[key]):
                n = src.shape[1]
                fn = lambda e, src=src, co=co, n=n: e.dma_start(out=v[:, :, co:co + n], in_=src.rearrange("(c p) n -> p c n", p=128))
                if si == 0:
                    op0 = P.add("pool", fn, [], [t_WB[i], t_WB2[i]] if two else [t_WB[i]], dma=True, wload=True)
                else:
                    op1 = P.add("pool", fn, [], [], dma=True, wload=True)
                    op1.deps = set(op0.deps)
                    t_WB2[i].last_write = op1
                    t_WB2[i].reads = []
            wb_extra[0] = [t_WB2[i]] if two else []
            scr_tok[key] = [Tok(str(key))]
            P.add("act", lambda e: e.dma_start(out=dst_scr, in_=v), [t_WB[i]] + wb_extra[0], scr_tok[key], dma=True, wload=True)
        else:
            wb_extra[0] = []
            P.add("sp", lambda e: e.dma_start(out=v, in_=dst_scr), scr_tok[key] + [t_WB2[i]], [t_WB[i]], dma=True, wload=True)
        return v, t_WB[i]

    def setup():
        m0 = AR.off
        stg = AR.alloc([128, 128], F32)
        t_stg = Tok("stg")
        LD(lambda e: e.dma_start(out=stg[0:96, :], in_=pvecs), [], [t_stg])
        pb, pt = small_rot.get()
        T(lambda e, pb=pb: e.transpose(pb[:, 0:96], stg[0:96, :], K["ident_f"][0:96, 0:96]), [t_stg, Kt["ident_f"]], [pt])
        act(PV, pb[:, 0:96], AF.Copy, [pt], [t_PV])
        for l in range(DEPTH):
            for j in range(4):
                LD(lambda e, l=l, j=j: e.dma_start(out=stg[0:88, :], in_=convp[l, j]), [], [t_stg])
                pb, pt = small_rot.get()
                T(lambda e, pb=pb: e.transpose(pb[:, 0:88], stg[0:88, :], K["ident_f"][0:88, 0:88]), [t_stg, Kt["ident_f"]], [pt])
                act(CW[l][:, j, :], pb[:, 0:88], AF.Copy, [pt], [t_CW[l]])
            hn32 = AR.alloc([128, 3 * HV], F32)
            t_hn = Tok("hn32")
            LD(lambda e, l=l, hn32=hn32: e.dma_start(out=hn32, in_=head_norm[l].partition_broadcast(128)), [], [t_hn])
            V(lambda e, l=l, hn32=hn32: e.tensor_copy(out=HNW[l], in_=hn32), [t_hn], [t_HNW[l]])
            gl32 = AR.alloc([16, 512], F32)
            t_gl = Tok("gl32")
            LD(lambda e, l=l, gl32=gl32: e.dma_start(out=gl32, in_=gla_w_lr[l]), [], [t_gl])
            V(lambda e, l=l, gl32=gl32: e.tensor_copy(out=GLW[l], in_=gl32), [t_gl], [t_GLW[l]])
        tmp = AR.alloc([128, 8, 4], F32)
        t_tmp = Tok("lbtmp")
        l0 = PV[:, 80:84]
        l1 = PV[:, 84:88]
        mx, e0, e1, sm, r, s0, s1, cs = [tmp[:, i, :] for i in range(8)]
        tt(mx, l0, l1, ALU.max, [t_PV], [t_tmp])
        tt(e0, l0, mx, ALU.subtract, [t_PV, t_tmp], [t_tmp])
        tt(e1, l1, mx, ALU.subtract, [t_PV, t_tmp], [t_tmp])
        act(e0, e0, AF.Exp, [t_tmp], [t_tmp])
        act(e1, e1, AF.Exp, [t_tmp], [t_tmp])
        tt(sm, e0, e1, ALU.add, [t_tmp], [t_tmp])
        V(lambda e: e.reciprocal(out=r, in_=sm), [t_tmp], [t_tmp])
        tt(s0, e0, r, ALU.mult, [t_tmp], [t_tmp])
        tt(s1, e1, r, ALU.mult, [t_tmp], [t_tmp])
        tt(cs, s0, s1, ALU.add, [t_tmp], [t_tmp])
        tt(LBV[:, 0, 0, :], s0, s0, ALU.subtract, [t_tmp], [t_LBV])
        tt(LBV[:, 1, 0, :], cs, s0, ALU.subtract, [t_tmp], [t_LBV])
        for l in range(DEPTH):
            ts(LBV[:, l, 1, :], LBV[:, l, 0, :], -1.0, 1.0, ALU.mult, ALU.add, [t_LBV], [t_LBV])
            ts(LBV[:, l, 2, :], LBV[:, l, 0, :], 1.0, -1.0, ALU.mult, ALU.add, [t_LBV], [t_LBV])
            ts(NEGB[:, l, :], PV[:, 88 + 4 * l:92 + 4 * l], -1.0, None, ALU.mult, ALU.bypass, [t_PV], [t_LBV])
        for l in range(DEPTH):
            V(lambda e, l=l: e.memset(CARRY[l], 0.0), [], t_CARRY[l])
        return m0

    def compute_mod(MOD, t_MOD):
        m0 = AR.off
        c_sb = AR.alloc([33, D], F32)
        s_sb = AR.alloc([33, D], F32)
        scT = AR.alloc([128, 16, 33], F32)
        badaT = AR.alloc([128, 48], F32)
        stg = AR.alloc([128, 128], F32)
        scTb = AR.alloc([128, 16, 33], BF16)
        MODH = [AR.alloc([128, 48, 33], F32) for _ in range(DEPTH)]
        MODF = AR.alloc([128, 96, 33], F32)
        omisb = AR.alloc([128, 1], F32)
        t_MODH = [Tok() for _ in range(DEPTH)]
        t_MODF, t_om = Tok(), Tok()
        t_scTb = Tok()
        t_c, t_s, t_scT, t_ba, t_stg = Tok(), Tok(), Tok(), Tok(), Tok()
        LD(lambda e: e.dma_start(out=c_sb, in_=cvec), [], [t_c])
        act(s_sb, c_sb, AF.Sigmoid, [t_c], [t_s])
        tt(s_sb, s_sb, c_sb, ALU.mult, [t_c, t_s], [t_s])
        ts(omisb, K["isb"][:, 0:1], -1.0, 1.0, ALU.mult, ALU.add, [Kt["isb"]], [t_om])
        for c in range(16):
            pb, pt = small_rot.get()
            T(lambda e, c=c, pb=pb: e.transpose(pb[:, 0:33], s_sb[:, c * 128:(c + 1) * 128], K["ident_f"][0:33, 0:33]),
              [t_s, Kt["ident_f"]], [pt])
            act(scT[:, c, :], pb[:, 0:33], AF.Copy, [pt], [t_scT])
        V(lambda e: e.tensor_copy(out=scTb, in_=scT), [t_scT], [t_scTb])
        for l in range(DEPTH):
            LD(lambda e, l=l: e.dma_start(out=stg[0:48, :], in_=b_ada[l]), [], [t_stg])
            pb, pt = small_rot.get()
            T(lambda e, pb=pb: e.transpose(pb[:, 0:48], stg[0:48, :], K["ident_f"][0:48, 0:48]), [t_stg, Kt["ident_f"]], [pt])
            act(badaT, pb[:, 0:48], AF.Copy, [pt], [t_ba])
            for fb in range(24):
                i = wb_i[0] % len(WB)
                wb_i[0] += 1
                wv = WB[i].rearrange("p (c n) -> p c n", c=16)
                P.add("pool", lambda e, l=l, fb=fb, wv=wv: e.dma_start(
                    out=wv, in_=w_ada[l, :, fb * 256:(fb + 1) * 256].rearrange("(c p) n -> p c n", p=128)),
                    [], [t_WB[i]], dma=True, wload=True)
                for ft in range(2):
                    fi = fb * 2 + ft
                    pb, pt = small_rot.get()
                    mm(pb[:, 0:33], pt, [(wv[:, c, ft * 128:(ft + 1) * 128], scTb[:, c, :]) for c in range(16)], [t_WB[i], t_scTb])
                    act(MODH[l][:, fi, :], pb[:, 0:33], AF.Identity, [pt, t_ba], [t_MODH[l]], bias=badaT[:, fi:fi + 1], scale=1.0)
            ST_(lambda e, l=l: e.dma_start(out=exm_s[l], in_=MODH[l][:].rearrange("p a b -> p (a b)")), [t_MODH[l]], [t_exm_s[l]])
        for l in range(DEPTH):
            P.add("pool", lambda e, l=l: e.collective_compute("AllGather", ALU.bypass, replica_groups=pairs,
                                                              ins=[exm_s[l].opt()], outs=[exm_d[l].opt()]), [t_exm_s[l]], [t_exm_d[l]])
        for l in range(DEPTH):
            LD(lambda e, l=l: e.dma_start(out=MODF[:].rearrange("p (r a) b -> p r (a b)", r=2), in_=exm_d[l].rearrange("(r p) n -> p r n", p=128)),
               [t_exm_d[l]], [t_MODF])
            ts(MOD[l][:, :, 0:16], MODF[:, :, 0:16], omisb[:, 0:1], None, ALU.mult, ALU.bypass, [t_MODF, t_om], [t_MOD[l]])
            stt(MOD[l][:, :, 0:16], MODF[:, :, 16:32], K["isb"][:, 0:1], MOD[l][:, :, 0:16], ALU.mult, ALU.add,
                [t_MODF, Kt["isb"], t_MOD[l]], [t_MOD[l]])
            V(lambda e, l=l: e.tensor_copy(out=MOD[l][:, :, 16], in_=MODF[:, :, 32]), [t_MODF, t_MOD[l]], [t_MOD[l]])
            for (c0, pv0) in ((16, 16 * l), (64, 32 + 16 * l)):
                stt(MOD[l][:, c0:c0 + 16, :], MOD[l][:, c0:c0 + 16, :], 1.0,
                    PV[:, pv0:pv0 + 16].unsqueeze(2).broadcast_to([128, 16, 17]),
                    ALU.add, ALU.mult, [t_MOD[l], t_PV], [t_MOD[l]])
            for k, c0 in enumerate((16, 0, 32, 64, 48, 80)):
                V(lambda e, l=l, k=k, c0=c0: e.tensor_copy(out=MODP[l][:, k, :], in_=MOD[l][:, c0:c0 + 16, 16]),
                  [t_MOD[l]], [t_MODP[l]])
        return m0

    dbg_outs = {}

    def run_pass(kind, MOD=None, t_MOD=None):
        prompt = kind == "p"
        cx = {"sp": 0, "ph1": False, "store_x": False}
        Tn = TP if prompt else TSAMP
        tiles = [(i * 128, 128) for i in range(4)] if prompt else [(0, 64)]
        nt = len(tiles)
        m0 = AR.off
        xT = AR.alloc([128, 16, Tn], F32)
        t_xT = [Tok(f"xT{c}") for c in range(16)]
        hT = AR.alloc([128, 16, Tn], BF16)
        t_hT = [Tok(f"hT{c}") for c in range(16)]
        r1_words = max(NFT * Tn // 2, 16 * Tn + 2048) + 64
        r1_off = AR.off
        R1 = Arena(AR.ap[:, r1_off:r1_off + r1_words], r1_words)
        AR.off += r1_words
        uT = R1.alloc([128, 3, 8, Tn], BF16)
        v_tok = R1.alloc([128, nt, MV], BF16)
        sg_tok = R1.alloc([128, nt, MV], BF16)
        R1b = Arena(AR.ap[:, r1_off:r1_off + r1_words], r1_words)
        actT = R1b.alloc([128, NFT, Tn], BF16)
        R1c = Arena(AR.ap[:, r1_off:r1_off + r1_words], r1_words)
        yT = R1c.alloc([128, 16, Tn], F32)
        stg = R1c.alloc([128, D], F32)
        mT = Arena(AR.ap[:, r1_off + 3 * 8 * Tn // 2 + 0: r1_off + r1_words], r1_words).alloc([128, 16, Tn], BF16) \
            if prompt else AR.alloc([128, 16, Tn], BF16)
        t_v = [[Tok() for _ in range(4)] for _ in range(nt)]
        t_sg = [[Tok() for _ in range(4)] for _ in range(nt)]
        t_uT = [[Tok() for _ in range(nt)] for _ in range(3)]
        t_stgi = Tok("stgi")
        t_yT = Tok("yT")
        qT = AR.alloc([128, 4, Tn], BF16)
        kT = AR.alloc([128, 4, Tn], BF16)
        t_qT = [Tok() for _ in range(4)]
        t_kT = [Tok() for _ in range(4)]
        NTMP = 6
        TWS = [520] * 6 if prompt else [128, 128, 1040, 1040, 128, 128]
        tmp = [AR.alloc([128, TWS[i]], F32) for i in range(NTMP)]
        t_tmp = [Tok(f"tmp{i}") for i in range(NTMP)]
        acc = AR.alloc([128, 2, Tn], F32)
        t_acc = Tok("acc")
        rstd = AR.alloc([128, Tn], F32)
        t_rstd = Tok("rstd")
        COS = AR.alloc([128, Tn], F32)
        SIN = AR.alloc([128, Tn], F32)
        t_rope = Tok("rope")
        EL = AR.alloc([128, 4, 16], F32)
        t_EL = [Tok() for _ in range(4)]
        PT_ = [AR.alloc([128, 128], BF16) for _ in range(4)]
        t_PT = [Tok() for _ in range(4)]
        ktok = [AR.alloc([128, 128], BF16) for _ in range(4)]
        t_ktok = [Tok() for _ in range(4)]
        u_tok = AR.alloc([128, MV], BF16)
        t_utok = Tok("utok")
        kve = [AR.alloc([128, HV], F32) for _ in range(4)]
        t_kve = [Tok() for _ in range(4)]
        sq256 = [AR.alloc([128, HV], F32) for _ in range(4)]
        t_sq256 = [Tok() for _ in range(4)]
        ssv = AR.alloc([128, 8], F32)
        t_ssv = Tok("ssv")
        glrT = AR.alloc([16, Tn], BF16)
        t_glrT = Tok("glrT")
        if prompt:
            SF = AR.alloc([128, 4, HV], F32)
            Sbf = AR.alloc([128, 4, HV], BF16)
            t_SF = [Tok() for _ in range(4)]
            t_Sbf = [Tok() for _ in range(4)]
            KM = [AR.alloc([128, 4, 128], BF16) for _ in range(4)]
            QM = [AR.alloc([128, 4, 128], BF16) for _ in range(4)]
            t_KM = [Tok() for _ in range(4)]
            t_QM = [Tok() for _ in range(4)]
            cstage = stg[0:2, :]
            t_cstage = Tok("cstage")
            MASKS = {1: K["mask1"], 4: K["mask4"]}
            t_MASKS = {1: Kt["mask1"], 4: Kt["mask4"]}
            CM = K["cm4"]
            RM = K["rm4"]
            t_CMRM = [Kt["cm4"], Kt["rm4"]]
            RST = {1: K["rst_g"], 2: K["rst_h"]}
            t_RST = {1: Kt["rst_g"], 2: Kt["rst_h"]}
            GQ, GK, ELR = K["gq"], K["gk"], K["elr"]
            t_G = [Kt["gq"], Kt["gk"], Kt["elr"]]
            xtl = AR.alloc([128, 32], F32)
            t_xtl = Tok("xtl")
            xtin = AR.alloc([128, 32], F32)
            sqt = AR.alloc([128, 32], F32)
            rs2 = AR.alloc([128, 2], F32)
            h2t = AR.alloc([128, 16, 2], BF16)
            t_xtin, t_sqt, t_rs2, t_h2t = Tok(), Tok(), Tok(), Tok()
        else:
            r3_off = AR.off
            r3_words = 2 * NSEQ * HV + 64
            R3 = Arena(AR.ap[:, r3_off:r3_off + r3_words], r3_words)
            AR.off += r3_words
            SS = [R3.alloc([128, NSEQ, HV], F32) for _ in range(2)]
            R3b = Arena(AR.ap[:, r3_off:r3_off + r3_words], r3_words)
            t_SS = [Tok() for _ in range(2)]
            SSb = AR.alloc([128, NSEQ, HV], BF16)
            t_SSb = Tok()
            SSo = AR.alloc([128, NSEQ, HV], F32)
            t_SSo = Tok()
            KM = [AR.alloc([64, 16, 128], BF16)]
            QM = [AR.alloc([128, 16, 64], BF16)]
            t_KM = [Tok()]
            t_QM = [Tok()]
            msk = AR.alloc([128, 128], BF16)
            cms = AR.alloc([128, 16, 64], BF16)
            rms = AR.alloc([128, 16], F32)
            rsts = AR.alloc([128, 64], BF16)
            gqs = AR.alloc([128, 4, 64], F32)
            gks = AR.alloc([128, 4, 64], F32)
            elrs = AR.alloc([128, 4], F32)
            t_sc = Tok("sconst")
            for dst, nm in ((msk, "masks"), (cms, "cms"), (rms, "rms"), (rsts, "rst_s"), (gqs, "gqs"), (gks, "gks"),
                            (elrs, "elrs"), (COS, "cos_s"), (SIN, "sin_s")):
                LD(lambda e, dst=dst, nm=nm: e.dma_start(out=dst, in_=cst[nm]), [], [t_sc if nm not in ("cos_s", "sin_s") else t_rope])
            MASKS = {16: msk}
            t_MASKS = {16: t_sc}
            CM, RM = cms, rms
            t_CMRM = [t_sc]
            RST = {1: rsts, 2: rsts}
            t_RST = {1: t_sc, 2: t_sc}
            GQ, GK, ELR = gqs, gks, elrs
            t_G = [t_sc]
            convT = AR.alloc([128, 88, 32], F32)
            t_convT = Tok("convT")
            outc = [R3b.alloc([32, 2816], F32) for _ in range(2)]
            t_outc = [Tok("outc0"), Tok("outc1")]
            cstg = R3b.alloc([32, 2048], F32)
            t_cstg = Tok("cstg")
            U6 = [AR.alloc([128, NSEQ, 6], F32) for _ in range(2)]
            t_U6 = [Tok() for _ in range(2)]
            while len(WB) < 12 and AR.off + 2056 <= AR.words:
                WB.append(AR.alloc([128, 4096], BF16))
                t_WB.append(Tok(f"wbx{len(WB)}"))

        def load_tokens(xsrc):
            for (t0, rows) in tiles:
                LD(lambda e, t0=t0, rows=rows: e.dma_start(out=stg[0:rows, :], in_=xsrc[t0:t0 + rows, :]), [], [t_stgi])
                for g in range(4):
                    pb, pt = main_rot.get()
                    for cc in range(4):
                        c = g * 4 + cc
                        T(lambda e, pb=pb, cc=cc, c=c, rows=rows: e.transpose(
                            pb[:, cc * rows:(cc + 1) * rows], stg[0:rows, c * 128:(c + 1) * 128], K["ident_f"][0:rows, 0:rows]),
                          [t_stgi, Kt["ident_f"]], [pt])
                    act(xT[:, g * 4:g * 4 + 4, t0:t0 + rows], pb[:, 0:4 * rows].rearrange("p (a b) -> p a b", a=4),
                        AF.Copy, [pt], t_xT[g * 4:g * 4 + 4])

        def store_x(sp, c):
            ST_(lambda e: e.dma_start(out=xsw[sp][:, c, :], in_=xT[:, c, :]), [t_xT[c]], [t_xsw[sp][c]])

        def load_x(sp):
            for c in range(16):
                LD(lambda e, c=c: e.dma_start(out=xT[:, c, :], in_=xsw[sp][:, c, :]), [t_xsw[sp][c]], [t_xT[c]])

        def load_rope(sp):
            LD(lambda e: e.dma_start(out=COS, in_=cst["cos_p"][:, sp * TP:(sp + 1) * TP]), [], [t_rope])
            LD(lambda e: e.dma_start(out=SIN, in_=cst["sin_p"][:, sp * TP:(sp + 1) * TP]), [], [t_rope])

        if prompt:
            for sp in range(NSP):
                load_tokens(xp[sp * TP:(sp + 1) * TP])
                for c in range(16):
                    store_x(sp, c)
        else:
            load_tokens(xs)
        P.fence()

        def rms_rstd():
            pss, pst_ = small_rot.get()
            for c in range(16):
                k = c % 2
                act(tmp[k][:, 0:Tn], xT[:, c, :], AF.Square, [t_xT[c]], [t_tmp[k]])
                T(lambda e, c=c, k=k: e.matmul(pss[:, 0:Tn], lhsT=K["ones_f"], rhs=tmp[k][:, 0:Tn], start=(c == 0), stop=(c == 15)),
                  [t_tmp[k], Kt["ones_f"]], [pst_])
            ts(rstd, pss[:, 0:Tn], 1.0 / D, EPS, ALU.mult, ALU.add, [pst_], [t_rstd])
            act(rstd, rstd, AF.Sqrt, [t_rstd], [t_rstd])
            V(lambda e: e.reciprocal(out=rstd, in_=rstd), [t_rstd], [t_rstd])

        def modulated_norm(l, which):
            rms_rstd()
            if prompt:
                ka, kb = (0, 1) if which == 1 else (3, 4)
                for c in range(16):
                    k = 2 + c % 2
                    stt(tmp[k][:, 0:Tn], xT[:, c, :], MODP[l][:, ka, c:c + 1], rstd, ALU.mult, ALU.mult,
                        [t_xT[c], t_MODP[l], t_rstd], [t_tmp[k]])
                    act(hT[:, c, :], tmp[k][:, 0:Tn], AF.Identity, [t_tmp[k], t_MODP[l]], [t_hT[c]],
                        bias=MODP[l][:, kb, c:c + 1], scale=1.0)
            else:
                ca, cb = (16, 0) if which == 1 else (64, 48)
                t1 = tmp[2][:, 0:1024].rearrange("p (c t) -> p c t", c=16)
                tt(t1, xT, rstd.unsqueeze(1).broadcast_to([128, 16, Tn]), ALU.mult, t_xT + [t_rstd], [t_tmp[2]])
                t1v = tmp[2][:, 0:1024].rearrange("p (c s t) -> p c s t", c=16, s=NSEQ)
                t2v = tmp[3][:, 0:1024].rearrange("p (c s t) -> p c s t", c=16, s=NSEQ)
                Av = MOD[l][:, ca:ca + 16, 0:NSEQ].unsqueeze(3).broadcast_to([128, 16, NSEQ, TS])
                Bv = MOD[l][:, cb:cb + 16, 0:NSEQ].unsqueeze(3).broadcast_to([128, 16, NSEQ, TS])
                tt(t2v, t1v, Av, ALU.mult, [t_tmp[2], t_MOD[l]], [t_tmp[3]])
                tt(hT.rearrange("p c (s t) -> p c s t", s=NSEQ), t2v, Bv, ALU.add, [t_tmp[3], t_MOD[l]], t_hT)

        def residual_add(l, which, f, pm, pmt):
            if prompt:
                kg = 2 if which == 1 else 5
                stt(xT[:, f, :], pm[:, 0:Tn], MODP[l][:, kg, f:f + 1], xT[:, f, :], ALU.mult, ALU.add,
                    [pmt, t_MODP[l], t_xT[f]], [t_xT[f]])
                store_x(cx["sp"], f)
            else:
                cg = 32 if which == 1 else 80
                Gv = MOD[l][:, cg + f, 0:NSEQ].unsqueeze(2).broadcast_to([128, NSEQ, TS])
                tv = tmp[4][:, 0:Tn].rearrange("p (s t) -> p s t", s=NSEQ)
                tt(tv, pm[:, 0:Tn].rearrange("p (s t) -> p s t", s=NSEQ), Gv, ALU.mult, [pmt, t_MOD[l]], [t_tmp[4]])
                tt(xT[:, f, :], xT[:, f, :], tmp[4][:, 0:Tn], ALU.add, [t_tmp[4], t_xT[f]], [t_xT[f]])

        nbt = {0: 1, 1: 1, 2: 4} if prompt else {0: 16, 1: 16, 2: 16}
        bsz = {b: tiles[0][1] // nbt[b] for b in range(3)}

        def mixer_branch(l, b):
            nb = nbt[b]
            bs = bsz[b]
            ph1 = cx["ph1"]
            sp = cx["sp"]
            sv = prompt and ph1
            ld = prompt and not ph1
            if b == 1 and not ld:
                wg_, wgt = load_w(("glr", l))
                pb, pt = small_rot.get()
                mm(pb[0:16, 0:Tn], pt, [(wg_[:, c, 0:16], hT[:, c, :]) for c in range(16)], [wgt] + t_hT)
                act(glrT, pb[0:16, 0:Tn], AF.Copy, [pt], [t_glrT])
            if ld:
                ST_(lambda e: e.dma_start(out=kT, in_=kTs[sp][b]), [t_kTs[sp][b]], t_kT)
                if b != 0:
                    ST_(lambda e: e.dma_start(out=EL[:].rearrange("p a b -> p (a b)"), in_=els[sp][b]), [t_els[sp][b]], t_EL)
            for hp in range(2):
                if not ld:
                    kb_, kbt = load_w(("k", l, b, hp))
                if not ph1:
                    qb_, qbt = load_w(("q", l, b, hp))
                for hh in range(2):
                    h = hp * 2 + hh
                    if not ld:
                        pk, pkt = main_rot.get()
                        mm(pk[:, 0:Tn], pkt, [(kb_[:, c, hh * 128:(hh + 1) * 128], hT[:, c, :]) for c in range(16)], [kbt] + t_hT)
                    X = [tmp[i][:, 0:Tn] for i in range(NTMP)]
                    if b == 0:
                        def rope_path(pz, pzt, G, dstT, dtok):
                            raw = tmp[0].bitcast(BF16)[:, 0:Tn]
                            act(raw, pz[:, 0:Tn], AF.Copy, [pzt], [t_tmp[0]])
                            psw, pswt = small_rot.get()
                            T(lambda e: e.matmul(psw[:, 0:Tn], lhsT=K["pswap"], rhs=raw, start=True, stop=True),
                              [t_tmp[0], Kt["pswap"]], [pswt])
                            tt(X[1], raw, COS, ALU.mult, [t_tmp[0], t_rope], [t_tmp[1]])
                            tt(X[2], psw[:, 0:Tn], SIN, ALU.mult, [pswt, t_rope], [t_tmp[2]])
                            tt(X[3], X[1], X[2], ALU.add, [t_tmp[1], t_tmp[2]], [t_tmp[3]])
                            if prompt:
                                Gv = G[:, h, :].unsqueeze(1).broadcast_to([128, nt, 128])
                                tt(dstT[:, h, :].rearrange("p (i t) -> p i t", i=nt), X[3].rearrange("p (i t) -> p i t", i=nt),
                                   Gv, ALU.mult, [t_tmp[3]] + t_G, [dtok[h]])
                            else:
                                tt(dstT[:, h, :], X[3], G[:, h, :], ALU.mult, [t_tmp[3]] + t_G, [dtok[h]])
                        if not ld:
                            rope_path(pk, pkt, GK, kT, t_kT)
                        if not ph1:
                            pq, pqt = main_rot.get()
                            mm(pq[:, 0:Tn], pqt, [(qb_[:, c, hh * 128:(hh + 1) * 128], hT[:, c, :]) for c in range(16)], [qbt] + t_hT)
                            rope_path(pq, pqt, GQ, qT, t_qT)
                    else:
                        if not ld:
                            if b == 1:
                                px, pxt = small_rot.get()
                                T(lambda e, h=h, px=px: e.matmul(px[:, 0:Tn], lhsT=GLW[l][:, h * 128:(h + 1) * 128], rhs=glrT, start=True, stop=True),
                                  [t_GLW[l], t_glrT], [pxt])
                                act(X[0], px[:, 0:Tn], AF.Exp, [pxt, t_LBV], [t_tmp[0]], bias=NEGB[:, l, h:h + 1], scale=-1.0)
                                act(X[1], X[0], AF.Ln, [t_tmp[0]], [t_tmp[1]], bias=1.0, scale=1.0)
                                sc_q, sc_k = -1.0 / 16.0, 1.0 / 16.0
                            else:
                                act(X[0], pk[:, 0:Tn], AF.Sigmoid, [pkt], [t_tmp[0]])
                                act(X[1], X[0], AF.Ln, [t_tmp[0], t_LBV], [t_tmp[1]],
                                    bias=LBV[:, l, 0, h:h + 1], scale=LBV[:, l, 1, h:h + 1])
                                sc_q, sc_k = 1.0, -1.0
                            V(lambda e, b=b: e.tensor_tensor_scan(out=X[2], data0=RST[b][:, 0:Tn], data1=X[1], initial=0.0,
                                                                  op0=ALU.mult, op1=ALU.add), [t_tmp[1], t_RST[b]], [t_tmp[2]])
                            act(X[3], X[2], AF.Exp, [t_tmp[2]], [t_tmp[3]], scale=sc_q)
                            act(X[4], X[2], AF.Exp, [t_tmp[2]], [t_tmp[4]], scale=sc_k)
                            nblk = Tn // bs
                            A(lambda e, h=h, bs=bs, nblk=nblk: e.activation(out=EL[:, h, 0:nblk], in_=tmp[3][:, bs - 1:Tn:bs], func=AF.Copy),
                              [t_tmp[3]], [t_EL[h]])
                            if b == 1:
                                tt(kT[:, h, :], pk[:, 0:Tn], X[4], ALU.mult, [pkt, t_tmp[4]], [t_kT[h]])
                            else:
                                ts(X[5], X[0], LBV[:, l, 2, h:h + 1], LBV[:, l, 1, h:h + 1], ALU.mult, ALU.add, [t_tmp[0], t_LBV], [t_tmp[5]])
                                tt(kT[:, h, :], X[5], X[4], ALU.mult, [t_tmp[5], t_tmp[4]], [t_kT[h]])
                            if sv:
                                ST_(lambda e, h=h: e.dma_start(out=eqs[sp][b][:, h, :], in_=tmp[3][:, 0:Tn]), [t_tmp[3]], [t_eqs[sp][b][h]])
                        else:
                            ST_(lambda e, h=h: e.dma_start(out=tmp[3][:, 0:Tn], in_=eqs[sp][b][:, h, :]), [t_eqs[sp][b][h]], [t_tmp[3]])
                        if ph1:
                            continue
                        pq, pqt = main_rot.get()
                        mm(pq[:, 0:Tn], pqt, [(qb_[:, c, hh * 128:(hh + 1) * 128], hT[:, c, :]) for c in range(16)], [qbt] + t_hT)
                        if b == 1:
                            stt(qT[:, h, :], pq[:, 0:Tn], HK ** -0.5, X[3], ALU.mult, ALU.mult, [pqt, t_tmp[3]], [t_qT[h]])
                        else:
                            act(X[0], pq[:, 0:Tn], AF.Sigmoid, [pqt], [t_tmp[0]])
                            tt(X[1], pq[:, 0:Tn], X[0], ALU.mult, [pqt, t_tmp[0]], [t_tmp[1]])
                            tt(qT[:, h, :], X[1], X[3], ALU.mult, [t_tmp[1], t_tmp[3]], [t_qT[h]])
            if sv:
                ST_(lambda e: e.dma_start(out=kTs[sp][b], in_=kT), t_kT, [t_kTs[sp][b]])
                if b != 0:
                    ST_(lambda e: e.dma_start(out=els[sp][b], in_=EL[:].rearrange("p a b -> p (a b)")), t_EL, [t_els[sp][b]])
            if ld:
                ST_(lambda e: e.dma_start(out=v_tok, in_=vTs[sp][b]), [t_vTs[sp][b]], [t for row in t_v for t in row])
            for h in range(4 if not ld else 0):
                wv_, wvt = load_w(("v", l, b, h))
                for i, (t0, rows) in enumerate(tiles):
                    pv, pvt = main_rot.get()
                    mm(pv[0:rows, 0:256], pvt, [(hT[:, c, t0:t0 + rows], wv_[:, c, :]) for c in range(16)], [wvt] + t_hT)
                    act(v_tok[0:rows, i, h * 256:(h + 1) * 256], pv[0:rows, 0:256], AF.Copy, [pvt], [t_v[i][h]])
            if sv:
                ST_(lambda e: e.dma_start(out=vTs[sp][b], in_=v_tok), [t for row in t_v for t in row], [t_vTs[sp][b]])
            for h in range(0 if not ph1 else 4, 4):
                wg_, wgt = load_w(("g", l, b, h))
                for i, (t0, rows) in enumerate(tiles):
                    pg, pgt = main_rot.get()
                    mm(pg[0:rows, 0:256], pgt, [(hT[:, c, t0:t0 + rows], wg_[:, c, :]) for c in range(16)], [wgt] + t_hT)
                    k = h % 2
                    act(sq256[k][0:rows, :], pg[0:rows, 0:256], AF.Sigmoid, [pgt], [t_sq256[k]])
                    tt(sq256[k][0:rows, :], pg[0:rows, 0:256], sq256[k][0:rows, :], ALU.mult, [pgt, t_sq256[k]], [t_sq256[k]])
                    tt(sg_tok[0:rows, i, h * 256:(h + 1) * 256], sq256[k][0:rows, :], HNW[l][0:rows, b * HV:(b + 1) * HV], ALU.mult,
                       [t_sq256[k], t_HNW[l]], [t_sg[i][h]])
            def el_col(h, blk):
                if b == 0:
                    return ELR[:, h:h + 1], t_G
                return EL[:, h, blk:blk + 1], [t_EL[h]]

            if prompt:
                if ph1:
                    if sp == 0:
                        V(lambda e: e.memset(SF, 0.0), [], t_SF)
                    else:
                        LD(lambda e: e.dma_start(out=SF, in_=ph1st[b].rearrange("h k v -> k h v")), [t_ph1st[b]], t_SF)
                elif sp == 0:
                    LD(lambda e: e.dma_start(out=SF, in_=exd[l][b * 512:(b + 1) * 512, :].rearrange("(h k) v -> k h v", k=HK)), [t_exd[l]], t_SF)
                    ts(SF, SF, K["isb"][:, 0:1], None, ALU.mult, ALU.bypass, t_SF + [Kt["isb"]], t_SF)
                else:
                    LD(lambda e: e.dma_start(out=SF, in_=pst[b][l].rearrange("h k v -> k h v")), [t_pst[b][l]], t_SF)
                if not ph1:
                    act(Sbf, SF, AF.Copy, t_SF, t_Sbf)
            hgroups = [[0, 1, 2, 3]] if prompt else [[0], [1], [2], [3]]
            ss_i = [0]
            for i, (t0, C) in enumerate(tiles):
                for hg in hgroups:
                    ctx = {}
                    H = list(enumerate(hg))
                    if not prompt:
                        for hi, h in H:
                            k = ss_i[0] % 2
                            ss_i[0] += 1
                            LD(lambda e, h=h, k=k: e.dma_start(out=SS[k], in_=st_in[b][l, :, h].rearrange("s k v -> k s v")), [], [t_SS[k]])
                            V(lambda e, k=k: e.tensor_copy(out=SSb[:, 0:8, :], in_=SS[k][:, 0:8, :]), [t_SS[k]], [t_SSb])
                            act(SSb[:, 8:16, :], SS[k][:, 8:16, :], AF.Copy, [t_SS[k]], [t_SSb])
                            ctx[h] = k
                    qts = {h: qT[:, h, t0:t0 + C] for _, h in H}
                    kts = {h: kT[:, h, t0:t0 + C] for _, h in H}
                    vts = {h: v_tok[0:C, i, h * 256:(h + 1) * 256] for _, h in H}
                    pscb, psct = banks[6][:], bank_tok[6]
                    if not ph1:
                        for hi, h in H:
                            T(lambda e, hi=hi, C=C, pscb=pscb, kt_=kts[h], qt_=qts[h]: e.matmul(pscb[0:C, hi * 128:hi * 128 + C], lhsT=kt_, rhs=qt_, start=True, stop=True),
                              [t_kT[h], t_qT[h]], [psct])
                        for hi, h in H:
                            tt(PT_[h][0:C, 0:C], pscb[0:C, hi * 128:hi * 128 + C], MASKS[nb][0:C, 0:C], ALU.mult, [psct, t_MASKS[nb]], [t_PT[h]])
                    for hi, h in H:
                        T(lambda e, hi=hi, C=C, kt_=kts[h]: e.transpose(bank_bf[0:C, hi * 128:(hi + 1) * 128], kt_, K["ident_b"]),
                          [t_kT[h], Kt["ident_b"]], [bank_tok[7]])
                    for hi, h in H:
                        act(ktok[h][0:C, :], bank_bf[0:C, hi * 128:(hi + 1) * 128], AF.Copy, [bank_tok[7]], [t_ktok[h]])
                    if nb > 1:
                        for hi, h in H:
                            km, tkm = KM[h % len(KM)], t_KM[h % len(KM)]
                            tt(km[0:C, 0:nb, :], ktok[h][0:C, :].unsqueeze(1).broadcast_to([C, nb, 128]),
                               RM[0:C, 0:nb].unsqueeze(2).broadcast_to([C, nb, 128]), ALU.mult, [t_ktok[h]] + t_CMRM, [tkm])
                        if not ph1:
                            for hi, h in H:
                                qm, tqm = QM[h % len(QM)], t_QM[h % len(QM)]
                                tt(qm[:, 0:nb, 0:C], qts[h].unsqueeze(1).broadcast_to([128, nb, C]), CM[:, 0:nb, 0:C], ALU.mult,
                                   [t_qT[h]] + t_CMRM, [tqm])
                    po = {h: (banks[hi][:], bank_tok[hi]) for hi, h in H}
                    pkvs = {h: (banks[4 + hi // 2][:, (hi % 2) * 256:(hi % 2) * 256 + 256], bank_tok[4 + hi // 2]) for hi, h in H}
                    for j in range(nb):
                        blk = i * nb + j
                        if not ph1:
                            for hi, h in H:
                                pob, pot = po[h]
                                if j == 0:
                                    T(lambda e, pob=pob, h=h, C=C, vt_=vts[h]: e.matmul(pob[0:C, 0:256], lhsT=PT_[h][0:C, 0:C], rhs=vt_, start=True, stop=False),
                                      [t_PT[h], t_v[i][h]], [pot])
                                lhs = qts[h] if nb == 1 else QM[h % len(QM)][:, j, 0:C]
                                lt = [t_qT[h]] if nb == 1 else [t_QM[h % len(QM)]]
                                if prompt:
                                    srhs, srt = Sbf[:, h, :], [t_Sbf[h]]
                                else:
                                    srhs, srt = SSb[:, j, :], [t_SSb]
                                T(lambda e, pob=pob, lhs=lhs, srhs=srhs, C=C, j=j: e.matmul(pob[0:C, 0:256], lhsT=lhs, rhs=srhs, start=False, stop=(j == nb - 1)),
                                  lt + srt, [pot])
                        for hi, h in H:
                            pkv, pkvt = pkvs[h]
                            klhs = ktok[h][0:C, :] if nb == 1 else KM[h % len(KM)][0:C, j, :]
                            klt = [t_ktok[h]] if nb == 1 else [t_KM[h % len(KM)]]
                            T(lambda e, pkv=pkv, klhs=klhs, vt_=vts[h]: e.matmul(pkv, lhsT=klhs, rhs=vt_, start=True, stop=True),
                              klt + [t_v[i][h]], [pkvt])
                        for hi, h in H:
                            pkv, pkvt = pkvs[h]
                            elc, elt = el_col(h, blk)
                            act(kve[hi], pkv, AF.Identity, [pkvt] + elt, [t_kve[hi]], scale=elc)
                        for hi, h in H:
                            elc, elt = el_col(h, blk)
                            if prompt:
                                stt(SF[:, h, :], SF[:, h, :], elc, kve[hi], ALU.mult, ALU.add, [t_SF[h], t_kve[hi]] + elt, [t_SF[h]])
                            else:
                                stt(SSo[:, j, :], SS[ctx[h]][:, j, :], elc, kve[hi], ALU.mult, ALU.add, [t_SS[ctx[h]], t_kve[hi]] + elt, [t_SSo])
                        if prompt and not ph1:
                            for hi, h in H:
                                act(Sbf[:, h, :], SF[:, h, :], AF.Copy, [t_SF[h]], [t_Sbf[h]])
                    if not ph1:
                        h0 = hg[0]
                        nh_ = len(hg)
                        for hi, h in H:
                            pob, pot = po[h]
                            act(sq256[hi][0:C, :], pob[0:C, 0:256], AF.Square, [pot], [t_sq256[hi]])
                        for hi, h in H:
                            V(lambda e, hi=hi, h=h, C=C: e.tensor_reduce(out=ssv[0:C, h:h + 1], in_=sq256[hi][0:C, :], axis=AX.X, op=ALU.add),
                              [t_sq256[hi]], [t_ssv])
                        sv_ = ssv[0:C, h0:h0 + nh_]
                        ts(sv_, sv_, 1.0 / HV, EPS, ALU.mult, ALU.add, [t_ssv], [t_ssv])
                        act(sv_, sv_, AF.Sqrt, [t_ssv], [t_ssv])
                        V(lambda e, sv_=sv_: e.reciprocal(out=sv_, in_=sv_), [t_ssv], [t_ssv])
                        for hi, h in H:
                            pob, pot = po[h]
                            stt(u_tok[0:C, h * 256:(h + 1) * 256], pob[0:C, 0:256], ssv[0:C, h:h + 1], sg_tok[0:C, i, h * 256:(h + 1) * 256],
                                ALU.mult, ALU.mult, [pot, t_ssv, t_sg[i][h]], [t_utok])
                    if not prompt:
                        for hi, h in H:
                            ST_(lambda e, h=h: e.dma_start(out=sst[b][l, :, h].rearrange("s k v -> k s v"), in_=SSo), [t_SSo], [t_sst])
                            outs_final.append(P.streams["act"][-1])
                for c in range(8 if not ph1 else 0):
                    T(lambda e, c=c, C=C: e.transpose(bank_bf[:, c * C:(c + 1) * C], u_tok[0:C, c * 128:(c + 1) * 128], K["ident_b"][0:C, 0:C]),
                      [t_utok, Kt["ident_b"]], [bank_tok[7]])
                if not ph1:
                    act(uT[:, b, :, t0:t0 + C], bank_bf[:, 0:8 * C].rearrange("p (c t) -> p c t", c=8), AF.Copy, [bank_tok[7]], [t_uT[b][i]])
            if prompt:
                if ph1 and sp == 0:
                    ST_(lambda e: e.dma_start(out=ph1st[b].rearrange("h k v -> k h v"), in_=SF), t_SF, [t_ph1st[b]])
                elif ph1:
                    ST_(lambda e: e.dma_start(out=exs[l][b * 512:(b + 1) * 512, :].rearrange("(h k) v -> k h v", k=HK), in_=SF), t_SF, [t_exs[l][b]])
                else:
                    ST_(lambda e: e.dma_start(out=pst[b][l].rearrange("h k v -> k h v"), in_=SF), t_SF, [t_pst[b][l]])
                    if sp == NSP - 1:
                        outs_final.append(P.streams["act"][-1])

        def merge_and_out(l):
            P.fence()
            for fp in range(8):
                for b in range(3):
                    wbr, wbrt = load_w(("br", l, b, fp))
                    wmg, wmgt = load_w(("mg", l, b, fp))
                    for ft in range(2):
                        f = fp * 2 + ft
                        po_, pot = main_rot.get()
                        mm(po_[:, 0:Tn], pot, [(wbr[:, c, ft * 128:(ft + 1) * 128], uT[:, b, c, :]) for c in range(8)], [wbrt] + t_uT[b])
                        pg, pgt = main_rot.get()
                        mm(pg[:, 0:Tn], pgt, [(wmg[:, c, ft * 128:(ft + 1) * 128], hT[:, c, :]) for c in range(16)], [wmgt] + t_hT)
                        k = ft
                        act(tmp[k][:, 0:Tn], pg[:, 0:Tn], AF.Sigmoid, [pgt], [t_tmp[k]])
                        if b == 0:
                            tt(acc[:, ft, :], po_[:, 0:Tn], tmp[k][:, 0:Tn], ALU.mult, [pot, t_tmp[k]], [t_acc])
                        elif b == 1:
                            tt(tmp[k][:, 0:Tn], po_[:, 0:Tn], tmp[k][:, 0:Tn], ALU.mult, [pot, t_tmp[k]], [t_tmp[k]])
                            tt(acc[:, ft, :], acc[:, ft, :], tmp[k][:, 0:Tn], ALU.add, [t_tmp[k], t_acc], [t_acc])
                        else:
                            tt(tmp[k][:, 0:Tn], po_[:, 0:Tn], tmp[k][:, 0:Tn], ALU.mult, [pot, t_tmp[k]], [t_tmp[k]])
                            tt(mT[:, f, :], acc[:, ft, :], tmp[k][:, 0:Tn], ALU.add, [t_tmp[k], t_acc], [t_mT])
            for fp in range(8):
                wo, wot = load_w(("out", l, fp))
                for ft in range(2):
                    f = fp * 2 + ft
                    pm, pmt = main_rot.get()
                    mm(pm[:, 0:Tn], pmt, [(wo[:, c, ft * 128:(ft + 1) * 128], mT[:, c, :]) for c in range(16)], [wot, t_mT])
                    residual_add(l, 1, f, pm, pmt)
            if prompt and cx["sp"] == NSP - 1:
                V(lambda e: e.tensor_copy(out=xtl[:].rearrange("p (c t) -> p c t", t=2), in_=xT[:, :, Tn - 2:Tn]), t_xT, [t_xtl])
                ST_(lambda e: e.dma_start(out=exs2[l], in_=xtl), [t_xtl], [t_exs2[l]])
            P.fence()

        def ffn(l):
            modulated_norm(l, 2)
            P.fence()
            if prompt and cx["sp"] == 0:
                LD(lambda e: e.dma_start(out=xtin, in_=exd2[l][0:128, :]), [t_exd2[l]], [t_xtin])
                act(sqt, xtin, AF.Square, [t_xtin], [t_sqt])
                pss2, pss2t = small_rot.get()
                for c in range(16):
                    T(lambda e, c=c, pss2=pss2: e.matmul(pss2[:, 0:2], lhsT=K["ones_f"], rhs=sqt[:, 2 * c:2 * c + 2], start=(c == 0), stop=(c == 15)),
                      [t_sqt, Kt["ones_f"]], [pss2t])
                ts(rs2, pss2[:, 0:2], 1.0 / D, EPS, ALU.mult, ALU.add, [pss2t], [t_rs2])
                act(rs2, rs2, AF.Sqrt, [t_rs2], [t_rs2])
                V(lambda e: e.reciprocal(out=rs2, in_=rs2), [t_rs2], [t_rs2])
                for c in range(16):
                    stt(sqt[:, 2 * c:2 * c + 2], xtin[:, 2 * c:2 * c + 2], MODP[l][:, 3, c:c + 1], rs2, ALU.mult, ALU.mult,
                        [t_xtin, t_MODP[l], t_rs2, t_sqt], [t_sqt])
                    act(h2t[:, c, :], sqt[:, 2 * c:2 * c + 2], AF.Identity, [t_sqt, t_MODP[l]], [t_h2t], bias=MODP[l][:, 4, c:c + 1], scale=1.0)
            if not prompt:
                for g in range(6):
                    ncol = min(2048, 2 * DFF - g * 2048)
                    LD(lambda e, g=g, ncol=ncol: e.dma_start(out=cstg[:, 0:ncol], in_=sconv_in[l, :, g * 2048:g * 2048 + ncol]), [], [t_cstg])
                    for q4 in range(0, ncol // 128, 4):
                        pb, pt = small_rot.get()
                        for cc in range(4):
                            T(lambda e, pb=pb, cc=cc, q4=q4: e.transpose(pb[:, cc * 32:(cc + 1) * 32], cstg[:, (q4 + cc) * 128:(q4 + cc + 1) * 128],
                                                                          K["ident_f"][0:32, 0:32]), [t_cstg, Kt["ident_f"]], [pt])
                        act(convT[:, g * 16 + q4:g * 16 + q4 + 4, :], pb[:, 0:128].rearrange("p (a b) -> p a b", a=4), AF.Copy, [pt], [t_convT])
            for j in range(NFT):
                wu, wut = load_w(("up", l, j))
                wux = list(wb_extra[0])
                res = []
                for half in range(2):
                    tile_idx = half * NFT + j
                    pu, put = main_rot.get()
                    mm(pu[:, 0:Tn], put, [(wu[:, c, half * 128:(half + 1) * 128], hT[:, c, :]) for c in range(16)], [wut] + wux + t_hT)
                    o3 = 3 * half
                    w0 = CW[l][:, 0, tile_idx:tile_idx + 1]
                    w1 = CW[l][:, 1, tile_idx:tile_idx + 1]
                    w2 = CW[l][:, 2, tile_idx:tile_idx + 1]
                    cb = CW[l][:, 3, tile_idx:tile_idx + 1]
                    if prompt:
                        U = tmp[o3]
                        tc_ = t_CARRY[l][tile_idx]
                        if cx["sp"] == 0:
                            pu2, pu2t = small_rot.get()
                            mm(pu2[:, 0:2], pu2t, [(wu[:, c, half * 128:(half + 1) * 128], h2t[:, c, :]) for c in range(16)], [wut, t_h2t] + wux)
                            ts(U[:, 0:2], pu2[:, 0:2], K["isb"][:, 0:1], None, ALU.mult, ALU.bypass, [pu2t, Kt["isb"]], [t_tmp[o3]])
                        else:
                            V(lambda e, U=U, tile_idx=tile_idx: e.tensor_copy(out=U[:, 0:2], in_=CARRY[l][:, tile_idx, :]), [tc_], [t_tmp[o3]])
                        act(U[:, 2:2 + Tn], pu[:, 0:Tn], AF.Copy, [put], [t_tmp[o3]])
                        V(lambda e, U=U, tile_idx=tile_idx: e.tensor_copy(out=CARRY[l][:, tile_idx, :], in_=U[:, Tn:Tn + 2]), [t_tmp[o3]], [tc_])
                        c1, c2 = tmp[o3 + 1][:, 0:Tn], tmp[o3 + 2][:, 0:Tn]
                        ts(c1, U[:, 0:Tn], w0, cb, ALU.mult, ALU.add, [t_tmp[o3], t_CW[l]], [t_tmp[o3 + 1]])
                        stt(c2, U[:, 1:Tn + 1], w1, c1, ALU.mult, ALU.add, [t_tmp[o3], t_tmp[o3 + 1], t_CW[l]], [t_tmp[o3 + 2]])
                        stt(c1, U[:, 2:Tn + 2], w2, c2, ALU.mult, ALU.add, [t_tmp[o3], t_tmp[o3 + 2], t_CW[l]], [t_tmp[o3 + 1]])
                        res.append((c1, t_tmp[o3 + 1], tmp[o3 + 2][:, 0:Tn], t_tmp[o3 + 2]))
                    else:
                        U = U6[half]
                        tu = t_U6[half]
                        V(lambda e, U=U, tile_idx=tile_idx: e.tensor_copy(out=U[:, :, 0:2], in_=convT[:, tile_idx, :].rearrange("p (s r) -> p s r", r=2)),
                          [t_convT], [tu])
                        act(U[:, :, 2:6], pu[:, 0:Tn].rearrange("p (s t) -> p s t", t=TS), AF.Copy, [put], [tu])
                        pb, pt = small_rot.get()
                        raw = tmp[o3][:, 0:32].rearrange("p (s r) -> p s r", r=2)
                        V(lambda e, U=U, raw=raw: e.tensor_copy(out=raw, in_=U[:, :, 4:6]), [tu], [t_tmp[o3]])
                        T(lambda e, pb=pb, o3=o3: e.transpose(pb[0:32, 0:128], tmp[o3][:, 0:32], K["ident_f"]), [t_tmp[o3], Kt["ident_f"]], [pt])
                        act(outc[half][:, (tile_idx % 22) * 128:(tile_idx % 22 + 1) * 128], pb[0:32, 0:128], AF.Copy, [pt], [t_outc[half]])
                        c1 = tmp[o3 + 1][:, 0:Tn].rearrange("p (s t) -> p s t", t=TS)
                        c2 = tmp[o3 + 2][:, 0:Tn].rearrange("p (s t) -> p s t", t=TS)
                        ts(c1, U[:, :, 0:4], w0, cb, ALU.mult, ALU.add, [tu, t_CW[l]], [t_tmp[o3 + 1]])
                        stt(c2, U[:, :, 1:5], w1, c1, ALU.mult, ALU.add, [tu, t_tmp[o3 + 1], t_CW[l]], [t_tmp[o3 + 2]])
                        stt(c1, U[:, :, 2:6], w2, c2, ALU.mult, ALU.add, [tu, t_tmp[o3 + 2], t_CW[l]], [t_tmp[o3 + 1]])
                        res.append((tmp[o3 + 1][:, 0:Tn], t_tmp[o3 + 1], tmp[o3 + 2][:, 0:Tn], t_tmp[o3 + 2]))
                (ca, tca, sa, tsa), (cbv, tcb, _, _) = res
                act(sa, ca, AF.Silu, [tca], [tsa])
                tt(actT[:, j, :], sa, cbv, ALU.mult, [tsa, tcb], [t_act[j]])
                if (not prompt) and j % 22 == 21:
                    for half in range(2):
                        grp = half * 2 + j // 22
                        ST_(lambda e, half=half, grp=grp: e.dma_start(out=sconv_o[l][:, grp * 2816:(grp + 1) * 2816], in_=outc[half]),
                            [t_outc[half]], [t_sconv_o])
                        outs_final.append(P.streams["act"][-1])
            for f in range(16):
                pf, pft = main_rot.get()
                for kh in range(2):
                    wd, wdt = load_w(("dn", l, f, kh))
                    for c in range(22):
                        T(lambda e, pf=pf, wd=wd, c=c, kh=kh: e.matmul(pf[:, 0:Tn], lhsT=wd[:, c, :], rhs=actT[:, kh * 22 + c, :],
                                                                     start=(kh == 0 and c == 0), stop=(kh == 1 and c == 21)),
                          [wdt, t_act[kh * 22 + c]], [pft])
                residual_add(l, 2, f, pf, pft)
            P.fence()

        t_mT = Tok("mT")
        t_act = [Tok(f"act{j}") for j in range(NFT)]

        def conv_out(l):
            for g in range(6):
                n = min(16, 88 - g * 16)
                for q4 in range(0, n, 4):
                    pb, pt = small_rot.get()
                    for cc in range(4):
                        ti = g * 16 + q4 + cc
                        T(lambda e, pb=pb, cc=cc, ti=ti, l=l: e.transpose(pb[0:2, cc * 128:(cc + 1) * 128], CARRY[l][:, ti, :], K["ident_f"]),
                          [t_CARRY[l][ti], Kt["ident_f"]], [pt])
                    act(cstage[:, q4 * 128:(q4 + 4) * 128], pb[0:2, 0:512], AF.Copy, [pt], [t_cstage])
                ST_(lambda e, g=g, n=n, l=l: e.dma_start(out=pconv[l][:, g * 2048:g * 2048 + n * 128], in_=cstage[:, 0:n * 128]), [t_cstage], [t_pconv])
                outs_final.append(P.streams["act"][-1])
            P.fence()

        def final_out(ydst):
            rms_rstd()
            for c in range(16):
                stt(yT[:, c, :], xT[:, c, :], PV[:, 64 + c:65 + c], rstd, ALU.mult, ALU.mult, [t_xT[c], t_PV, t_rstd], [t_yT])
            for (t0, rows) in tiles:
                for g in range(4):
                    pb, pt = main_rot.get()
                    for cc in range(4):
                        c = g * 4 + cc
                        T(lambda e, pb=pb, cc=cc, c=c, t0=t0, rows=rows: e.transpose(pb[0:rows, cc * 128:(cc + 1) * 128], yT[:, c, t0:t0 + rows], K["ident_f"]),
                          [t_yT, Kt["ident_f"]], [pt])
                    act(stg[0:rows, g * 512:(g + 1) * 512], pb[0:rows, 0:512], AF.Copy, [pt], [t_stgo])
                ST_(lambda e, t0=t0, rows=rows: e.dma_start(out=ydst[t0:t0 + rows, :], in_=stg[0:rows, :]), [t_stgo], [])
                outs_final.append(P.streams["act"][-1])
            P.fence()

        if not prompt:
            for l in range(DEPTH):
                modulated_norm(l, 1)
                for b in range(3):
                    mixer_branch(l, b)
                merge_and_out(l)
                ffn(l)
            final_out(ys)
        else:
            for l in range(DEPTH):
                cx["ph1"] = True
                for sp in range(NSP):
                    cx["sp"] = sp
                    load_x(sp)
                    load_rope(sp)
                    modulated_norm(l, 1)
                    for b in range(3):
                        mixer_branch(l, b)
                    P.fence()
                cx["ph1"] = False
                if no_cc:
                    LD(lambda e, l=l: e.dma_start(out=exd[l][0:1536, :], in_=exs[l]), t_exs[l], [t_exd[l]])
                else:
                    P.add("pool", lambda e, l=l: e.collective_compute("AllGather", ALU.bypass, replica_groups=pairs,
                                                                      ins=[exs[l].opt()], outs=[exd[l].opt()]), t_exs[l], [t_exd[l]])
                emit_casts(("ffn", l))
                for sp in range(NSP):
                    cx["sp"] = sp
                    load_x(sp)
                    load_rope(sp)
                    modulated_norm(l, 1)
                    for b in range(3):
                        mixer_branch(l, b)
                    merge_and_out(l)
                if no_cc:
                    LD(lambda e, l=l: e.dma_start(out=exd2[l][0:128, :], in_=exs2[l]), [t_exs2[l]], [t_exd2[l]])
                else:
                    P.add("pool", lambda e, l=l: e.collective_compute("AllGather", ALU.bypass, replica_groups=pairs,
                                                                      ins=[exs2[l].opt()], outs=[exd2[l].opt()]), [t_exs2[l]], [t_exd2[l]])
                if l + 1 < DEPTH:
                    emit_casts(("mix", l + 1))
                for sp in range(NSP):
                    cx["sp"] = sp
                    load_x(sp)
                    ffn(l)
                conv_out(l)
            for sp in range(NSP):
                load_x(sp)
                final_out(yp[sp * TP:(sp + 1) * TP])
        AR.off = m0

    t_pst = [[Tok() for _ in range(DEPTH)] for _ in range(3)]
    t_sst = Tok()
    t_sconv_o = Tok()
    t_pconv = Tok()
    t_stgo = Tok("stgo")
    outs_final = []

    setup()
    MOD = [AR.alloc([128, 96, 17], F32) for _ in range(DEPTH)]
    t_MOD = [Tok("MOD") for _ in range(DEPTH)]
    compute_mod(MOD, t_MOD)
    for l in range(DEPTH):
        LD(lambda e, l=l, M=MOD: e.dma_start(out=modsc[l], in_=M[l][:].rearrange("p a b -> p (a b)")), [t_MOD[l]], [t_modsc[l]])
    emit_casts(("mix", 0))
    P.fence()
    AR.off = persist_mark
    run_pass("p")
    AR.off = persist_mark
    MOD = [AR.alloc([128, 96, 17], F32) for _ in range(DEPTH)]
    t_MOD = [Tok("MOD") for _ in range(DEPTH)]
    for l in range(DEPTH):
        LD(lambda e, l=l, M=MOD: e.dma_start(out=M[l][:].rearrange("p a b -> p (a b)"), in_=modsc[l]), [t_modsc[l]], [t_MOD[l]])
    run_pass("s", MOD, t_MOD)
    P.final_waits = outs_final
    P.emit(st)
    st.close()
    return nc


_CACHE = {}


def _prep_inputs(inp):
    f32 = lambda a: np.ascontiguousarray(np.asarray(a, dtype=np.float32))
    consts = host_consts()
    pv = np.zeros((96, 128), np.float32)
    nm, nf, fn = f32(inp["norm_mix"]), f32(inp["norm_ffn"]), f32(inp["final_norm"])
    pv[0:16] = nm[0].reshape(16, 128)
    pv[16:32] = nm[1].reshape(16, 128)
    pv[32:48] = nf[0].reshape(16, 128)
    pv[48:64] = nf[1].reshape(16, 128)
    pv[64:80] = fn.reshape(16, 128)
    lbl = f32(inp["hgrn_lb_logits"])
    pv[80:84] = lbl[0].reshape(4, 128)
    pv[84:88] = lbl[1].reshape(4, 128)
    gb = f32(inp["gla_b_lr"])
    pv[88:92] = gb[0].reshape(4, 128)
    pv[92:96] = gb[1].reshape(4, 128)
    cw, cb = f32(inp["ffn_conv_w"]), f32(inp["ffn_conv_b"])
    convp = np.concatenate([cw.reshape(DEPTH, 3, 88, 128), cb.reshape(DEPTH, 1, 88, 128)], axis=1)
    shared = {
        "w_in": f32(inp["w_in"]), "gla_w_lr": f32(inp["gla_w_lr"]), "pvecs": pv,
        "head_norm": f32(inp["head_norm"]).reshape(DEPTH, 3 * HV), "w_branch": f32(inp["w_branch"]),
        "w_out": f32(inp["w_out"]),
        "ffn_w_up": f32(inp["ffn_w_up"]), "convp": np.ascontiguousarray(convp), "ffn_w_down": f32(inp["ffn_w_down"]),
    }
    for n, s, dt in CONST_SPECS:
        if n not in ("cos_p", "sin_p", "isb"):
            shared["k_" + n] = np.ascontiguousarray(consts[n])
    x_prompt, x_sample = f32(inp["x_prompt"]), f32(inp["x_sample"])
    c_prompt, c_sample = f32(inp["c_prompt"]), f32(inp["c_sample"])
    sts = [f32(inp["state_ret"]), f32(inp["state_gla"]), f32(inp["state_hgrn"])]
    sc = f32(inp["state_conv"])
    w_ada_f, b_ada_f = f32(inp["w_ada"]), f32(inp["b_ada"])
    maps = []
    for c in range(N_CORES):
        b = c // 2
        hf = c % 2
        s0 = c * NSEQ
        m = dict(shared)
        m["xp"] = np.ascontiguousarray(x_prompt[b, hf * HALF:(hf + 1) * HALF])
        m["k_cos_p"] = np.ascontiguousarray(consts["cos_p"][:, hf * HALF:(hf + 1) * HALF])
        m["k_sin_p"] = np.ascontiguousarray(consts["sin_p"][:, hf * HALF:(hf + 1) * HALF])
        m["k_isb"] = np.full((128, 1), float(hf), np.float32)
        m["xs"] = np.ascontiguousarray(x_sample[s0:s0 + NSEQ].reshape(TSAMP, D))
        sa = (c - hf) * NSEQ
        m["cvec"] = np.ascontiguousarray(np.concatenate([c_sample[sa:sa + 2 * NSEQ], c_prompt[b:b + 1]], axis=0))
        m["w_ada"] = np.ascontiguousarray(w_ada_f[:, :, hf * 3 * D:(hf + 1) * 3 * D])
        m["b_ada"] = np.ascontiguousarray(b_ada_f[:, hf * 3 * D:(hf + 1) * 3 * D].reshape(DEPTH, 48, 128))
        for k in range(3):
            m[f"st_in{k}"] = np.ascontiguousarray(sts[k][:, s0:s0 + NSEQ])
        m["sconv_in"] = np.ascontiguousarray(sc[:, s0:s0 + NSEQ].reshape(DEPTH, NSEQ * 2, 2 * DFF))
        maps.append(m)
    return maps


def kernel(**inputs):
    if "nc" not in _CACHE:
        _CACHE["nc"] = build_program()
    nc = _CACHE["nc"]
    maps = _prep_inputs(inputs)
    res = run_bass_kernel_spmd(nc, maps, core_ids=list(range(N_CORES)))
    R = res.results
    y_prompt = np.stack([np.concatenate([R[2 * b]["yp"], R[2 * b + 1]["yp"]], axis=0) for b in range(4)]).astype(np.float32)
    y_sample = np.concatenate([R[c]["ys"].reshape(NSEQ, TS, D) for c in range(N_CORES)], axis=0).astype(np.float32)
    outs = [y_prompt, y_sample]
    for k in range(3):
        outs.append(np.stack([R[2 * b + 1][f"pst{k}"] for b in range(4)], axis=1).astype(np.float32))
    outs.append(np.stack([R[2 * b + 1]["pconv"] for b in range(4)], axis=1).astype(np.float32))
    for k in range(3):
        outs.append(np.concatenate([R[c][f"sst{k}"] for c in range(N_CORES)], axis=1).astype(np.float32))
    outs.append(np.concatenate([R[c]["sconv_o"].reshape(DEPTH, NSEQ, 2, 2 * DFF) for c in range(N_CORES)], axis=1).astype(np.float32))
    return tuple(outs)
```
